# Optimizing a Trainium2 kernel written in Bass

```python
import math
import jax, jax.numpy as jnp
from jax import lax
import numpy as np

D_MODEL = 1024
BATCH = 8
SEQ = 4096
DEPTH = 2

N_MIXERS = 2
EXPAND = 2
BRANCH = EXPAND * D_MODEL
EPS = 1e-6

DA_HEAD = 64
DA_HEADS = BRANCH // (2 * DA_HEAD)
DA_VDIM = 2 * DA_HEAD
Q_BLOCK = 128

S5_GROUP = 16
S5_GROUPS = BRANCH // S5_GROUP
S5_STATE = 64
S5_CHUNK = 128
DT_MIN = 1e-3
DT_MAX = 1e-1

kernel_name = "hybrid_diffattn_s5_gated"


def rms_norm(x, g):
    xf = x.astype(jnp.float32)
    y = xf * lax.rsqrt(jnp.mean(xf * xf, axis=-1, keepdims=True) + EPS)
    return (y * g.astype(jnp.float32)).astype(x.dtype)


def alibi_slopes(n_heads):
    return jnp.asarray(2.0 ** (-8.0 * np.arange(1, n_heads + 1) / n_heads), dtype=jnp.float32)


def lambda_init(layer_idx):
    return 0.8 - 0.6 * math.exp(-0.3 * layer_idx)


def diff_attention_layer(x, norm_g, w_in, q_norm_g, k_norm_g, lam_q1, lam_k1,
                         lam_q2, lam_k2, head_norm_g, w_out, layer_idx):
    b, s, _ = x.shape
    h = rms_norm(x, norm_g)
    q, k, v, z = jnp.split(h @ w_in, 4, axis=-1)
    q = rms_norm(q.reshape(b, s, DA_HEADS, 2, DA_HEAD), q_norm_g) * (DA_HEAD ** -0.5)
    k = rms_norm(k.reshape(b, s, DA_HEADS, 2, DA_HEAD), k_norm_g)
    v = v.reshape(b, s, DA_HEADS, DA_VDIM)

    f32 = jnp.float32
    lam_0 = lambda_init(layer_idx)
    lam = (jnp.exp(jnp.sum(lam_q1.astype(f32) * lam_k1.astype(f32)))
           - jnp.exp(jnp.sum(lam_q2.astype(f32) * lam_k2.astype(f32))) + lam_0)
    slopes = alibi_slopes(DA_HEADS)

    outs = []
    for blk in range(s // Q_BLOCK):
        q0 = blk * Q_BLOCK
        kv_len = q0 + Q_BLOCK
        qb = q[:, q0:kv_len]
        kb = k[:, :kv_len]
        vb = v[:, :kv_len]
        scores = jnp.einsum('bqhmd,bkhmd->bhmqk', qb, kb,
                            preferred_element_type=f32)
        dist = (jnp.arange(q0, kv_len)[:, None] - jnp.arange(kv_len)[None, :]).astype(f32)
        bias = jnp.where(dist >= 0, -slopes[:, None, None] * dist, -jnp.inf)
        probs = jax.nn.softmax(scores + bias[None, :, None], axis=-1)
        weights = probs[:, :, 0] - lam * probs[:, :, 1]
        outs.append(jnp.einsum('bhqk,bkhe->bqhe', weights.astype(v.dtype), vb))
    o = jnp.concatenate(outs, axis=1)
    o = rms_norm(o, head_norm_g) * (1.0 - lam_0)
    o = o.reshape(b, s, BRANCH) * jax.nn.silu(z)
    return x + o @ w_out


def _ssm_combine(e1, e2):
    a1r, a1i, b1r, b1i = e1
    a2r, a2i, b2r, b2i = e2
    return (a2r * a1r - a2i * a1i,
            a2r * a1i + a2i * a1r,
            a2r * b1r - a2i * b1i + b2r,
            a2r * b1i + a2i * b1r + b2i)


def s5_layer(x, norm_g, w_in, lam_re, lam_im, log_dt, b_re, b_im, c_re, c_im,
             d_skip, w_glu, b_glu, w_out):
    b, s, _ = x.shape
    f32 = jnp.float32
    h = rms_norm(x, norm_g)
    u, z = jnp.split(h @ w_in, 2, axis=-1)

    dt = jnp.exp(log_dt.astype(f32))[:, None]
    lr, li = lam_re.astype(f32), lam_im.astype(f32)
    mag = jnp.exp(lr * dt)
    ab_re, ab_im = mag * jnp.cos(li * dt), mag * jnp.sin(li * dt)
    den = lr * lr + li * li
    nr, ni = ab_re - 1.0, ab_im
    g_re = (nr * lr + ni * li) / den
    g_im = (ni * lr - nr * li) / den
    br, bi = b_re.astype(f32), b_im.astype(f32)
    bb_re = g_re[..., None] * br - g_im[..., None] * bi
    bb_im = g_re[..., None] * bi + g_im[..., None] * br
    cr, ci = c_re.astype(f32), c_im.astype(f32)

    n_chunks = s // S5_CHUNK
    ug = u.astype(f32).reshape(b, n_chunks, S5_CHUNK, S5_GROUPS, S5_GROUP)
    ug = jnp.moveaxis(ug, 1, 0)
    a_shape = (b, S5_CHUNK, S5_GROUPS, S5_STATE)
    a_re = jnp.broadcast_to(ab_re, a_shape)
    a_im = jnp.broadcast_to(ab_im, a_shape)

    def chunk_step(carry, u_c):
        h_re, h_im = carry
        bu_re = jnp.einsum('blgc,gpc->blgp', u_c, bb_re)
        bu_im = jnp.einsum('blgc,gpc->blgp', u_c, bb_im)
        acc_re, acc_im, st_re, st_im = lax.associative_scan(
            _ssm_combine, (a_re, a_im, bu_re, bu_im), axis=1)
        st_re = acc_re * h_re[:, None] - acc_im * h_im[:, None] + st_re
        st_im = acc_re * h_im[:, None] + acc_im * h_re[:, None] + st_im
        y = (jnp.einsum('blgp,gcp->blgc', st_re, cr)
             - jnp.einsum('blgp,gcp->blgc', st_im, ci))
        return (st_re[:, -1], st_im[:, -1]), y

    h0 = jnp.zeros((b, S5_GROUPS, S5_STATE), f32)
    _, ys = lax.scan(chunk_step, (h0, h0), ug)
    y = jnp.moveaxis(ys, 0, 1).reshape(b, s, BRANCH)
    y = y + d_skip.astype(f32) * u.astype(f32)
    y = jax.nn.gelu(y)
    y = y * jax.nn.sigmoid(y @ w_glu.astype(f32) + b_glu.astype(f32))
    y = y.astype(x.dtype) * jax.nn.silu(z)
    return x + y @ w_out


def setup_inputs(seed: int = 0) -> dict:
    key = jax.random.key(seed)
    ks = jax.random.split(key, 32)
    nrm = jax.random.normal
    f32 = jnp.float32
    D, E, G, P, C = D_MODEL, BRANCH, S5_GROUPS, S5_STATE, S5_GROUP
    inp = {}
    inp['x'] = nrm(ks[0], (BATCH, SEQ, D), f32)
    inp['l0_norm_g'] = 1.0 + 0.02 * nrm(ks[1], (D,), f32)
    inp['l0_w_in'] = nrm(ks[2], (D, 4 * E), f32) * D ** -0.5
    inp['l0_q_norm_g'] = 1.0 + 0.02 * nrm(ks[3], (DA_HEAD,), f32)
    inp['l0_k_norm_g'] = 1.0 + 0.02 * nrm(ks[4], (DA_HEAD,), f32)
    inp['l0_lam_q1'] = 0.1 * nrm(ks[5], (DA_HEAD,), f32)
    inp['l0_lam_k1'] = 0.1 * nrm(ks[6], (DA_HEAD,), f32)
    inp['l0_lam_q2'] = 0.1 * nrm(ks[7], (DA_HEAD,), f32)
    inp['l0_lam_k2'] = 0.1 * nrm(ks[8], (DA_HEAD,), f32)
    inp['l0_head_norm_g'] = 1.0 + 0.02 * nrm(ks[9], (DA_VDIM,), f32)
    inp['l0_w_out'] = nrm(ks[10], (E, D), f32) * E ** -0.5
    inp['l1_norm_g'] = 1.0 + 0.02 * nrm(ks[11], (D,), f32)
    inp['l1_w_in'] = nrm(ks[12], (D, 2 * E), f32) * D ** -0.5
    n = jnp.arange(P, dtype=f32)
    inp['l1_lam_re'] = -0.5 + 0.01 * nrm(ks[13], (G, P), f32)
    inp['l1_lam_im'] = jnp.pi * n[None, :] + 0.01 * nrm(ks[14], (G, P), f32)
    inp['l1_log_dt'] = jax.random.uniform(ks[15], (G,), f32, math.log(DT_MIN), math.log(DT_MAX))
    inp['l1_b_re'] = nrm(ks[16], (G, P, C), f32) * (2.0 * C) ** -0.5
    inp['l1_b_im'] = nrm(ks[17], (G, P, C), f32) * (2.0 * C) ** -0.5
    inp['l1_c_re'] = nrm(ks[18], (G, C, P), f32) * (2.0 * P) ** -0.5
    inp['l1_c_im'] = nrm(ks[19], (G, C, P), f32) * (2.0 * P) ** -0.5
    inp['l1_d'] = 1.0 + 0.1 * nrm(ks[20], (E,), f32)
    inp['l1_w_glu'] = nrm(ks[21], (E, E), f32) * E ** -0.5
    inp['l1_b_glu'] = 0.01 * nrm(ks[22], (E,), f32)
    inp['l1_w_out'] = nrm(ks[23], (E, D), f32) * E ** -0.5
    return inp


def reference(x, l0_norm_g, l0_w_in, l0_q_norm_g, l0_k_norm_g, l0_lam_q1, l0_lam_k1,
              l0_lam_q2, l0_lam_k2, l0_head_norm_g, l0_w_out,
              l1_norm_g, l1_w_in, l1_lam_re, l1_lam_im, l1_log_dt, l1_b_re, l1_b_im,
              l1_c_re, l1_c_im, l1_d, l1_w_glu, l1_b_glu, l1_w_out):
    mixers = [
        lambda h: diff_attention_layer(h, l0_norm_g, l0_w_in, l0_q_norm_g, l0_k_norm_g,
                                       l0_lam_q1, l0_lam_k1, l0_lam_q2, l0_lam_k2,
                                       l0_head_norm_g, l0_w_out, 0),
        lambda h: s5_layer(h, l1_norm_g, l1_w_in, l1_lam_re, l1_lam_im, l1_log_dt,
                           l1_b_re, l1_b_im, l1_c_re, l1_c_im, l1_d, l1_w_glu,
                           l1_b_glu, l1_w_out),
    ]
    for i in range(DEPTH):
        x = mixers[i % N_MIXERS](x)
    return x
```

```python
import math
from contextlib import ExitStack

import numpy as np
import ml_dtypes

import concourse.bass as bass
import concourse.mybir as mybir
from concourse.bass_utils import run_bass_kernel_spmd

F32 = mybir.dt.float32
BF16 = mybir.dt.bfloat16
AF = mybir.ActivationFunctionType
ALU = mybir.AluOpType
AX = mybir.AxisListType

S = 4096
D = 1024
E = 2048
NH = 16
EPS = 1e-6
LAM0 = 0.8 - 0.6 * math.exp(-0.3 * 0)
NEG = -30000.0
EPOCH = 12000


class Prog:
    def __init__(self, nc, stack, n_dma_sems=48):
        self.nc = nc
        self.stack = stack
        self.eng = {"pe": nc.tensor, "act": nc.scalar, "dve": nc.vector,
                    "pool": nc.gpsimd, "sp": nc.sync}
        self._nsem = 0
        self.cur = {}
        for e in ("pe", "act", "dve", "pool"):
            self.cur[e] = [self._newsem(e), 0]
        n_sw = max(8, n_dma_sems // 3)
        self.dma_pool = {"sp": [[self._newsem("dmah"), 0] for _ in range(n_dma_sems - n_sw)],
                         "pool": [[self._newsem("dmas"), 0] for _ in range(n_sw)]}
        self.dma_rr = {"sp": 0, "pool": 0}
        self.dma_sems = self.dma_pool["sp"] + self.dma_pool["pool"]
        self.waited = {}
        self.lastw = {}
        self.readers = {}
        self.ninst = 0

    def _newsem(self, name):
        self._nsem += 1
        return self.stack.enter_context(self.nc.semaphore("s_%s_%d" % (name, self._nsem)))

    def _wait(self, e, h):
        sem, val, src = h
        if src == "pe" and e == "pe":
            return
        k = (e, id(sem))
        if self.waited.get(k, 0) >= val:
            return
        self.waited[k] = val
        self.eng[e].wait_ge(sem, val)

    def deps(self, e, reads, writes):
        for r in reads:
            h = self.lastw.get(r)
            if h is not None:
                self._wait(e, h)
        for w in writes:
            h = self.lastw.get(w)
            if h is not None:
                self._wait(e, h)
            for h in self.readers.get(w, ()):
                self._wait(e, h)

    def _commit(self, h, reads, writes):
        for r in reads:
            lst = self.readers.setdefault(r, [])
            lst[:] = [x for x in lst if x[0] is not h[0]]
            lst.append(h)
        for w in writes:
            self.lastw[w] = h
            self.readers[w] = []

    def op(self, e, fn, reads=(), writes=()):
        self.deps(e, reads, writes)
        c = self.cur[e]
        if c[1] >= EPOCH:
            c = self.cur[e] = [self._newsem(e), 0]
        ins = fn()
        c[1] += 1
        ins.then_inc(c[0], 1)
        self.ninst += 1
        h = (c[0], c[1], e)
        self._commit(h, reads, writes)
        return h

    def dma(self, q, out, in_, reads=(), writes=(), **kw):
        self.deps(q, reads, writes)
        pool_ = self.dma_pool[q]
        slot = pool_[self.dma_rr[q]]
        self.dma_rr[q] = (self.dma_rr[q] + 1) % len(pool_)
        if slot[1] > 0:
            self._wait(q, (slot[0], slot[1], "dma"))
        ins = self.eng[q].dma_start(out=out, in_=in_, **kw)
        slot[1] += 16
        ins.then_inc(slot[0], 16)
        h = (slot[0], slot[1], "dma")
        self._commit(h, reads, writes)
        return h

    def barrier(self):
        hs = [(slot[0], slot[1], "dma") for slot in self.dma_sems if slot[1] > 0]
        hs += [(c[0], c[1], "bar") for c in self.cur.values() if c[1] > 0]
        for e in ("sp", "pe", "act", "dve", "pool"):
            for h in hs:
                self._wait(e, h)

    def finish(self):
        for slot in self.dma_sems:
            if slot[1] > 0:
                self._wait("sp", (slot[0], slot[1], "dma"))
        for e, c in self.cur.items():
            if c[1] > 0:
                self._wait("sp", (c[0], c[1], e))


def _split3(x):
    x = np.asarray(x, np.float32)
    hi = x.astype(ml_dtypes.bfloat16)
    r1 = x - hi.astype(np.float32)
    mid = r1.astype(ml_dtypes.bfloat16)
    r2 = r1 - mid.astype(np.float32)
    lo = r2.astype(ml_dtypes.bfloat16)
    return hi, mid, lo


def host_consts():
    bf = ml_dtypes.bfloat16
    c = {}
    c["c_ident"] = np.eye(128, dtype=np.float32).astype(bf)
    blk = np.zeros((128, 128), np.float32)
    blk[:64, :64] = 1.0 / 64
    blk[64:, 64:] = 1.0 / 64
    c["c_blk"] = blk.astype(bf)
    mask = np.zeros((128, 4, 512), np.float32)
    ki = np.arange(128)[:, None]
    qi = np.arange(512)[None, :]
    for j in range(4):
        mask[:, j, :] = np.where(128 * j + ki <= qi, 0.0, NEG)
    c["c_mask"] = mask.astype(bf)
    pos = np.arange(S, dtype=np.float64)
    qb = np.zeros((NH, 6, S), bf)
    kb = np.zeros((NH, 6, S), bf)
    for h in range(NH):
        slope = 2.0 ** (-8.0 * (h + 1) / NH)
        a, b_, c_ = _split3((-slope * pos).astype(np.float32))
        qb[h, 0], qb[h, 1], qb[h, 2] = a, b_, c_
        qb[h, 3:6] = 1.0
        a, b_, c_ = _split3((slope * pos).astype(np.float32))
        kb[h, 0:3] = 1.0
        kb[h, 3], kb[h, 4], kb[h, 5] = a, b_, c_
    c["c_qb"] = qb
    c["c_kb"] = kb
    return c


CONST_SHAPES = {
    "c_ident": ([128, 128], BF16), "c_blk": ([128, 128], BF16),
    "c_mask": ([128, 4, 512], BF16), "c_qb": ([NH, 6, S], BF16), "c_kb": ([NH, 6, S], BF16),
}

L0_PARAMS = {
    "l0_norm_g": [D], "l0_w_in": [D, 4 * E], "l0_q_norm_g": [64], "l0_k_norm_g": [64],
    "l0_lam_q1": [64], "l0_lam_k1": [64], "l0_lam_q2": [64], "l0_lam_k2": [64],
    "l0_head_norm_g": [128], "l0_w_out": [E, D],
}


def vec2(ap):
    return ap.rearrange("(n o) -> n o", o=1)


def alloc_norm_bufs(nc, st, tagp):
    T = lambda name, shape, dt: st.enter_context(nc.sbuf_tensor(name, shape, dt))
    return {"xt": [T(tagp + "xt%d" % i, [128, D], F32) for i in range(2)],
            "xn": [T(tagp + "xn%d" % i, [128, D], BF16) for i in range(2)],
            "junk": T(tagp + "junk", [128, D], BF16), "stat": T(tagp + "stat", [128, 8], F32)}


def emit_norm_transpose(nc, P, st, x_src, hT, gbc, ident, banksb, tagp, NT=32, x_keep=None):
    if x_keep is None:
        x_keep = alloc_norm_bufs(nc, st, tagp)
    xt, xn, junk, stat = x_keep["xt"], x_keep["xn"], x_keep["junk"], x_keep["stat"]
    for t in range(NT):
        b = t % 2
        P.dma("sp", xt[b][:], x_src[t * 128:(t + 1) * 128, :], writes=[(tagp, "xt", b)])
        P.op("act", lambda: nc.scalar.activation(out=junk[:], in_=xt[b][:], func=AF.Square,
                                                 accum_out=stat[:, b:b + 1]),
             reads=[(tagp, "xt", b)], writes=[(tagp, "junk"), (tagp, "ss", b)])
        P.op("act", lambda: nc.scalar.activation(out=stat[:, 2 + b:3 + b], in_=stat[:, b:b + 1], func=AF.Sqrt,
                                                 scale=1.0 / D, bias=EPS),
             reads=[(tagp, "ss", b)], writes=[(tagp, "sd", b)])
        P.op("dve", lambda: nc.vector.reciprocal(stat[:, 4 + b:5 + b], stat[:, 2 + b:3 + b]),
             reads=[(tagp, "sd", b)], writes=[(tagp, "rs", b)])
        P.op("dve", lambda: nc.vector.scalar_tensor_tensor(xn[b][:], xt[b][:], stat[:, 4 + b:5 + b], gbc[:],
                                                           ALU.mult, ALU.mult),
             reads=[(tagp, "xt", b), (tagp, "rs", b), "gbc"], writes=[(tagp, "xn", b)])
        pb = banksb[t % 2]
        for j in range(8):
            P.op("pe", lambda: nc.tensor.transpose(pb[:, j * 128:(j + 1) * 128], xn[b][:, j * 128:(j + 1) * 128],
                                                   ident[:]),
                 reads=[(tagp, "xn", b), "ident"], writes=[("pbb", t % 2)])
        P.op("dve", lambda: nc.vector.tensor_copy(hT[:, :, t * 128:(t + 1) * 128],
                                                  pb[:, :].rearrange("p (j c) -> p j c", j=8)),
             reads=[("pbb", t % 2)], writes=[("hT", t // 4)])


def emit_layer0(nc, P, x, prm, cst, x1_out, gscr, heads=None, NQ=8, dump=None):
    heads = list(range(NH)) if heads is None else list(heads)
    with ExitStack() as st:
        T = lambda name, shape, dt: st.enter_context(nc.sbuf_tensor(name, shape, dt))
        banks = [st.enter_context(nc.psum_tensor("bank%d" % i, [128, 512], F32)) for i in range(6)]
        banksb = [st.enter_context(nc.psum_tensor("bankb%d" % i, [128, 1024], BF16)) for i in range(2)]
        ident = T("ident", [128, 128], BF16)
        blk = T("blk", [128, 128], BF16)
        mask = T("mask", [128, 4, 512], BF16)
        gbc = T("gbc", [128, D], F32)
        small = T("small", [128, 16], F32)
        lamv = T("lamv", [128, 4, 64], F32)
        lamj = T("lamj", [128, 64], F32)
        ghn = T("ghn", [128, 128], F32)
        P.dma("sp", ident[:], cst["c_ident"][:, :], writes=["ident"])
        P.dma("sp", blk[:], cst["c_blk"][:, :], writes=["blk"])
        P.dma("sp", mask[:], cst["c_mask"][:, :, :], writes=["mask"])
        P.dma("sp", gbc[:], prm["l0_norm_g"].partition_broadcast(128), writes=["gbc"])
        P.dma("sp", ghn[:], prm["l0_head_norm_g"].partition_broadcast(128), writes=["ghn"])
        for m in range(2):
            P.dma("sp", small[m * 64:(m + 1) * 64, 0:1], vec2(prm["l0_q_norm_g"]), writes=["small"])
            P.dma("sp", small[m * 64:(m + 1) * 64, 1:2], vec2(prm["l0_k_norm_g"]), writes=["small"])
        for i, nm in enumerate(["l0_lam_q1", "l0_lam_k1", "l0_lam_q2", "l0_lam_k2"]):
            P.dma("sp", lamv[:, i, :], prm[nm].partition_broadcast(128), writes=["lamv"])
        P.op("dve", lambda: nc.vector.tensor_scalar(small[:, 0:1], small[:, 0:1], 0.125, None, ALU.mult),
             reads=["small"], writes=["small"])
        for i in range(2):
            P.op("dve", lambda: nc.vector.tensor_tensor(lamj[:], lamv[:, 2 * i, :], lamv[:, 2 * i + 1, :], ALU.mult),
                 reads=["lamv"], writes=["lamj"])
            P.op("dve", lambda: nc.vector.reduce_sum(small[:, 4 + i:5 + i], lamj[:], axis=AX.X),
                 reads=["lamj"], writes=["small"])
        P.op("act", lambda: nc.scalar.activation(out=small[:, 6:8], in_=small[:, 4:6], func=AF.Exp),
             reads=["small"], writes=["small"])
        P.op("dve", lambda: nc.vector.scalar_tensor_tensor(small[:, 2:3], small[:, 7:8], -LAM0, small[:, 6:7],
                                                           ALU.add, ALU.subtract),
             reads=["small"], writes=["small"])
        P.op("dve", lambda: nc.vector.tensor_scalar(ghn[:], ghn[:], 1.0 - LAM0, None, ALU.mult),
             reads=["ghn"], writes=["ghn"])

        hT = T("hT", [128, 8, S], BF16)
        with ExitStack() as stA:
            emit_norm_transpose(nc, P, stA, x, hT, gbc, ident, banksb, "A", NT=4 * NQ)
            P.barrier()

        with ExitStack() as stH:
            TH = lambda name, shape, dt: stH.enter_context(nc.sbuf_tensor(name, shape, dt))
            QA = [TH("QA%d" % m, [128, S], BF16) for m in range(2)]
            KA = [TH("KA%d" % m, [128, S], BF16) for m in range(2)]
            VA = TH("VA", [128, 32, 130], BF16)
            ZT = TH("ZT", [128, S], BF16)
            GT = [TH("GT%d" % i, [128, S], BF16) for i in range(2)]
            Wh = [TH("Wh%d" % i, [128, 8, 4, 128], BF16) for i in range(2)]
            sq = [TH("sq%d" % i, [128, 512], BF16) for i in range(2)]
            rst = [TH("rst%d" % i, [128, 512], F32) for i in range(2)]
            pbuf = [TH("pbuf%d" % i, [128, 512], BF16) for i in range(4)]
            fin = TH("fin", [128, 4, 8], F32)
            o0 = TH("o0", [128, 4, 128], F32)
            o1 = TH("o1", [128, 4, 128], F32)
            onb = TH("onb", [128, 2, 4, 128], BF16)
            junk2 = TH("junk2", [128, 128], BF16)
            pending = []

            def flush_pending():
                while pending:
                    pending.pop(0)()
            P.op("pool", lambda: nc.gpsimd.memset(VA[:, :, 128:130], 1.0), writes=["VAones"])
            epsb = TH("epsb", [128, 1], F32)
            P.op("pool", lambda: nc.gpsimd.memset(epsb[:], EPS), writes=["epsb"])
            accs = [[TH("accs%d_%d" % (i, k), [128, 390], F32) for k in range(3)] for i in range(2)]

            w_in = prm["l0_w_in"].rearrange("(j p) f -> p j f", p=128)

            def load_head_weights(h):
                wb = Wh[h % 2]
                for sec in range(4):
                    P.dma("pool", wb[:, :, sec, :], w_in[:, :, sec * E + h * 128: sec * E + (h + 1) * 128],
                          writes=[("Wh", h % 2, sec)])

            load_head_weights(heads[0])
            for hi_, h in enumerate(heads):
                wb = Wh[h % 2]
                if hi_ + 1 < len(heads):
                    load_head_weights(heads[hi_ + 1])
                for m in range(2):
                    P.dma("sp", QA[m][64:70, :], cst["c_qb"][h, :, :], writes=[("QAb", m)])
                    P.dma("sp", KA[m][64:70, :], cst["c_kb"][h, :, :], writes=[("KAb", m)])
                for nt in range(NQ):
                    tok = slice(nt * 512, (nt + 1) * 512)
                    hkeys = [("hT", nt)]
                    for sec, bk in ((0, 0), (1, 1), (3, 2)):
                        for j in range(8):
                            P.op("pe", lambda: nc.tensor.matmul(banks[bk][:], wb[:, j, sec, :], hT[:, j, tok],
                                                                start=(j == 0), stop=(j == 7)),
                                 reads=hkeys + [("Wh", h % 2, sec)], writes=[("bk", bk)])
                    for qi, (bk, tiles, gcol, key) in enumerate(((0, QA, 0, "QA"), (1, KA, 1, "KA"))):
                        P.op("act", lambda: nc.scalar.activation(out=sq[qi][:], in_=banks[bk][:], func=AF.Square),
                             reads=[("bk", bk)], writes=[("sq", qi)])
                        P.op("pe", lambda: nc.tensor.matmul(banks[3 + qi][:], blk[:], sq[qi][:], start=True, stop=True),
                             reads=["blk", ("sq", qi)], writes=[("bk", 3 + qi)])
                        P.op("act", lambda: nc.scalar.activation(out=rst[qi][:], in_=banks[3 + qi][:], func=AF.Ln,
                                                                 scale=1.0, bias=epsb[:, 0:1]),
                             reads=[("bk", 3 + qi), "epsb"], writes=[("rst", qi)])
                        P.op("act", lambda: nc.scalar.activation(out=rst[qi][:], in_=rst[qi][:], func=AF.Exp, scale=-0.5),
                             reads=[("rst", qi)], writes=[("rst", qi)])
                        for m in range(2):
                            ps = slice(m * 64, (m + 1) * 64)
                            P.op("dve", lambda: nc.vector.scalar_tensor_tensor(
                                tiles[m][0:64, tok], banks[bk][ps, :], small[ps, gcol:gcol + 1], rst[qi][ps, :],
                                ALU.mult, ALU.mult),
                                reads=[("bk", bk), ("rst", qi), "small"], writes=[(key, m, nt)])
                    P.op("act", lambda: nc.scalar.activation(out=ZT[:, tok], in_=banks[2][:], func=AF.Silu),
                         reads=[("bk", 2)], writes=[("ZT", nt)])
                    for i in range(4):
                        kt = nt * 4 + i
                        for j in range(8):
                            P.op("pe", lambda: nc.tensor.matmul(banks[5][:, i * 128:(i + 1) * 128],
                                                                hT[:, j, kt * 128:(kt + 1) * 128], wb[:, j, 2, :],
                                                                start=(i == 0 and j == 0), stop=(j == 7),
                                                                skip_group_check=True),
                                 reads=hkeys + [("Wh", h % 2, 2)], writes=[("bk", 5)])
                    P.op("act", lambda: nc.scalar.activation(
                        out=VA[:, nt * 4:(nt + 1) * 4, 0:128],
                        in_=banks[5][:, :].rearrange("p (i c) -> p i c", i=4), func=AF.Copy),
                        reads=[("bk", 5)], writes=[("VA", nt)])

                gt = GT[h % 2]
                ti = 0
                for Q in range(NQ):
                    qs = slice(Q * 512, (Q + 1) * 512)
                    tiles = [(m, kt) for m in range(2) for kt in range(4 * Q + 4)]
                    accreg = {}
                    slots = [(4, 0), (4, 1), (4, 2), (5, 0), (5, 1), (5, 2), (3, 0), (3, 1)]
                    for idx, (qb, m) in enumerate([(qb, m) for m in range(2) for qb in range(4)]):
                        accreg[(qb, m)] = slots[idx]
                    started = set()

                    def emit_S(i):
                        m, kt = tiles[i]
                        sb = (ti + i) % 3
                        diag = kt >= 4 * Q
                        P.op("pe", lambda: nc.tensor.matmul(banks[sb][:], KA[m][0:70, kt * 128:(kt + 1) * 128],
                                                            QA[m][0:70, qs], start=True, stop=not diag),
                             reads=[("KA", m, kt // 4), ("KAb", m), ("QA", m, Q), ("QAb", m)], writes=[("bk", sb)])
                        if diag:
                            P.op("pe", lambda: nc.tensor.matmul(banks[sb][:], ident[:], mask[:, kt - 4 * Q, :],
                                                                start=False, stop=True),
                                 reads=["ident", "mask"], writes=[("bk", sb)])

                    def emit_EP(i):
                        m, kt = tiles[i]
                        sb = (ti + i) % 3
                        pb = (ti + i) % 4
                        P.op("act", lambda: nc.scalar.activation(out=pbuf[pb][:], in_=banks[sb][:], func=AF.Exp),
                             reads=[("bk", sb)], writes=[("pbuf", pb)])
                        j = kt - 4 * Q
                        for qb in range(4):
                            if j >= 0 and qb < j:
                                continue
                            bk, r = accreg[(qb, m)]
                            first = bk not in started
                            started.add(bk)
                            P.op("pe", lambda: nc.tensor.matmul(banks[bk][:, r * 130:r * 130 + 129],
                                                                pbuf[pb][:, qb * 128:(qb + 1) * 128], VA[:, kt, 0:129],
                                                                start=first, stop=False, skip_group_check=True),
                                 reads=[("pbuf", pb), ("VA", kt // 4), "VAones"], writes=[("bk", bk)])

                    n = len(tiles)
                    emit_S(0)
                    if n > 1:
                        emit_S(1)
                    for i in range(n):
                        if i + 2 < n:
                            emit_S(i + 2)
                        emit_EP(i)
                        if i == min(n - 1, 24):
                            flush_pending()
                    ti += n
                    ab = Q % 2
                    sidx = {4: 0, 5: 1, 3: 2}
                    for bk_, ncol in ((4, 390), (5, 390), (3, 260)):
                        P.op("dve", lambda: nc.vector.tensor_copy(accs[ab][sidx[bk_]][:, 0:ncol], banks[bk_][:, 0:ncol]),
                             reads=[("bk", bk_)], writes=[("accs", ab, sidx[bk_])])
                    for qb in range(4):
                        b0, r0 = accreg[(qb, 0)]
                        b1, r1 = accreg[(qb, 1)]
                        s0, s1 = accs[ab][sidx[b0]], accs[ab][sidx[b1]]
                        k0, k1 = ("accs", ab, sidx[b0]), ("accs", ab, sidx[b1])
                        a0 = s0[:, r0 * 130:r0 * 130 + 128]
                        d0 = s0[:, r0 * 130 + 128:r0 * 130 + 129]
                        a1 = s1[:, r1 * 130:r1 * 130 + 128]
                        d1 = s1[:, r1 * 130 + 128:r1 * 130 + 129]
                        fk = ("fin", qb)
                        P.op("dve", lambda: nc.vector.reciprocal(fin[:, qb, 0:1], d0), reads=[k0], writes=[fk])
                        P.op("dve", lambda: nc.vector.reciprocal(fin[:, qb, 1:2], d1), reads=[k1], writes=[fk])
                        P.op("dve", lambda: nc.vector.tensor_tensor(fin[:, qb, 2:3], fin[:, qb, 1:2], small[:, 2:3], ALU.mult),
                             reads=[fk, "small"], writes=[fk])
                        P.op("dve", lambda: nc.vector.tensor_scalar(o0[:, qb, :], a0, fin[:, qb, 0:1], None, ALU.mult),
                             reads=[k0, fk], writes=[("o0", qb)])
                        P.op("dve", lambda: nc.vector.scalar_tensor_tensor(o1[:, qb, :], a1, fin[:, qb, 2:3], o0[:, qb, :],
                                                                           ALU.mult, ALU.add),
                             reads=[k1, fk, ("o0", qb)], writes=[("o1", qb)])
                    ob = Q % 2
                    for qb in range(4):
                        fk = ("fin", qb)
                        P.op("act", lambda: nc.scalar.activation(out=junk2[:], in_=o1[:, qb, :], func=AF.Square,
                                                                 accum_out=fin[:, qb, 3:4]),
                             reads=[("o1", qb)], writes=["junk2", fk])
                        P.op("act", lambda: nc.scalar.activation(out=fin[:, qb, 4:5], in_=fin[:, qb, 3:4], func=AF.Sqrt,
                                                                 scale=1.0 / 128, bias=EPS),
                             reads=[fk], writes=[fk])
                        P.op("dve", lambda: nc.vector.reciprocal(fin[:, qb, 5:6], fin[:, qb, 4:5]), reads=[fk], writes=[fk])
                        P.op("dve", lambda: nc.vector.scalar_tensor_tensor(onb[:, ob, qb, :], o1[:, qb, :], fin[:, qb, 5:6], ghn[:],
                                                                           ALU.mult, ALU.mult),
                             reads=[("o1", qb), fk, "ghn"], writes=[("onb", ob, qb)])

                    def tail(Q=Q, ob=ob, qs=qs, gt=gt, h=h):
                        tb = banksb[Q % 2]
                        for qb in range(4):
                            P.op("pe", lambda: nc.tensor.transpose(tb[:, qb * 128:(qb + 1) * 128], onb[:, ob, qb, :], ident[:]),
                                 reads=[("onb", ob, qb), "ident"], writes=[("pbb", Q % 2)])
                        P.op("dve", lambda: nc.vector.tensor_tensor(gt[:, qs], tb[:, 0:512], ZT[:, qs], ALU.mult),
                             reads=[("pbb", Q % 2), ("ZT", Q)], writes=[("GT", h % 2)])
                    pending.append(tail)
                flush_pending()
                P.dma("sp", gscr[:, h, 0:NQ * 512], gt[:, 0:NQ * 512], reads=[("GT", h % 2)], writes=[("gscr", h)])
                if dump is not None and h == heads[-1]:
                    n = NQ * 512
                    allk = list(P.lastw.keys())
                    P.dma("sp", dump["d_hT"][:, :, :], hT[:, :, 0:n], reads=allk)
                    for m in range(2):
                        P.dma("sp", dump["d_QA"][m, :, :], QA[m][0:70, 0:n], reads=allk)
                        P.dma("sp", dump["d_KA"][m, :, :], KA[m][0:70, 0:n], reads=allk)
                    P.dma("sp", dump["d_VA"][:, :, :], VA[:, 0:4 * NQ, :], reads=allk)
                    P.dma("sp", dump["d_ZT"][:, :], ZT[:, 0:n], reads=allk)
                    P.dma("sp", dump["d_GT"][:, :], gt[:, 0:n], reads=allk)

        if dump is not None:
            return
        P.barrier()
        with ExitStack() as stD:
            TD = lambda name, shape, dt: stD.enter_context(nc.sbuf_tensor(name, shape, dt))
            Wo = TD("Wo", [128, NH, D], BF16)
            Gt = [TD("Gt%d" % i, [128, NH, 512], BF16) for i in range(2)]
            xr = [TD("xr%d" % i, [128, D], F32) for i in range(2)]
            x1t = [TD("x1t%d" % i, [128, D], F32) for i in range(2)]
            w_out = prm["l0_w_out"].rearrange("(h p) f -> p h f", p=128)
            for hh in range(0, NH, 4):
                P.dma("pool", Wo[:, hh:hh + 4, :], w_out[:, hh:hh + 4, :], writes=[("Wo", hh // 4)])
            for Tt in range(NQ):
                g = Gt[Tt % 2]
                P.dma("sp", g[:], gscr[:, :, Tt * 512:(Tt + 1) * 512], reads=[("gscr", h) for h in heads],
                      writes=[("Gt", Tt % 2)])
                for tt in range(4):
                    t = Tt * 4 + tt
                    b = t % 2
                    P.dma("sp", xr[b][:], x[t * 128:(t + 1) * 128, :], writes=[("xr", b)])
                    for half in range(2):
                        bk = (t * 2 + half) % 4
                        for h in range(NH):
                            P.op("pe", lambda: nc.tensor.matmul(banks[bk][:], g[:, h, tt * 128:(tt + 1) * 128],
                                                                Wo[:, h, half * 512:(half + 1) * 512],
                                                                start=(h == 0), stop=(h == NH - 1)),
                                 reads=[("Gt", Tt % 2), ("Wo", h // 4)], writes=[("bk", bk)])
                        P.op("dve", lambda: nc.vector.tensor_tensor(x1t[b][:, half * 512:(half + 1) * 512], banks[bk][:],
                                                                    xr[b][:, half * 512:(half + 1) * 512], ALU.add),
                             reads=[("bk", bk), ("xr", b)], writes=[("x1t", b)])
                    P.dma("sp", x1_out[t * 128:(t + 1) * 128, :], x1t[b][:], reads=[("x1t", b)], writes=[("x1", t)])


def build_l0(NQ=8):
    nc = bass.Bass("TRN2", target_bir_lowering=False)
    x = nc.dram_tensor("x", [S, D], F32, kind="ExternalInput").ap()
    prm = {k: nc.dram_tensor(k, shp, F32, kind="ExternalInput").ap() for k, shp in L0_PARAMS.items()}
    cst = {k: nc.dram_tensor(k, shp, dt, kind="ExternalInput").ap() for k, (shp, dt) in CONST_SHAPES.items()}
    out = nc.dram_tensor("out", [S, D], F32, kind="ExternalOutput").ap()
    gscr = nc.dram_tensor("gscr", [128, NH, S], BF16, kind="Internal").ap()
    with ExitStack() as st:
        P = Prog(nc, st)
        emit_layer0(nc, P, x, prm, cst, out, gscr, NQ=NQ)
        P.finish()
        build_l0.stats = {"ninst": P.ninst, "nsem": P._nsem, "dma_uses": sum(sl[1] // 16 for sl in P.dma_sems),
                          "cur": {e: c[1] for e, c in P.cur.items()}}
    return nc


def build_l0_debug(heads=(0,), NQ=2):
    nc = bass.Bass("TRN2", target_bir_lowering=False)
    x = nc.dram_tensor("x", [S, D], F32, kind="ExternalInput").ap()
    prm = {k: nc.dram_tensor(k, shp, F32, kind="ExternalInput").ap() for k, shp in L0_PARAMS.items()}
    cst = {k: nc.dram_tensor(k, shp, dt, kind="ExternalInput").ap() for k, (shp, dt) in CONST_SHAPES.items()}
    n = NQ * 512
    dshapes = {"d_hT": [128, 8, n], "d_QA": [2, 70, n], "d_KA": [2, 70, n], "d_VA": [128, 4 * NQ, 130],
               "d_ZT": [128, n], "d_GT": [128, n]}
    dump = {k: nc.dram_tensor(k, shp, BF16, kind="ExternalOutput").ap() for k, shp in dshapes.items()}
    gscr = nc.dram_tensor("gscr", [128, NH, S], BF16, kind="Internal").ap()
    with ExitStack() as st:
        P = Prog(nc, st)
        emit_layer0(nc, P, x, prm, cst, None, gscr, heads=heads, NQ=NQ, dump=dump)
        P.finish()
    return nc


def run_layer0_spmd(inputs, n_cores=8):
    nc = build_l0()
    cst = host_consts()
    in_maps = []
    for b in range(n_cores):
        m = {"x": np.ascontiguousarray(inputs["x"][b])}
        for k in L0_PARAMS:
            m[k] = np.ascontiguousarray(inputs[k])
        m.update(cst)
        in_maps.append(m)
    res = run_bass_kernel_spmd(nc, in_maps, core_ids=list(range(n_cores)))
    return np.stack([np.asarray(r["out"]) for r in res.results], axis=0)


TWO_PI = 6.283185307179586
MAGIC = 12582912.0
L1_PARAMS = {
    "l1_norm_g": [D], "l1_w_in": [D, 2 * E], "l1_lam_re": [128, 64], "l1_lam_im": [128, 64], "l1_log_dt": [128],
    "l1_b_re": [128, 64, 16], "l1_b_im": [128, 64, 16], "l1_c_re": [128, 16, 64], "l1_c_im": [128, 16, 64],
    "l1_d": [E], "l1_w_glu": [E, E], "l1_b_glu": [E], "l1_w_out": [E, D],
}


def emit_s5_tables(nc, P, st, prm, NPOW=17):
    T = lambda name, shape, dt: st.enter_context(nc.sbuf_tensor(name, shape, dt))
    tb = T("s5tb", [128, 24, 64], F32)
    pw = T("s5pw", [128, NPOW, 2, 64], F32)
    LR, LI, LDT, DT, MAG, TH, K_, SN, CS, ABR, ABI, DEN, NR, GR, GI, T1, T2 = range(17)
    V = lambda i: tb[:, i, :]
    for g2 in range(2):
        ps = slice(g2 * 64, (g2 + 1) * 64)
        for i, nm in ((LR, "l1_lam_re"), (LI, "l1_lam_im")):
            src = prm[nm].rearrange("(pr g2) p -> g2 p pr", g2=2)
            P.dma("sp", tb[ps, i, :], src[g2], writes=[("tb", i, g2)], allow_slow_non_contiguous=True)
        src = prm["l1_log_dt"].rearrange("(pr g2) -> g2 pr", g2=2)
        P.dma("sp", tb[ps, LDT, :], src[g2:g2 + 1, :].to_broadcast([64, 64]), writes=[("tb", LDT, g2)],
              allow_slow_non_contiguous=True)
    rd = lambda *idx: [("tb", i, g2) for i in idx for g2 in range(2)] + [("tb", i) for i in idx]
    op = P.op
    op("act", lambda: nc.scalar.activation(out=V(DT), in_=V(LDT), func=AF.Exp), reads=rd(LDT), writes=[("tb", DT)])
    op("dve", lambda: nc.vector.tensor_tensor(V(T1), V(LR), V(DT), ALU.mult), reads=rd(LR, DT), writes=[("tb", T1)])
    op("act", lambda: nc.scalar.activation(out=V(MAG), in_=V(T1), func=AF.Exp), reads=rd(T1), writes=[("tb", MAG)])
    op("dve", lambda: nc.vector.tensor_tensor(V(TH), V(LI), V(DT), ALU.mult), reads=rd(LI, DT), writes=[("tb", TH)])
    for dst, shift in ((SN, 0.0), (CS, math.pi / 2)):
        op("dve", lambda: nc.vector.tensor_scalar(V(T2), V(TH), shift, 1.0 / TWO_PI, ALU.add, ALU.mult),
           reads=rd(TH), writes=[("tb", T2)])
        op("dve", lambda: nc.vector.tensor_scalar(V(K_), V(T2), MAGIC, None, ALU.add), reads=rd(T2), writes=[("tb", K_)])
        op("dve", lambda: nc.vector.tensor_scalar(V(K_), V(K_), MAGIC, None, ALU.subtract), reads=rd(K_), writes=[("tb", K_)])
        op("dve", lambda: nc.vector.scalar_tensor_tensor(V(T2), V(K_), -TWO_PI, V(TH), ALU.mult, ALU.add),
           reads=rd(K_, TH), writes=[("tb", T2)])
        op("act", lambda: nc.scalar.activation(out=V(dst), in_=V(T2), func=AF.Sin, scale=1.0, bias=shift),
           reads=rd(T2), writes=[("tb", dst)])
    op("dve", lambda: nc.vector.tensor_tensor(V(ABR), V(MAG), V(CS), ALU.mult), reads=rd(MAG, CS), writes=[("tb", ABR)])
    op("dve", lambda: nc.vector.tensor_tensor(V(ABI), V(MAG), V(SN), ALU.mult), reads=rd(MAG, SN), writes=[("tb", ABI)])
    op("dve", lambda: nc.vector.tensor_tensor(V(T1), V(LR), V(LR), ALU.mult), reads=rd(LR), writes=[("tb", T1)])
    op("dve", lambda: nc.vector.tensor_tensor(V(T2), V(LI), V(LI), ALU.mult), reads=rd(LI), writes=[("tb", T2)])
    op("dve", lambda: nc.vector.tensor_tensor(V(DEN), V(T1), V(T2), ALU.add), reads=rd(T1, T2), writes=[("tb", DEN)])
    op("dve", lambda: nc.vector.reciprocal(V(DEN), V(DEN)), reads=rd(DEN), writes=[("tb", DEN)])
    op("dve", lambda: nc.vector.tensor_scalar(V(NR), V(ABR), -1.0, None, ALU.add), reads=rd(ABR), writes=[("tb", NR)])
    op("dve", lambda: nc.vector.tensor_tensor(V(T1), V(NR), V(LR), ALU.mult), reads=rd(NR, LR), writes=[("tb", T1)])
    op("dve", lambda: nc.vector.tensor_tensor(V(T2), V(ABI), V(LI), ALU.mult), reads=rd(ABI, LI), writes=[("tb", T2)])
    op("dve", lambda: nc.vector.tensor_tensor(V(GR), V(T1), V(T2), ALU.add), reads=rd(T1, T2), writes=[("tb", GR)])
    op("dve", lambda: nc.vector.tensor_tensor(V(GR), V(GR), V(DEN), ALU.mult), reads=rd(GR, DEN), writes=[("tb", GR)])
    op("dve", lambda: nc.vector.tensor_tensor(V(T1), V(ABI), V(LR), ALU.mult), reads=rd(ABI, LR), writes=[("tb", T1)])
    op("dve", lambda: nc.vector.tensor_tensor(V(T2), V(NR), V(LI), ALU.mult), reads=rd(NR, LI), writes=[("tb", T2)])
    op("dve", lambda: nc.vector.tensor_tensor(V(GI), V(T1), V(T2), ALU.subtract), reads=rd(T1, T2), writes=[("tb", GI)])
    op("dve", lambda: nc.vector.tensor_tensor(V(GI), V(GI), V(DEN), ALU.mult), reads=rd(GI, DEN), writes=[("tb", GI)])
    op("pool", lambda: nc.gpsimd.memset(pw[:, 0, 0, :], 1.0), writes=[("pw", 0)])
    op("pool", lambda: nc.gpsimd.memset(pw[:, 0, 1, :], 0.0), writes=[("pw", 0)])
    for t in range(1, NPOW):
        pr_, pi_ = pw[:, t - 1, 0, :], pw[:, t - 1, 1, :]
        op("dve", lambda: nc.vector.tensor_tensor(V(T1), pr_, V(ABR), ALU.mult), reads=[("pw", t - 1)] + rd(ABR), writes=[("tb", T1)])
        op("dve", lambda: nc.vector.tensor_tensor(V(T2), pi_, V(ABI), ALU.mult), reads=[("pw", t - 1)] + rd(ABI), writes=[("tb", T2)])
        op("dve", lambda: nc.vector.tensor_tensor(pw[:, t, 0, :], V(T1), V(T2), ALU.subtract), reads=rd(T1, T2), writes=[("pw", t)])
        op("dve", lambda: nc.vector.tensor_tensor(V(T1), pr_, V(ABI), ALU.mult), reads=[("pw", t - 1)] + rd(ABI), writes=[("tb", T1)])
        op("dve", lambda: nc.vector.tensor_tensor(V(T2), pi_, V(ABR), ALU.mult), reads=[("pw", t - 1)] + rd(ABR), writes=[("tb", T2)])
        op("dve", lambda: nc.vector.tensor_tensor(pw[:, t, 1, :], V(T1), V(T2), ALU.add), reads=rd(T1, T2), writes=[("pw", t)])
    return {"tb": tb, "pw": pw, "idx": dict(ABR=ABR, ABI=ABI, GR=GR, GI=GI)}


def build_s5_tables_debug():
    nc = bass.Bass("TRN2", target_bir_lowering=False)
    prm = {k: nc.dram_tensor(k, shp, F32, kind="ExternalInput").ap() for k, shp in L1_PARAMS.items()
           if k in ("l1_lam_re", "l1_lam_im", "l1_log_dt")}
    d_tb = nc.dram_tensor("d_tb", [128, 24, 64], F32, kind="ExternalOutput").ap()
    d_pw = nc.dram_tensor("d_pw", [128, 17, 2, 64], F32, kind="ExternalOutput").ap()
    with ExitStack() as st:
        P = Prog(nc, st)
        r = emit_s5_tables(nc, P, st, prm)
        allk = list(P.lastw.keys())
        P.dma("sp", d_tb[:, :, :], r["tb"][:], reads=allk)
        P.dma("sp", d_pw[:, :, :, :], r["pw"][:], reads=allk)
        P.finish()
    return nc


def emit_s5_operands(nc, P, st, prm, tabs, ident, gts, o_K, o_WS, o_WC, banks, banksb, o_BB=None):
    T = lambda name, shape, dt: st.enter_context(nc.sbuf_tensor(name, shape, dt))
    tb, pw, ix = tabs["tb"], tabs["pw"], tabs["idx"]
    Bs = [T("s5B%d" % i, [128, 64, 16], F32) for i in range(2)]
    Cs = [T("s5C%d" % i, [128, 64, 16], F32) for i in range(2)]
    BB = [T("s5BB%d" % i, [128, 64, 16], F32) for i in range(2)]
    t1 = T("s5t1", [128, 64, 16], F32)
    t2 = T("s5t2", [128, 64, 16], F32)
    t3 = T("s5t3", [128, 64, 16], F32)
    t4 = T("s5t4", [128, 64, 16], F32)
    pwn = T("s5pwn", [128, 17, 64], F32)
    dsk = T("s5dsk", [128, 16], F32)
    XB2 = [T("s5XB%d" % i, [128, 16, 2, 4, 32], BF16) for i in range(2)]
    WC2 = [T("s5WC%d" % i, [128, 16, 2, 4, 32], BF16) for i in range(2)]
    CB2 = [T("s5CB%d" % i, [128, 2, 4, 32], BF16) for i in range(2)]
    Kbd2 = [T("s5Kbd%d" % i, [128, 16, 128], BF16) for i in range(2)]
    WS2 = [T("s5WS%d" % i, [128, 16, 2, 128], BF16) for i in range(2)]
    identf = T("s5idf", [128, 128], F32)
    op = P.op
    for g2 in range(2):
        ps = slice(g2 * 64, (g2 + 1) * 64)
        for i, nm in enumerate(("l1_b_re", "l1_b_im")):
            P.dma("sp", Bs[i][ps, :, :], prm[nm].rearrange("(pr g2) p c -> g2 p pr c", g2=2)[g2], writes=[("Bs", i, g2)])
        for i, nm in enumerate(("l1_c_re", "l1_c_im")):
            src = prm[nm].rearrange("(pr g2) co p -> g2 p pr co", g2=2)[g2]
            for pr_ in range(64):
                P.dma("sp", Cs[i][ps, pr_, :], src[:, pr_, :],
                      writes=[("Cs", i, g2, pr_ // 8)], allow_slow_non_contiguous=True)
    P.dma("sp", dsk[:], prm["l1_d"].rearrange("(gt p) -> p gt", p=128), writes=["dsk"], allow_slow_non_contiguous=True)
    op("dve", lambda: nc.vector.tensor_copy(identf[:], ident[:]), reads=["ident"], writes=["identf"])
    for i in range(2):
        for tl, key in ((XB2[i], "XB"), (WC2[i], "WC"), (CB2[i], "CB"), (Kbd2[i], "Kbd")):
            op("pool", lambda: nc.gpsimd.memset(tl[:], 0.0), writes=[key + str(i)])
    rB = [("Bs", i, g2) for i in range(2) for g2 in range(2)]
    rC = [("Cs", i, g2, b) for i in range(2) for g2 in range(2) for b in range(8)]
    bc = lambda ap2, n: ap2.unsqueeze(2).to_broadcast([128, n, 16])
    GR, GI = tb[:, ix["GR"], :], tb[:, ix["GI"], :]
    op("dve", lambda: nc.vector.tensor_tensor(t1[:], Bs[0][:], bc(GR, 64), ALU.mult), reads=rB + [("tb", ix["GR"])], writes=["t1"])
    op("dve", lambda: nc.vector.tensor_tensor(t2[:], Bs[1][:], bc(GI, 64), ALU.mult), reads=rB + [("tb", ix["GI"])], writes=["t2"])
    op("dve", lambda: nc.vector.tensor_tensor(BB[0][:], t1[:], t2[:], ALU.subtract), reads=["t1", "t2"], writes=["BB0"])
    op("dve", lambda: nc.vector.tensor_tensor(t1[:], Bs[1][:], bc(GR, 64), ALU.mult), reads=rB + [("tb", ix["GR"])], writes=["t1"])
    op("dve", lambda: nc.vector.tensor_tensor(t2[:], Bs[0][:], bc(GI, 64), ALU.mult), reads=rB + [("tb", ix["GI"])], writes=["t2"])
    op("dve", lambda: nc.vector.tensor_tensor(BB[1][:], t1[:], t2[:], ALU.add), reads=["t1", "t2"], writes=["BB1"])
    if o_BB is not None:
        for i in range(2):
            P.dma("sp", o_BB[i, :, :, :], BB[i][:], reads=["BB%d" % i])

    def cmul_blocks(dst, k, Ar, Ai, Pr, Pi, prs, neg_im, rA, rP, wkey):
        a, b_ = t1[:, 0:4, :], t1[:, 4:8, :]
        c_, d_ = t2[:, 0:4, :], t2[:, 4:8, :]
        op("dve", lambda: nc.vector.tensor_tensor(a, Ar[:, prs, :], bc(Pr[:, prs], 4), ALU.mult), reads=rA + rP, writes=["t1"])
        op("dve", lambda: nc.vector.tensor_tensor(b_, Ai[:, prs, :], bc(Pi[:, prs], 4), ALU.mult), reads=rA + rP, writes=["t1"])
        op("dve", lambda: nc.vector.tensor_tensor(c_, Ar[:, prs, :], bc(Pi[:, prs], 4), ALU.mult), reads=rA + rP, writes=["t2"])
        op("dve", lambda: nc.vector.tensor_tensor(d_, Ai[:, prs, :], bc(Pr[:, prs], 4), ALU.mult), reads=rA + rP, writes=["t2"])
        for g2 in range(2):
            ps = slice(g2 * 64, (g2 + 1) * 64)
            cs = slice(g2 * 16, (g2 + 1) * 16)
            op("dve", lambda: nc.vector.tensor_tensor(dst[ps, k, 0, :, cs], a[ps], b_[ps], ALU.subtract),
               reads=["t1"], writes=[wkey])
            if neg_im:
                op("dve", lambda: nc.vector.scalar_tensor_tensor(dst[ps, k, 1, :, cs], c_[ps], -1.0, d_[ps], ALU.mult, ALU.subtract),
                   reads=["t2"], writes=[wkey])
            else:
                op("dve", lambda: nc.vector.tensor_tensor(dst[ps, k, 1, :, cs], c_[ps], d_[ps], ALU.add),
                   reads=["t2"], writes=[wkey])

    rPW = [("pw", t) for t in range(17)]
    op("dve", lambda: nc.vector.tensor_scalar(pwn[:], pw[:, :, 1, :], -1.0, None, ALU.mult), reads=rPW, writes=["pwn"])

    def cmul_all_d(dst, Ar, Ai, d0, prs, neg_im, rA, wkey):
        V4 = lambda t: t[:, :, :].rearrange("p (d j) c -> p d j c", d=16)
        a, b_, c_, d_ = V4(t1), V4(t2), V4(t3), V4(t4)
        bA = lambda A: A[:, prs, :].unsqueeze(1).to_broadcast([128, 16, 4, 16])
        Pr = pw[:, d0:d0 + 16, 0, prs].unsqueeze(3).to_broadcast([128, 16, 4, 16])
        Pi = pw[:, d0:d0 + 16, 1, prs].unsqueeze(3).to_broadcast([128, 16, 4, 16])
        Pin = pwn[:, d0:d0 + 16, prs].unsqueeze(3).to_broadcast([128, 16, 4, 16])
        op("dve", lambda: nc.vector.tensor_tensor(a, bA(Ar), Pr, ALU.mult), reads=rA + rPW, writes=["t1"])
        op("dve", lambda: nc.vector.tensor_tensor(b_, bA(Ai), Pi, ALU.mult), reads=rA + rPW, writes=["t2"])
        op("dve", lambda: nc.vector.tensor_tensor(c_, bA(Ar), Pin if neg_im else Pi, ALU.mult), reads=rA + rPW + ["pwn"], writes=["t3"])
        op("dve", lambda: nc.vector.tensor_tensor(d_, bA(Ai), Pr, ALU.mult), reads=rA + rPW, writes=["t4"])
        for g2 in range(2):
            ps = slice(g2 * 64, (g2 + 1) * 64)
            cs = slice(g2 * 16, (g2 + 1) * 16)
            op("dve", lambda: nc.vector.tensor_tensor(dst[ps, :, 0, :, cs], a[ps], b_[ps], ALU.subtract),
               reads=["t1", "t2"], writes=[wkey])
            op("dve", lambda: nc.vector.tensor_tensor(dst[ps, :, 1, :, cs], c_[ps], d_[ps], ALU.subtract if neg_im else ALU.add),
               reads=["t3", "t4"], writes=[wkey])

    for gi, gt in enumerate(gts):
        pb_ = str(gi % 2)
        XB, WC, CB, Kbd, WS = XB2[gi % 2], WC2[gi % 2], CB2[gi % 2], Kbd2[gi % 2], WS2[gi % 2]
        kXB, kWC, kCB, kKbd, kWS = "XB" + pb_, "WC" + pb_, "CB" + pb_, "Kbd" + pb_, "WS" + pb_
        prs = slice(gt * 4, gt * 4 + 4)
        cmul_all_d(XB, BB[0], BB[1], 0, prs, False, ["BB0", "BB1"], kXB)
        cmul_all_d(WC, Cs[0], Cs[1], 1, prs, True, rC, kWC)
        for g2 in range(2):
            ps = slice(g2 * 64, (g2 + 1) * 64)
            cs = slice(g2 * 16, (g2 + 1) * 16)
            for i in range(2):
                op("dve", lambda: nc.vector.tensor_copy(CB[ps, i, :, cs], Cs[i][ps, prs, :]), reads=rC, writes=[kCB])
        op("dve", lambda: nc.vector.tensor_scalar(CB[:, 1, :, :], CB[:, 1, :, :], -1.0, None, ALU.mult), reads=[kCB], writes=[kCB])
        for j in range(4):
            kb = banks[j % 2]
            for d in range(16):
                for i in range(2):
                    op("pe", lambda: nc.tensor.matmul(kb[0:32, d * 32:(d + 1) * 32], XB[:, d, i, j, :], CB[:, i, j, :],
                                                      start=(d == 0 and i == 0), stop=(i == 1), skip_group_check=True),
                       reads=[kXB, kCB], writes=[("kbk", j % 2)])
            op("dve", lambda: nc.vector.tensor_copy(Kbd[j * 32:(j + 1) * 32, :, j * 32:(j + 1) * 32],
                                                    kb[0:32, :].rearrange("p (d c) -> p d c", d=16)),
               reads=[("kbk", j % 2)], writes=[kKbd])
        op("dve", lambda: nc.vector.scalar_tensor_tensor(Kbd[:, 0, :], identf[:], dsk[:, gt:gt + 1], Kbd[:, 0, :], ALU.mult, ALU.add),
           reads=["identf", "dsk", kKbd], writes=[kKbd])
        P.dma("sp", o_K[gt, :, :, :], Kbd[:], reads=[kKbd], writes=[("oK", gt)])
        for j in range(4):
            for i in range(2):
                for half in range(2):
                    tbk = banksb[(j * 4 + i * 2 + half) % 2]
                    for mm in range(8):
                        m_ = half * 8 + mm
                        op("pe", lambda: nc.tensor.transpose(tbk[0:32, mm * 128:(mm + 1) * 128], XB[:, 15 - m_, i, j, :], ident[:]),
                           reads=[kXB, "ident"], writes=[("tbk", (j * 4 + i * 2 + half) % 2)])
                    op("dve", lambda: nc.vector.tensor_copy(WS[j * 32:(j + 1) * 32, half * 8:(half + 1) * 8, i, :],
                                                            tbk[0:32, :].rearrange("p (m c) -> p m c", m=8)),
                       reads=[("tbk", (j * 4 + i * 2 + half) % 2)], writes=[kWS])
        P.dma("sp", o_WS[gt, :, :, :, :], WS[:], reads=[kWS], writes=[("oWS", gt)])
        P.dma("sp", o_WC[gt, :, :, :, :, :], WC[:], reads=[kWC], writes=[("oWC", gt)])


def build_s5_operands_debug(gts=(0, 5)):
    nc = bass.Bass("TRN2", target_bir_lowering=False)
    prm = {k: nc.dram_tensor(k, shp, F32, kind="ExternalInput").ap() for k, shp in L1_PARAMS.items()
           if k in ("l1_lam_re", "l1_lam_im", "l1_log_dt", "l1_b_re", "l1_b_im", "l1_c_re", "l1_c_im", "l1_d")}
    c_ident = nc.dram_tensor("c_ident", [128, 128], BF16, kind="ExternalInput").ap()
    o_K = nc.dram_tensor("o_K", [16, 128, 16, 128], BF16, kind="ExternalOutput").ap()
    o_WS = nc.dram_tensor("o_WS", [16, 128, 16, 2, 128], BF16, kind="ExternalOutput").ap()
    o_WC = nc.dram_tensor("o_WC", [16, 128, 16, 2, 4, 32], BF16, kind="ExternalOutput").ap()
    o_BB = nc.dram_tensor("o_BB", [2, 128, 64, 16], F32, kind="ExternalOutput").ap()
    with ExitStack() as st:
        P = Prog(nc, st)
        banks = [st.enter_context(nc.psum_tensor("bank%d" % i, [128, 512], F32)) for i in range(2)]
        banksb = [st.enter_context(nc.psum_tensor("bankb%d" % i, [128, 1024], BF16)) for i in range(2)]
        ident = st.enter_context(nc.sbuf_tensor("ident", [128, 128], BF16))
        P.dma("sp", ident[:], c_ident[:, :], writes=["ident"])
        tabs = emit_s5_tables(nc, P, st, prm)
        emit_s5_operands(nc, P, st, prm, tabs, ident, list(gts), o_K, o_WS, o_WC, banks, banksb, o_BB=o_BB)
        P.finish()
    return nc


NCH = 64
NTS = NCH * 16
KS_LEVELS = NCH.bit_length() - 1


def alloc_ks(nc, st):
    T = lambda name, shape, dt: st.enter_context(nc.sbuf_tensor(name, shape, dt))
    return {"ks": T("s5ks", [128, 6, 2, 64], F32), "kt1": T("s5kt1", [128, 64], F32), "kt2": T("s5kt2", [128, 64], F32),
            "ksn": T("s5ksn", [128, 6, 64], F32)}


def emit_ks_table(nc, P, st, tabs, bufs=None):
    if bufs is None:
        bufs = alloc_ks(nc, st)
    pw = tabs["pw"]
    ks, kt1, kt2 = bufs["ks"], bufs["kt1"], bufs["kt2"]
    op = P.op
    op("dve", lambda: nc.vector.tensor_copy(ks[:, 0, :, :], pw[:, 16, :, :]), reads=[("pw", 16)], writes=[("ks", 0)])
    for k in range(1, 6):
        a, b_ = ks[:, k - 1, 0, :], ks[:, k - 1, 1, :]
        op("dve", lambda: nc.vector.tensor_tensor(kt1[:], a, a, ALU.mult), reads=[("ks", k - 1)], writes=["kt1"])
        op("dve", lambda: nc.vector.tensor_tensor(kt2[:], b_, b_, ALU.mult), reads=[("ks", k - 1)], writes=["kt2"])
        op("dve", lambda: nc.vector.tensor_tensor(ks[:, k, 0, :], kt1[:], kt2[:], ALU.subtract), reads=["kt1", "kt2"], writes=[("ks", k)])
        op("dve", lambda: nc.vector.tensor_tensor(kt1[:], a, b_, ALU.mult), reads=[("ks", k - 1)], writes=["kt1"])
        op("dve", lambda: nc.vector.tensor_scalar(ks[:, k, 1, :], kt1[:], 2.0, None, ALU.mult), reads=["kt1"], writes=[("ks", k)])
    ksn = bufs["ksn"]
    op("dve", lambda: nc.vector.tensor_scalar(ksn[:], ks[:, :, 1, :], -1.0, None, ALU.mult), reads=[("ks", k) for k in range(6)], writes=["ksn"])
    return ks, ksn


def emit_s5_core(nc, P, gt, uT, Kbd, WS, WC, ks, ksn, carry, hbuf, hprev, Yb, Sb, yout, first_tile, tag="", ukey=("uT",), ykey=("yout",)):
    op = P.op
    uv = uT[:, :].rearrange("p (n m) -> p m n", m=16)
    h0 = hbuf[0]
    for j in range(4):
        rows = slice(j * 32, (j + 1) * 32)
        first = True
        for ri in range(2):
            for m in range(16):
                op("pe", lambda: nc.tensor.matmul(Sb[j][:, ri * NCH:(ri + 1) * NCH], WS[rows, m, ri, :], uv[rows, m, :],
                                                  start=first, stop=(m == 15), skip_group_check=True,
                                                  tile_position=(32 * j, 0)),
                   reads=[ukey, ("WS" + tag,)], writes=[("Sb", j)])
                first = False
        op("dve", lambda: nc.vector.tensor_copy(h0[:, j, :, :], Sb[j][:, 0:2 * NCH].rearrange("p (r n) -> p r n", r=2)),
           reads=[("Sb", j)], writes=[("hb", 0)])
    for j in range(4):
        pr = gt * 4 + j
        ar, ai, nai = ks[:, 0, 0, pr:pr + 1], ks[:, 0, 1, pr:pr + 1], ksn[:, 0, pr:pr + 1]
        if not first_tile:
            cr, ci = carry[:, j, 0:1], carry[:, j, 1:2]
            op("dve", lambda: nc.vector.scalar_tensor_tensor(h0[:, j, 0, 0:1], cr, ar, h0[:, j, 0, 0:1], ALU.mult, ALU.add),
               reads=[("carry", gt), ("hb", 0), ("ks", 0)], writes=[("hb", 0)])
            op("dve", lambda: nc.vector.scalar_tensor_tensor(h0[:, j, 0, 0:1], ci, nai, h0[:, j, 0, 0:1], ALU.mult, ALU.add),
               reads=[("carry", gt), ("hb", 0), "ksn"], writes=[("hb", 0)])
            op("dve", lambda: nc.vector.scalar_tensor_tensor(h0[:, j, 1, 0:1], ci, ar, h0[:, j, 1, 0:1], ALU.mult, ALU.add),
               reads=[("carry", gt), ("hb", 0), ("ks", 0)], writes=[("hb", 0)])
            op("dve", lambda: nc.vector.scalar_tensor_tensor(h0[:, j, 1, 0:1], cr, ai, h0[:, j, 1, 0:1], ALU.mult, ALU.add),
               reads=[("carry", gt), ("hb", 0), ("ks", 0)], writes=[("hb", 0)])
    cur = 0
    for k in range(KS_LEVELS):
        s = 1 << k
        src, dst = hbuf[cur], hbuf[1 - cur]
        rk = [("hb", cur), ("ks", k), "ksn"]
        wk = [("hb", 1 - cur)]
        op("dve", lambda: nc.vector.tensor_copy(dst[:, :, :, 0:s], src[:, :, :, 0:s]), reads=rk, writes=wk)
        for j in range(4):
            pr = gt * 4 + j
            ar, ai, nai = ks[:, k, 0, pr:pr + 1], ks[:, k, 1, pr:pr + 1], ksn[:, k, pr:pr + 1]
            sr, si = src[:, j, 0, 0:NCH - s], src[:, j, 1, 0:NCH - s]
            op("dve", lambda: nc.vector.scalar_tensor_tensor(dst[:, j, 0, s:NCH], sr, ar, src[:, j, 0, s:NCH], ALU.mult, ALU.add), reads=rk, writes=wk)
            op("dve", lambda: nc.vector.scalar_tensor_tensor(dst[:, j, 0, s:NCH], si, nai, dst[:, j, 0, s:NCH], ALU.mult, ALU.add), reads=rk + wk, writes=wk)
            op("dve", lambda: nc.vector.scalar_tensor_tensor(dst[:, j, 1, s:NCH], si, ar, src[:, j, 1, s:NCH], ALU.mult, ALU.add), reads=rk, writes=wk)
            op("dve", lambda: nc.vector.scalar_tensor_tensor(dst[:, j, 1, s:NCH], sr, ai, dst[:, j, 1, s:NCH], ALU.mult, ALU.add), reads=rk + wk, writes=wk)
        cur = 1 - cur
    H = hbuf[cur]
    hk = [("hp",)]
    if first_tile:
        op("pool", lambda: nc.gpsimd.memset(hprev[:, :, :, 0:1], 0.0), writes=hk)
    else:
        op("dve", lambda: nc.vector.tensor_copy(hprev[:, :, :, 0], carry[:, :, :]), reads=[("carry", gt)], writes=hk)
    op("dve", lambda: nc.vector.tensor_copy(hprev[:, :, :, 1:NCH], H[:, :, :, 0:NCH - 1]), reads=[("hb", cur)], writes=hk)
    op("dve", lambda: nc.vector.tensor_copy(carry[:, :, :], H[:, :, :, NCH - 1]), reads=[("hb", cur)] + hk, writes=[("carry", gt)])
    started = set()
    for m in range(16):
        bk = m // 8
        yb = Yb[bk]
        cols = slice((m % 8) * NCH, (m % 8 + 1) * NCH)
        for mp in range(m + 1):
            f_ = bk not in started
            started.add(bk)
            op("pe", lambda: nc.tensor.matmul(yb[:, cols], Kbd[:, m - mp, :], uv[:, mp, :], start=f_, stop=False, skip_group_check=True),
               reads=[ukey, ("Kbd" + tag,)], writes=[("Yb", bk)])
    for m in range(16):
        bk = m // 8
        yb = Yb[bk]
        cols = slice((m % 8) * NCH, (m % 8 + 1) * NCH)
        for j in range(4):
            for ri in range(2):
                op("pe", lambda: nc.tensor.matmul(yb[j * 32:(j + 1) * 32, cols], WC[:, m, ri, j, :], hprev[:, j, ri, :],
                                                  start=False, stop=(ri == 1), skip_group_check=True,
                                                  tile_position=(0, 32 * j)),
                   reads=hk + [("WC" + tag,)], writes=[("Yb", bk)])
    yv = yout[:, :].rearrange("p (n m) -> p m n", m=16)
    for bk in range(2):
        op("act", lambda: nc.scalar.activation(out=yv[:, bk * 8:(bk + 1) * 8, :], in_=Yb[bk][:, 0:8 * NCH].rearrange("p (m n) -> p m n", m=8), func=AF.Copy),
           reads=[("Yb", bk)], writes=[ykey])


def build_s5_core_debug(gt=0, ntiles=2):
    nc = bass.Bass("TRN2", target_bir_lowering=False)
    names = ("l1_lam_re", "l1_lam_im", "l1_log_dt", "l1_b_re", "l1_b_im", "l1_c_re", "l1_c_im", "l1_d")
    prm = {k: nc.dram_tensor(k, L1_PARAMS[k], F32, kind="ExternalInput").ap() for k in names}
    c_ident = nc.dram_tensor("c_ident", [128, 128], BF16, kind="ExternalInput").ap()
    u_in = nc.dram_tensor("u_in", [128, ntiles * NTS], BF16, kind="ExternalInput").ap()
    y_out = nc.dram_tensor("y_out", [128, ntiles * NTS], F32, kind="ExternalOutput").ap()
    o_K = nc.dram_tensor("o_K", [16, 128, 16, 128], BF16, kind="Internal").ap()
    o_WS = nc.dram_tensor("o_WS", [16, 128, 16, 2, 128], BF16, kind="Internal").ap()
    o_WC = nc.dram_tensor("o_WC", [16, 128, 16, 2, 4, 32], BF16, kind="Internal").ap()
    with ExitStack() as st:
        P = Prog(nc, st)
        T = lambda name, shape, dt: st.enter_context(nc.sbuf_tensor(name, shape, dt))
        banks = [st.enter_context(nc.psum_tensor("bank%d" % i, [128, 512], F32)) for i in range(6)]
        banksb = [st.enter_context(nc.psum_tensor("bankb%d" % i, [128, 1024], BF16)) for i in range(2)]
        ident = T("ident", [128, 128], BF16)
        P.dma("sp", ident[:], c_ident[:, :], writes=["ident"])
        tabs = emit_s5_tables(nc, P, st, prm)
        ks, ksn = emit_ks_table(nc, P, st, tabs)
        with ExitStack() as st2:
            emit_s5_operands(nc, P, st2, prm, tabs, ident, [gt], o_K, o_WS, o_WC, banks, banksb)
            P.barrier()
        Kbd = T("mKbd", [128, 16, 128], BF16); WS = T("mWS", [128, 16, 2, 128], BF16); WC = T("mWC", [128, 16, 2, 4, 32], BF16)
        P.dma("sp", Kbd[:], o_K[gt, :, :, :], reads=[("oK", gt)], writes=[("Kbd",)])
        P.dma("sp", WS[:], o_WS[gt, :, :, :, :], reads=[("oWS", gt)], writes=[("WS",)])
        P.dma("sp", WC[:], o_WC[gt, :, :, :, :, :], reads=[("oWC", gt)], writes=[("WC",)])
        uT = T("muT", [128, NTS], BF16); yo = T("myo", [128, NTS], F32)
        carry = T("mcarry", [128, 4, 2], F32)
        hbuf = [T("mhb%d" % i, [128, 4, 2, NCH], F32) for i in range(2)]
        hprev = T("mhp", [128, 4, 2, NCH], BF16)
        for t in range(ntiles):
            P.dma("sp", uT[:], u_in[:, t * NTS:(t + 1) * NTS], writes=[("uT",)])
            emit_s5_core(nc, P, gt, uT, Kbd, WS, WC, ks, ksn, carry, hbuf, hprev, banks[0:2], banks[2:6], yo, first_tile=(t == 0))
            P.dma("sp", y_out[:, t * NTS:(t + 1) * NTS], yo[:], reads=[("yout",)])
        P.finish()
    return nc


NSUP_FULL = S // NTS


def emit_layer1(nc, P, x1_src, prm, c_ident, out, o_K, o_WS, o_WC, NSUP=NSUP_FULL, d_ys5=None, probe_no_restream=False):
    with ExitStack() as st:
        T = lambda name, shape, dt: st.enter_context(nc.sbuf_tensor(name, shape, dt))
        banks = [st.enter_context(nc.psum_tensor("l1bank%d" % i, [128, 512], F32)) for i in range(6)]
        banksb = [st.enter_context(nc.psum_tensor("l1bankb%d" % i, [128, 1024], BF16)) for i in range(2)]
        ident = T("l1ident", [128, 128], BF16)
        gbc = T("l1gbc", [128, D], F32)
        bglu = T("l1bglu", [128, 16], F32)
        carry = T("l1carry", [128, 16, 4, 2], F32)
        Wo = T("l1Wo", [128, 16, D], BF16)
        P.dma("sp", ident[:], c_ident[:, :], writes=["ident"])
        P.dma("sp", gbc[:], prm["l1_norm_g"].partition_broadcast(128), writes=["gbc"])
        P.dma("sp", bglu[:], prm["l1_b_glu"].rearrange("(ft p) -> p ft", p=128), writes=["bglu"], allow_slow_non_contiguous=True)
        w_out = prm["l1_w_out"].rearrange("(k p) f -> p k f", p=128)
        for kk in range(0, 16, 4):
            P.dma("pool", Wo[:, kk:kk + 4, :], w_out[:, kk:kk + 4, :], writes=[("Wo", kk // 4)])
        ksb = alloc_ks(nc, st)
        with ExitStack() as st0:
            tabs = emit_s5_tables(nc, P, st0, prm)
            ks, ksn = emit_ks_table(nc, P, st0, tabs, bufs=ksb)
            emit_s5_operands(nc, P, st0, prm, tabs, ident, list(range(16)), o_K, o_WS, o_WC, banks, banksb)
            P.barrier()
        nb = alloc_norm_bufs(nc, st, "L1A")
        hT = T("l1hT", [128, 8, NTS], BF16)
        uT = [T("l1uT%d" % i, [128, NTS], BF16) for i in range(2)]
        yo = [T("l1yo%d" % i, [128, NTS], F32) for i in range(2)]
        YG = T("l1YG", [128, 16, NTS], BF16)
        G1 = T("l1G1", [128, 16, 512], BF16)
        hbuf = [T("l1hb%d" % i, [128, 4, 2, NCH], F32) for i in range(2)]
        hprev = [T("l1hp%d" % i, [128, 4, 2, NCH], BF16) for i in range(2)]
        Kbd = [T("l1Kbd%d" % i, [128, 16, 128], BF16) for i in range(2)]
        WS = [T("l1WS%d" % i, [128, 16, 2, 128], BF16) for i in range(2)]
        WC = [T("l1WC%d" % i, [128, 16, 2, 4, 32], BF16) for i in range(2)]
        Wu = [T("l1Wu%d" % i, [128, 8, 128], BF16) for i in range(2)]
        Wg = [T("l1Wg%d" % i, [128, 16, 128], BF16) for i in range(2)]
        Wz = [T("l1Wz%d" % i, [128, 8, 128], BF16) for i in range(2)]
        sg = T("l1sg", [128, 512], BF16)
        sz = T("l1sz", [128, 512], BF16)
        tg = T("l1tg", [128, 512], BF16)
        xr = [T("l1xr%d" % i, [128, D], F32) for i in range(2)]
        w_in = prm["l1_w_in"].rearrange("(j p) f -> p j f", p=128)
        w_glu = prm["l1_w_glu"].rearrange("(k p) f -> p k f", p=128)
        op = P.op

        loaded = set()

        def load_gt(gt, b):
            if probe_no_restream:
                if ("gt", b) in loaded:
                    return
                loaded.add(("gt", b))
            tag = str(b)
            P.dma("sp", Kbd[b][:], o_K[gt, :, :, :], reads=[("oK", gt)], writes=[("Kbd" + tag,)])
            P.dma("sp", WS[b][:], o_WS[gt, :, :, :, :], reads=[("oWS", gt)], writes=[("WS" + tag,)])
            P.dma("sp", WC[b][:], o_WC[gt, :, :, :, :, :], reads=[("oWC", gt)], writes=[("WC" + tag,)])
            P.dma("pool", Wu[b][:], w_in[:, :, gt * 128:(gt + 1) * 128], writes=[("Wu", b)])

        def load_ft(ft, b):
            if probe_no_restream:
                if ("ft", b) in loaded:
                    return
                loaded.add(("ft", b))
            P.dma("pool", Wg[b][:], w_glu[:, :, ft * 128:(ft + 1) * 128], writes=[("Wg", b)])
            P.dma("pool", Wz[b][:], w_in[:, :, E + ft * 128:E + (ft + 1) * 128], writes=[("Wz", b)])

        for Tt in range(NSUP):
            r0 = Tt * NTS
            emit_norm_transpose(nc, P, st, x1_src[r0:r0 + NTS, :], hT, gbc, ident, banksb, "L1A", NT=NTS // 128, x_keep=nb)
            hTk = [("hT", q) for q in range(NTS // 512)]
            Yb, Sb = banks[0:2], banks[2:6]
            Ub = [banksb[hf][:, :].bitcast(F32) for hf in range(2)]

            def U_stage(gt, b):
                for hf in range(NTS // 512):
                    cs = slice(hf * 512, (hf + 1) * 512)
                    for j in range(8):
                        op("pe", lambda: nc.tensor.matmul(Ub[hf][:, 0:512], Wu[b][:, j, :], hT[:, j, cs], start=(j == 0), stop=(j == 7)),
                           reads=hTk + [("Wu", b)], writes=[("pbb", hf)])
                    op("act", lambda: nc.scalar.activation(out=uT[b][:, cs], in_=Ub[hf][:, 0:512], func=AF.Copy),
                       reads=[("pbb", hf)], writes=[("uT", b)])

            def ST_stage(gt, b):
                uv = uT[b][:, :].rearrange("p (n m) -> p m n", m=16)
                h0 = hbuf[0]
                for j in range(4):
                    rows = slice(j * 32, (j + 1) * 32)
                    first = True
                    for ri in range(2):
                        for m in range(16):
                            op("pe", lambda: nc.tensor.matmul(Sb[j][:, ri * NCH:(ri + 1) * NCH], WS[b][rows, m, ri, :], uv[rows, m, :],
                                                              start=first, stop=(m == 15), skip_group_check=True, tile_position=(32 * j, 0)),
                               reads=[("uT", b), ("WS" + str(b),)], writes=[("Sb", j)])
                            first = False
                    op("dve", lambda: nc.vector.tensor_copy(h0[:, j, :, :], Sb[j][:, 0:2 * NCH].rearrange("p (r n) -> p r n", r=2)),
                       reads=[("Sb", j)], writes=[("hb", 0)])

            def SCAN_stage(gt, b, first_tile):
                h0 = hbuf[0]
                cg = carry[:, gt]
                ck = ("carry", gt)
                for j in range(4):
                    pr = gt * 4 + j
                    ar, ai, nai = ks[:, 0, 0, pr:pr + 1], ks[:, 0, 1, pr:pr + 1], ksn[:, 0, pr:pr + 1]
                    if not first_tile:
                        cr, ci = cg[:, j, 0:1], cg[:, j, 1:2]
                        for (dst_, src_, sc_) in ((0, cr, ar), (0, ci, nai), (1, ci, ar), (1, cr, ai)):
                            op("dve", lambda: nc.vector.scalar_tensor_tensor(h0[:, j, dst_, 0:1], src_, sc_, h0[:, j, dst_, 0:1], ALU.mult, ALU.add),
                               reads=[ck, ("hb", 0), ("ks", 0), "ksn"], writes=[("hb", 0)])
                cur = 0
                for k in range(KS_LEVELS):
                    sft = 1 << k
                    src, dst = hbuf[cur], hbuf[1 - cur]
                    rk = [("hb", cur), ("ks", k), "ksn"]
                    wk = [("hb", 1 - cur)]
                    op("dve", lambda: nc.vector.tensor_copy(dst[:, :, :, 0:sft], src[:, :, :, 0:sft]), reads=rk, writes=wk)
                    for j in range(4):
                        pr = gt * 4 + j
                        ar, ai, nai = ks[:, k, 0, pr:pr + 1], ks[:, k, 1, pr:pr + 1], ksn[:, k, pr:pr + 1]
                        sr, si = src[:, j, 0, 0:NCH - sft], src[:, j, 1, 0:NCH - sft]
                        op("dve", lambda: nc.vector.scalar_tensor_tensor(dst[:, j, 0, sft:NCH], sr, ar, src[:, j, 0, sft:NCH], ALU.mult, ALU.add), reads=rk, writes=wk)
                        op("dve", lambda: nc.vector.scalar_tensor_tensor(dst[:, j, 0, sft:NCH], si, nai, dst[:, j, 0, sft:NCH], ALU.mult, ALU.add), reads=rk + wk, writes=wk)
                        op("dve", lambda: nc.vector.scalar_tensor_tensor(dst[:, j, 1, sft:NCH], si, ar, src[:, j, 1, sft:NCH], ALU.mult, ALU.add), reads=rk, writes=wk)
                        op("dve", lambda: nc.vector.scalar_tensor_tensor(dst[:, j, 1, sft:NCH], sr, ai, dst[:, j, 1, sft:NCH], ALU.mult, ALU.add), reads=rk + wk, writes=wk)
                    cur = 1 - cur
                H = hbuf[cur]
                hp = hprev[b]
                hk = [("hp", b)]
                if first_tile:
                    op("pool", lambda: nc.gpsimd.memset(hp[:, :, :, 0:1], 0.0), writes=hk)
                else:
                    op("dve", lambda: nc.vector.tensor_copy(hp[:, :, :, 0], cg[:, :, :]), reads=[ck], writes=hk)
                op("dve", lambda: nc.vector.tensor_copy(hp[:, :, :, 1:NCH], H[:, :, :, 0:NCH - 1]), reads=[("hb", cur)], writes=hk)
                op("dve", lambda: nc.vector.tensor_copy(cg[:, :, :], H[:, :, :, NCH - 1]), reads=[("hb", cur)] + hk, writes=[ck])

            def INTRA_stage(gt, b):
                uv = uT[b][:, :].rearrange("p (n m) -> p m n", m=16)
                started = set()
                for m in range(16):
                    bk = m // 8
                    cols = slice((m % 8) * NCH, (m % 8 + 1) * NCH)
                    for mp in range(m + 1):
                        f_ = bk not in started
                        started.add(bk)
                        op("pe", lambda: nc.tensor.matmul(Yb[bk][:, cols], Kbd[b][:, m - mp, :], uv[:, mp, :], start=f_, stop=False, skip_group_check=True),
                           reads=[("uT", b), ("Kbd" + str(b),)], writes=[("Yb", bk)])

            def INTER_stage(gt, b):
                for m in range(16):
                    bk = m // 8
                    cols = slice((m % 8) * NCH, (m % 8 + 1) * NCH)
                    for j in range(4):
                        for ri in range(2):
                            op("pe", lambda: nc.tensor.matmul(Yb[bk][j * 32:(j + 1) * 32, cols], WC[b][:, m, ri, j, :], hprev[b][:, j, ri, :],
                                                              start=False, stop=(ri == 1), skip_group_check=True, tile_position=(0, 32 * j)),
                               reads=[("hp", b), ("WC" + str(b),)], writes=[("Yb", bk)])
                yv = yo[b][:, :].rearrange("p (n m) -> p m n", m=16)
                for bk in range(2):
                    op("act", lambda: nc.scalar.activation(out=yv[:, bk * 8:(bk + 1) * 8, :], in_=Yb[bk][:, 0:8 * NCH].rearrange("p (m n) -> p m n", m=8), func=AF.Copy),
                       reads=[("Yb", bk)], writes=[("yo", b)])

            load_gt(0, 0)
            U_stage(0, 0)
            ST_stage(0, 0)
            for gt in range(16):
                b = gt % 2
                if gt + 1 < 16:
                    load_gt(gt + 1, 1 - b)
                INTRA_stage(gt, b)
                SCAN_stage(gt, b, first_tile=(Tt == 0))
                if gt + 1 < 16:
                    U_stage(gt + 1, 1 - b)
                    ST_stage(gt + 1, 1 - b)
                INTER_stage(gt, b)
                if d_ys5 is not None:
                    P.dma("sp", d_ys5[gt * 128:(gt + 1) * 128, r0:r0 + NTS], yo[b][:], reads=[("yo", b)])
                op("act", lambda: nc.scalar.activation(out=YG[:, gt, :], in_=yo[b][:], func=AF.Gelu_apprx_tanh),
                   reads=[("yo", b)], writes=[("YG", gt)])
            ygk = [("YG", k) for k in range(16)]
            g1k = [("G1", k) for k in range(16)]
            for hf in range(NTS // 512):
                cs = slice(hf * 512, (hf + 1) * 512)
                load_ft(0, 0)
                for ft in range(16):
                    b = ft % 2
                    if ft + 1 < 16:
                        load_ft(ft + 1, 1 - b)
                    pg, pz = banks[b], banks[2 + b]
                    for k in range(16):
                        op("pe", lambda: nc.tensor.matmul(pg[:, :], Wg[b][:, k, :], YG[:, k, cs], start=(k == 0), stop=(k == 15)),
                           reads=ygk + [("Wg", b)], writes=[("Yb", b)])
                    for j in range(8):
                        op("pe", lambda: nc.tensor.matmul(pz[:, :], Wz[b][:, j, :], hT[:, j, cs], start=(j == 0), stop=(j == 7)),
                           reads=hTk + [("Wz", b)], writes=[("Sb", b)])
                    op("act", lambda: nc.scalar.activation(out=sg[:], in_=pg[:, :], func=AF.Sigmoid, bias=bglu[:, ft:ft + 1], scale=1.0),
                       reads=[("Yb", b), "bglu"], writes=["sg"])
                    op("act", lambda: nc.scalar.activation(out=sz[:], in_=pz[:, :], func=AF.Silu), reads=[("Sb", b)], writes=["sz"])
                    op("dve", lambda: nc.vector.tensor_tensor(tg[:], YG[:, ft, cs], sg[:], ALU.mult), reads=[("YG", ft), "sg"], writes=["tg"])
                    op("dve", lambda: nc.vector.tensor_tensor(G1[:, ft, :], tg[:], sz[:], ALU.mult), reads=["tg", "sz"], writes=[("G1", ft)])
                for tt in range(4):
                    t = (r0 + hf * 512) // 128 + tt
                    b = t % 2
                    P.dma("sp", xr[b][:], x1_src[t * 128:(t + 1) * 128, :], writes=[("xr", b)])
                    for half in range(2):
                        bk = 2 + (t * 2 + half) % 4
                        for k in range(16):
                            op("pe", lambda: nc.tensor.matmul(banks[bk][:], G1[:, k, tt * 128:(tt + 1) * 128], Wo[:, k, half * 512:(half + 1) * 512],
                                                              start=(k == 0), stop=(k == 15)),
                               reads=g1k + [("Wo", k // 4)], writes=[("Sb", bk - 2)])
                        op("dve", lambda: nc.vector.tensor_tensor(xr[b][:, half * 512:(half + 1) * 512], banks[bk][:],
                                                                  xr[b][:, half * 512:(half + 1) * 512], ALU.add),
                           reads=[("Sb", bk - 2), ("xr", b)], writes=[("xr", b)])
                    P.dma("sp", out[t * 128:(t + 1) * 128, :], xr[b][:], reads=[("xr", b)], writes=[("x2", t)])


def build_l1(NSUP=NSUP_FULL, debug=False, probe_no_restream=False):
    nc = bass.Bass("TRN2", target_bir_lowering=False)
    x1 = nc.dram_tensor("x1", [S, D], F32, kind="ExternalInput").ap()
    prm = {k: nc.dram_tensor(k, shp, F32, kind="ExternalInput").ap() for k, shp in L1_PARAMS.items()}
    c_ident = nc.dram_tensor("c_ident", [128, 128], BF16, kind="ExternalInput").ap()
    out = nc.dram_tensor("out", [S, D], F32, kind="ExternalOutput").ap()
    d_ys5 = nc.dram_tensor("d_ys5", [E, S], F32, kind="ExternalOutput").ap() if debug else None
    o_K = nc.dram_tensor("o_K", [16, 128, 16, 128], BF16, kind="Internal").ap()
    o_WS = nc.dram_tensor("o_WS", [16, 128, 16, 2, 128], BF16, kind="Internal").ap()
    o_WC = nc.dram_tensor("o_WC", [16, 128, 16, 2, 4, 32], BF16, kind="Internal").ap()
    with ExitStack() as st:
        P = Prog(nc, st)
        emit_layer1(nc, P, x1, prm, c_ident, out, o_K, o_WS, o_WC, NSUP=NSUP, d_ys5=d_ys5, probe_no_restream=probe_no_restream)
        P.finish()
        build_l1.stats = {"ninst": P.ninst, "nsem": P._nsem}
    return nc


def run_layer1_spmd(inputs, x1, n_cores=8):
    nc = build_l1()
    ident = host_consts()["c_ident"]
    in_maps = []
    for b in range(n_cores):
        m = {"x1": np.ascontiguousarray(x1[b]), "c_ident": ident}
        for k in L1_PARAMS:
            m[k] = np.ascontiguousarray(inputs[k])
        in_maps.append(m)
    res = run_bass_kernel_spmd(nc, in_maps, core_ids=list(range(n_cores)))
    return np.stack([np.asarray(r["out"]) for r in res.results], axis=0)


def build_fused(NQ=8, NSUP=NSUP_FULL):
    nc = bass.Bass("TRN2", target_bir_lowering=False)
    x = nc.dram_tensor("x", [S, D], F32, kind="ExternalInput").ap()
    prm = {k: nc.dram_tensor(k, shp, F32, kind="ExternalInput").ap() for k, shp in {**L0_PARAMS, **L1_PARAMS}.items()}
    cst = {k: nc.dram_tensor(k, shp, dt, kind="ExternalInput").ap() for k, (shp, dt) in CONST_SHAPES.items()}
    out = nc.dram_tensor("out", [S, D], F32, kind="ExternalOutput").ap()
    gscr = nc.dram_tensor("gscr", [128, NH, S], BF16, kind="Internal").ap()
    x1scr = nc.dram_tensor("x1scr", [S, D], F32, kind="Internal").ap()
    o_K = nc.dram_tensor("o_K", [16, 128, 16, 128], BF16, kind="Internal").ap()
    o_WS = nc.dram_tensor("o_WS", [16, 128, 16, 2, 128], BF16, kind="Internal").ap()
    o_WC = nc.dram_tensor("o_WC", [16, 128, 16, 2, 4, 32], BF16, kind="Internal").ap()
    with ExitStack() as st:
        P = Prog(nc, st)
        emit_layer0(nc, P, x, prm, cst, x1scr, gscr, NQ=NQ)
        P.barrier()
        emit_layer1(nc, P, x1scr, prm, cst["c_ident"], out, o_K, o_WS, o_WC, NSUP=NSUP)
        P.finish()
        build_fused.stats = {"ninst": P.ninst, "nsem": P._nsem}
    return nc


def fused_in_maps(inputs, n_cores=8):
    cst = host_consts()
    maps = []
    for b in range(n_cores):
        m = {"x": np.ascontiguousarray(inputs["x"][b])}
        for k in list(L0_PARAMS) + list(L1_PARAMS):
            m[k] = np.ascontiguousarray(inputs[k])
        m.update(cst)
        maps.append(m)
    return maps


FUSED = True


def kernel(**inputs):
    inputs = {k: np.asarray(v) for k, v in inputs.items()}
    if FUSED:
        nc = build_fused()
        res = run_bass_kernel_spmd(nc, fused_in_maps(inputs), core_ids=list(range(8)))
        return np.stack([np.asarray(r["out"]) for r in res.results], axis=0).astype(np.float32, copy=False)
    x1 = run_layer0_spmd(inputs)
    x2 = run_layer1_spmd(inputs, x1)
    return x2.astype(np.float32, copy=False)
```

```python
import math
from contextlib import ExitStack

import numpy as np
import ml_dtypes

import concourse.bass as bass
import concourse.mybir as mybir
from concourse.bass_utils import run_bass_kernel_spmd

F32 = mybir.dt.float32
BF16 = mybir.dt.bfloat16
AF = mybir.ActivationFunctionType
ALU = mybir.AluOpType
AX = mybir.AxisListType

S = 4096
D = 1024
E = 2048
NH = 16
EPS = 1e-6
LAM0 = 0.8 - 0.6 * math.exp(-0.3 * 0)
NEG = -30000.0
EPOCH = 12000


class Prog:
    def __init__(self, nc, stack, n_dma_sems=48):
        self.nc = nc
        self.stack = stack
        self.eng = {"pe": nc.tensor, "act": nc.scalar, "dve": nc.vector,
                    "pool": nc.gpsimd, "sp": nc.sync}
        self._nsem = 0
        self.cur = {}
        for e in ("pe", "act", "dve", "pool"):
            self.cur[e] = [self._newsem(e), 0]
        n_sw = max(8, n_dma_sems // 3)
        self.dma_pool = {"sp": [[self._newsem("dmah"), 0] for _ in range(n_dma_sems - n_sw)],
                         "pool": [[self._newsem("dmas"), 0] for _ in range(n_sw)]}
        self.dma_rr = {"sp": 0, "pool": 0}
        self.dma_sems = self.dma_pool["sp"] + self.dma_pool["pool"]
        self.waited = {}
        self.lastw = {}
        self.readers = {}
        self.ninst = 0

    def _newsem(self, name):
        self._nsem += 1
        return self.stack.enter_context(self.nc.semaphore("s_%s_%d" % (name, self._nsem)))

    def _wait(self, e, h):
        sem, val, src = h
        if src == "pe" and e == "pe":
            return
        k = (e, id(sem))
        if self.waited.get(k, 0) >= val:
            return
        self.waited[k] = val
        self.eng[e].wait_ge(sem, val)

    def deps(self, e, reads, writes):
        for r in reads:
            h = self.lastw.get(r)
            if h is not None:
                self._wait(e, h)
        for w in writes:
            h = self.lastw.get(w)
            if h is not None:
                self._wait(e, h)
            for h in self.readers.get(w, ()):
                self._wait(e, h)

    def _commit(self, h, reads, writes):
        for r in reads:
            lst = self.readers.setdefault(r, [])
            lst[:] = [x for x in lst if x[0] is not h[0]]
            lst.append(h)
        for w in writes:
            self.lastw[w] = h
            self.readers[w] = []

    def op(self, e, fn, reads=(), writes=()):
        self.deps(e, reads, writes)
        c = self.cur[e]
        if c[1] >= EPOCH:
            c = self.cur[e] = [self._newsem(e), 0]
        ins = fn()
        c[1] += 1
        ins.then_inc(c[0], 1)
        self.ninst += 1
        h = (c[0], c[1], e)
        self._commit(h, reads, writes)
        return h

    def dma(self, q, out, in_, reads=(), writes=(), **kw):
        self.deps(q, reads, writes)
        pool_ = self.dma_pool[q]
        slot = pool_[self.dma_rr[q]]
        self.dma_rr[q] = (self.dma_rr[q] + 1) % len(pool_)
        if slot[1] > 0:
            self._wait(q, (slot[0], slot[1], "dma"))
        ins = self.eng[q].dma_start(out=out, in_=in_, **kw)
        slot[1] += 16
        ins.then_inc(slot[0], 16)
        h = (slot[0], slot[1], "dma")
        self._commit(h, reads, writes)
        return h

    def barrier(self):
        hs = [(slot[0], slot[1], "dma") for slot in self.dma_sems if slot[1] > 0]
        hs += [(c[0], c[1], "bar") for c in self.cur.values() if c[1] > 0]
        for e in ("sp", "pe", "act", "dve", "pool"):
            for h in hs:
                self._wait(e, h)

    def finish(self):
        for slot in self.dma_sems:
            if slot[1] > 0:
                self._wait("sp", (slot[0], slot[1], "dma"))
        for e, c in self.cur.items():
            if c[1] > 0:
                self._wait("sp", (c[0], c[1], e))


def _split3(x):
    x = np.asarray(x, np.float32)
    hi = x.astype(ml_dtypes.bfloat16)
    r1 = x - hi.astype(np.float32)
    mid = r1.astype(ml_dtypes.bfloat16)
    r2 = r1 - mid.astype(np.float32)
    lo = r2.astype(ml_dtypes.bfloat16)
    return hi, mid, lo


def host_consts():
    bf = ml_dtypes.bfloat16
    c = {}
    c["c_ident"] = np.eye(128, dtype=np.float32).astype(bf)
    blk = np.zeros((128, 128), np.float32)
    blk[:64, :64] = 1.0 / 64
    blk[64:, 64:] = 1.0 / 64
    c["c_blk"] = blk.astype(bf)
    mask = np.zeros((128, 4, 512), np.float32)
    ki = np.arange(128)[:, None]
    qi = np.arange(512)[None, :]
    for j in range(4):
        mask[:, j, :] = np.where(128 * j + ki <= qi, 0.0, NEG)
    c["c_mask"] = mask.astype(bf)
    pos = np.arange(S, dtype=np.float64)
    qb = np.zeros((NH, 6, S), bf)
    kb = np.zeros((NH, 6, S), bf)
    for h in range(NH):
        slope = 2.0 ** (-8.0 * (h + 1) / NH)
        a, b_, c_ = _split3((-slope * pos).astype(np.float32))
        qb[h, 0], qb[h, 1], qb[h, 2] = a, b_, c_
        qb[h, 3:6] = 1.0
        a, b_, c_ = _split3((slope * pos).astype(np.float32))
        kb[h, 0:3] = 1.0
        kb[h, 3], kb[h, 4], kb[h, 5] = a, b_, c_
    c["c_qb"] = qb
    c["c_kb"] = kb
    return c


CONST_SHAPES = {
    "c_ident": ([128, 128], BF16), "c_blk": ([128, 128], BF16),
    "c_mask": ([128, 4, 512], BF16), "c_qb": ([NH, 6, S], BF16), "c_kb": ([NH, 6, S], BF16),
}

L0_PARAMS = {
    "l0_norm_g": [D], "l0_w_in": [D, 4 * E], "l0_q_norm_g": [64], "l0_k_norm_g": [64],
    "l0_lam_q1": [64], "l0_lam_k1": [64], "l0_lam_q2": [64], "l0_lam_k2": [64],
    "l0_head_norm_g": [128], "l0_w_out": [E, D],
}


def vec2(ap):
    return ap.rearrange("(n o) -> n o", o=1)


def alloc_norm_bufs(nc, st, tagp):
    T = lambda name, shape, dt: st.enter_context(nc.sbuf_tensor(name, shape, dt))
    return {"xt": [T(tagp + "xt%d" % i, [128, D], F32) for i in range(2)],
            "xn": [T(tagp + "xn%d" % i, [128, D], BF16) for i in range(2)],
            "junk": T(tagp + "junk", [128, D], BF16), "stat": T(tagp + "stat", [128, 8], F32)}


def emit_norm_transpose(nc, P, st, x_src, hT, gbc, ident, banksb, tagp, NT=32, x_keep=None):
    if x_keep is None:
        x_keep = alloc_norm_bufs(nc, st, tagp)
    xt, xn, junk, stat = x_keep["xt"], x_keep["xn"], x_keep["junk"], x_keep["stat"]
    for t in range(NT):
        b = t % 2
        P.dma("sp", xt[b][:], x_src[t * 128:(t + 1) * 128, :], writes=[(tagp, "xt", b)])
        P.op("act", lambda: nc.scalar.activation(out=junk[:], in_=xt[b][:], func=AF.Square,
                                                 accum_out=stat[:, b:b + 1]),
             reads=[(tagp, "xt", b)], writes=[(tagp, "junk"), (tagp, "ss", b)])
        P.op("act", lambda: nc.scalar.activation(out=stat[:, 2 + b:3 + b], in_=stat[:, b:b + 1], func=AF.Sqrt,
                                                 scale=1.0 / D, bias=EPS),
             reads=[(tagp, "ss", b)], writes=[(tagp, "sd", b)])
        P.op("dve", lambda: nc.vector.reciprocal(stat[:, 4 + b:5 + b], stat[:, 2 + b:3 + b]),
             reads=[(tagp, "sd", b)], writes=[(tagp, "rs", b)])
        P.op("dve", lambda: nc.vector.scalar_tensor_tensor(xn[b][:], xt[b][:], stat[:, 4 + b:5 + b], gbc[:],
                                                           ALU.mult, ALU.mult),
             reads=[(tagp, "xt", b), (tagp, "rs", b), "gbc"], writes=[(tagp, "xn", b)])
        pb = banksb[t % 2]
        for j in range(8):
            P.op("pe", lambda: nc.tensor.transpose(pb[:, j * 128:(j + 1) * 128], xn[b][:, j * 128:(j + 1) * 128],
                                                   ident[:]),
                 reads=[(tagp, "xn", b), "ident"], writes=[("pbb", t % 2)])
        P.op("dve", lambda: nc.vector.tensor_copy(hT[:, :, t * 128:(t + 1) * 128],
                                                  pb[:, :].rearrange("p (j c) -> p j c", j=8)),
             reads=[("pbb", t % 2)], writes=[("hT", t // 4)])


def emit_layer0(nc, P, x, prm, cst, x1_out, gscr, heads=None, NQ=8, dump=None):
    heads = list(range(NH)) if heads is None else list(heads)
    with ExitStack() as st:
        T = lambda name, shape, dt: st.enter_context(nc.sbuf_tensor(name, shape, dt))
        banks = [st.enter_context(nc.psum_tensor("bank%d" % i, [128, 512], F32)) for i in range(6)]
        banksb = [st.enter_context(nc.psum_tensor("bankb%d" % i, [128, 1024], BF16)) for i in range(2)]
        ident = T("ident", [128, 128], BF16)
        blk = T("blk", [128, 128], BF16)
        mask = T("mask", [128, 4, 512], BF16)
        gbc = T("gbc", [128, D], F32)
        small = T("small", [128, 16], F32)
        lamv = T("lamv", [128, 4, 64], F32)
        lamj = T("lamj", [128, 64], F32)
        ghn = T("ghn", [128, 128], F32)
        P.dma("sp", ident[:], cst["c_ident"][:, :], writes=["ident"])
        P.dma("sp", blk[:], cst["c_blk"][:, :], writes=["blk"])
        P.dma("sp", mask[:], cst["c_mask"][:, :, :], writes=["mask"])
        P.dma("sp", gbc[:], prm["l0_norm_g"].partition_broadcast(128), writes=["gbc"])
        P.dma("sp", ghn[:], prm["l0_head_norm_g"].partition_broadcast(128), writes=["ghn"])
        for m in range(2):
            P.dma("sp", small[m * 64:(m + 1) * 64, 0:1], vec2(prm["l0_q_norm_g"]), writes=["small"])
            P.dma("sp", small[m * 64:(m + 1) * 64, 1:2], vec2(prm["l0_k_norm_g"]), writes=["small"])
        for i, nm in enumerate(["l0_lam_q1", "l0_lam_k1", "l0_lam_q2", "l0_lam_k2"]):
            P.dma("sp", lamv[:, i, :], prm[nm].partition_broadcast(128), writes=["lamv"])
        P.op("dve", lambda: nc.vector.tensor_scalar(small[:, 0:1], small[:, 0:1], 0.125, None, ALU.mult),
             reads=["small"], writes=["small"])
        for i in range(2):
            P.op("dve", lambda: nc.vector.tensor_tensor(lamj[:], lamv[:, 2 * i, :], lamv[:, 2 * i + 1, :], ALU.mult),
                 reads=["lamv"], writes=["lamj"])
            P.op("dve", lambda: nc.vector.reduce_sum(small[:, 4 + i:5 + i], lamj[:], axis=AX.X),
                 reads=["lamj"], writes=["small"])
        P.op("act", lambda: nc.scalar.activation(out=small[:, 6:8], in_=small[:, 4:6], func=AF.Exp),
             reads=["small"], writes=["small"])
        P.op("dve", lambda: nc.vector.scalar_tensor_tensor(small[:, 2:3], small[:, 7:8], -LAM0, small[:, 6:7],
                                                           ALU.add, ALU.subtract),
             reads=["small"], writes=["small"])
        P.op("dve", lambda: nc.vector.tensor_scalar(ghn[:], ghn[:], 1.0 - LAM0, None, ALU.mult),
             reads=["ghn"], writes=["ghn"])

        hT = T("hT", [128, 8, S], BF16)
        with ExitStack() as stA:
            emit_norm_transpose(nc, P, stA, x, hT, gbc, ident, banksb, "A", NT=4 * NQ)
            P.barrier()

        with ExitStack() as stH:
            TH = lambda name, shape, dt: stH.enter_context(nc.sbuf_tensor(name, shape, dt))
            QA = [TH("QA%d" % m, [128, S], BF16) for m in range(2)]
            KA = [TH("KA%d" % m, [128, S], BF16) for m in range(2)]
            VA = TH("VA", [128, 32, 130], BF16)
            ZT = TH("ZT", [128, S], BF16)
            GT = [TH("GT%d" % i, [128, S], BF16) for i in range(2)]
            Wh = [TH("Wh%d" % i, [128, 8, 4, 128], BF16) for i in range(2)]
            sq = [TH("sq%d" % i, [128, 512], BF16) for i in range(2)]
            rst = [TH("rst%d" % i, [128, 512], F32) for i in range(2)]
            pbuf = [TH("pbuf%d" % i, [128, 512], BF16) for i in range(4)]
            fin = TH("fin", [128, 4, 8], F32)
            o0 = TH("o0", [128, 4, 128], F32)
            o1 = TH("o1", [128, 4, 128], F32)
            onb = TH("onb", [128, 2, 4, 128], BF16)
            junk2 = TH("junk2", [128, 128], BF16)
            pending = []

            def flush_pending():
                while pending:
                    pending.pop(0)()
            P.op("pool", lambda: nc.gpsimd.memset(VA[:, :, 128:130], 1.0), writes=["VAones"])
            epsb = TH("epsb", [128, 1], F32)
            P.op("pool", lambda: nc.gpsimd.memset(epsb[:], EPS), writes=["epsb"])
            accs = [[TH("accs%d_%d" % (i, k), [128, 390], F32) for k in range(3)] for i in range(2)]

            w_in = prm["l0_w_in"].rearrange("(j p) f -> p j f", p=128)

            def load_head_weights(h):
                wb = Wh[h % 2]
                for sec in range(4):
                    P.dma("pool", wb[:, :, sec, :], w_in[:, :, sec * E + h * 128: sec * E + (h + 1) * 128],
                          writes=[("Wh", h % 2, sec)])

            load_head_weights(heads[0])
            for hi_, h in enumerate(heads):
                wb = Wh[h % 2]
                if hi_ + 1 < len(heads):
                    load_head_weights(heads[hi_ + 1])
                for m in range(2):
                    P.dma("sp", QA[m][64:70, :], cst["c_qb"][h, :, :], writes=[("QAb", m)])
                    P.dma("sp", KA[m][64:70, :], cst["c_kb"][h, :, :], writes=[("KAb", m)])
                for nt in range(NQ):
                    tok = slice(nt * 512, (nt + 1) * 512)
                    hkeys = [("hT", nt)]
                    for sec, bk in ((0, 0), (1, 1), (3, 2)):
                        for j in range(8):
                            P.op("pe", lambda: nc.tensor.matmul(banks[bk][:], wb[:, j, sec, :], hT[:, j, tok],
                                                                start=(j == 0), stop=(j == 7)),
                                 reads=hkeys + [("Wh", h % 2, sec)], writes=[("bk", bk)])
                    for qi, bk in ((0, 0), (1, 1)):
                        P.op("act", lambda: nc.scalar.activation(out=sq[qi][:], in_=banks[bk][:], func=AF.Square),
                             reads=[("bk", bk)], writes=[("sq", qi)])
                    for qi in range(2):
                        P.op("pe", lambda: nc.tensor.matmul(banks[3 + qi][:], blk[:], sq[qi][:], start=True, stop=True),
                             reads=["blk", ("sq", qi)], writes=[("bk", 3 + qi)])
                    for qi, (bk, tiles, gcol, key) in enumerate(((0, QA, 0, "QA"), (1, KA, 1, "KA"))):
                        P.op("act", lambda: nc.scalar.activation(out=rst[qi][:], in_=banks[3 + qi][:], func=AF.Ln,
                                                                 scale=1.0, bias=epsb[:, 0:1]),
                             reads=[("bk", 3 + qi), "epsb"], writes=[("rst", qi)])
                        P.op("act", lambda: nc.scalar.activation(out=rst[qi][:], in_=rst[qi][:], func=AF.Exp, scale=-0.5),
                             reads=[("rst", qi)], writes=[("rst", qi)])
                        for m in range(2):
                            ps = slice(m * 64, (m + 1) * 64)
                            P.op("dve", lambda: nc.vector.scalar_tensor_tensor(
                                tiles[m][0:64, tok], banks[bk][ps, :], small[ps, gcol:gcol + 1], rst[qi][ps, :],
                                ALU.mult, ALU.mult),
                                reads=[("bk", bk), ("rst", qi), "small"], writes=[(key, m, nt)])
                    P.op("act", lambda: nc.scalar.activation(out=ZT[:, tok], in_=banks[2][:], func=AF.Silu),
                         reads=[("bk", 2)], writes=[("ZT", nt)])
                    for i in range(4):
                        kt = nt * 4 + i
                        for j in range(8):
                            P.op("pe", lambda: nc.tensor.matmul(banks[5][:, i * 128:(i + 1) * 128],
                                                                hT[:, j, kt * 128:(kt + 1) * 128], wb[:, j, 2, :],
                                                                start=(i == 0 and j == 0), stop=(j == 7),
                                                                skip_group_check=True),
                                 reads=hkeys + [("Wh", h % 2, 2)], writes=[("bk", 5)])
                    P.op("act", lambda: nc.scalar.activation(
                        out=VA[:, nt * 4:(nt + 1) * 4, 0:128],
                        in_=banks[5][:, :].rearrange("p (i c) -> p i c", i=4), func=AF.Copy),
                        reads=[("bk", 5)], writes=[("VA", nt)])

                gt = GT[h % 2]
                ti = 0
                for Q in range(NQ):
                    qs = slice(Q * 512, (Q + 1) * 512)
                    tiles = [(m, kt) for m in range(2) for kt in range(4 * Q + 4)]
                    accreg = {}
                    slots = [(4, 0), (4, 1), (4, 2), (5, 0), (5, 1), (5, 2), (3, 0), (3, 1)]
                    for idx, (qb, m) in enumerate([(qb, m) for m in range(2) for qb in range(4)]):
                        accreg[(qb, m)] = slots[idx]
                    started = set()

                    def emit_S(i):
                        m, kt = tiles[i]
                        sb = (ti + i) % 3
                        diag = kt >= 4 * Q
                        P.op("pe", lambda: nc.tensor.matmul(banks[sb][:], KA[m][0:70, kt * 128:(kt + 1) * 128],
                                                            QA[m][0:70, qs], start=True, stop=not diag),
                             reads=[("KA", m, kt // 4), ("KAb", m), ("QA", m, Q), ("QAb", m)], writes=[("bk", sb)])
                        if diag:
                            P.op("pe", lambda: nc.tensor.matmul(banks[sb][:], ident[:], mask[:, kt - 4 * Q, :],
                                                                start=False, stop=True),
                                 reads=["ident", "mask"], writes=[("bk", sb)])

                    def emit_EP(i):
                        m, kt = tiles[i]
                        sb = (ti + i) % 3
                        pb = (ti + i) % 4
                        P.op("act", lambda: nc.scalar.activation(out=pbuf[pb][:], in_=banks[sb][:], func=AF.Exp),
                             reads=[("bk", sb)], writes=[("pbuf", pb)])
                        j = kt - 4 * Q
                        for qb in range(4):
                            if j >= 0 and qb < j:
                                continue
                            bk, r = accreg[(qb, m)]
                            first = bk not in started
                            started.add(bk)
                            P.op("pe", lambda: nc.tensor.matmul(banks[bk][:, r * 130:r * 130 + 129],
                                                                pbuf[pb][:, qb * 128:(qb + 1) * 128], VA[:, kt, 0:129],
                                                                start=first, stop=False, skip_group_check=True),
                                 reads=[("pbuf", pb), ("VA", kt // 4), "VAones"], writes=[("bk", bk)])

                    n = len(tiles)
                    emit_S(0)
                    if n > 1:
                        emit_S(1)
                    for i in range(n):
                        if i + 2 < n:
                            emit_S(i + 2)
                        emit_EP(i)
                        if i == min(n - 1, 24):
                            flush_pending()
                    ti += n
                    ab = Q % 2
                    sidx = {4: 0, 5: 1, 3: 2}
                    for bk_, ncol in ((4, 390), (5, 390), (3, 260)):
                        P.op("dve", lambda: nc.vector.tensor_copy(accs[ab][sidx[bk_]][:, 0:ncol], banks[bk_][:, 0:ncol]),
                             reads=[("bk", bk_)], writes=[("accs", ab, sidx[bk_])])
                    def tail(Q=Q, qs=qs, gt=gt, h=h, ab=ab, accreg=accreg):
                        ob = Q % 2
                        for qb in range(4):
                            b0, r0 = accreg[(qb, 0)]
                            b1, r1 = accreg[(qb, 1)]
                            s0, s1 = accs[ab][sidx[b0]], accs[ab][sidx[b1]]
                            k0, k1 = ("accs", ab, sidx[b0]), ("accs", ab, sidx[b1])
                            a0 = s0[:, r0 * 130:r0 * 130 + 128]
                            d0 = s0[:, r0 * 130 + 128:r0 * 130 + 129]
                            a1 = s1[:, r1 * 130:r1 * 130 + 128]
                            d1 = s1[:, r1 * 130 + 128:r1 * 130 + 129]
                            fk = ("fin", qb)
                            P.op("dve", lambda: nc.vector.reciprocal(fin[:, qb, 0:1], d0), reads=[k0], writes=[fk])
                            P.op("dve", lambda: nc.vector.reciprocal(fin[:, qb, 1:2], d1), reads=[k1], writes=[fk])
                            P.op("dve", lambda: nc.vector.tensor_tensor(fin[:, qb, 2:3], fin[:, qb, 1:2], small[:, 2:3], ALU.mult),
                                 reads=[fk, "small"], writes=[fk])
                            P.op("dve", lambda: nc.vector.tensor_scalar(o0[:, qb, :], a0, fin[:, qb, 0:1], None, ALU.mult),
                                 reads=[k0, fk], writes=[("o0", qb)])
                            P.op("dve", lambda: nc.vector.scalar_tensor_tensor(o1[:, qb, :], a1, fin[:, qb, 2:3], o0[:, qb, :],
                                                                               ALU.mult, ALU.add),
                                 reads=[k1, fk, ("o0", qb)], writes=[("o1", qb)])
                        ob = Q % 2
                        for qb in range(4):
                            fk = ("fin", qb)
                            P.op("act", lambda: nc.scalar.activation(out=junk2[:], in_=o1[:, qb, :], func=AF.Square,
                                                                     accum_out=fin[:, qb, 3:4]),
                                 reads=[("o1", qb)], writes=["junk2", fk])
                            P.op("act", lambda: nc.scalar.activation(out=fin[:, qb, 4:5], in_=fin[:, qb, 3:4], func=AF.Sqrt,
                                                                     scale=1.0 / 128, bias=EPS),
                                 reads=[fk], writes=[fk])
                            P.op("dve", lambda: nc.vector.reciprocal(fin[:, qb, 5:6], fin[:, qb, 4:5]), reads=[fk], writes=[fk])
                            P.op("dve", lambda: nc.vector.scalar_tensor_tensor(onb[:, ob, qb, :], o1[:, qb, :], fin[:, qb, 5:6], ghn[:],
                                                                               ALU.mult, ALU.mult),
                                 reads=[("o1", qb), fk, "ghn"], writes=[("onb", ob, qb)])
                        tb = banksb[Q % 2]
                        for qb in range(4):
                            P.op("pe", lambda: nc.tensor.transpose(tb[:, qb * 128:(qb + 1) * 128], onb[:, ob, qb, :], ident[:]),
                                 reads=[("onb", ob, qb), "ident"], writes=[("pbb", Q % 2)])
                        P.op("dve", lambda: nc.vector.tensor_tensor(gt[:, qs], tb[:, 0:512], ZT[:, qs], ALU.mult),
                             reads=[("pbb", Q % 2), ("ZT", Q)], writes=[("GT", h % 2)])
                    pending.append(tail)
                flush_pending()
                P.dma("sp", gscr[:, h, 0:NQ * 512], gt[:, 0:NQ * 512], reads=[("GT", h % 2)], writes=[("gscr", h)])
                if dump is not None and h == heads[-1]:
                    n = NQ * 512
                    allk = list(P.lastw.keys())
                    P.dma("sp", dump["d_hT"][:, :, :], hT[:, :, 0:n], reads=allk)
                    for m in range(2):
                        P.dma("sp", dump["d_QA"][m, :, :], QA[m][0:70, 0:n], reads=allk)
                        P.dma("sp", dump["d_KA"][m, :, :], KA[m][0:70, 0:n], reads=allk)
                    P.dma("sp", dump["d_VA"][:, :, :], VA[:, 0:4 * NQ, :], reads=allk)
                    P.dma("sp", dump["d_ZT"][:, :], ZT[:, 0:n], reads=allk)
                    P.dma("sp", dump["d_GT"][:, :], gt[:, 0:n], reads=allk)

        if dump is not None:
            return
        P.barrier()
        with ExitStack() as stD:
            TD = lambda name, shape, dt: stD.enter_context(nc.sbuf_tensor(name, shape, dt))
            Wo = TD("Wo", [128, NH, D], BF16)
            Gt = [TD("Gt%d" % i, [128, NH, 512], BF16) for i in range(2)]
            xr = [TD("xr%d" % i, [128, D], F32) for i in range(2)]
            x1t = [TD("x1t%d" % i, [128, D], F32) for i in range(2)]
            w_out = prm["l0_w_out"].rearrange("(h p) f -> p h f", p=128)
            for hh in range(0, NH, 4):
                P.dma("pool", Wo[:, hh:hh + 4, :], w_out[:, hh:hh + 4, :], writes=[("Wo", hh // 4)])
            for Tt in range(NQ):
                g = Gt[Tt % 2]
                P.dma("sp", g[:], gscr[:, :, Tt * 512:(Tt + 1) * 512], reads=[("gscr", h) for h in heads],
                      writes=[("Gt", Tt % 2)])
                for tt in range(4):
                    t = Tt * 4 + tt
                    b = t % 2
                    P.dma("sp", xr[b][:], x[t * 128:(t + 1) * 128, :], writes=[("xr", b)])
                    for half in range(2):
                        bk = (t * 2 + half) % 4
                        for h in range(NH):
                            P.op("pe", lambda: nc.tensor.matmul(banks[bk][:], g[:, h, tt * 128:(tt + 1) * 128],
                                                                Wo[:, h, half * 512:(half + 1) * 512],
                                                                start=(h == 0), stop=(h == NH - 1)),
                                 reads=[("Gt", Tt % 2), ("Wo", h // 4)], writes=[("bk", bk)])
                        P.op("dve", lambda: nc.vector.tensor_tensor(x1t[b][:, half * 512:(half + 1) * 512], banks[bk][:],
                                                                    xr[b][:, half * 512:(half + 1) * 512], ALU.add),
                             reads=[("bk", bk), ("xr", b)], writes=[("x1t", b)])
                    P.dma("sp", x1_out[t * 128:(t + 1) * 128, :], x1t[b][:], reads=[("x1t", b)], writes=[("x1", t)])


def build_l0(NQ=8):
    nc = bass.Bass("TRN2", target_bir_lowering=False)
    x = nc.dram_tensor("x", [S, D], F32, kind="ExternalInput").ap()
    prm = {k: nc.dram_tensor(k, shp, F32, kind="ExternalInput").ap() for k, shp in L0_PARAMS.items()}
    cst = {k: nc.dram_tensor(k, shp, dt, kind="ExternalInput").ap() for k, (shp, dt) in CONST_SHAPES.items()}
    out = nc.dram_tensor("out", [S, D], F32, kind="ExternalOutput").ap()
    gscr = nc.dram_tensor("gscr", [128, NH, S], BF16, kind="Internal").ap()
    with ExitStack() as st:
        P = Prog(nc, st)
        emit_layer0(nc, P, x, prm, cst, out, gscr, NQ=NQ)
        P.finish()
        build_l0.stats = {"ninst": P.ninst, "nsem": P._nsem, "dma_uses": sum(sl[1] // 16 for sl in P.dma_sems),
                          "cur": {e: c[1] for e, c in P.cur.items()}}
    return nc


def build_l0_debug(heads=(0,), NQ=2):
    nc = bass.Bass("TRN2", target_bir_lowering=False)
    x = nc.dram_tensor("x", [S, D], F32, kind="ExternalInput").ap()
    prm = {k: nc.dram_tensor(k, shp, F32, kind="ExternalInput").ap() for k, shp in L0_PARAMS.items()}
    cst = {k: nc.dram_tensor(k, shp, dt, kind="ExternalInput").ap() for k, (shp, dt) in CONST_SHAPES.items()}
    n = NQ * 512
    dshapes = {"d_hT": [128, 8, n], "d_QA": [2, 70, n], "d_KA": [2, 70, n], "d_VA": [128, 4 * NQ, 130],
               "d_ZT": [128, n], "d_GT": [128, n]}
    dump = {k: nc.dram_tensor(k, shp, BF16, kind="ExternalOutput").ap() for k, shp in dshapes.items()}
    gscr = nc.dram_tensor("gscr", [128, NH, S], BF16, kind="Internal").ap()
    with ExitStack() as st:
        P = Prog(nc, st)
        emit_layer0(nc, P, x, prm, cst, None, gscr, heads=heads, NQ=NQ, dump=dump)
        P.finish()
    return nc


def run_layer0_spmd(inputs, n_cores=8):
    nc = build_l0()
    cst = host_consts()
    in_maps = []
    for b in range(n_cores):
        m = {"x": np.ascontiguousarray(inputs["x"][b])}
        for k in L0_PARAMS:
            m[k] = np.ascontiguousarray(inputs[k])
        m.update(cst)
        in_maps.append(m)
    res = run_bass_kernel_spmd(nc, in_maps, core_ids=list(range(n_cores)))
    return np.stack([np.asarray(r["out"]) for r in res.results], axis=0)


TWO_PI = 6.283185307179586
MAGIC = 12582912.0
L1_PARAMS = {
    "l1_norm_g": [D], "l1_w_in": [D, 2 * E], "l1_lam_re": [128, 64], "l1_lam_im": [128, 64], "l1_log_dt": [128],
    "l1_b_re": [128, 64, 16], "l1_b_im": [128, 64, 16], "l1_c_re": [128, 16, 64], "l1_c_im": [128, 16, 64],
    "l1_d": [E], "l1_w_glu": [E, E], "l1_b_glu": [E], "l1_w_out": [E, D],
}


def emit_s5_tables(nc, P, st, prm, NPOW=17):
    T = lambda name, shape, dt: st.enter_context(nc.sbuf_tensor(name, shape, dt))
    tb = T("s5tb", [128, 24, 64], F32)
    pw = T("s5pw", [128, NPOW, 2, 64], F32)
    LR, LI, LDT, DT, MAG, TH, K_, SN, CS, ABR, ABI, DEN, NR, GR, GI, T1, T2 = range(17)
    V = lambda i: tb[:, i, :]
    for g2 in range(2):
        ps = slice(g2 * 64, (g2 + 1) * 64)
        for i, nm in ((LR, "l1_lam_re"), (LI, "l1_lam_im")):
            src = prm[nm].rearrange("(pr g2) p -> g2 p pr", g2=2)
            P.dma("sp", tb[ps, i, :], src[g2], writes=[("tb", i, g2)], allow_slow_non_contiguous=True)
        src = prm["l1_log_dt"].rearrange("(pr g2) -> g2 pr", g2=2)
        P.dma("sp", tb[ps, LDT, :], src[g2:g2 + 1, :].to_broadcast([64, 64]), writes=[("tb", LDT, g2)],
              allow_slow_non_contiguous=True)
    rd = lambda *idx: [("tb", i, g2) for i in idx for g2 in range(2)] + [("tb", i) for i in idx]
    op = P.op
    op("act", lambda: nc.scalar.activation(out=V(DT), in_=V(LDT), func=AF.Exp), reads=rd(LDT), writes=[("tb", DT)])
    op("dve", lambda: nc.vector.tensor_tensor(V(T1), V(LR), V(DT), ALU.mult), reads=rd(LR, DT), writes=[("tb", T1)])
    op("act", lambda: nc.scalar.activation(out=V(MAG), in_=V(T1), func=AF.Exp), reads=rd(T1), writes=[("tb", MAG)])
    op("dve", lambda: nc.vector.tensor_tensor(V(TH), V(LI), V(DT), ALU.mult), reads=rd(LI, DT), writes=[("tb", TH)])
    for dst, shift in ((SN, 0.0), (CS, math.pi / 2)):
        op("dve", lambda: nc.vector.tensor_scalar(V(T2), V(TH), shift, 1.0 / TWO_PI, ALU.add, ALU.mult),
           reads=rd(TH), writes=[("tb", T2)])
        op("dve", lambda: nc.vector.tensor_scalar(V(K_), V(T2), MAGIC, None, ALU.add), reads=rd(T2), writes=[("tb", K_)])
        op("dve", lambda: nc.vector.tensor_scalar(V(K_), V(K_), MAGIC, None, ALU.subtract), reads=rd(K_), writes=[("tb", K_)])
        op("dve", lambda: nc.vector.scalar_tensor_tensor(V(T2), V(K_), -TWO_PI, V(TH), ALU.mult, ALU.add),
           reads=rd(K_, TH), writes=[("tb", T2)])
        op("act", lambda: nc.scalar.activation(out=V(dst), in_=V(T2), func=AF.Sin, scale=1.0, bias=shift),
           reads=rd(T2), writes=[("tb", dst)])
    op("dve", lambda: nc.vector.tensor_tensor(V(ABR), V(MAG), V(CS), ALU.mult), reads=rd(MAG, CS), writes=[("tb", ABR)])
    op("dve", lambda: nc.vector.tensor_tensor(V(ABI), V(MAG), V(SN), ALU.mult), reads=rd(MAG, SN), writes=[("tb", ABI)])
    op("dve", lambda: nc.vector.tensor_tensor(V(T1), V(LR), V(LR), ALU.mult), reads=rd(LR), writes=[("tb", T1)])
    op("dve", lambda: nc.vector.tensor_tensor(V(T2), V(LI), V(LI), ALU.mult), reads=rd(LI), writes=[("tb", T2)])
    op("dve", lambda: nc.vector.tensor_tensor(V(DEN), V(T1), V(T2), ALU.add), reads=rd(T1, T2), writes=[("tb", DEN)])
    op("dve", lambda: nc.vector.reciprocal(V(DEN), V(DEN)), reads=rd(DEN), writes=[("tb", DEN)])
    op("dve", lambda: nc.vector.tensor_scalar(V(NR), V(ABR), -1.0, None, ALU.add), reads=rd(ABR), writes=[("tb", NR)])
    op("dve", lambda: nc.vector.tensor_tensor(V(T1), V(NR), V(LR), ALU.mult), reads=rd(NR, LR), writes=[("tb", T1)])
    op("dve", lambda: nc.vector.tensor_tensor(V(T2), V(ABI), V(LI), ALU.mult), reads=rd(ABI, LI), writes=[("tb", T2)])
    op("dve", lambda: nc.vector.tensor_tensor(V(GR), V(T1), V(T2), ALU.add), reads=rd(T1, T2), writes=[("tb", GR)])
    op("dve", lambda: nc.vector.tensor_tensor(V(GR), V(GR), V(DEN), ALU.mult), reads=rd(GR, DEN), writes=[("tb", GR)])
    op("dve", lambda: nc.vector.tensor_tensor(V(T1), V(ABI), V(LR), ALU.mult), reads=rd(ABI, LR), writes=[("tb", T1)])
    op("dve", lambda: nc.vector.tensor_tensor(V(T2), V(NR), V(LI), ALU.mult), reads=rd(NR, LI), writes=[("tb", T2)])
    op("dve", lambda: nc.vector.tensor_tensor(V(GI), V(T1), V(T2), ALU.subtract), reads=rd(T1, T2), writes=[("tb", GI)])
    op("dve", lambda: nc.vector.tensor_tensor(V(GI), V(GI), V(DEN), ALU.mult), reads=rd(GI, DEN), writes=[("tb", GI)])
    op("pool", lambda: nc.gpsimd.memset(pw[:, 0, 0, :], 1.0), writes=[("pw", 0)])
    op("pool", lambda: nc.gpsimd.memset(pw[:, 0, 1, :], 0.0), writes=[("pw", 0)])
    for t in range(1, NPOW):
        pr_, pi_ = pw[:, t - 1, 0, :], pw[:, t - 1, 1, :]
        op("dve", lambda: nc.vector.tensor_tensor(V(T1), pr_, V(ABR), ALU.mult), reads=[("pw", t - 1)] + rd(ABR), writes=[("tb", T1)])
        op("dve", lambda: nc.vector.tensor_tensor(V(T2), pi_, V(ABI), ALU.mult), reads=[("pw", t - 1)] + rd(ABI), writes=[("tb", T2)])
        op("dve", lambda: nc.vector.tensor_tensor(pw[:, t, 0, :], V(T1), V(T2), ALU.subtract), reads=rd(T1, T2), writes=[("pw", t)])
        op("dve", lambda: nc.vector.tensor_tensor(V(T1), pr_, V(ABI), ALU.mult), reads=[("pw", t - 1)] + rd(ABI), writes=[("tb", T1)])
        op("dve", lambda: nc.vector.tensor_tensor(V(T2), pi_, V(ABR), ALU.mult), reads=[("pw", t - 1)] + rd(ABR), writes=[("tb", T2)])
        op("dve", lambda: nc.vector.tensor_tensor(pw[:, t, 1, :], V(T1), V(T2), ALU.add), reads=rd(T1, T2), writes=[("pw", t)])
    return {"tb": tb, "pw": pw, "idx": dict(ABR=ABR, ABI=ABI, GR=GR, GI=GI)}


def build_s5_tables_debug():
    nc = bass.Bass("TRN2", target_bir_lowering=False)
    prm = {k: nc.dram_tensor(k, shp, F32, kind="ExternalInput").ap() for k, shp in L1_PARAMS.items()
           if k in ("l1_lam_re", "l1_lam_im", "l1_log_dt")}
    d_tb = nc.dram_tensor("d_tb", [128, 24, 64], F32, kind="ExternalOutput").ap()
    d_pw = nc.dram_tensor("d_pw", [128, 17, 2, 64], F32, kind="ExternalOutput").ap()
    with ExitStack() as st:
        P = Prog(nc, st)
        r = emit_s5_tables(nc, P, st, prm)
        allk = list(P.lastw.keys())
        P.dma("sp", d_tb[:, :, :], r["tb"][:], reads=allk)
        P.dma("sp", d_pw[:, :, :, :], r["pw"][:], reads=allk)
        P.finish()
    return nc


def emit_s5_operands(nc, P, st, prm, tabs, ident, gts, o_K, o_WS, o_WC, banks, banksb, o_BB=None):
    T = lambda name, shape, dt: st.enter_context(nc.sbuf_tensor(name, shape, dt))
    tb, pw, ix = tabs["tb"], tabs["pw"], tabs["idx"]
    Bs = [T("s5B%d" % i, [128, 64, 16], F32) for i in range(2)]
    Cs = [T("s5C%d" % i, [128, 64, 16], F32) for i in range(2)]
    BB = [T("s5BB%d" % i, [128, 64, 16], F32) for i in range(2)]
    t1 = T("s5t1", [128, 64, 16], F32)
    t2 = T("s5t2", [128, 64, 16], F32)
    t3 = T("s5t3", [128, 64, 16], F32)
    t4 = T("s5t4", [128, 64, 16], F32)
    pwn = T("s5pwn", [128, 17, 64], F32)
    dsk = T("s5dsk", [128, 16], F32)
    XB2 = [T("s5XB%d" % i, [128, 16, 2, 4, 32], BF16) for i in range(2)]
    WC2 = [T("s5WC%d" % i, [128, 16, 2, 4, 32], BF16) for i in range(2)]
    CB2 = [T("s5CB%d" % i, [128, 2, 4, 32], BF16) for i in range(2)]
    Kbd2 = [T("s5Kbd%d" % i, [128, 16, 128], BF16) for i in range(2)]
    WS2 = [T("s5WS%d" % i, [128, 16, 2, 128], BF16) for i in range(2)]
    identf = T("s5idf", [128, 128], F32)
    op = P.op
    for g2 in range(2):
        ps = slice(g2 * 64, (g2 + 1) * 64)
        for i, nm in enumerate(("l1_b_re", "l1_b_im")):
            P.dma("sp", Bs[i][ps, :, :], prm[nm].rearrange("(pr g2) p c -> g2 p pr c", g2=2)[g2], writes=[("Bs", i, g2)])
        for i, nm in enumerate(("l1_c_re", "l1_c_im")):
            src = prm[nm].rearrange("(pr g2) co p -> g2 p pr co", g2=2)[g2]
            for pr_ in range(64):
                P.dma("sp", Cs[i][ps, pr_, :], src[:, pr_, :],
                      writes=[("Cs", i, g2, pr_ // 8)], allow_slow_non_contiguous=True)
    P.dma("sp", dsk[:], prm["l1_d"].rearrange("(gt p) -> p gt", p=128), writes=["dsk"], allow_slow_non_contiguous=True)
    op("dve", lambda: nc.vector.tensor_copy(identf[:], ident[:]), reads=["ident"], writes=["identf"])
    for i in range(2):
        for tl, key in ((XB2[i], "XB"), (WC2[i], "WC"), (CB2[i], "CB"), (Kbd2[i], "Kbd")):
            op("pool", lambda: nc.gpsimd.memset(tl[:], 0.0), writes=[key + str(i)])
    rB = [("Bs", i, g2) for i in range(2) for g2 in range(2)]
    rC = [("Cs", i, g2, b) for i in range(2) for g2 in range(2) for b in range(8)]
    bc = lambda ap2, n: ap2.unsqueeze(2).to_broadcast([128, n, 16])
    GR, GI = tb[:, ix["GR"], :], tb[:, ix["GI"], :]
    op("dve", lambda: nc.vector.tensor_tensor(t1[:], Bs[0][:], bc(GR, 64), ALU.mult), reads=rB + [("tb", ix["GR"])], writes=["t1"])
    op("dve", lambda: nc.vector.tensor_tensor(t2[:], Bs[1][:], bc(GI, 64), ALU.mult), reads=rB + [("tb", ix["GI"])], writes=["t2"])
    op("dve", lambda: nc.vector.tensor_tensor(BB[0][:], t1[:], t2[:], ALU.subtract), reads=["t1", "t2"], writes=["BB0"])
    op("dve", lambda: nc.vector.tensor_tensor(t1[:], Bs[1][:], bc(GR, 64), ALU.mult), reads=rB + [("tb", ix["GR"])], writes=["t1"])
    op("dve", lambda: nc.vector.tensor_tensor(t2[:], Bs[0][:], bc(GI, 64), ALU.mult), reads=rB + [("tb", ix["GI"])], writes=["t2"])
    op("dve", lambda: nc.vector.tensor_tensor(BB[1][:], t1[:], t2[:], ALU.add), reads=["t1", "t2"], writes=["BB1"])
    if o_BB is not None:
        for i in range(2):
            P.dma("sp", o_BB[i, :, :, :], BB[i][:], reads=["BB%d" % i])

    def cmul_blocks(dst, k, Ar, Ai, Pr, Pi, prs, neg_im, rA, rP, wkey):
        a, b_ = t1[:, 0:4, :], t1[:, 4:8, :]
        c_, d_ = t2[:, 0:4, :], t2[:, 4:8, :]
        op("dve", lambda: nc.vector.tensor_tensor(a, Ar[:, prs, :], bc(Pr[:, prs], 4), ALU.mult), reads=rA + rP, writes=["t1"])
        op("dve", lambda: nc.vector.tensor_tensor(b_, Ai[:, prs, :], bc(Pi[:, prs], 4), ALU.mult), reads=rA + rP, writes=["t1"])
        op("dve", lambda: nc.vector.tensor_tensor(c_, Ar[:, prs, :], bc(Pi[:, prs], 4), ALU.mult), reads=rA + rP, writes=["t2"])
        op("dve", lambda: nc.vector.tensor_tensor(d_, Ai[:, prs, :], bc(Pr[:, prs], 4), ALU.mult), reads=rA + rP, writes=["t2"])
        for g2 in range(2):
            ps = slice(g2 * 64, (g2 + 1) * 64)
            cs = slice(g2 * 16, (g2 + 1) * 16)
            op("dve", lambda: nc.vector.tensor_tensor(dst[ps, k, 0, :, cs], a[ps], b_[ps], ALU.subtract),
               reads=["t1"], writes=[wkey])
            if neg_im:
                op("dve", lambda: nc.vector.scalar_tensor_tensor(dst[ps, k, 1, :, cs], c_[ps], -1.0, d_[ps], ALU.mult, ALU.subtract),
                   reads=["t2"], writes=[wkey])
            else:
                op("dve", lambda: nc.vector.tensor_tensor(dst[ps, k, 1, :, cs], c_[ps], d_[ps], ALU.add),
                   reads=["t2"], writes=[wkey])

    rPW = [("pw", t) for t in range(17)]
    op("dve", lambda: nc.vector.tensor_scalar(pwn[:], pw[:, :, 1, :], -1.0, None, ALU.mult), reads=rPW, writes=["pwn"])

    def cmul_all_d(dst, Ar, Ai, d0, prs, neg_im, rA, wkey):
        V4 = lambda t: t[:, :, :].rearrange("p (d j) c -> p d j c", d=16)
        a, b_, c_, d_ = V4(t1), V4(t2), V4(t3), V4(t4)
        bA = lambda A: A[:, prs, :].unsqueeze(1).to_broadcast([128, 16, 4, 16])
        Pr = pw[:, d0:d0 + 16, 0, prs].unsqueeze(3).to_broadcast([128, 16, 4, 16])
        Pi = pw[:, d0:d0 + 16, 1, prs].unsqueeze(3).to_broadcast([128, 16, 4, 16])
        Pin = pwn[:, d0:d0 + 16, prs].unsqueeze(3).to_broadcast([128, 16, 4, 16])
        op("dve", lambda: nc.vector.tensor_tensor(a, bA(Ar), Pr, ALU.mult), reads=rA + rPW, writes=["t1"])
        op("dve", lambda: nc.vector.tensor_tensor(b_, bA(Ai), Pi, ALU.mult), reads=rA + rPW, writes=["t2"])
        op("dve", lambda: nc.vector.tensor_tensor(c_, bA(Ar), Pin if neg_im else Pi, ALU.mult), reads=rA + rPW + ["pwn"], writes=["t3"])
        op("dve", lambda: nc.vector.tensor_tensor(d_, bA(Ai), Pr, ALU.mult), reads=rA + rPW, writes=["t4"])
        for g2 in range(2):
            ps = slice(g2 * 64, (g2 + 1) * 64)
            cs = slice(g2 * 16, (g2 + 1) * 16)
            op("dve", lambda: nc.vector.tensor_tensor(dst[ps, :, 0, :, cs], a[ps], b_[ps], ALU.subtract),
               reads=["t1", "t2"], writes=[wkey])
            op("dve", lambda: nc.vector.tensor_tensor(dst[ps, :, 1, :, cs], c_[ps], d_[ps], ALU.subtract if neg_im else ALU.add),
               reads=["t3", "t4"], writes=[wkey])

    for gi, gt in enumerate(gts):
        pb_ = str(gi % 2)
        XB, WC, CB, Kbd, WS = XB2[gi % 2], WC2[gi % 2], CB2[gi % 2], Kbd2[gi % 2], WS2[gi % 2]
        kXB, kWC, kCB, kKbd, kWS = "XB" + pb_, "WC" + pb_, "CB" + pb_, "Kbd" + pb_, "WS" + pb_
        prs = slice(gt * 4, gt * 4 + 4)
        cmul_all_d(XB, BB[0], BB[1], 0, prs, False, ["BB0", "BB1"], kXB)
        cmul_all_d(WC, Cs[0], Cs[1], 1, prs, True, rC, kWC)
        for g2 in range(2):
            ps = slice(g2 * 64, (g2 + 1) * 64)
            cs = slice(g2 * 16, (g2 + 1) * 16)
            for i in range(2):
                op("dve", lambda: nc.vector.tensor_copy(CB[ps, i, :, cs], Cs[i][ps, prs, :]), reads=rC, writes=[kCB])
        op("dve", lambda: nc.vector.tensor_scalar(CB[:, 1, :, :], CB[:, 1, :, :], -1.0, None, ALU.mult), reads=[kCB], writes=[kCB])
        for j in range(4):
            kb = banks[j % 2]
            for d in range(16):
                for i in range(2):
                    op("pe", lambda: nc.tensor.matmul(kb[0:32, d * 32:(d + 1) * 32], XB[:, d, i, j, :], CB[:, i, j, :],
                                                      start=(d == 0 and i == 0), stop=(i == 1), skip_group_check=True),
                       reads=[kXB, kCB], writes=[("kbk", j % 2)])
            op("act", lambda: nc.scalar.activation(out=Kbd[j * 32:(j + 1) * 32, :, j * 32:(j + 1) * 32],
                                                   in_=kb[0:32, :].rearrange("p (d c) -> p d c", d=16), func=AF.Copy),
               reads=[("kbk", j % 2)], writes=[kKbd])
        op("dve", lambda: nc.vector.scalar_tensor_tensor(Kbd[:, 0, :], identf[:], dsk[:, gt:gt + 1], Kbd[:, 0, :], ALU.mult, ALU.add),
           reads=["identf", "dsk", kKbd], writes=[kKbd])
        P.dma("sp", o_K[gt, :, :, :], Kbd[:], reads=[kKbd], writes=[("oK", gt)])
        for j in range(4):
            for i in range(2):
                for half in range(2):
                    tbk = banksb[(j * 4 + i * 2 + half) % 2]
                    for mm in range(8):
                        m_ = half * 8 + mm
                        op("pe", lambda: nc.tensor.transpose(tbk[0:32, mm * 128:(mm + 1) * 128], XB[:, 15 - m_, i, j, :], ident[:]),
                           reads=[kXB, "ident"], writes=[("tbk", (j * 4 + i * 2 + half) % 2)])
                    op("act", lambda: nc.scalar.activation(out=WS[j * 32:(j + 1) * 32, half * 8:(half + 1) * 8, i, :],
                                                           in_=tbk[0:32, :].rearrange("p (m c) -> p m c", m=8), func=AF.Copy),
                       reads=[("tbk", (j * 4 + i * 2 + half) % 2)], writes=[kWS])
        P.dma("sp", o_WS[gt, :, :, :, :], WS[:], reads=[kWS], writes=[("oWS", gt)])
        P.dma("sp", o_WC[gt, :, :, :, :, :], WC[:], reads=[kWC], writes=[("oWC", gt)])


def build_s5_operands_debug(gts=(0, 5)):
    nc = bass.Bass("TRN2", target_bir_lowering=False)
    prm = {k: nc.dram_tensor(k, shp, F32, kind="ExternalInput").ap() for k, shp in L1_PARAMS.items()
           if k in ("l1_lam_re", "l1_lam_im", "l1_log_dt", "l1_b_re", "l1_b_im", "l1_c_re", "l1_c_im", "l1_d")}
    c_ident = nc.dram_tensor("c_ident", [128, 128], BF16, kind="ExternalInput").ap()
    o_K = nc.dram_tensor("o_K", [16, 128, 16, 128], BF16, kind="ExternalOutput").ap()
    o_WS = nc.dram_tensor("o_WS", [16, 128, 16, 2, 128], BF16, kind="ExternalOutput").ap()
    o_WC = nc.dram_tensor("o_WC", [16, 128, 16, 2, 4, 32], BF16, kind="ExternalOutput").ap()
    o_BB = nc.dram_tensor("o_BB", [2, 128, 64, 16], F32, kind="ExternalOutput").ap()
    with ExitStack() as st:
        P = Prog(nc, st)
        banks = [st.enter_context(nc.psum_tensor("bank%d" % i, [128, 512], F32)) for i in range(2)]
        banksb = [st.enter_context(nc.psum_tensor("bankb%d" % i, [128, 1024], BF16)) for i in range(2)]
        ident = st.enter_context(nc.sbuf_tensor("ident", [128, 128], BF16))
        P.dma("sp", ident[:], c_ident[:, :], writes=["ident"])
        tabs = emit_s5_tables(nc, P, st, prm)
        emit_s5_operands(nc, P, st, prm, tabs, ident, list(gts), o_K, o_WS, o_WC, banks, banksb, o_BB=o_BB)
        P.finish()
    return nc


NCH = 64
NTS = NCH * 16
KS_LEVELS = NCH.bit_length() - 1


def alloc_ks(nc, st):
    T = lambda name, shape, dt: st.enter_context(nc.sbuf_tensor(name, shape, dt))
    return {"ks": T("s5ks", [128, 6, 2, 64], F32), "kt1": T("s5kt1", [128, 64], F32), "kt2": T("s5kt2", [128, 64], F32),
            "ksn": T("s5ksn", [128, 6, 64], F32)}


def emit_ks_table(nc, P, st, tabs, bufs=None):
    if bufs is None:
        bufs = alloc_ks(nc, st)
    pw = tabs["pw"]
    ks, kt1, kt2 = bufs["ks"], bufs["kt1"], bufs["kt2"]
    op = P.op
    op("dve", lambda: nc.vector.tensor_copy(ks[:, 0, :, :], pw[:, 16, :, :]), reads=[("pw", 16)], writes=[("ks", 0)])
    for k in range(1, 6):
        a, b_ = ks[:, k - 1, 0, :], ks[:, k - 1, 1, :]
        op("dve", lambda: nc.vector.tensor_tensor(kt1[:], a, a, ALU.mult), reads=[("ks", k - 1)], writes=["kt1"])
        op("dve", lambda: nc.vector.tensor_tensor(kt2[:], b_, b_, ALU.mult), reads=[("ks", k - 1)], writes=["kt2"])
        op("dve", lambda: nc.vector.tensor_tensor(ks[:, k, 0, :], kt1[:], kt2[:], ALU.subtract), reads=["kt1", "kt2"], writes=[("ks", k)])
        op("dve", lambda: nc.vector.tensor_tensor(kt1[:], a, b_, ALU.mult), reads=[("ks", k - 1)], writes=["kt1"])
        op("dve", lambda: nc.vector.tensor_scalar(ks[:, k, 1, :], kt1[:], 2.0, None, ALU.mult), reads=["kt1"], writes=[("ks", k)])
    ksn = bufs["ksn"]
    op("dve", lambda: nc.vector.tensor_scalar(ksn[:], ks[:, :, 1, :], -1.0, None, ALU.mult), reads=[("ks", k) for k in range(6)], writes=["ksn"])
    return ks, ksn


def emit_s5_core(nc, P, gt, uT, Kbd, WS, WC, ks, ksn, carry, hbuf, hprev, Yb, Sb, yout, first_tile, tag="", ukey=("uT",), ykey=("yout",)):
    op = P.op
    uv = uT[:, :].rearrange("p (n m) -> p m n", m=16)
    h0 = hbuf[0]
    for j in range(4):
        rows = slice(j * 32, (j + 1) * 32)
        first = True
        for ri in range(2):
            for m in range(16):
                op("pe", lambda: nc.tensor.matmul(Sb[j][:, ri * NCH:(ri + 1) * NCH], WS[rows, m, ri, :], uv[rows, m, :],
                                                  start=first, stop=(m == 15), skip_group_check=True,
                                                  tile_position=(32 * j, 0)),
                   reads=[ukey, ("WS" + tag,)], writes=[("Sb", j)])
                first = False
        op("dve", lambda: nc.vector.tensor_copy(h0[:, j, :, :], Sb[j][:, 0:2 * NCH].rearrange("p (r n) -> p r n", r=2)),
           reads=[("Sb", j)], writes=[("hb", 0)])
    for j in range(4):
        pr = gt * 4 + j
        ar, ai, nai = ks[:, 0, 0, pr:pr + 1], ks[:, 0, 1, pr:pr + 1], ksn[:, 0, pr:pr + 1]
        if not first_tile:
            cr, ci = carry[:, j, 0:1], carry[:, j, 1:2]
            op("dve", lambda: nc.vector.scalar_tensor_tensor(h0[:, j, 0, 0:1], cr, ar, h0[:, j, 0, 0:1], ALU.mult, ALU.add),
               reads=[("carry", gt), ("hb", 0), ("ks", 0)], writes=[("hb", 0)])
            op("dve", lambda: nc.vector.scalar_tensor_tensor(h0[:, j, 0, 0:1], ci, nai, h0[:, j, 0, 0:1], ALU.mult, ALU.add),
               reads=[("carry", gt), ("hb", 0), "ksn"], writes=[("hb", 0)])
            op("dve", lambda: nc.vector.scalar_tensor_tensor(h0[:, j, 1, 0:1], ci, ar, h0[:, j, 1, 0:1], ALU.mult, ALU.add),
               reads=[("carry", gt), ("hb", 0), ("ks", 0)], writes=[("hb", 0)])
            op("dve", lambda: nc.vector.scalar_tensor_tensor(h0[:, j, 1, 0:1], cr, ai, h0[:, j, 1, 0:1], ALU.mult, ALU.add),
               reads=[("carry", gt), ("hb", 0), ("ks", 0)], writes=[("hb", 0)])
    cur = 0
    for k in range(KS_LEVELS):
        s = 1 << k
        src, dst = hbuf[cur], hbuf[1 - cur]
        rk = [("hb", cur), ("ks", k), "ksn"]
        wk = [("hb", 1 - cur)]
        op("dve", lambda: nc.vector.tensor_copy(dst[:, :, :, 0:s], src[:, :, :, 0:s]), reads=rk, writes=wk)
        for j in range(4):
            pr = gt * 4 + j
            ar, ai, nai = ks[:, k, 0, pr:pr + 1], ks[:, k, 1, pr:pr + 1], ksn[:, k, pr:pr + 1]
            sr, si = src[:, j, 0, 0:NCH - s], src[:, j, 1, 0:NCH - s]
            op("dve", lambda: nc.vector.scalar_tensor_tensor(dst[:, j, 0, s:NCH], sr, ar, src[:, j, 0, s:NCH], ALU.mult, ALU.add), reads=rk, writes=wk)
            op("dve", lambda: nc.vector.scalar_tensor_tensor(dst[:, j, 0, s:NCH], si, nai, dst[:, j, 0, s:NCH], ALU.mult, ALU.add), reads=rk + wk, writes=wk)
            op("dve", lambda: nc.vector.scalar_tensor_tensor(dst[:, j, 1, s:NCH], si, ar, src[:, j, 1, s:NCH], ALU.mult, ALU.add), reads=rk, writes=wk)
            op("dve", lambda: nc.vector.scalar_tensor_tensor(dst[:, j, 1, s:NCH], sr, ai, dst[:, j, 1, s:NCH], ALU.mult, ALU.add), reads=rk + wk, writes=wk)
        cur = 1 - cur
    H = hbuf[cur]
    hk = [("hp",)]
    if first_tile:
        op("pool", lambda: nc.gpsimd.memset(hprev[:, :, :, 0:1], 0.0), writes=hk)
    else:
        op("dve", lambda: nc.vector.tensor_copy(hprev[:, :, :, 0], carry[:, :, :]), reads=[("carry", gt)], writes=hk)
    op("dve", lambda: nc.vector.tensor_copy(hprev[:, :, :, 1:NCH], H[:, :, :, 0:NCH - 1]), reads=[("hb", cur)], writes=hk)
    op("dve", lambda: nc.vector.tensor_copy(carry[:, :, :], H[:, :, :, NCH - 1]), reads=[("hb", cur)] + hk, writes=[("carry", gt)])
    started = set()
    for m in range(16):
        bk = m // 8
        yb = Yb[bk]
        cols = slice((m % 8) * NCH, (m % 8 + 1) * NCH)
        for mp in range(m + 1):
            f_ = bk not in started
            started.add(bk)
            op("pe", lambda: nc.tensor.matmul(yb[:, cols], Kbd[:, m - mp, :], uv[:, mp, :], start=f_, stop=False, skip_group_check=True),
               reads=[ukey, ("Kbd" + tag,)], writes=[("Yb", bk)])
    for m in range(16):
        bk = m // 8
        yb = Yb[bk]
        cols = slice((m % 8) * NCH, (m % 8 + 1) * NCH)
        for j in range(4):
            for ri in range(2):
                op("pe", lambda: nc.tensor.matmul(yb[j * 32:(j + 1) * 32, cols], WC[:, m, ri, j, :], hprev[:, j, ri, :],
                                                  start=False, stop=(ri == 1), skip_group_check=True,
                                                  tile_position=(0, 32 * j)),
                   reads=hk + [("WC" + tag,)], writes=[("Yb", bk)])
    yv = yout[:, :].rearrange("p (n m) -> p m n", m=16)
    for bk in range(2):
        op("act", lambda: nc.scalar.activation(out=yv[:, bk * 8:(bk + 1) * 8, :], in_=Yb[bk][:, 0:8 * NCH].rearrange("p (m n) -> p m n", m=8), func=AF.Copy),
           reads=[("Yb", bk)], writes=[ykey])


def build_s5_core_debug(gt=0, ntiles=2):
    nc = bass.Bass("TRN2", target_bir_lowering=False)
    names = ("l1_lam_re", "l1_lam_im", "l1_log_dt", "l1_b_re", "l1_b_im", "l1_c_re", "l1_c_im", "l1_d")
    prm = {k: nc.dram_tensor(k, L1_PARAMS[k], F32, kind="ExternalInput").ap() for k in names}
    c_ident = nc.dram_tensor("c_ident", [128, 128], BF16, kind="ExternalInput").ap()
    u_in = nc.dram_tensor("u_in", [128, ntiles * NTS], BF16, kind="ExternalInput").ap()
    y_out = nc.dram_tensor("y_out", [128, ntiles * NTS], F32, kind="ExternalOutput").ap()
    o_K = nc.dram_tensor("o_K", [16, 128, 16, 128], BF16, kind="Internal").ap()
    o_WS = nc.dram_tensor("o_WS", [16, 128, 16, 2, 128], BF16, kind="Internal").ap()
    o_WC = nc.dram_tensor("o_WC", [16, 128, 16, 2, 4, 32], BF16, kind="Internal").ap()
    with ExitStack() as st:
        P = Prog(nc, st)
        T = lambda name, shape, dt: st.enter_context(nc.sbuf_tensor(name, shape, dt))
        banks = [st.enter_context(nc.psum_tensor("bank%d" % i, [128, 512], F32)) for i in range(6)]
        banksb = [st.enter_context(nc.psum_tensor("bankb%d" % i, [128, 1024], BF16)) for i in range(2)]
        ident = T("ident", [128, 128], BF16)
        P.dma("sp", ident[:], c_ident[:, :], writes=["ident"])
        tabs = emit_s5_tables(nc, P, st, prm)
        ks, ksn = emit_ks_table(nc, P, st, tabs)
        with ExitStack() as st2:
            emit_s5_operands(nc, P, st2, prm, tabs, ident, [gt], o_K, o_WS, o_WC, banks, banksb)
            P.barrier()
        Kbd = T("mKbd", [128, 16, 128], BF16); WS = T("mWS", [128, 16, 2, 128], BF16); WC = T("mWC", [128, 16, 2, 4, 32], BF16)
        P.dma("sp", Kbd[:], o_K[gt, :, :, :], reads=[("oK", gt)], writes=[("Kbd",)])
        P.dma("sp", WS[:], o_WS[gt, :, :, :, :], reads=[("oWS", gt)], writes=[("WS",)])
        P.dma("sp", WC[:], o_WC[gt, :, :, :, :, :], reads=[("oWC", gt)], writes=[("WC",)])
        uT = T("muT", [128, NTS], BF16); yo = T("myo", [128, NTS], F32)
        carry = T("mcarry", [128, 4, 2], F32)
        hbuf = [T("mhb%d" % i, [128, 4, 2, NCH], F32) for i in range(2)]
        hprev = T("mhp", [128, 4, 2, NCH], BF16)
        for t in range(ntiles):
            P.dma("sp", uT[:], u_in[:, t * NTS:(t + 1) * NTS], writes=[("uT",)])
            emit_s5_core(nc, P, gt, uT, Kbd, WS, WC, ks, ksn, carry, hbuf, hprev, banks[0:2], banks[2:6], yo, first_tile=(t == 0))
            P.dma("sp", y_out[:, t * NTS:(t + 1) * NTS], yo[:], reads=[("yout",)])
        P.finish()
    return nc


NSUP_FULL = S // NTS


def emit_layer1(nc, P, x1_src, prm, c_ident, out, o_K, o_WS, o_WC, NSUP=NSUP_FULL, d_ys5=None, probe_no_restream=False):
    with ExitStack() as st:
        T = lambda name, shape, dt: st.enter_context(nc.sbuf_tensor(name, shape, dt))
        banks = [st.enter_context(nc.psum_tensor("l1bank%d" % i, [128, 512], F32)) for i in range(6)]
        banksb = [st.enter_context(nc.psum_tensor("l1bankb%d" % i, [128, 1024], BF16)) for i in range(2)]
        ident = T("l1ident", [128, 128], BF16)
        gbc = T("l1gbc", [128, D], F32)
        bglu = T("l1bglu", [128, 16], F32)
        carry = T("l1carry", [128, 16, 4, 2], F32)
        Wo = T("l1Wo", [128, 16, D], BF16)
        P.dma("sp", ident[:], c_ident[:, :], writes=["ident"])
        P.dma("sp", gbc[:], prm["l1_norm_g"].partition_broadcast(128), writes=["gbc"])
        P.dma("sp", bglu[:], prm["l1_b_glu"].rearrange("(ft p) -> p ft", p=128), writes=["bglu"], allow_slow_non_contiguous=True)
        w_out = prm["l1_w_out"].rearrange("(k p) f -> p k f", p=128)
        for kk in range(0, 16, 4):
            P.dma("pool", Wo[:, kk:kk + 4, :], w_out[:, kk:kk + 4, :], writes=[("Wo", kk // 4)])
        ksb = alloc_ks(nc, st)
        with ExitStack() as st0:
            tabs = emit_s5_tables(nc, P, st0, prm)
            ks, ksn = emit_ks_table(nc, P, st0, tabs, bufs=ksb)
            emit_s5_operands(nc, P, st0, prm, tabs, ident, list(range(16)), o_K, o_WS, o_WC, banks, banksb)
            P.barrier()
        nb = alloc_norm_bufs(nc, st, "L1A")
        hT = T("l1hT", [128, 8, NTS], BF16)
        uT = [T("l1uT%d" % i, [128, NTS], BF16) for i in range(2)]
        yo = [T("l1yo%d" % i, [128, NTS], F32) for i in range(2)]
        YG = T("l1YG", [128, 16, NTS], BF16)
        G1 = T("l1G1", [128, 16, 512], BF16)
        hbuf = [T("l1hb%d" % i, [128, 4, 2, NCH], F32) for i in range(2)]
        hprev = [T("l1hp%d" % i, [128, 4, 2, NCH], BF16) for i in range(2)]
        Kbd = [T("l1Kbd%d" % i, [128, 16, 128], BF16) for i in range(2)]
        WS = [T("l1WS%d" % i, [128, 16, 2, 128], BF16) for i in range(2)]
        WC = [T("l1WC%d" % i, [128, 16, 2, 4, 32], BF16) for i in range(2)]
        Wu = [T("l1Wu%d" % i, [128, 8, 128], BF16) for i in range(2)]
        Wg = [T("l1Wg%d" % i, [128, 16, 128], BF16) for i in range(2)]
        Wz = [T("l1Wz%d" % i, [128, 8, 128], BF16) for i in range(2)]
        sg = T("l1sg", [128, 512], BF16)
        sz = T("l1sz", [128, 512], BF16)
        tg = T("l1tg", [128, 512], BF16)
        xr = [T("l1xr%d" % i, [128, D], F32) for i in range(2)]
        w_in = prm["l1_w_in"].rearrange("(j p) f -> p j f", p=128)
        w_glu = prm["l1_w_glu"].rearrange("(k p) f -> p k f", p=128)
        op = P.op

        loaded = set()

        def load_gt(gt, b):
            if probe_no_restream:
                if ("gt", b) in loaded:
                    return
                loaded.add(("gt", b))
            tag = str(b)
            P.dma("sp", Kbd[b][:], o_K[gt, :, :, :], reads=[("oK", gt)], writes=[("Kbd" + tag,)])
            P.dma("sp", WS[b][:], o_WS[gt, :, :, :, :], reads=[("oWS", gt)], writes=[("WS" + tag,)])
            P.dma("sp", WC[b][:], o_WC[gt, :, :, :, :, :], reads=[("oWC", gt)], writes=[("WC" + tag,)])
            P.dma("pool", Wu[b][:], w_in[:, :, gt * 128:(gt + 1) * 128], writes=[("Wu", b)])

        def load_ft(ft, b):
            if probe_no_restream:
                if ("ft", b) in loaded:
                    return
                loaded.add(("ft", b))
            P.dma("pool", Wg[b][:], w_glu[:, :, ft * 128:(ft + 1) * 128], writes=[("Wg", b)])
            P.dma("pool", Wz[b][:], w_in[:, :, E + ft * 128:E + (ft + 1) * 128], writes=[("Wz", b)])

        for Tt in range(NSUP):
            r0 = Tt * NTS
            emit_norm_transpose(nc, P, st, x1_src[r0:r0 + NTS, :], hT, gbc, ident, banksb, "L1A", NT=NTS // 128, x_keep=nb)
            hTk = [("hT", q) for q in range(NTS // 512)]
            Yb, Sb = banks[0:2], banks[2:6]
            Ub = [banksb[hf][:, :].bitcast(F32) for hf in range(2)]

            def U_stage(gt, b):
                for hf in range(NTS // 512):
                    cs = slice(hf * 512, (hf + 1) * 512)
                    for j in range(8):
                        op("pe", lambda: nc.tensor.matmul(Ub[hf][:, 0:512], Wu[b][:, j, :], hT[:, j, cs], start=(j == 0), stop=(j == 7)),
                           reads=hTk + [("Wu", b)], writes=[("pbb", hf)])
                    op("act", lambda: nc.scalar.activation(out=uT[b][:, cs], in_=Ub[hf][:, 0:512], func=AF.Copy),
                       reads=[("pbb", hf)], writes=[("uT", b)])

            def ST_stage(gt, b):
                uv = uT[b][:, :].rearrange("p (n m) -> p m n", m=16)
                h0 = hbuf[0]
                for j in range(4):
                    rows = slice(j * 32, (j + 1) * 32)
                    first = True
                    for ri in range(2):
                        for m in range(16):
                            op("pe", lambda: nc.tensor.matmul(Sb[j][:, ri * NCH:(ri + 1) * NCH], WS[b][rows, m, ri, :], uv[rows, m, :],
                                                              start=first, stop=(m == 15), skip_group_check=True, tile_position=(32 * j, 0)),
                               reads=[("uT", b), ("WS" + str(b),)], writes=[("Sb", j)])
                            first = False
                    op("dve", lambda: nc.vector.tensor_copy(h0[:, j, :, :], Sb[j][:, 0:2 * NCH].rearrange("p (r n) -> p r n", r=2)),
                       reads=[("Sb", j)], writes=[("hb", 0)])

            def SCAN_stage(gt, b, first_tile):
                h0 = hbuf[0]
                cg = carry[:, gt]
                ck = ("carry", gt)
                for j in range(4):
                    pr = gt * 4 + j
                    ar, ai, nai = ks[:, 0, 0, pr:pr + 1], ks[:, 0, 1, pr:pr + 1], ksn[:, 0, pr:pr + 1]
                    if not first_tile:
                        cr, ci = cg[:, j, 0:1], cg[:, j, 1:2]
                        for (dst_, src_, sc_) in ((0, cr, ar), (0, ci, nai), (1, ci, ar), (1, cr, ai)):
                            op("dve", lambda: nc.vector.scalar_tensor_tensor(h0[:, j, dst_, 0:1], src_, sc_, h0[:, j, dst_, 0:1], ALU.mult, ALU.add),
                               reads=[ck, ("hb", 0), ("ks", 0), "ksn"], writes=[("hb", 0)])
                cur = 0
                for k in range(KS_LEVELS):
                    sft = 1 << k
                    src, dst = hbuf[cur], hbuf[1 - cur]
                    rk = [("hb", cur), ("ks", k), "ksn"]
                    wk = [("hb", 1 - cur)]
                    op("dve", lambda: nc.vector.tensor_copy(dst[:, :, :, 0:sft], src[:, :, :, 0:sft]), reads=rk, writes=wk)
                    for j in range(4):
                        pr = gt * 4 + j
                        ar, ai, nai = ks[:, k, 0, pr:pr + 1], ks[:, k, 1, pr:pr + 1], ksn[:, k, pr:pr + 1]
                        sr, si = src[:, j, 0, 0:NCH - sft], src[:, j, 1, 0:NCH - sft]
                        op("dve", lambda: nc.vector.scalar_tensor_tensor(dst[:, j, 0, sft:NCH], sr, ar, src[:, j, 0, sft:NCH], ALU.mult, ALU.add), reads=rk, writes=wk)
                        op("dve", lambda: nc.vector.scalar_tensor_tensor(dst[:, j, 0, sft:NCH], si, nai, dst[:, j, 0, sft:NCH], ALU.mult, ALU.add), reads=rk + wk, writes=wk)
                        op("dve", lambda: nc.vector.scalar_tensor_tensor(dst[:, j, 1, sft:NCH], si, ar, src[:, j, 1, sft:NCH], ALU.mult, ALU.add), reads=rk, writes=wk)
                        op("dve", lambda: nc.vector.scalar_tensor_tensor(dst[:, j, 1, sft:NCH], sr, ai, dst[:, j, 1, sft:NCH], ALU.mult, ALU.add), reads=rk + wk, writes=wk)
                    cur = 1 - cur
                H = hbuf[cur]
                hp = hprev[b]
                hk = [("hp", b)]
                if first_tile:
                    op("pool", lambda: nc.gpsimd.memset(hp[:, :, :, 0:1], 0.0), writes=hk)
                else:
                    op("dve", lambda: nc.vector.tensor_copy(hp[:, :, :, 0], cg[:, :, :]), reads=[ck], writes=hk)
                op("dve", lambda: nc.vector.tensor_copy(hp[:, :, :, 1:NCH], H[:, :, :, 0:NCH - 1]), reads=[("hb", cur)], writes=hk)
                op("dve", lambda: nc.vector.tensor_copy(cg[:, :, :], H[:, :, :, NCH - 1]), reads=[("hb", cur)] + hk, writes=[ck])

            def INTRA_stage(gt, b):
                uv = uT[b][:, :].rearrange("p (n m) -> p m n", m=16)
                started = set()
                for m in range(16):
                    bk = m // 8
                    cols = slice((m % 8) * NCH, (m % 8 + 1) * NCH)
                    for mp in range(m + 1):
                        f_ = bk not in started
                        started.add(bk)
                        op("pe", lambda: nc.tensor.matmul(Yb[bk][:, cols], Kbd[b][:, m - mp, :], uv[:, mp, :], start=f_, stop=False, skip_group_check=True),
                           reads=[("uT", b), ("Kbd" + str(b),)], writes=[("Yb", bk)])

            def INTER_stage(gt, b):
                for m in range(16):
                    bk = m // 8
                    cols = slice((m % 8) * NCH, (m % 8 + 1) * NCH)
                    for j in range(4):
                        for ri in range(2):
                            op("pe", lambda: nc.tensor.matmul(Yb[bk][j * 32:(j + 1) * 32, cols], WC[b][:, m, ri, j, :], hprev[b][:, j, ri, :],
                                                              start=False, stop=(ri == 1), skip_group_check=True, tile_position=(0, 32 * j)),
                               reads=[("hp", b), ("WC" + str(b),)], writes=[("Yb", bk)])
                yv = yo[b][:, :].rearrange("p (n m) -> p m n", m=16)
                for bk in range(2):
                    op("act", lambda: nc.scalar.activation(out=yv[:, bk * 8:(bk + 1) * 8, :], in_=Yb[bk][:, 0:8 * NCH].rearrange("p (m n) -> p m n", m=8), func=AF.Copy),
                       reads=[("Yb", bk)], writes=[("yo", b)])

            load_gt(0, 0)
            U_stage(0, 0)
            ST_stage(0, 0)
            for gt in range(16):
                b = gt % 2
                if gt + 1 < 16:
                    load_gt(gt + 1, 1 - b)
                INTRA_stage(gt, b)
                SCAN_stage(gt, b, first_tile=(Tt == 0))
                if gt + 1 < 16:
                    U_stage(gt + 1, 1 - b)
                    ST_stage(gt + 1, 1 - b)
                INTER_stage(gt, b)
                if d_ys5 is not None:
                    P.dma("sp", d_ys5[gt * 128:(gt + 1) * 128, r0:r0 + NTS], yo[b][:], reads=[("yo", b)])
                op("act", lambda: nc.scalar.activation(out=YG[:, gt, :], in_=yo[b][:], func=AF.Gelu_apprx_tanh),
                   reads=[("yo", b)], writes=[("YG", gt)])
            ygk = [("YG", k) for k in range(16)]
            g1k = [("G1", k) for k in range(16)]
            for hf in range(NTS // 512):
                cs = slice(hf * 512, (hf + 1) * 512)
                load_ft(0, 0)
                for ft in range(16):
                    b = ft % 2
                    if ft + 1 < 16:
                        load_ft(ft + 1, 1 - b)
                    pg, pz = banks[b], banks[2 + b]
                    for k in range(16):
                        op("pe", lambda: nc.tensor.matmul(pg[:, :], Wg[b][:, k, :], YG[:, k, cs], start=(k == 0), stop=(k == 15)),
                           reads=ygk + [("Wg", b)], writes=[("Yb", b)])
                    for j in range(8):
                        op("pe", lambda: nc.tensor.matmul(pz[:, :], Wz[b][:, j, :], hT[:, j, cs], start=(j == 0), stop=(j == 7)),
                           reads=hTk + [("Wz", b)], writes=[("Sb", b)])
                    op("act", lambda: nc.scalar.activation(out=sg[:], in_=pg[:, :], func=AF.Sigmoid, bias=bglu[:, ft:ft + 1], scale=1.0),
                       reads=[("Yb", b), "bglu"], writes=["sg"])
                    op("act", lambda: nc.scalar.activation(out=sz[:], in_=pz[:, :], func=AF.Silu), reads=[("Sb", b)], writes=["sz"])
                    op("dve", lambda: nc.vector.tensor_tensor(tg[:], YG[:, ft, cs], sg[:], ALU.mult), reads=[("YG", ft), "sg"], writes=["tg"])
                    op("dve", lambda: nc.vector.tensor_tensor(G1[:, ft, :], tg[:], sz[:], ALU.mult), reads=["tg", "sz"], writes=[("G1", ft)])
                for tt in range(4):
                    t = (r0 + hf * 512) // 128 + tt
                    b = t % 2
                    P.dma("sp", xr[b][:], x1_src[t * 128:(t + 1) * 128, :], writes=[("xr", b)])
                    for half in range(2):
                        bk = 2 + (t * 2 + half) % 4
                        for k in range(16):
                            op("pe", lambda: nc.tensor.matmul(banks[bk][:], G1[:, k, tt * 128:(tt + 1) * 128], Wo[:, k, half * 512:(half + 1) * 512],
                                                              start=(k == 0), stop=(k == 15)),
                               reads=g1k + [("Wo", k // 4)], writes=[("Sb", bk - 2)])
                        op("dve", lambda: nc.vector.tensor_tensor(xr[b][:, half * 512:(half + 1) * 512], banks[bk][:],
                                                                  xr[b][:, half * 512:(half + 1) * 512], ALU.add),
                           reads=[("Sb", bk - 2), ("xr", b)], writes=[("xr", b)])
                    P.dma("sp", out[t * 128:(t + 1) * 128, :], xr[b][:], reads=[("xr", b)], writes=[("x2", t)])


def build_l1(NSUP=NSUP_FULL, debug=False, probe_no_restream=False):
    nc = bass.Bass("TRN2", target_bir_lowering=False)
    x1 = nc.dram_tensor("x1", [S, D], F32, kind="ExternalInput").ap()
    prm = {k: nc.dram_tensor(k, shp, F32, kind="ExternalInput").ap() for k, shp in L1_PARAMS.items()}
    c_ident = nc.dram_tensor("c_ident", [128, 128], BF16, kind="ExternalInput").ap()
    out = nc.dram_tensor("out", [S, D], F32, kind="ExternalOutput").ap()
    d_ys5 = nc.dram_tensor("d_ys5", [E, S], F32, kind="ExternalOutput").ap() if debug else None
    o_K = nc.dram_tensor("o_K", [16, 128, 16, 128], BF16, kind="Internal").ap()
    o_WS = nc.dram_tensor("o_WS", [16, 128, 16, 2, 128], BF16, kind="Internal").ap()
    o_WC = nc.dram_tensor("o_WC", [16, 128, 16, 2, 4, 32], BF16, kind="Internal").ap()
    with ExitStack() as st:
        P = Prog(nc, st)
        emit_layer1(nc, P, x1, prm, c_ident, out, o_K, o_WS, o_WC, NSUP=NSUP, d_ys5=d_ys5, probe_no_restream=probe_no_restream)
        P.finish()
        build_l1.stats = {"ninst": P.ninst, "nsem": P._nsem}
    return nc


def run_layer1_spmd(inputs, x1, n_cores=8):
    nc = build_l1()
    ident = host_consts()["c_ident"]
    in_maps = []
    for b in range(n_cores):
        m = {"x1": np.ascontiguousarray(x1[b]), "c_ident": ident}
        for k in L1_PARAMS:
            m[k] = np.ascontiguousarray(inputs[k])
        in_maps.append(m)
    res = run_bass_kernel_spmd(nc, in_maps, core_ids=list(range(n_cores)))
    return np.stack([np.asarray(r["out"]) for r in res.results], axis=0)


def build_fused(NQ=8, NSUP=NSUP_FULL):
    nc = bass.Bass("TRN2", target_bir_lowering=False)
    x = nc.dram_tensor("x", [S, D], F32, kind="ExternalInput").ap()
    prm = {k: nc.dram_tensor(k, shp, F32, kind="ExternalInput").ap() for k, shp in {**L0_PARAMS, **L1_PARAMS}.items()}
    cst = {k: nc.dram_tensor(k, shp, dt, kind="ExternalInput").ap() for k, (shp, dt) in CONST_SHAPES.items()}
    out = nc.dram_tensor("out", [S, D], F32, kind="ExternalOutput").ap()
    gscr = nc.dram_tensor("gscr", [128, NH, S], BF16, kind="Internal").ap()
    x1scr = nc.dram_tensor("x1scr", [S, D], F32, kind="Internal").ap()
    o_K = nc.dram_tensor("o_K", [16, 128, 16, 128], BF16, kind="Internal").ap()
    o_WS = nc.dram_tensor("o_WS", [16, 128, 16, 2, 128], BF16, kind="Internal").ap()
    o_WC = nc.dram_tensor("o_WC", [16, 128, 16, 2, 4, 32], BF16, kind="Internal").ap()
    with ExitStack() as st:
        P = Prog(nc, st)
        emit_layer0(nc, P, x, prm, cst, x1scr, gscr, NQ=NQ)
        P.barrier()
        emit_layer1(nc, P, x1scr, prm, cst["c_ident"], out, o_K, o_WS, o_WC, NSUP=NSUP)
        P.finish()
        build_fused.stats = {"ninst": P.ninst, "nsem": P._nsem}
    return nc


def fused_in_maps(inputs, n_cores=8):
    cst = host_consts()
    maps = []
    for b in range(n_cores):
        m = {"x": np.ascontiguousarray(inputs["x"][b])}
        for k in list(L0_PARAMS) + list(L1_PARAMS):
            m[k] = np.ascontiguousarray(inputs[k])
        m.update(cst)
        maps.append(m)
    return maps


FUSED = True


def kernel(**inputs):
    inputs = {k: np.asarray(v) for k, v in inputs.items()}
    if FUSED:
        nc = build_fused()
        res = run_bass_kernel_spmd(nc, fused_in_maps(inputs), core_ids=list(range(8)))
        return np.stack([np.asarray(r["out"]) for r in res.results], axis=0).astype(np.float32, copy=False)
    x1 = run_layer0_spmd(inputs)
    x2 = run_layer1_spmd(inputs, x1)
    return x2.astype(np.float32, copy=False)
```

```python
import math
from contextlib import ExitStack

import numpy as np
import ml_dtypes

import concourse.bass as bass
import concourse.mybir as mybir
from concourse.bass_utils import run_bass_kernel_spmd

F32 = mybir.dt.float32
BF16 = mybir.dt.bfloat16
AF = mybir.ActivationFunctionType
ALU = mybir.AluOpType
AX = mybir.AxisListType

S = 4096
D = 1024
E = 2048
NH = 16
EPS = 1e-6
LAM0 = 0.8 - 0.6 * math.exp(-0.3 * 0)
NEG = -30000.0
EPOCH = 12000


class Prog:
    def __init__(self, nc, stack, n_dma_sems=48):
        self.nc = nc
        self.stack = stack
        self.eng = {"pe": nc.tensor, "act": nc.scalar, "dve": nc.vector,
                    "pool": nc.gpsimd, "sp": nc.sync}
        self._nsem = 0
        self.cur = {}
        for e in ("pe", "act", "dve", "pool"):
            self.cur[e] = [self._newsem(e), 0]
        n_sw = max(8, n_dma_sems // 3)
        self.dma_pool = {"sp": [[self._newsem("dmah"), 0] for _ in range(n_dma_sems - n_sw)],
                         "pool": [[self._newsem("dmas"), 0] for _ in range(n_sw)]}
        self.dma_rr = {"sp": 0, "pool": 0}
        self.dma_sems = self.dma_pool["sp"] + self.dma_pool["pool"]
        self.waited = {}
        self.lastw = {}
        self.readers = {}
        self.ninst = 0

    def _newsem(self, name):
        self._nsem += 1
        return self.stack.enter_context(self.nc.semaphore("s_%s_%d" % (name, self._nsem)))

    def _wait(self, e, h):
        sem, val, src = h
        if src == "pe" and e == "pe":
            return
        k = (e, id(sem))
        if self.waited.get(k, 0) >= val:
            return
        self.waited[k] = val
        self.eng[e].wait_ge(sem, val)

    def deps(self, e, reads, writes):
        for r in reads:
            h = self.lastw.get(r)
            if h is not None:
                self._wait(e, h)
        for w in writes:
            h = self.lastw.get(w)
            if h is not None:
                self._wait(e, h)
            for h in self.readers.get(w, ()):
                self._wait(e, h)

    def _commit(self, h, reads, writes):
        for r in reads:
            lst = self.readers.setdefault(r, [])
            lst[:] = [x for x in lst if x[0] is not h[0]]
            lst.append(h)
        for w in writes:
            self.lastw[w] = h
            self.readers[w] = []

    def op(self, e, fn, reads=(), writes=()):
        self.deps(e, reads, writes)
        c = self.cur[e]
        if c[1] >= EPOCH:
            c = self.cur[e] = [self._newsem(e), 0]
        ins = fn()
        c[1] += 1
        ins.then_inc(c[0], 1)
        self.ninst += 1
        h = (c[0], c[1], e)
        self._commit(h, reads, writes)
        return h

    def dma(self, q, out, in_, reads=(), writes=(), **kw):
        self.deps(q, reads, writes)
        pool_ = self.dma_pool[q]
        slot = pool_[self.dma_rr[q]]
        self.dma_rr[q] = (self.dma_rr[q] + 1) % len(pool_)
        if slot[1] > 0:
            self._wait(q, (slot[0], slot[1], "dma"))
        ins = self.eng[q].dma_start(out=out, in_=in_, **kw)
        slot[1] += 16
        ins.then_inc(slot[0], 16)
        h = (slot[0], slot[1], "dma")
        self._commit(h, reads, writes)
        return h

    def barrier(self):
        hs = [(slot[0], slot[1], "dma") for slot in self.dma_sems if slot[1] > 0]
        hs += [(c[0], c[1], "bar") for c in self.cur.values() if c[1] > 0]
        for e in ("sp", "pe", "act", "dve", "pool"):
            for h in hs:
                self._wait(e, h)

    def finish(self):
        for slot in self.dma_sems:
            if slot[1] > 0:
                self._wait("sp", (slot[0], slot[1], "dma"))
        for e, c in self.cur.items():
            if c[1] > 0:
                self._wait("sp", (c[0], c[1], e))


def _split3(x):
    x = np.asarray(x, np.float32)
    hi = x.astype(ml_dtypes.bfloat16)
    r1 = x - hi.astype(np.float32)
    mid = r1.astype(ml_dtypes.bfloat16)
    r2 = r1 - mid.astype(np.float32)
    lo = r2.astype(ml_dtypes.bfloat16)
    return hi, mid, lo


def host_consts():
    bf = ml_dtypes.bfloat16
    c = {}
    c["c_ident"] = np.eye(128, dtype=np.float32).astype(bf)
    blk = np.zeros((128, 128), np.float32)
    blk[:64, :64] = 1.0 / 64
    blk[64:, 64:] = 1.0 / 64
    c["c_blk"] = blk.astype(bf)
    mask = np.zeros((128, 4, 512), np.float32)
    ki = np.arange(128)[:, None]
    qi = np.arange(512)[None, :]
    for j in range(4):
        mask[:, j, :] = np.where(128 * j + ki <= qi, 0.0, NEG)
    c["c_mask"] = mask.astype(bf)
    pos = np.arange(S, dtype=np.float64)
    qb = np.zeros((NH, 6, S), bf)
    kb = np.zeros((NH, 6, S), bf)
    for h in range(NH):
        slope = 2.0 ** (-8.0 * (h + 1) / NH)
        a, b_, c_ = _split3((-slope * pos).astype(np.float32))
        qb[h, 0], qb[h, 1], qb[h, 2] = a, b_, c_
        qb[h, 3:6] = 1.0
        a, b_, c_ = _split3((slope * pos).astype(np.float32))
        kb[h, 0:3] = 1.0
        kb[h, 3], kb[h, 4], kb[h, 5] = a, b_, c_
    c["c_qb"] = qb
    c["c_kb"] = kb
    return c


CONST_SHAPES = {
    "c_ident": ([128, 128], BF16), "c_blk": ([128, 128], BF16),
    "c_mask": ([128, 4, 512], BF16), "c_qb": ([NH, 6, S], BF16), "c_kb": ([NH, 6, S], BF16),
}

L0_PARAMS = {
    "l0_norm_g": [D], "l0_w_in": [D, 4 * E], "l0_q_norm_g": [64], "l0_k_norm_g": [64],
    "l0_lam_q1": [64], "l0_lam_k1": [64], "l0_lam_q2": [64], "l0_lam_k2": [64],
    "l0_head_norm_g": [128], "l0_w_out": [E, D],
}


def vec2(ap):
    return ap.rearrange("(n o) -> n o", o=1)


def alloc_norm_bufs(nc, st, tagp):
    T = lambda name, shape, dt: st.enter_context(nc.sbuf_tensor(name, shape, dt))
    return {"xt": [T(tagp + "xt%d" % i, [128, D], F32) for i in range(2)],
            "xn": [T(tagp + "xn%d" % i, [128, D], BF16) for i in range(2)],
            "junk": T(tagp + "junk", [128, D], BF16), "stat": T(tagp + "stat", [128, 8], F32)}


def emit_norm_transpose(nc, P, st, x_src, hT, gbc, ident, banksb, tagp, NT=32, x_keep=None):
    if x_keep is None:
        x_keep = alloc_norm_bufs(nc, st, tagp)
    xt, xn, junk, stat = x_keep["xt"], x_keep["xn"], x_keep["junk"], x_keep["stat"]
    for t in range(NT):
        b = t % 2
        P.dma("sp", xt[b][:], x_src[t * 128:(t + 1) * 128, :], writes=[(tagp, "xt", b)])
        P.op("act", lambda: nc.scalar.activation(out=junk[:], in_=xt[b][:], func=AF.Square,
                                                 accum_out=stat[:, b:b + 1]),
             reads=[(tagp, "xt", b)], writes=[(tagp, "junk"), (tagp, "ss", b)])
        P.op("act", lambda: nc.scalar.activation(out=stat[:, 2 + b:3 + b], in_=stat[:, b:b + 1], func=AF.Sqrt,
                                                 scale=1.0 / D, bias=EPS),
             reads=[(tagp, "ss", b)], writes=[(tagp, "sd", b)])
        P.op("dve", lambda: nc.vector.reciprocal(stat[:, 4 + b:5 + b], stat[:, 2 + b:3 + b]),
             reads=[(tagp, "sd", b)], writes=[(tagp, "rs", b)])
        P.op("dve", lambda: nc.vector.scalar_tensor_tensor(xn[b][:], xt[b][:], stat[:, 4 + b:5 + b], gbc[:],
                                                           ALU.mult, ALU.mult),
             reads=[(tagp, "xt", b), (tagp, "rs", b), "gbc"], writes=[(tagp, "xn", b)])
        pb = banksb[t % 2]
        for j in range(8):
            P.op("pe", lambda: nc.tensor.transpose(pb[:, j * 128:(j + 1) * 128], xn[b][:, j * 128:(j + 1) * 128],
                                                   ident[:]),
                 reads=[(tagp, "xn", b), "ident"], writes=[("pbb", t % 2)])
        P.op("dve", lambda: nc.vector.tensor_copy(hT[:, :, t * 128:(t + 1) * 128],
                                                  pb[:, :].rearrange("p (j c) -> p j c", j=8)),
             reads=[("pbb", t % 2)], writes=[("hT", t // 4)])


def emit_layer0(nc, P, x, prm, cst, x1_out, gscr, heads=None, NQ=8, dump=None):
    heads = list(range(NH)) if heads is None else list(heads)
    with ExitStack() as st:
        T = lambda name, shape, dt: st.enter_context(nc.sbuf_tensor(name, shape, dt))
        banks = [st.enter_context(nc.psum_tensor("bank%d" % i, [128, 512], F32)) for i in range(6)]
        banksb = [st.enter_context(nc.psum_tensor("bankb%d" % i, [128, 1024], BF16)) for i in range(2)]
        ident = T("ident", [128, 128], BF16)
        blk = T("blk", [128, 128], BF16)
        mask = T("mask", [128, 4, 512], BF16)
        gbc = T("gbc", [128, D], F32)
        small = T("small", [128, 16], F32)
        lamv = T("lamv", [128, 4, 64], F32)
        lamj = T("lamj", [128, 64], F32)
        ghn = T("ghn", [128, 128], F32)
        P.dma("sp", ident[:], cst["c_ident"][:, :], writes=["ident"])
        P.dma("sp", blk[:], cst["c_blk"][:, :], writes=["blk"])
        P.dma("sp", mask[:], cst["c_mask"][:, :, :], writes=["mask"])
        P.dma("sp", gbc[:], prm["l0_norm_g"].partition_broadcast(128), writes=["gbc"])
        P.dma("sp", ghn[:], prm["l0_head_norm_g"].partition_broadcast(128), writes=["ghn"])
        for m in range(2):
            P.dma("sp", small[m * 64:(m + 1) * 64, 0:1], vec2(prm["l0_q_norm_g"]), writes=["small"])
            P.dma("sp", small[m * 64:(m + 1) * 64, 1:2], vec2(prm["l0_k_norm_g"]), writes=["small"])
        for i, nm in enumerate(["l0_lam_q1", "l0_lam_k1", "l0_lam_q2", "l0_lam_k2"]):
            P.dma("sp", lamv[:, i, :], prm[nm].partition_broadcast(128), writes=["lamv"])
        P.op("dve", lambda: nc.vector.tensor_scalar(small[:, 0:1], small[:, 0:1], 0.125, None, ALU.mult),
             reads=["small"], writes=["small"])
        for i in range(2):
            P.op("dve", lambda: nc.vector.tensor_tensor(lamj[:], lamv[:, 2 * i, :], lamv[:, 2 * i + 1, :], ALU.mult),
                 reads=["lamv"], writes=["lamj"])
            P.op("dve", lambda: nc.vector.reduce_sum(small[:, 4 + i:5 + i], lamj[:], axis=AX.X),
                 reads=["lamj"], writes=["small"])
        P.op("act", lambda: nc.scalar.activation(out=small[:, 6:8], in_=small[:, 4:6], func=AF.Exp),
             reads=["small"], writes=["small"])
        P.op("dve", lambda: nc.vector.scalar_tensor_tensor(small[:, 2:3], small[:, 7:8], -LAM0, small[:, 6:7],
                                                           ALU.add, ALU.subtract),
             reads=["small"], writes=["small"])
        P.op("dve", lambda: nc.vector.tensor_scalar(ghn[:], ghn[:], 1.0 - LAM0, None, ALU.mult),
             reads=["ghn"], writes=["ghn"])

        hT = T("hT", [128, 8, S], BF16)
        with ExitStack() as stA:
            emit_norm_transpose(nc, P, stA, x, hT, gbc, ident, banksb, "A", NT=4 * NQ)
            P.barrier()

        with ExitStack() as stH:
            TH = lambda name, shape, dt: stH.enter_context(nc.sbuf_tensor(name, shape, dt))
            QA = [TH("QA%d" % m, [128, S], BF16) for m in range(2)]
            KA = [TH("KA%d" % m, [128, S], BF16) for m in range(2)]
            VA = TH("VA", [128, 32, 130], BF16)
            ZT = TH("ZT", [128, S], BF16)
            GT = [TH("GT%d" % i, [128, S], BF16) for i in range(2)]
            Wh = [TH("Wh%d" % i, [128, 8, 4, 128], BF16) for i in range(2)]
            sq = [TH("sq%d" % i, [128, 512], BF16) for i in range(2)]
            rst = [TH("rst%d" % i, [128, 512], F32) for i in range(2)]
            pbuf = [TH("pbuf%d" % i, [128, 512], BF16) for i in range(4)]
            fin = TH("fin", [128, 4, 8], F32)
            o0 = TH("o0", [128, 4, 128], F32)
            o1 = TH("o1", [128, 4, 128], F32)
            onb = TH("onb", [128, 2, 4, 128], BF16)
            junk2 = TH("junk2", [128, 128], BF16)
            junk2f = TH("junk2f", [128, 128], F32)
            pending = []

            def flush_pending():
                while pending:
                    pending.pop(0)()
            P.op("pool", lambda: nc.gpsimd.memset(VA[:, :, 128:130], 1.0), writes=["VAones"])
            epsb = TH("epsb", [128, 1], F32)
            P.op("pool", lambda: nc.gpsimd.memset(epsb[:], EPS), writes=["epsb"])
            accs = [[TH("accs%d_%d" % (i, k), [128, 390], F32) for k in range(3)] for i in range(2)]

            w_in = prm["l0_w_in"].rearrange("(j p) f -> p j f", p=128)

            def load_head_weights(h):
                wb = Wh[h % 2]
                for sec in range(4):
                    P.dma("pool", wb[:, :, sec, :], w_in[:, :, sec * E + h * 128: sec * E + (h + 1) * 128],
                          writes=[("Wh", h % 2, sec)])

            load_head_weights(heads[0])
            for hi_, h in enumerate(heads):
                wb = Wh[h % 2]
                if hi_ + 1 < len(heads):
                    load_head_weights(heads[hi_ + 1])
                for m in range(2):
                    P.dma("sp", QA[m][64:70, :], cst["c_qb"][h, :, :], writes=[("QAb", m)])
                    P.dma("sp", KA[m][64:70, :], cst["c_kb"][h, :, :], writes=[("KAb", m)])
                for nt in range(NQ):
                    tok = slice(nt * 512, (nt + 1) * 512)
                    hkeys = [("hT", nt)]
                    for sec, bk in ((0, 0), (1, 1), (3, 2)):
                        for j in range(8):
                            P.op("pe", lambda: nc.tensor.matmul(banks[bk][:], wb[:, j, sec, :], hT[:, j, tok],
                                                                start=(j == 0), stop=(j == 7)),
                                 reads=hkeys + [("Wh", h % 2, sec)], writes=[("bk", bk)])
                    for qi, bk in ((0, 0), (1, 1)):
                        P.op("act", lambda: nc.scalar.activation(out=sq[qi][:], in_=banks[bk][:], func=AF.Square),
                             reads=[("bk", bk)], writes=[("sq", qi)])
                    for qi in range(2):
                        P.op("pe", lambda: nc.tensor.matmul(banks[3 + qi][:], blk[:], sq[qi][:], start=True, stop=True),
                             reads=["blk", ("sq", qi)], writes=[("bk", 3 + qi)])
                    for qi, (bk, tiles, gcol, key) in enumerate(((0, QA, 0, "QA"), (1, KA, 1, "KA"))):
                        P.op("act", lambda: nc.scalar.activation(out=rst[qi][:], in_=banks[3 + qi][:], func=AF.Ln,
                                                                 scale=1.0, bias=epsb[:, 0:1]),
                             reads=[("bk", 3 + qi), "epsb"], writes=[("rst", qi)])
                        P.op("act", lambda: nc.scalar.activation(out=rst[qi][:], in_=rst[qi][:], func=AF.Exp, scale=-0.5),
                             reads=[("rst", qi)], writes=[("rst", qi)])
                        for m in range(2):
                            ps = slice(m * 64, (m + 1) * 64)
                            P.op("dve", lambda: nc.vector.scalar_tensor_tensor(
                                tiles[m][0:64, tok], banks[bk][ps, :], small[ps, gcol:gcol + 1], rst[qi][ps, :],
                                ALU.mult, ALU.mult),
                                reads=[("bk", bk), ("rst", qi), "small"], writes=[(key, m, nt)])
                    P.op("act", lambda: nc.scalar.activation(out=ZT[:, tok], in_=banks[2][:], func=AF.Silu),
                         reads=[("bk", 2)], writes=[("ZT", nt)])
                    for i in range(4):
                        kt = nt * 4 + i
                        for j in range(8):
                            P.op("pe", lambda: nc.tensor.matmul(banks[5][:, i * 128:(i + 1) * 128],
                                                                hT[:, j, kt * 128:(kt + 1) * 128], wb[:, j, 2, :],
                                                                start=(i == 0 and j == 0), stop=(j == 7),
                                                                skip_group_check=True),
                                 reads=hkeys + [("Wh", h % 2, 2)], writes=[("bk", 5)])
                    P.op("dve", lambda: nc.vector.tensor_copy(
                        VA[:, nt * 4:(nt + 1) * 4, 0:128],
                        banks[5][:, :].rearrange("p (i c) -> p i c", i=4)),
                        reads=[("bk", 5)], writes=[("VA", nt)])

                gt = GT[h % 2]
                ti = 0
                for Q in range(NQ):
                    qs = slice(Q * 512, (Q + 1) * 512)
                    tiles = [(m, kt) for m in range(2) for kt in range(4 * Q + 4)]
                    accreg = {}
                    slots = [(4, 0), (4, 1), (4, 2), (5, 0), (5, 1), (5, 2), (3, 0), (3, 1)]
                    for idx, (qb, m) in enumerate([(qb, m) for m in range(2) for qb in range(4)]):
                        accreg[(qb, m)] = slots[idx]
                    started = set()

                    def emit_S(i):
                        m, kt = tiles[i]
                        sb = (ti + i) % 3
                        diag = kt >= 4 * Q
                        P.op("pe", lambda: nc.tensor.matmul(banks[sb][:], KA[m][0:70, kt * 128:(kt + 1) * 128],
                                                            QA[m][0:70, qs], start=True, stop=not diag),
                             reads=[("KA", m, kt // 4), ("KAb", m), ("QA", m, Q), ("QAb", m)], writes=[("bk", sb)])
                        if diag:
                            P.op("pe", lambda: nc.tensor.matmul(banks[sb][:], ident[:], mask[:, kt - 4 * Q, :],
                                                                start=False, stop=True),
                                 reads=["ident", "mask"], writes=[("bk", sb)])

                    def emit_EP(i):
                        m, kt = tiles[i]
                        sb = (ti + i) % 3
                        pb = (ti + i) % 4
                        P.op("act", lambda: nc.scalar.activation(out=pbuf[pb][:], in_=banks[sb][:], func=AF.Exp),
                             reads=[("bk", sb)], writes=[("pbuf", pb)])
                        j = kt - 4 * Q
                        for qb in range(4):
                            if j >= 0 and qb < j:
                                continue
                            bk, r = accreg[(qb, m)]
                            first = bk not in started
                            started.add(bk)
                            P.op("pe", lambda: nc.tensor.matmul(banks[bk][:, r * 130:r * 130 + 129],
                                                                pbuf[pb][:, qb * 128:(qb + 1) * 128], VA[:, kt, 0:129],
                                                                start=first, stop=False, skip_group_check=True),
                                 reads=[("pbuf", pb), ("VA", kt // 4), "VAones"], writes=[("bk", bk)])

                    n = len(tiles)
                    emit_S(0)
                    if n > 1:
                        emit_S(1)
                    for i in range(n):
                        if i + 2 < n:
                            emit_S(i + 2)
                        emit_EP(i)
                        if i == min(n - 1, 24):
                            flush_pending()
                    ti += n
                    ab = Q % 2
                    sidx = {4: 0, 5: 1, 3: 2}
                    for bk_, ncol in ((4, 390), (5, 390), (3, 260)):
                        P.op("dve", lambda: nc.vector.tensor_copy(accs[ab][sidx[bk_]][:, 0:ncol], banks[bk_][:, 0:ncol]),
                             reads=[("bk", bk_)], writes=[("accs", ab, sidx[bk_])])
                    def tail(Q=Q, qs=qs, gt=gt, h=h, ab=ab, accreg=accreg):
                        ob = Q % 2
                        for qb in range(4):
                            b0, r0 = accreg[(qb, 0)]
                            b1, r1 = accreg[(qb, 1)]
                            s0, s1 = accs[ab][sidx[b0]], accs[ab][sidx[b1]]
                            k0, k1 = ("accs", ab, sidx[b0]), ("accs", ab, sidx[b1])
                            a0 = s0[:, r0 * 130:r0 * 130 + 128]
                            d0 = s0[:, r0 * 130 + 128:r0 * 130 + 129]
                            a1 = s1[:, r1 * 130:r1 * 130 + 128]
                            d1 = s1[:, r1 * 130 + 128:r1 * 130 + 129]
                            fk = ("fin", qb)
                            P.op("dve", lambda: nc.vector.reciprocal(fin[:, qb, 0:1], d0), reads=[k0], writes=[fk])
                            P.op("dve", lambda: nc.vector.reciprocal(fin[:, qb, 1:2], d1), reads=[k1], writes=[fk])
                            P.op("dve", lambda: nc.vector.tensor_tensor(fin[:, qb, 2:3], fin[:, qb, 1:2], small[:, 2:3], ALU.mult),
                                 reads=[fk, "small"], writes=[fk])
                            P.op("dve", lambda: nc.vector.tensor_scalar(o0[:, qb, :], a0, fin[:, qb, 0:1], None, ALU.mult),
                                 reads=[k0, fk], writes=[("o0", qb)])
                            P.op("dve", lambda: nc.vector.scalar_tensor_tensor(o1[:, qb, :], a1, fin[:, qb, 2:3], o0[:, qb, :],
                                                                               ALU.mult, ALU.add),
                                 reads=[k1, fk, ("o0", qb)], writes=[("o1", qb)])
                        ob = Q % 2
                        fks = [("fin", qb) for qb in range(4)]
                        for qb in range(4):
                            P.op("dve", lambda: nc.vector.scalar_tensor_tensor(junk2f[:], o1[:, qb, :], 1.0, o1[:, qb, :], ALU.mult, ALU.mult,
                                                                               accum_out=fin[:, qb, 3:4]),
                                 reads=[("o1", qb)], writes=["junk2f", ("fin", qb)])
                        P.op("act", lambda: nc.scalar.activation(out=fin[:, :, 4:5], in_=fin[:, :, 3:4], func=AF.Sqrt,
                                                                 scale=1.0 / 128, bias=EPS),
                             reads=fks, writes=fks)
                        P.op("dve", lambda: nc.vector.reciprocal(fin[:, :, 5:6], fin[:, :, 4:5]), reads=fks, writes=fks)
                        for qb in range(4):
                            P.op("dve", lambda: nc.vector.scalar_tensor_tensor(onb[:, ob, qb, :], o1[:, qb, :], fin[:, qb, 5:6], ghn[:],
                                                                               ALU.mult, ALU.mult),
                                 reads=[("o1", qb), ("fin", qb), "ghn"], writes=[("onb", ob, qb)])
                        tb = banksb[Q % 2]
                        for qb in range(4):
                            P.op("pe", lambda: nc.tensor.transpose(tb[:, qb * 128:(qb + 1) * 128], onb[:, ob, qb, :], ident[:]),
                                 reads=[("onb", ob, qb), "ident"], writes=[("pbb", Q % 2)])
                        P.op("dve", lambda: nc.vector.tensor_tensor(gt[:, qs], tb[:, 0:512], ZT[:, qs], ALU.mult),
                             reads=[("pbb", Q % 2), ("ZT", Q)], writes=[("GT", h % 2)])
                    pending.append(tail)
                flush_pending()
                P.dma("sp", gscr[:, h, 0:NQ * 512], gt[:, 0:NQ * 512], reads=[("GT", h % 2)], writes=[("gscr", h)])
                if dump is not None and h == heads[-1]:
                    n = NQ * 512
                    allk = list(P.lastw.keys())
                    P.dma("sp", dump["d_hT"][:, :, :], hT[:, :, 0:n], reads=allk)
                    for m in range(2):
                        P.dma("sp", dump["d_QA"][m, :, :], QA[m][0:70, 0:n], reads=allk)
                        P.dma("sp", dump["d_KA"][m, :, :], KA[m][0:70, 0:n], reads=allk)
                    P.dma("sp", dump["d_VA"][:, :, :], VA[:, 0:4 * NQ, :], reads=allk)
                    P.dma("sp", dump["d_ZT"][:, :], ZT[:, 0:n], reads=allk)
                    P.dma("sp", dump["d_GT"][:, :], gt[:, 0:n], reads=allk)

        if dump is not None:
            return
        P.barrier()
        with ExitStack() as stD:
            TD = lambda name, shape, dt: stD.enter_context(nc.sbuf_tensor(name, shape, dt))
            Wo = TD("Wo", [128, NH, D], BF16)
            Gt = [TD("Gt%d" % i, [128, NH, 512], BF16) for i in range(2)]
            xr = [TD("xr%d" % i, [128, D], F32) for i in range(2)]
            x1t = [TD("x1t%d" % i, [128, D], F32) for i in range(2)]
            w_out = prm["l0_w_out"].rearrange("(h p) f -> p h f", p=128)
            for hh in range(0, NH, 4):
                P.dma("pool", Wo[:, hh:hh + 4, :], w_out[:, hh:hh + 4, :], writes=[("Wo", hh // 4)])
            for Tt in range(NQ):
                g = Gt[Tt % 2]
                P.dma("sp", g[:], gscr[:, :, Tt * 512:(Tt + 1) * 512], reads=[("gscr", h) for h in heads],
                      writes=[("Gt", Tt % 2)])
                for tt in range(4):
                    t = Tt * 4 + tt
                    b = t % 2
                    P.dma("sp", xr[b][:], x[t * 128:(t + 1) * 128, :], writes=[("xr", b)])
                    for half in range(2):
                        bk = (t * 2 + half) % 4
                        for h in range(NH):
                            P.op("pe", lambda: nc.tensor.matmul(banks[bk][:], g[:, h, tt * 128:(tt + 1) * 128],
                                                                Wo[:, h, half * 512:(half + 1) * 512],
                                                                start=(h == 0), stop=(h == NH - 1)),
                                 reads=[("Gt", Tt % 2), ("Wo", h // 4)], writes=[("bk", bk)])
                        P.op("dve", lambda: nc.vector.tensor_tensor(x1t[b][:, half * 512:(half + 1) * 512], banks[bk][:],
                                                                    xr[b][:, half * 512:(half + 1) * 512], ALU.add),
                             reads=[("bk", bk), ("xr", b)], writes=[("x1t", b)])
                    P.dma("sp", x1_out[t * 128:(t + 1) * 128, :], x1t[b][:], reads=[("x1t", b)], writes=[("x1", t)])


def build_l0(NQ=8):
    nc = bass.Bass("TRN2", target_bir_lowering=False)
    x = nc.dram_tensor("x", [S, D], F32, kind="ExternalInput").ap()
    prm = {k: nc.dram_tensor(k, shp, F32, kind="ExternalInput").ap() for k, shp in L0_PARAMS.items()}
    cst = {k: nc.dram_tensor(k, shp, dt, kind="ExternalInput").ap() for k, (shp, dt) in CONST_SHAPES.items()}
    out = nc.dram_tensor("out", [S, D], F32, kind="ExternalOutput").ap()
    gscr = nc.dram_tensor("gscr", [128, NH, S], BF16, kind="Internal").ap()
    with ExitStack() as st:
        P = Prog(nc, st)
        emit_layer0(nc, P, x, prm, cst, out, gscr, NQ=NQ)
        P.finish()
        build_l0.stats = {"ninst": P.ninst, "nsem": P._nsem, "dma_uses": sum(sl[1] // 16 for sl in P.dma_sems),
                          "cur": {e: c[1] for e, c in P.cur.items()}}
    return nc


def build_l0_debug(heads=(0,), NQ=2):
    nc = bass.Bass("TRN2", target_bir_lowering=False)
    x = nc.dram_tensor("x", [S, D], F32, kind="ExternalInput").ap()
    prm = {k: nc.dram_tensor(k, shp, F32, kind="ExternalInput").ap() for k, shp in L0_PARAMS.items()}
    cst = {k: nc.dram_tensor(k, shp, dt, kind="ExternalInput").ap() for k, (shp, dt) in CONST_SHAPES.items()}
    n = NQ * 512
    dshapes = {"d_hT": [128, 8, n], "d_QA": [2, 70, n], "d_KA": [2, 70, n], "d_VA": [128, 4 * NQ, 130],
               "d_ZT": [128, n], "d_GT": [128, n]}
    dump = {k: nc.dram_tensor(k, shp, BF16, kind="ExternalOutput").ap() for k, shp in dshapes.items()}
    gscr = nc.dram_tensor("gscr", [128, NH, S], BF16, kind="Internal").ap()
    with ExitStack() as st:
        P = Prog(nc, st)
        emit_layer0(nc, P, x, prm, cst, None, gscr, heads=heads, NQ=NQ, dump=dump)
        P.finish()
    return nc


def run_layer0_spmd(inputs, n_cores=8):
    nc = build_l0()
    cst = host_consts()
    in_maps = []
    for b in range(n_cores):
        m = {"x": np.ascontiguousarray(inputs["x"][b])}
        for k in L0_PARAMS:
            m[k] = np.ascontiguousarray(inputs[k])
        m.update(cst)
        in_maps.append(m)
    res = run_bass_kernel_spmd(nc, in_maps, core_ids=list(range(n_cores)))
    return np.stack([np.asarray(r["out"]) for r in res.results], axis=0)


TWO_PI = 6.283185307179586
MAGIC = 12582912.0
L1_PARAMS = {
    "l1_norm_g": [D], "l1_w_in": [D, 2 * E], "l1_lam_re": [128, 64], "l1_lam_im": [128, 64], "l1_log_dt": [128],
    "l1_b_re": [128, 64, 16], "l1_b_im": [128, 64, 16], "l1_c_re": [128, 16, 64], "l1_c_im": [128, 16, 64],
    "l1_d": [E], "l1_w_glu": [E, E], "l1_b_glu": [E], "l1_w_out": [E, D],
}


def emit_s5_tables(nc, P, st, prm, NPOW=17):
    T = lambda name, shape, dt: st.enter_context(nc.sbuf_tensor(name, shape, dt))
    tb = T("s5tb", [128, 24, 64], F32)
    pw = T("s5pw", [128, NPOW, 2, 64], F32)
    LR, LI, LDT, DT, MAG, TH, K_, SN, CS, ABR, ABI, DEN, NR, GR, GI, T1, T2 = range(17)
    V = lambda i: tb[:, i, :]
    for g2 in range(2):
        ps = slice(g2 * 64, (g2 + 1) * 64)
        for i, nm in ((LR, "l1_lam_re"), (LI, "l1_lam_im")):
            src = prm[nm].rearrange("(pr g2) p -> g2 p pr", g2=2)
            P.dma("sp", tb[ps, i, :], src[g2], writes=[("tb", i, g2)], allow_slow_non_contiguous=True)
        src = prm["l1_log_dt"].rearrange("(pr g2) -> g2 pr", g2=2)
        P.dma("sp", tb[ps, LDT, :], src[g2:g2 + 1, :].to_broadcast([64, 64]), writes=[("tb", LDT, g2)],
              allow_slow_non_contiguous=True)
    rd = lambda *idx: [("tb", i, g2) for i in idx for g2 in range(2)] + [("tb", i) for i in idx]
    op = P.op
    op("act", lambda: nc.scalar.activation(out=V(DT), in_=V(LDT), func=AF.Exp), reads=rd(LDT), writes=[("tb", DT)])
    op("dve", lambda: nc.vector.tensor_tensor(V(T1), V(LR), V(DT), ALU.mult), reads=rd(LR, DT), writes=[("tb", T1)])
    op("act", lambda: nc.scalar.activation(out=V(MAG), in_=V(T1), func=AF.Exp), reads=rd(T1), writes=[("tb", MAG)])
    op("dve", lambda: nc.vector.tensor_tensor(V(TH), V(LI), V(DT), ALU.mult), reads=rd(LI, DT), writes=[("tb", TH)])
    for dst, shift in ((SN, 0.0), (CS, math.pi / 2)):
        op("dve", lambda: nc.vector.tensor_scalar(V(T2), V(TH), shift, 1.0 / TWO_PI, ALU.add, ALU.mult),
           reads=rd(TH), writes=[("tb", T2)])
        op("dve", lambda: nc.vector.tensor_scalar(V(K_), V(T2), MAGIC, None, ALU.add), reads=rd(T2), writes=[("tb", K_)])
        op("dve", lambda: nc.vector.tensor_scalar(V(K_), V(K_), MAGIC, None, ALU.subtract), reads=rd(K_), writes=[("tb", K_)])
        op("dve", lambda: nc.vector.scalar_tensor_tensor(V(T2), V(K_), -TWO_PI, V(TH), ALU.mult, ALU.add),
           reads=rd(K_, TH), writes=[("tb", T2)])
        op("act", lambda: nc.scalar.activation(out=V(dst), in_=V(T2), func=AF.Sin, scale=1.0, bias=shift),
           reads=rd(T2), writes=[("tb", dst)])
    op("dve", lambda: nc.vector.tensor_tensor(V(ABR), V(MAG), V(CS), ALU.mult), reads=rd(MAG, CS), writes=[("tb", ABR)])
    op("dve", lambda: nc.vector.tensor_tensor(V(ABI), V(MAG), V(SN), ALU.mult), reads=rd(MAG, SN), writes=[("tb", ABI)])
    op("dve", lambda: nc.vector.tensor_tensor(V(T1), V(LR), V(LR), ALU.mult), reads=rd(LR), writes=[("tb", T1)])
    op("dve", lambda: nc.vector.tensor_tensor(V(T2), V(LI), V(LI), ALU.mult), reads=rd(LI), writes=[("tb", T2)])
    op("dve", lambda: nc.vector.tensor_tensor(V(DEN), V(T1), V(T2), ALU.add), reads=rd(T1, T2), writes=[("tb", DEN)])
    op("dve", lambda: nc.vector.reciprocal(V(DEN), V(DEN)), reads=rd(DEN), writes=[("tb", DEN)])
    op("dve", lambda: nc.vector.tensor_scalar(V(NR), V(ABR), -1.0, None, ALU.add), reads=rd(ABR), writes=[("tb", NR)])
    op("dve", lambda: nc.vector.tensor_tensor(V(T1), V(NR), V(LR), ALU.mult), reads=rd(NR, LR), writes=[("tb", T1)])
    op("dve", lambda: nc.vector.tensor_tensor(V(T2), V(ABI), V(LI), ALU.mult), reads=rd(ABI, LI), writes=[("tb", T2)])
    op("dve", lambda: nc.vector.tensor_tensor(V(GR), V(T1), V(T2), ALU.add), reads=rd(T1, T2), writes=[("tb", GR)])
    op("dve", lambda: nc.vector.tensor_tensor(V(GR), V(GR), V(DEN), ALU.mult), reads=rd(GR, DEN), writes=[("tb", GR)])
    op("dve", lambda: nc.vector.tensor_tensor(V(T1), V(ABI), V(LR), ALU.mult), reads=rd(ABI, LR), writes=[("tb", T1)])
    op("dve", lambda: nc.vector.tensor_tensor(V(T2), V(NR), V(LI), ALU.mult), reads=rd(NR, LI), writes=[("tb", T2)])
    op("dve", lambda: nc.vector.tensor_tensor(V(GI), V(T1), V(T2), ALU.subtract), reads=rd(T1, T2), writes=[("tb", GI)])
    op("dve", lambda: nc.vector.tensor_tensor(V(GI), V(GI), V(DEN), ALU.mult), reads=rd(GI, DEN), writes=[("tb", GI)])
    op("pool", lambda: nc.gpsimd.memset(pw[:, 0, 0, :], 1.0), writes=[("pw", 0)])
    op("pool", lambda: nc.gpsimd.memset(pw[:, 0, 1, :], 0.0), writes=[("pw", 0)])
    for t in range(1, NPOW):
        pr_, pi_ = pw[:, t - 1, 0, :], pw[:, t - 1, 1, :]
        op("dve", lambda: nc.vector.tensor_tensor(V(T1), pr_, V(ABR), ALU.mult), reads=[("pw", t - 1)] + rd(ABR), writes=[("tb", T1)])
        op("dve", lambda: nc.vector.tensor_tensor(V(T2), pi_, V(ABI), ALU.mult), reads=[("pw", t - 1)] + rd(ABI), writes=[("tb", T2)])
        op("dve", lambda: nc.vector.tensor_tensor(pw[:, t, 0, :], V(T1), V(T2), ALU.subtract), reads=rd(T1, T2), writes=[("pw", t)])
        op("dve", lambda: nc.vector.tensor_tensor(V(T1), pr_, V(ABI), ALU.mult), reads=[("pw", t - 1)] + rd(ABI), writes=[("tb", T1)])
        op("dve", lambda: nc.vector.tensor_tensor(V(T2), pi_, V(ABR), ALU.mult), reads=[("pw", t - 1)] + rd(ABR), writes=[("tb", T2)])
        op("dve", lambda: nc.vector.tensor_tensor(pw[:, t, 1, :], V(T1), V(T2), ALU.add), reads=rd(T1, T2), writes=[("pw", t)])
    return {"tb": tb, "pw": pw, "idx": dict(ABR=ABR, ABI=ABI, GR=GR, GI=GI)}


def build_s5_tables_debug():
    nc = bass.Bass("TRN2", target_bir_lowering=False)
    prm = {k: nc.dram_tensor(k, shp, F32, kind="ExternalInput").ap() for k, shp in L1_PARAMS.items()
           if k in ("l1_lam_re", "l1_lam_im", "l1_log_dt")}
    d_tb = nc.dram_tensor("d_tb", [128, 24, 64], F32, kind="ExternalOutput").ap()
    d_pw = nc.dram_tensor("d_pw", [128, 17, 2, 64], F32, kind="ExternalOutput").ap()
    with ExitStack() as st:
        P = Prog(nc, st)
        r = emit_s5_tables(nc, P, st, prm)
        allk = list(P.lastw.keys())
        P.dma("sp", d_tb[:, :, :], r["tb"][:], reads=allk)
        P.dma("sp", d_pw[:, :, :, :], r["pw"][:], reads=allk)
        P.finish()
    return nc


def emit_s5_operands(nc, P, st, prm, tabs, ident, gts, o_K, o_WS, o_WC, banks, banksb, o_BB=None):
    T = lambda name, shape, dt: st.enter_context(nc.sbuf_tensor(name, shape, dt))
    tb, pw, ix = tabs["tb"], tabs["pw"], tabs["idx"]
    Bs = [T("s5B%d" % i, [128, 64, 16], F32) for i in range(2)]
    Cs = [T("s5C%d" % i, [128, 64, 16], F32) for i in range(2)]
    BB = [T("s5BB%d" % i, [128, 64, 16], F32) for i in range(2)]
    t1 = T("s5t1", [128, 64, 16], F32)
    t2 = T("s5t2", [128, 64, 16], F32)
    t3 = T("s5t3", [128, 64, 16], F32)
    t4 = T("s5t4", [128, 64, 16], F32)
    pwn = T("s5pwn", [128, 17, 64], F32)
    dsk = T("s5dsk", [128, 16], F32)
    XB2 = [T("s5XB%d" % i, [128, 16, 2, 4, 32], BF16) for i in range(2)]
    WC2 = [T("s5WC%d" % i, [128, 16, 2, 4, 32], BF16) for i in range(2)]
    CB2 = [T("s5CB%d" % i, [128, 2, 4, 32], BF16) for i in range(2)]
    Kbd2 = [T("s5Kbd%d" % i, [128, 16, 128], BF16) for i in range(2)]
    WS2 = [T("s5WS%d" % i, [128, 16, 2, 128], BF16) for i in range(2)]
    identf = T("s5idf", [128, 128], F32)
    op = P.op
    for g2 in range(2):
        ps = slice(g2 * 64, (g2 + 1) * 64)
        for i, nm in enumerate(("l1_b_re", "l1_b_im")):
            P.dma("sp", Bs[i][ps, :, :], prm[nm].rearrange("(pr g2) p c -> g2 p pr c", g2=2)[g2], writes=[("Bs", i, g2)])
        for i, nm in enumerate(("l1_c_re", "l1_c_im")):
            src = prm[nm].rearrange("(pr g2) co p -> g2 p pr co", g2=2)[g2]
            for pr_ in range(64):
                P.dma("sp", Cs[i][ps, pr_, :], src[:, pr_, :],
                      writes=[("Cs", i, g2, pr_ // 8)], allow_slow_non_contiguous=True)
    P.dma("sp", dsk[:], prm["l1_d"].rearrange("(gt p) -> p gt", p=128), writes=["dsk"], allow_slow_non_contiguous=True)
    op("dve", lambda: nc.vector.tensor_copy(identf[:], ident[:]), reads=["ident"], writes=["identf"])
    for i in range(2):
        for tl, key in ((XB2[i], "XB"), (WC2[i], "WC"), (CB2[i], "CB"), (Kbd2[i], "Kbd")):
            op("pool", lambda: nc.gpsimd.memset(tl[:], 0.0), writes=[key + str(i)])
    rB = [("Bs", i, g2) for i in range(2) for g2 in range(2)]
    rC = [("Cs", i, g2, b) for i in range(2) for g2 in range(2) for b in range(8)]
    bc = lambda ap2, n: ap2.unsqueeze(2).to_broadcast([128, n, 16])
    GR, GI = tb[:, ix["GR"], :], tb[:, ix["GI"], :]
    op("dve", lambda: nc.vector.tensor_tensor(t1[:], Bs[0][:], bc(GR, 64), ALU.mult), reads=rB + [("tb", ix["GR"])], writes=["t1"])
    op("dve", lambda: nc.vector.tensor_tensor(t2[:], Bs[1][:], bc(GI, 64), ALU.mult), reads=rB + [("tb", ix["GI"])], writes=["t2"])
    op("dve", lambda: nc.vector.tensor_tensor(BB[0][:], t1[:], t2[:], ALU.subtract), reads=["t1", "t2"], writes=["BB0"])
    op("dve", lambda: nc.vector.tensor_tensor(t1[:], Bs[1][:], bc(GR, 64), ALU.mult), reads=rB + [("tb", ix["GR"])], writes=["t1"])
    op("dve", lambda: nc.vector.tensor_tensor(t2[:], Bs[0][:], bc(GI, 64), ALU.mult), reads=rB + [("tb", ix["GI"])], writes=["t2"])
    op("dve", lambda: nc.vector.tensor_tensor(BB[1][:], t1[:], t2[:], ALU.add), reads=["t1", "t2"], writes=["BB1"])
    if o_BB is not None:
        for i in range(2):
            P.dma("sp", o_BB[i, :, :, :], BB[i][:], reads=["BB%d" % i])

    def cmul_blocks(dst, k, Ar, Ai, Pr, Pi, prs, neg_im, rA, rP, wkey):
        a, b_ = t1[:, 0:4, :], t1[:, 4:8, :]
        c_, d_ = t2[:, 0:4, :], t2[:, 4:8, :]
        op("dve", lambda: nc.vector.tensor_tensor(a, Ar[:, prs, :], bc(Pr[:, prs], 4), ALU.mult), reads=rA + rP, writes=["t1"])
        op("dve", lambda: nc.vector.tensor_tensor(b_, Ai[:, prs, :], bc(Pi[:, prs], 4), ALU.mult), reads=rA + rP, writes=["t1"])
        op("dve", lambda: nc.vector.tensor_tensor(c_, Ar[:, prs, :], bc(Pi[:, prs], 4), ALU.mult), reads=rA + rP, writes=["t2"])
        op("dve", lambda: nc.vector.tensor_tensor(d_, Ai[:, prs, :], bc(Pr[:, prs], 4), ALU.mult), reads=rA + rP, writes=["t2"])
        for g2 in range(2):
            ps = slice(g2 * 64, (g2 + 1) * 64)
            cs = slice(g2 * 16, (g2 + 1) * 16)
            op("dve", lambda: nc.vector.tensor_tensor(dst[ps, k, 0, :, cs], a[ps], b_[ps], ALU.subtract),
               reads=["t1"], writes=[wkey])
            if neg_im:
                op("dve", lambda: nc.vector.scalar_tensor_tensor(dst[ps, k, 1, :, cs], c_[ps], -1.0, d_[ps], ALU.mult, ALU.subtract),
                   reads=["t2"], writes=[wkey])
            else:
                op("dve", lambda: nc.vector.tensor_tensor(dst[ps, k, 1, :, cs], c_[ps], d_[ps], ALU.add),
                   reads=["t2"], writes=[wkey])

    rPW = [("pw", t) for t in range(17)]
    op("dve", lambda: nc.vector.tensor_scalar(pwn[:], pw[:, :, 1, :], -1.0, None, ALU.mult), reads=rPW, writes=["pwn"])

    def cmul_all_d(dst, Ar, Ai, d0, prs, neg_im, rA, wkey):
        V4 = lambda t: t[:, :, :].rearrange("p (d j) c -> p d j c", d=16)
        a, b_, c_, d_ = V4(t1), V4(t2), V4(t3), V4(t4)
        bA = lambda A: A[:, prs, :].unsqueeze(1).to_broadcast([128, 16, 4, 16])
        Pr = pw[:, d0:d0 + 16, 0, prs].unsqueeze(3).to_broadcast([128, 16, 4, 16])
        Pi = pw[:, d0:d0 + 16, 1, prs].unsqueeze(3).to_broadcast([128, 16, 4, 16])
        Pin = pwn[:, d0:d0 + 16, prs].unsqueeze(3).to_broadcast([128, 16, 4, 16])
        op("dve", lambda: nc.vector.tensor_tensor(a, bA(Ar), Pr, ALU.mult), reads=rA + rPW, writes=["t1"])
        op("dve", lambda: nc.vector.tensor_tensor(b_, bA(Ai), Pi, ALU.mult), reads=rA + rPW, writes=["t2"])
        op("dve", lambda: nc.vector.tensor_tensor(c_, bA(Ar), Pin if neg_im else Pi, ALU.mult), reads=rA + rPW + ["pwn"], writes=["t3"])
        op("dve", lambda: nc.vector.tensor_tensor(d_, bA(Ai), Pr, ALU.mult), reads=rA + rPW, writes=["t4"])
        for g2 in range(2):
            ps = slice(g2 * 64, (g2 + 1) * 64)
            cs = slice(g2 * 16, (g2 + 1) * 16)
            op("dve", lambda: nc.vector.tensor_tensor(dst[ps, :, 0, :, cs], a[ps], b_[ps], ALU.subtract),
               reads=["t1", "t2"], writes=[wkey])
            op("dve", lambda: nc.vector.tensor_tensor(dst[ps, :, 1, :, cs], c_[ps], d_[ps], ALU.subtract if neg_im else ALU.add),
               reads=["t3", "t4"], writes=[wkey])

    for gi, gt in enumerate(gts):
        pb_ = str(gi % 2)
        XB, WC, CB, Kbd, WS = XB2[gi % 2], WC2[gi % 2], CB2[gi % 2], Kbd2[gi % 2], WS2[gi % 2]
        kXB, kWC, kCB, kKbd, kWS = "XB" + pb_, "WC" + pb_, "CB" + pb_, "Kbd" + pb_, "WS" + pb_
        prs = slice(gt * 4, gt * 4 + 4)
        cmul_all_d(XB, BB[0], BB[1], 0, prs, False, ["BB0", "BB1"], kXB)
        cmul_all_d(WC, Cs[0], Cs[1], 1, prs, True, rC, kWC)
        for g2 in range(2):
            ps = slice(g2 * 64, (g2 + 1) * 64)
            cs = slice(g2 * 16, (g2 + 1) * 16)
            for i in range(2):
                op("dve", lambda: nc.vector.tensor_copy(CB[ps, i, :, cs], Cs[i][ps, prs, :]), reads=rC, writes=[kCB])
        op("dve", lambda: nc.vector.tensor_scalar(CB[:, 1, :, :], CB[:, 1, :, :], -1.0, None, ALU.mult), reads=[kCB], writes=[kCB])
        for j in range(4):
            kb = banks[j % 2]
            for d in range(16):
                for i in range(2):
                    op("pe", lambda: nc.tensor.matmul(kb[0:32, d * 32:(d + 1) * 32], XB[:, d, i, j, :], CB[:, i, j, :],
                                                      start=(d == 0 and i == 0), stop=(i == 1), skip_group_check=True),
                       reads=[kXB, kCB], writes=[("kbk", j % 2)])
            op("act", lambda: nc.scalar.activation(out=Kbd[j * 32:(j + 1) * 32, :, j * 32:(j + 1) * 32],
                                                   in_=kb[0:32, :].rearrange("p (d c) -> p d c", d=16), func=AF.Copy),
               reads=[("kbk", j % 2)], writes=[kKbd])
        op("dve", lambda: nc.vector.scalar_tensor_tensor(Kbd[:, 0, :], identf[:], dsk[:, gt:gt + 1], Kbd[:, 0, :], ALU.mult, ALU.add),
           reads=["identf", "dsk", kKbd], writes=[kKbd])
        P.dma("sp", o_K[gt, :, :, :], Kbd[:], reads=[kKbd], writes=[("oK", gt)])
        for j in range(4):
            for i in range(2):
                for half in range(2):
                    tbk = banksb[(j * 4 + i * 2 + half) % 2]
                    for mm in range(8):
                        m_ = half * 8 + mm
                        op("pe", lambda: nc.tensor.transpose(tbk[0:32, mm * 128:(mm + 1) * 128], XB[:, 15 - m_, i, j, :], ident[:]),
                           reads=[kXB, "ident"], writes=[("tbk", (j * 4 + i * 2 + half) % 2)])
                    op("act", lambda: nc.scalar.activation(out=WS[j * 32:(j + 1) * 32, half * 8:(half + 1) * 8, i, :],
                                                           in_=tbk[0:32, :].rearrange("p (m c) -> p m c", m=8), func=AF.Copy),
                       reads=[("tbk", (j * 4 + i * 2 + half) % 2)], writes=[kWS])
        P.dma("sp", o_WS[gt, :, :, :, :], WS[:], reads=[kWS], writes=[("oWS", gt)])
        P.dma("sp", o_WC[gt, :, :, :, :, :], WC[:], reads=[kWC], writes=[("oWC", gt)])


def build_s5_operands_debug(gts=(0, 5)):
    nc = bass.Bass("TRN2", target_bir_lowering=False)
    prm = {k: nc.dram_tensor(k, shp, F32, kind="ExternalInput").ap() for k, shp in L1_PARAMS.items()
           if k in ("l1_lam_re", "l1_lam_im", "l1_log_dt", "l1_b_re", "l1_b_im", "l1_c_re", "l1_c_im", "l1_d")}
    c_ident = nc.dram_tensor("c_ident", [128, 128], BF16, kind="ExternalInput").ap()
    o_K = nc.dram_tensor("o_K", [16, 128, 16, 128], BF16, kind="ExternalOutput").ap()
    o_WS = nc.dram_tensor("o_WS", [16, 128, 16, 2, 128], BF16, kind="ExternalOutput").ap()
    o_WC = nc.dram_tensor("o_WC", [16, 128, 16, 2, 4, 32], BF16, kind="ExternalOutput").ap()
    o_BB = nc.dram_tensor("o_BB", [2, 128, 64, 16], F32, kind="ExternalOutput").ap()
    with ExitStack() as st:
        P = Prog(nc, st)
        banks = [st.enter_context(nc.psum_tensor("bank%d" % i, [128, 512], F32)) for i in range(2)]
        banksb = [st.enter_context(nc.psum_tensor("bankb%d" % i, [128, 1024], BF16)) for i in range(2)]
        ident = st.enter_context(nc.sbuf_tensor("ident", [128, 128], BF16))
        P.dma("sp", ident[:], c_ident[:, :], writes=["ident"])
        tabs = emit_s5_tables(nc, P, st, prm)
        emit_s5_operands(nc, P, st, prm, tabs, ident, list(gts), o_K, o_WS, o_WC, banks, banksb, o_BB=o_BB)
        P.finish()
    return nc


NCH = 64
NTS = NCH * 16
KS_LEVELS = NCH.bit_length() - 1


def alloc_ks(nc, st):
    T = lambda name, shape, dt: st.enter_context(nc.sbuf_tensor(name, shape, dt))
    return {"ks": T("s5ks", [128, 6, 2, 64], F32), "kt1": T("s5kt1", [128, 64], F32), "kt2": T("s5kt2", [128, 64], F32),
            "ksn": T("s5ksn", [128, 6, 64], F32)}


def emit_ks_table(nc, P, st, tabs, bufs=None):
    if bufs is None:
        bufs = alloc_ks(nc, st)
    pw = tabs["pw"]
    ks, kt1, kt2 = bufs["ks"], bufs["kt1"], bufs["kt2"]
    op = P.op
    op("dve", lambda: nc.vector.tensor_copy(ks[:, 0, :, :], pw[:, 16, :, :]), reads=[("pw", 16)], writes=[("ks", 0)])
    for k in range(1, 6):
        a, b_ = ks[:, k - 1, 0, :], ks[:, k - 1, 1, :]
        op("dve", lambda: nc.vector.tensor_tensor(kt1[:], a, a, ALU.mult), reads=[("ks", k - 1)], writes=["kt1"])
        op("dve", lambda: nc.vector.tensor_tensor(kt2[:], b_, b_, ALU.mult), reads=[("ks", k - 1)], writes=["kt2"])
        op("dve", lambda: nc.vector.tensor_tensor(ks[:, k, 0, :], kt1[:], kt2[:], ALU.subtract), reads=["kt1", "kt2"], writes=[("ks", k)])
        op("dve", lambda: nc.vector.tensor_tensor(kt1[:], a, b_, ALU.mult), reads=[("ks", k - 1)], writes=["kt1"])
        op("dve", lambda: nc.vector.tensor_scalar(ks[:, k, 1, :], kt1[:], 2.0, None, ALU.mult), reads=["kt1"], writes=[("ks", k)])
    ksn = bufs["ksn"]
    op("dve", lambda: nc.vector.tensor_scalar(ksn[:], ks[:, :, 1, :], -1.0, None, ALU.mult), reads=[("ks", k) for k in range(6)], writes=["ksn"])
    return ks, ksn


def emit_s5_core(nc, P, gt, uT, Kbd, WS, WC, ks, ksn, carry, hbuf, hprev, Yb, Sb, yout, first_tile, tag="", ukey=("uT",), ykey=("yout",)):
    op = P.op
    uv = uT[:, :].rearrange("p (n m) -> p m n", m=16)
    h0 = hbuf[0]
    for j in range(4):
        rows = slice(j * 32, (j + 1) * 32)
        first = True
        for ri in range(2):
            for m in range(16):
                op("pe", lambda: nc.tensor.matmul(Sb[j][:, ri * NCH:(ri + 1) * NCH], WS[rows, m, ri, :], uv[rows, m, :],
                                                  start=first, stop=(m == 15), skip_group_check=True,
                                                  tile_position=(32 * j, 0)),
                   reads=[ukey, ("WS" + tag,)], writes=[("Sb", j)])
                first = False
        op("dve", lambda: nc.vector.tensor_copy(h0[:, j, :, :], Sb[j][:, 0:2 * NCH].rearrange("p (r n) -> p r n", r=2)),
           reads=[("Sb", j)], writes=[("hb", 0)])
    for j in range(4):
        pr = gt * 4 + j
        ar, ai, nai = ks[:, 0, 0, pr:pr + 1], ks[:, 0, 1, pr:pr + 1], ksn[:, 0, pr:pr + 1]
        if not first_tile:
            cr, ci = carry[:, j, 0:1], carry[:, j, 1:2]
            op("dve", lambda: nc.vector.scalar_tensor_tensor(h0[:, j, 0, 0:1], cr, ar, h0[:, j, 0, 0:1], ALU.mult, ALU.add),
               reads=[("carry", gt), ("hb", 0), ("ks", 0)], writes=[("hb", 0)])
            op("dve", lambda: nc.vector.scalar_tensor_tensor(h0[:, j, 0, 0:1], ci, nai, h0[:, j, 0, 0:1], ALU.mult, ALU.add),
               reads=[("carry", gt), ("hb", 0), "ksn"], writes=[("hb", 0)])
            op("dve", lambda: nc.vector.scalar_tensor_tensor(h0[:, j, 1, 0:1], ci, ar, h0[:, j, 1, 0:1], ALU.mult, ALU.add),
               reads=[("carry", gt), ("hb", 0), ("ks", 0)], writes=[("hb", 0)])
            op("dve", lambda: nc.vector.scalar_tensor_tensor(h0[:, j, 1, 0:1], cr, ai, h0[:, j, 1, 0:1], ALU.mult, ALU.add),
               reads=[("carry", gt), ("hb", 0), ("ks", 0)], writes=[("hb", 0)])
    cur = 0
    for k in range(KS_LEVELS):
        s = 1 << k
        src, dst = hbuf[cur], hbuf[1 - cur]
        rk = [("hb", cur), ("ks", k), "ksn"]
        wk = [("hb", 1 - cur)]
        op("dve", lambda: nc.vector.tensor_copy(dst[:, :, :, 0:s], src[:, :, :, 0:s]), reads=rk, writes=wk)
        for j in range(4):
            pr = gt * 4 + j
            ar, ai, nai = ks[:, k, 0, pr:pr + 1], ks[:, k, 1, pr:pr + 1], ksn[:, k, pr:pr + 1]
            sr, si = src[:, j, 0, 0:NCH - s], src[:, j, 1, 0:NCH - s]
            op("dve", lambda: nc.vector.scalar_tensor_tensor(dst[:, j, 0, s:NCH], sr, ar, src[:, j, 0, s:NCH], ALU.mult, ALU.add), reads=rk, writes=wk)
            op("dve", lambda: nc.vector.scalar_tensor_tensor(dst[:, j, 0, s:NCH], si, nai, dst[:, j, 0, s:NCH], ALU.mult, ALU.add), reads=rk + wk, writes=wk)
            op("dve", lambda: nc.vector.scalar_tensor_tensor(dst[:, j, 1, s:NCH], si, ar, src[:, j, 1, s:NCH], ALU.mult, ALU.add), reads=rk, writes=wk)
            op("dve", lambda: nc.vector.scalar_tensor_tensor(dst[:, j, 1, s:NCH], sr, ai, dst[:, j, 1, s:NCH], ALU.mult, ALU.add), reads=rk + wk, writes=wk)
        cur = 1 - cur
    H = hbuf[cur]
    hk = [("hp",)]
    if first_tile:
        op("pool", lambda: nc.gpsimd.memset(hprev[:, :, :, 0:1], 0.0), writes=hk)
    else:
        op("dve", lambda: nc.vector.tensor_copy(hprev[:, :, :, 0], carry[:, :, :]), reads=[("carry", gt)], writes=hk)
    op("dve", lambda: nc.vector.tensor_copy(hprev[:, :, :, 1:NCH], H[:, :, :, 0:NCH - 1]), reads=[("hb", cur)], writes=hk)
    op("dve", lambda: nc.vector.tensor_copy(carry[:, :, :], H[:, :, :, NCH - 1]), reads=[("hb", cur)] + hk, writes=[("carry", gt)])
    started = set()
    for m in range(16):
        bk = m // 8
        yb = Yb[bk]
        cols = slice((m % 8) * NCH, (m % 8 + 1) * NCH)
        for mp in range(m + 1):
            f_ = bk not in started
            started.add(bk)
            op("pe", lambda: nc.tensor.matmul(yb[:, cols], Kbd[:, m - mp, :], uv[:, mp, :], start=f_, stop=False, skip_group_check=True),
               reads=[ukey, ("Kbd" + tag,)], writes=[("Yb", bk)])
    for m in range(16):
        bk = m // 8
        yb = Yb[bk]
        cols = slice((m % 8) * NCH, (m % 8 + 1) * NCH)
        for j in range(4):
            for ri in range(2):
                op("pe", lambda: nc.tensor.matmul(yb[j * 32:(j + 1) * 32, cols], WC[:, m, ri, j, :], hprev[:, j, ri, :],
                                                  start=False, stop=(ri == 1), skip_group_check=True,
                                                  tile_position=(0, 32 * j)),
                   reads=hk + [("WC" + tag,)], writes=[("Yb", bk)])
    yv = yout[:, :].rearrange("p (n m) -> p m n", m=16)
    for bk in range(2):
        op("act", lambda: nc.scalar.activation(out=yv[:, bk * 8:(bk + 1) * 8, :], in_=Yb[bk][:, 0:8 * NCH].rearrange("p (m n) -> p m n", m=8), func=AF.Copy),
           reads=[("Yb", bk)], writes=[ykey])


def build_s5_core_debug(gt=0, ntiles=2):
    nc = bass.Bass("TRN2", target_bir_lowering=False)
    names = ("l1_lam_re", "l1_lam_im", "l1_log_dt", "l1_b_re", "l1_b_im", "l1_c_re", "l1_c_im", "l1_d")
    prm = {k: nc.dram_tensor(k, L1_PARAMS[k], F32, kind="ExternalInput").ap() for k in names}
    c_ident = nc.dram_tensor("c_ident", [128, 128], BF16, kind="ExternalInput").ap()
    u_in = nc.dram_tensor("u_in", [128, ntiles * NTS], BF16, kind="ExternalInput").ap()
    y_out = nc.dram_tensor("y_out", [128, ntiles * NTS], F32, kind="ExternalOutput").ap()
    o_K = nc.dram_tensor("o_K", [16, 128, 16, 128], BF16, kind="Internal").ap()
    o_WS = nc.dram_tensor("o_WS", [16, 128, 16, 2, 128], BF16, kind="Internal").ap()
    o_WC = nc.dram_tensor("o_WC", [16, 128, 16, 2, 4, 32], BF16, kind="Internal").ap()
    with ExitStack() as st:
        P = Prog(nc, st)
        T = lambda name, shape, dt: st.enter_context(nc.sbuf_tensor(name, shape, dt))
        banks = [st.enter_context(nc.psum_tensor("bank%d" % i, [128, 512], F32)) for i in range(6)]
        banksb = [st.enter_context(nc.psum_tensor("bankb%d" % i, [128, 1024], BF16)) for i in range(2)]
        ident = T("ident", [128, 128], BF16)
        P.dma("sp", ident[:], c_ident[:, :], writes=["ident"])
        tabs = emit_s5_tables(nc, P, st, prm)
        ks, ksn = emit_ks_table(nc, P, st, tabs)
        with ExitStack() as st2:
            emit_s5_operands(nc, P, st2, prm, tabs, ident, [gt], o_K, o_WS, o_WC, banks, banksb)
            P.barrier()
        Kbd = T("mKbd", [128, 16, 128], BF16); WS = T("mWS", [128, 16, 2, 128], BF16); WC = T("mWC", [128, 16, 2, 4, 32], BF16)
        P.dma("sp", Kbd[:], o_K[gt, :, :, :], reads=[("oK", gt)], writes=[("Kbd",)])
        P.dma("sp", WS[:], o_WS[gt, :, :, :, :], reads=[("oWS", gt)], writes=[("WS",)])
        P.dma("sp", WC[:], o_WC[gt, :, :, :, :, :], reads=[("oWC", gt)], writes=[("WC",)])
        uT = T("muT", [128, NTS], BF16); yo = T("myo", [128, NTS], F32)
        carry = T("mcarry", [128, 4, 2], F32)
        hbuf = [T("mhb%d" % i, [128, 4, 2, NCH], F32) for i in range(2)]
        hprev = T("mhp", [128, 4, 2, NCH], BF16)
        for t in range(ntiles):
            P.dma("sp", uT[:], u_in[:, t * NTS:(t + 1) * NTS], writes=[("uT",)])
            emit_s5_core(nc, P, gt, uT, Kbd, WS, WC, ks, ksn, carry, hbuf, hprev, banks[0:2], banks[2:6], yo, first_tile=(t == 0))
            P.dma("sp", y_out[:, t * NTS:(t + 1) * NTS], yo[:], reads=[("yout",)])
        P.finish()
    return nc


NSUP_FULL = S // NTS


def emit_layer1(nc, P, x1_src, prm, c_ident, out, o_K, o_WS, o_WC, NSUP=NSUP_FULL, d_ys5=None, probe_no_restream=False):
    with ExitStack() as st:
        T = lambda name, shape, dt: st.enter_context(nc.sbuf_tensor(name, shape, dt))
        banks = [st.enter_context(nc.psum_tensor("l1bank%d" % i, [128, 512], F32)) for i in range(6)]
        banksb = [st.enter_context(nc.psum_tensor("l1bankb%d" % i, [128, 1024], BF16)) for i in range(2)]
        ident = T("l1ident", [128, 128], BF16)
        gbc = T("l1gbc", [128, D], F32)
        bglu = T("l1bglu", [128, 16], F32)
        carry = T("l1carry", [128, 16, 4, 2], F32)
        Wo = T("l1Wo", [128, 16, D], BF16)
        P.dma("sp", ident[:], c_ident[:, :], writes=["ident"])
        P.dma("sp", gbc[:], prm["l1_norm_g"].partition_broadcast(128), writes=["gbc"])
        P.dma("sp", bglu[:], prm["l1_b_glu"].rearrange("(ft p) -> p ft", p=128), writes=["bglu"], allow_slow_non_contiguous=True)
        w_out = prm["l1_w_out"].rearrange("(k p) f -> p k f", p=128)
        for kk in range(0, 16, 4):
            P.dma("pool", Wo[:, kk:kk + 4, :], w_out[:, kk:kk + 4, :], writes=[("Wo", kk // 4)])
        ksb = alloc_ks(nc, st)
        with ExitStack() as st0:
            tabs = emit_s5_tables(nc, P, st0, prm)
            ks, ksn = emit_ks_table(nc, P, st0, tabs, bufs=ksb)
            emit_s5_operands(nc, P, st0, prm, tabs, ident, list(range(16)), o_K, o_WS, o_WC, banks, banksb)
            P.barrier()
        nb = alloc_norm_bufs(nc, st, "L1A")
        hT = T("l1hT", [128, 8, NTS], BF16)
        uT = [T("l1uT%d" % i, [128, NTS], BF16) for i in range(2)]
        yo = [T("l1yo%d" % i, [128, NTS], F32) for i in range(2)]
        YG = T("l1YG", [128, 16, NTS], BF16)
        G1 = T("l1G1", [128, 16, 512], BF16)
        hbuf = [T("l1hb%d" % i, [128, 4, 2, NCH], F32) for i in range(2)]
        hprev = [T("l1hp%d" % i, [128, 4, 2, NCH], BF16) for i in range(2)]
        Kbd = [T("l1Kbd%d" % i, [128, 16, 128], BF16) for i in range(2)]
        WS = [T("l1WS%d" % i, [128, 16, 2, 128], BF16) for i in range(2)]
        WC = [T("l1WC%d" % i, [128, 16, 2, 4, 32], BF16) for i in range(2)]
        Wu = [T("l1Wu%d" % i, [128, 8, 128], BF16) for i in range(2)]
        Wg = [T("l1Wg%d" % i, [128, 16, 128], BF16) for i in range(2)]
        Wz = [T("l1Wz%d" % i, [128, 8, 128], BF16) for i in range(2)]
        sg = T("l1sg", [128, 512], BF16)
        sz = T("l1sz", [128, 512], BF16)
        tg = T("l1tg", [128, 512], BF16)
        xr = [T("l1xr%d" % i, [128, D], F32) for i in range(2)]
        w_in = prm["l1_w_in"].rearrange("(j p) f -> p j f", p=128)
        w_glu = prm["l1_w_glu"].rearrange("(k p) f -> p k f", p=128)
        op = P.op

        loaded = set()

        def load_gt(gt, b):
            if probe_no_restream:
                if ("gt", b) in loaded:
                    return
                loaded.add(("gt", b))
            tag = str(b)
            P.dma("sp", Kbd[b][:], o_K[gt, :, :, :], reads=[("oK", gt)], writes=[("Kbd" + tag,)])
            P.dma("sp", WS[b][:], o_WS[gt, :, :, :, :], reads=[("oWS", gt)], writes=[("WS" + tag,)])
            P.dma("sp", WC[b][:], o_WC[gt, :, :, :, :, :], reads=[("oWC", gt)], writes=[("WC" + tag,)])
            P.dma("pool", Wu[b][:], w_in[:, :, gt * 128:(gt + 1) * 128], writes=[("Wu", b)])

        def load_ft(ft, b):
            if probe_no_restream:
                if ("ft", b) in loaded:
                    return
                loaded.add(("ft", b))
            P.dma("pool", Wg[b][:], w_glu[:, :, ft * 128:(ft + 1) * 128], writes=[("Wg", b)])
            P.dma("pool", Wz[b][:], w_in[:, :, E + ft * 128:E + (ft + 1) * 128], writes=[("Wz", b)])

        for Tt in range(NSUP):
            r0 = Tt * NTS
            emit_norm_transpose(nc, P, st, x1_src[r0:r0 + NTS, :], hT, gbc, ident, banksb, "L1A", NT=NTS // 128, x_keep=nb)
            hTk = [("hT", q) for q in range(NTS // 512)]
            Yb, Sb = banks[0:2], banks[2:6]
            Ub = [banksb[hf][:, :].bitcast(F32) for hf in range(2)]

            def U_stage(gt, b):
                for hf in range(NTS // 512):
                    cs = slice(hf * 512, (hf + 1) * 512)
                    for j in range(8):
                        op("pe", lambda: nc.tensor.matmul(Ub[hf][:, 0:512], Wu[b][:, j, :], hT[:, j, cs], start=(j == 0), stop=(j == 7)),
                           reads=hTk + [("Wu", b)], writes=[("pbb", hf)])
                    op("act", lambda: nc.scalar.activation(out=uT[b][:, cs], in_=Ub[hf][:, 0:512], func=AF.Copy),
                       reads=[("pbb", hf)], writes=[("uT", b)])

            def ST_stage(gt, b):
                uv = uT[b][:, :].rearrange("p (n m) -> p m n", m=16)
                h0 = hbuf[0]
                for j in range(4):
                    rows = slice(j * 32, (j + 1) * 32)
                    first = True
                    for ri in range(2):
                        for m in range(16):
                            op("pe", lambda: nc.tensor.matmul(Sb[j][:, ri * NCH:(ri + 1) * NCH], WS[b][rows, m, ri, :], uv[rows, m, :],
                                                              start=first, stop=(m == 15), skip_group_check=True, tile_position=(32 * j, 0)),
                               reads=[("uT", b), ("WS" + str(b),)], writes=[("Sb", j)])
                            first = False
                    op("dve", lambda: nc.vector.tensor_copy(h0[:, j, :, :], Sb[j][:, 0:2 * NCH].rearrange("p (r n) -> p r n", r=2)),
                       reads=[("Sb", j)], writes=[("hb", 0)])

            def SCAN_stage(gt, b, first_tile):
                h0 = hbuf[0]
                cg = carry[:, gt]
                ck = ("carry", gt)
                for j in range(4):
                    pr = gt * 4 + j
                    ar, ai, nai = ks[:, 0, 0, pr:pr + 1], ks[:, 0, 1, pr:pr + 1], ksn[:, 0, pr:pr + 1]
                    if not first_tile:
                        cr, ci = cg[:, j, 0:1], cg[:, j, 1:2]
                        for (dst_, src_, sc_) in ((0, cr, ar), (0, ci, nai), (1, ci, ar), (1, cr, ai)):
                            op("dve", lambda: nc.vector.scalar_tensor_tensor(h0[:, j, dst_, 0:1], src_, sc_, h0[:, j, dst_, 0:1], ALU.mult, ALU.add),
                               reads=[ck, ("hb", 0), ("ks", 0), "ksn"], writes=[("hb", 0)])
                cur = 0
                for k in range(KS_LEVELS):
                    sft = 1 << k
                    src, dst = hbuf[cur], hbuf[1 - cur]
                    rk = [("hb", cur), ("ks", k), "ksn"]
                    wk = [("hb", 1 - cur)]
                    op("dve", lambda: nc.vector.tensor_copy(dst[:, :, :, 0:sft], src[:, :, :, 0:sft]), reads=rk, writes=wk)
                    for j in range(4):
                        pr = gt * 4 + j
                        ar, ai, nai = ks[:, k, 0, pr:pr + 1], ks[:, k, 1, pr:pr + 1], ksn[:, k, pr:pr + 1]
                        sr, si = src[:, j, 0, 0:NCH - sft], src[:, j, 1, 0:NCH - sft]
                        op("dve", lambda: nc.vector.scalar_tensor_tensor(dst[:, j, 0, sft:NCH], sr, ar, src[:, j, 0, sft:NCH], ALU.mult, ALU.add), reads=rk, writes=wk)
                        op("dve", lambda: nc.vector.scalar_tensor_tensor(dst[:, j, 0, sft:NCH], si, nai, dst[:, j, 0, sft:NCH], ALU.mult, ALU.add), reads=rk + wk, writes=wk)
                        op("dve", lambda: nc.vector.scalar_tensor_tensor(dst[:, j, 1, sft:NCH], si, ar, src[:, j, 1, sft:NCH], ALU.mult, ALU.add), reads=rk, writes=wk)
                        op("dve", lambda: nc.vector.scalar_tensor_tensor(dst[:, j, 1, sft:NCH], sr, ai, dst[:, j, 1, sft:NCH], ALU.mult, ALU.add), reads=rk + wk, writes=wk)
                    cur = 1 - cur
                H = hbuf[cur]
                hp = hprev[b]
                hk = [("hp", b)]
                if first_tile:
                    op("pool", lambda: nc.gpsimd.memset(hp[:, :, :, 0:1], 0.0), writes=hk)
                else:
                    op("dve", lambda: nc.vector.tensor_copy(hp[:, :, :, 0], cg[:, :, :]), reads=[ck], writes=hk)
                op("dve", lambda: nc.vector.tensor_copy(hp[:, :, :, 1:NCH], H[:, :, :, 0:NCH - 1]), reads=[("hb", cur)], writes=hk)
                op("dve", lambda: nc.vector.tensor_copy(cg[:, :, :], H[:, :, :, NCH - 1]), reads=[("hb", cur)] + hk, writes=[ck])

            def INTRA_stage(gt, b):
                uv = uT[b][:, :].rearrange("p (n m) -> p m n", m=16)
                started = set()
                for m in range(16):
                    bk = m // 8
                    cols = slice((m % 8) * NCH, (m % 8 + 1) * NCH)
                    for mp in range(m + 1):
                        f_ = bk not in started
                        started.add(bk)
                        op("pe", lambda: nc.tensor.matmul(Yb[bk][:, cols], Kbd[b][:, m - mp, :], uv[:, mp, :], start=f_, stop=False, skip_group_check=True),
                           reads=[("uT", b), ("Kbd" + str(b),)], writes=[("Yb", bk)])

            def INTER_stage(gt, b):
                for m in range(16):
                    bk = m // 8
                    cols = slice((m % 8) * NCH, (m % 8 + 1) * NCH)
                    for j in range(4):
                        for ri in range(2):
                            op("pe", lambda: nc.tensor.matmul(Yb[bk][j * 32:(j + 1) * 32, cols], WC[b][:, m, ri, j, :], hprev[b][:, j, ri, :],
                                                              start=False, stop=(ri == 1), skip_group_check=True, tile_position=(0, 32 * j)),
                               reads=[("hp", b), ("WC" + str(b),)], writes=[("Yb", bk)])
                yv = yo[b][:, :].rearrange("p (n m) -> p m n", m=16)
                for bk in range(2):
                    op("act", lambda: nc.scalar.activation(out=yv[:, bk * 8:(bk + 1) * 8, :], in_=Yb[bk][:, 0:8 * NCH].rearrange("p (m n) -> p m n", m=8), func=AF.Copy),
                       reads=[("Yb", bk)], writes=[("yo", b)])

            load_gt(0, 0)
            U_stage(0, 0)
            ST_stage(0, 0)
            for gt in range(16):
                b = gt % 2
                if gt + 1 < 16:
                    load_gt(gt + 1, 1 - b)
                INTRA_stage(gt, b)
                SCAN_stage(gt, b, first_tile=(Tt == 0))
                if gt + 1 < 16:
                    U_stage(gt + 1, 1 - b)
                    ST_stage(gt + 1, 1 - b)
                INTER_stage(gt, b)
                if d_ys5 is not None:
                    P.dma("sp", d_ys5[gt * 128:(gt + 1) * 128, r0:r0 + NTS], yo[b][:], reads=[("yo", b)])
                op("act", lambda: nc.scalar.activation(out=YG[:, gt, :], in_=yo[b][:], func=AF.Gelu_apprx_tanh),
                   reads=[("yo", b)], writes=[("YG", gt)])
            ygk = [("YG", k) for k in range(16)]
            g1k = [("G1", k) for k in range(16)]
            for hf in range(NTS // 512):
                cs = slice(hf * 512, (hf + 1) * 512)
                load_ft(0, 0)
                for ft in range(16):
                    b = ft % 2
                    if ft + 1 < 16:
                        load_ft(ft + 1, 1 - b)
                    pg, pz = banks[b], banks[2 + b]
                    for k in range(16):
                        op("pe", lambda: nc.tensor.matmul(pg[:, :], Wg[b][:, k, :], YG[:, k, cs], start=(k == 0), stop=(k == 15)),
                           reads=ygk + [("Wg", b)], writes=[("Yb", b)])
                    for j in range(8):
                        op("pe", lambda: nc.tensor.matmul(pz[:, :], Wz[b][:, j, :], hT[:, j, cs], start=(j == 0), stop=(j == 7)),
                           reads=hTk + [("Wz", b)], writes=[("Sb", b)])
                    op("act", lambda: nc.scalar.activation(out=sg[:], in_=pg[:, :], func=AF.Sigmoid, bias=bglu[:, ft:ft + 1], scale=1.0),
                       reads=[("Yb", b), "bglu"], writes=["sg"])
                    op("act", lambda: nc.scalar.activation(out=sz[:], in_=pz[:, :], func=AF.Silu), reads=[("Sb", b)], writes=["sz"])
                    op("dve", lambda: nc.vector.tensor_tensor(tg[:], YG[:, ft, cs], sg[:], ALU.mult), reads=[("YG", ft), "sg"], writes=["tg"])
                    op("dve", lambda: nc.vector.tensor_tensor(G1[:, ft, :], tg[:], sz[:], ALU.mult), reads=["tg", "sz"], writes=[("G1", ft)])
                for tt in range(4):
                    t = (r0 + hf * 512) // 128 + tt
                    b = t % 2
                    P.dma("sp", xr[b][:], x1_src[t * 128:(t + 1) * 128, :], writes=[("xr", b)])
                    for half in range(2):
                        bk = 2 + (t * 2 + half) % 4
                        for k in range(16):
                            op("pe", lambda: nc.tensor.matmul(banks[bk][:], G1[:, k, tt * 128:(tt + 1) * 128], Wo[:, k, half * 512:(half + 1) * 512],
                                                              start=(k == 0), stop=(k == 15)),
                               reads=g1k + [("Wo", k // 4)], writes=[("Sb", bk - 2)])
                        op("dve", lambda: nc.vector.tensor_tensor(xr[b][:, half * 512:(half + 1) * 512], banks[bk][:],
                                                                  xr[b][:, half * 512:(half + 1) * 512], ALU.add),
                           reads=[("Sb", bk - 2), ("xr", b)], writes=[("xr", b)])
                    P.dma("sp", out[t * 128:(t + 1) * 128, :], xr[b][:], reads=[("xr", b)], writes=[("x2", t)])


def build_l1(NSUP=NSUP_FULL, debug=False, probe_no_restream=False):
    nc = bass.Bass("TRN2", target_bir_lowering=False)
    x1 = nc.dram_tensor("x1", [S, D], F32, kind="ExternalInput").ap()
    prm = {k: nc.dram_tensor(k, shp, F32, kind="ExternalInput").ap() for k, shp in L1_PARAMS.items()}
    c_ident = nc.dram_tensor("c_ident", [128, 128], BF16, kind="ExternalInput").ap()
    out = nc.dram_tensor("out", [S, D], F32, kind="ExternalOutput").ap()
    d_ys5 = nc.dram_tensor("d_ys5", [E, S], F32, kind="ExternalOutput").ap() if debug else None
    o_K = nc.dram_tensor("o_K", [16, 128, 16, 128], BF16, kind="Internal").ap()
    o_WS = nc.dram_tensor("o_WS", [16, 128, 16, 2, 128], BF16, kind="Internal").ap()
    o_WC = nc.dram_tensor("o_WC", [16, 128, 16, 2, 4, 32], BF16, kind="Internal").ap()
    with ExitStack() as st:
        P = Prog(nc, st)
        emit_layer1(nc, P, x1, prm, c_ident, out, o_K, o_WS, o_WC, NSUP=NSUP, d_ys5=d_ys5, probe_no_restream=probe_no_restream)
        P.finish()
        build_l1.stats = {"ninst": P.ninst, "nsem": P._nsem}
    return nc


def run_layer1_spmd(inputs, x1, n_cores=8):
    nc = build_l1()
    ident = host_consts()["c_ident"]
    in_maps = []
    for b in range(n_cores):
        m = {"x1": np.ascontiguousarray(x1[b]), "c_ident": ident}
        for k in L1_PARAMS:
            m[k] = np.ascontiguousarray(inputs[k])
        in_maps.append(m)
    res = run_bass_kernel_spmd(nc, in_maps, core_ids=list(range(n_cores)))
    return np.stack([np.asarray(r["out"]) for r in res.results], axis=0)


def build_fused(NQ=8, NSUP=NSUP_FULL):
    nc = bass.Bass("TRN2", target_bir_lowering=False)
    x = nc.dram_tensor("x", [S, D], F32, kind="ExternalInput").ap()
    prm = {k: nc.dram_tensor(k, shp, F32, kind="ExternalInput").ap() for k, shp in {**L0_PARAMS, **L1_PARAMS}.items()}
    cst = {k: nc.dram_tensor(k, shp, dt, kind="ExternalInput").ap() for k, (shp, dt) in CONST_SHAPES.items()}
    out = nc.dram_tensor("out", [S, D], F32, kind="ExternalOutput").ap()
    gscr = nc.dram_tensor("gscr", [128, NH, S], BF16, kind="Internal").ap()
    x1scr = nc.dram_tensor("x1scr", [S, D], F32, kind="Internal").ap()
    o_K = nc.dram_tensor("o_K", [16, 128, 16, 128], BF16, kind="Internal").ap()
    o_WS = nc.dram_tensor("o_WS", [16, 128, 16, 2, 128], BF16, kind="Internal").ap()
    o_WC = nc.dram_tensor("o_WC", [16, 128, 16, 2, 4, 32], BF16, kind="Internal").ap()
    with ExitStack() as st:
        P = Prog(nc, st)
        emit_layer0(nc, P, x, prm, cst, x1scr, gscr, NQ=NQ)
        P.barrier()
        emit_layer1(nc, P, x1scr, prm, cst["c_ident"], out, o_K, o_WS, o_WC, NSUP=NSUP)
        P.finish()
        build_fused.stats = {"ninst": P.ninst, "nsem": P._nsem}
    return nc


def fused_in_maps(inputs, n_cores=8):
    cst = host_consts()
    maps = []
    for b in range(n_cores):
        m = {"x": np.ascontiguousarray(inputs["x"][b])}
        for k in list(L0_PARAMS) + list(L1_PARAMS):
            m[k] = np.ascontiguousarray(inputs[k])
        m.update(cst)
        maps.append(m)
    return maps


FUSED = True


def kernel(**inputs):
    inputs = {k: np.asarray(v) for k, v in inputs.items()}
    if FUSED:
        nc = build_fused()
        res = run_bass_kernel_spmd(nc, fused_in_maps(inputs), core_ids=list(range(8)))
        return np.stack([np.asarray(r["out"]) for r in res.results], axis=0).astype(np.float32, copy=False)
    x1 = run_layer0_spmd(inputs)
    x2 = run_layer1_spmd(inputs, x1)
    return x2.astype(np.float32, copy=False)
```

```python
import math
from contextlib import ExitStack

import numpy as np
import ml_dtypes

import concourse.bass as bass
import concourse.mybir as mybir
from concourse.bass_utils import run_bass_kernel_spmd

F32 = mybir.dt.float32
BF16 = mybir.dt.bfloat16
AF = mybir.ActivationFunctionType
ALU = mybir.AluOpType
AX = mybir.AxisListType

S = 4096
D = 1024
E = 2048
NH = 16
EPS = 1e-6
LAM0 = 0.8 - 0.6 * math.exp(-0.3 * 0)
NEG = -30000.0
EPOCH = 12000


class Prog:
    def __init__(self, nc, stack, n_dma_sems=48):
        self.nc = nc
        self.stack = stack
        self.eng = {"pe": nc.tensor, "act": nc.scalar, "dve": nc.vector,
                    "pool": nc.gpsimd, "sp": nc.sync}
        self._nsem = 0
        self.cur = {}
        for e in ("pe", "act", "dve", "pool"):
            self.cur[e] = [self._newsem(e), 0]
        n_sw = max(8, n_dma_sems // 3)
        self.dma_pool = {"sp": [[self._newsem("dmah"), 0] for _ in range(n_dma_sems - n_sw)],
                         "pool": [[self._newsem("dmas"), 0] for _ in range(n_sw)]}
        self.dma_rr = {"sp": 0, "pool": 0}
        self.dma_sems = self.dma_pool["sp"] + self.dma_pool["pool"]
        self.waited = {}
        self.lastw = {}
        self.readers = {}
        self.ninst = 0

    def _newsem(self, name):
        self._nsem += 1
        return self.stack.enter_context(self.nc.semaphore("s_%s_%d" % (name, self._nsem)))

    def _wait(self, e, h):
        sem, val, src = h
        if src == "pe" and e == "pe":
            return
        k = (e, id(sem))
        if self.waited.get(k, 0) >= val:
            return
        self.waited[k] = val
        self.eng[e].wait_ge(sem, val)

    def deps(self, e, reads, writes):
        for r in reads:
            h = self.lastw.get(r)
            if h is not None:
                self._wait(e, h)
        for w in writes:
            h = self.lastw.get(w)
            if h is not None:
                self._wait(e, h)
            for h in self.readers.get(w, ()):
                self._wait(e, h)

    def _commit(self, h, reads, writes):
        for r in reads:
            lst = self.readers.setdefault(r, [])
            lst[:] = [x for x in lst if x[0] is not h[0]]
            lst.append(h)
        for w in writes:
            self.lastw[w] = h
            self.readers[w] = []

    def op(self, e, fn, reads=(), writes=()):
        self.deps(e, reads, writes)
        c = self.cur[e]
        if c[1] >= EPOCH:
            c = self.cur[e] = [self._newsem(e), 0]
        ins = fn()
        c[1] += 1
        ins.then_inc(c[0], 1)
        self.ninst += 1
        h = (c[0], c[1], e)
        self._commit(h, reads, writes)
        return h

    def dma(self, q, out, in_, reads=(), writes=(), **kw):
        self.deps(q, reads, writes)
        pool_ = self.dma_pool[q]
        slot = pool_[self.dma_rr[q]]
        self.dma_rr[q] = (self.dma_rr[q] + 1) % len(pool_)
        if slot[1] > 0:
            self._wait(q, (slot[0], slot[1], "dma"))
        ins = self.eng[q].dma_start(out=out, in_=in_, **kw)
        slot[1] += 16
        ins.then_inc(slot[0], 16)
        h = (slot[0], slot[1], "dma")
        self._commit(h, reads, writes)
        return h

    def barrier(self):
        hs = [(slot[0], slot[1], "dma") for slot in self.dma_sems if slot[1] > 0]
        hs += [(c[0], c[1], "bar") for c in self.cur.values() if c[1] > 0]
        for e in ("sp", "pe", "act", "dve", "pool"):
            for h in hs:
                self._wait(e, h)

    def finish(self):
        for slot in self.dma_sems:
            if slot[1] > 0:
                self._wait("sp", (slot[0], slot[1], "dma"))
        for e, c in self.cur.items():
            if c[1] > 0:
                self._wait("sp", (c[0], c[1], e))


def _split3(x):
    x = np.asarray(x, np.float32)
    hi = x.astype(ml_dtypes.bfloat16)
    r1 = x - hi.astype(np.float32)
    mid = r1.astype(ml_dtypes.bfloat16)
    r2 = r1 - mid.astype(np.float32)
    lo = r2.astype(ml_dtypes.bfloat16)
    return hi, mid, lo


def host_consts():
    bf = ml_dtypes.bfloat16
    c = {}
    c["c_ident"] = np.eye(128, dtype=np.float32).astype(bf)
    blk = np.zeros((128, 128), np.float32)
    blk[:64, :64] = 1.0 / 64
    blk[64:, 64:] = 1.0 / 64
    c["c_blk"] = blk.astype(bf)
    mask = np.zeros((128, 4, 512), np.float32)
    ki = np.arange(128)[:, None]
    qi = np.arange(512)[None, :]
    for j in range(4):
        mask[:, j, :] = np.where(128 * j + ki <= qi, 0.0, NEG)
    c["c_mask"] = mask.astype(bf)
    pos = np.arange(S, dtype=np.float64)
    qb = np.zeros((NH, 6, S), bf)
    kb = np.zeros((NH, 6, S), bf)
    for h in range(NH):
        slope = 2.0 ** (-8.0 * (h + 1) / NH)
        a, b_, c_ = _split3((-slope * pos).astype(np.float32))
        qb[h, 0], qb[h, 1], qb[h, 2] = a, b_, c_
        qb[h, 3:6] = 1.0
        a, b_, c_ = _split3((slope * pos).astype(np.float32))
        kb[h, 0:3] = 1.0
        kb[h, 3], kb[h, 4], kb[h, 5] = a, b_, c_
    c["c_qb"] = qb
    c["c_kb"] = kb
    return c


CONST_SHAPES = {
    "c_ident": ([128, 128], BF16), "c_blk": ([128, 128], BF16),
    "c_mask": ([128, 4, 512], BF16), "c_qb": ([NH, 6, S], BF16), "c_kb": ([NH, 6, S], BF16),
}

L0_PARAMS = {
    "l0_norm_g": [D], "l0_w_in": [D, 4 * E], "l0_q_norm_g": [64], "l0_k_norm_g": [64],
    "l0_lam_q1": [64], "l0_lam_k1": [64], "l0_lam_q2": [64], "l0_lam_k2": [64],
    "l0_head_norm_g": [128], "l0_w_out": [E, D],
}


def vec2(ap):
    return ap.rearrange("(n o) -> n o", o=1)


def alloc_norm_bufs(nc, st, tagp):
    T = lambda name, shape, dt: st.enter_context(nc.sbuf_tensor(name, shape, dt))
    return {"xt": [T(tagp + "xt%d" % i, [128, D], F32) for i in range(2)],
            "xn": [T(tagp + "xn%d" % i, [128, D], BF16) for i in range(2)],
            "junk": T(tagp + "junk", [128, D], BF16), "stat": T(tagp + "stat", [128, 8], F32)}


def emit_norm_transpose(nc, P, st, x_src, hT, gbc, ident, banksb, tagp, NT=32, x_keep=None):
    if x_keep is None:
        x_keep = alloc_norm_bufs(nc, st, tagp)
    xt, xn, junk, stat = x_keep["xt"], x_keep["xn"], x_keep["junk"], x_keep["stat"]
    for t in range(NT):
        b = t % 2
        P.dma("sp", xt[b][:], x_src[t * 128:(t + 1) * 128, :], writes=[(tagp, "xt", b)])
        P.op("act", lambda: nc.scalar.activation(out=junk[:], in_=xt[b][:], func=AF.Square,
                                                 accum_out=stat[:, b:b + 1]),
             reads=[(tagp, "xt", b)], writes=[(tagp, "junk"), (tagp, "ss", b)])
        P.op("act", lambda: nc.scalar.activation(out=stat[:, 2 + b:3 + b], in_=stat[:, b:b + 1], func=AF.Sqrt,
                                                 scale=1.0 / D, bias=EPS),
             reads=[(tagp, "ss", b)], writes=[(tagp, "sd", b)])
        P.op("dve", lambda: nc.vector.reciprocal(stat[:, 4 + b:5 + b], stat[:, 2 + b:3 + b]),
             reads=[(tagp, "sd", b)], writes=[(tagp, "rs", b)])
        P.op("dve", lambda: nc.vector.scalar_tensor_tensor(xn[b][:], xt[b][:], stat[:, 4 + b:5 + b], gbc[:],
                                                           ALU.mult, ALU.mult),
             reads=[(tagp, "xt", b), (tagp, "rs", b), "gbc"], writes=[(tagp, "xn", b)])
        pb = banksb[t % 2]
        for j in range(8):
            P.op("pe", lambda: nc.tensor.transpose(pb[:, j * 128:(j + 1) * 128], xn[b][:, j * 128:(j + 1) * 128],
                                                   ident[:]),
                 reads=[(tagp, "xn", b), "ident"], writes=[("pbb", t % 2)])
        P.op("dve", lambda: nc.vector.tensor_copy(hT[:, :, t * 128:(t + 1) * 128],
                                                  pb[:, :].rearrange("p (j c) -> p j c", j=8)),
             reads=[("pbb", t % 2)], writes=[("hT", t // 4)])


def emit_layer0(nc, P, x, prm, cst, x1_out, gscr, heads=None, NQ=8, dump=None):
    heads = list(range(NH)) if heads is None else list(heads)
    with ExitStack() as st:
        T = lambda name, shape, dt: st.enter_context(nc.sbuf_tensor(name, shape, dt))
        banks = [st.enter_context(nc.psum_tensor("bank%d" % i, [128, 512], F32)) for i in range(6)]
        banksb = [st.enter_context(nc.psum_tensor("bankb%d" % i, [128, 1024], BF16)) for i in range(2)]
        ident = T("ident", [128, 128], BF16)
        blk = T("blk", [128, 128], BF16)
        mask = T("mask", [128, 4, 512], BF16)
        gbc = T("gbc", [128, D], F32)
        small = T("small", [128, 16], F32)
        lamv = T("lamv", [128, 4, 64], F32)
        lamj = T("lamj", [128, 64], F32)
        ghn = T("ghn", [128, 128], F32)
        P.dma("sp", ident[:], cst["c_ident"][:, :], writes=["ident"])
        P.dma("sp", blk[:], cst["c_blk"][:, :], writes=["blk"])
        P.dma("sp", mask[:], cst["c_mask"][:, :, :], writes=["mask"])
        P.dma("sp", gbc[:], prm["l0_norm_g"].partition_broadcast(128), writes=["gbc"])
        P.dma("sp", ghn[:], prm["l0_head_norm_g"].partition_broadcast(128), writes=["ghn"])
        for m in range(2):
            P.dma("sp", small[m * 64:(m + 1) * 64, 0:1], vec2(prm["l0_q_norm_g"]), writes=["small"])
            P.dma("sp", small[m * 64:(m + 1) * 64, 1:2], vec2(prm["l0_k_norm_g"]), writes=["small"])
        for i, nm in enumerate(["l0_lam_q1", "l0_lam_k1", "l0_lam_q2", "l0_lam_k2"]):
            P.dma("sp", lamv[:, i, :], prm[nm].partition_broadcast(128), writes=["lamv"])
        P.op("dve", lambda: nc.vector.tensor_scalar(small[:, 0:1], small[:, 0:1], 0.125, None, ALU.mult),
             reads=["small"], writes=["small"])
        for i in range(2):
            P.op("dve", lambda: nc.vector.tensor_tensor(lamj[:], lamv[:, 2 * i, :], lamv[:, 2 * i + 1, :], ALU.mult),
                 reads=["lamv"], writes=["lamj"])
            P.op("dve", lambda: nc.vector.reduce_sum(small[:, 4 + i:5 + i], lamj[:], axis=AX.X),
                 reads=["lamj"], writes=["small"])
        P.op("act", lambda: nc.scalar.activation(out=small[:, 6:8], in_=small[:, 4:6], func=AF.Exp),
             reads=["small"], writes=["small"])
        P.op("dve", lambda: nc.vector.scalar_tensor_tensor(small[:, 2:3], small[:, 7:8], -LAM0, small[:, 6:7],
                                                           ALU.add, ALU.subtract),
             reads=["small"], writes=["small"])
        P.op("dve", lambda: nc.vector.tensor_scalar(ghn[:], ghn[:], 1.0 - LAM0, None, ALU.mult),
             reads=["ghn"], writes=["ghn"])

        hT = T("hT", [128, 8, S], BF16)
        with ExitStack() as stA:
            emit_norm_transpose(nc, P, stA, x, hT, gbc, ident, banksb, "A", NT=4 * NQ)
            P.barrier()

        with ExitStack() as stH:
            TH = lambda name, shape, dt: stH.enter_context(nc.sbuf_tensor(name, shape, dt))
            QA = [TH("QA%d" % m, [128, S], BF16) for m in range(2)]
            KA = [TH("KA%d" % m, [128, S], BF16) for m in range(2)]
            VA = TH("VA", [128, 32, 130], BF16)
            ZT = TH("ZT", [128, S], BF16)
            GT = [TH("GT%d" % i, [128, S], BF16) for i in range(2)]
            Wh = [TH("Wh%d" % i, [128, 8, 4, 128], BF16) for i in range(2)]
            sq = [TH("sq%d" % i, [128, 512], BF16) for i in range(2)]
            rst = [TH("rst%d" % i, [128, 512], F32) for i in range(2)]
            pbuf = [TH("pbuf%d" % i, [128, 512], BF16) for i in range(4)]
            fin = TH("fin", [128, 4, 8], F32)
            o0 = TH("o0", [128, 4, 128], F32)
            o1 = TH("o1", [128, 4, 128], F32)
            onb = TH("onb", [128, 2, 4, 128], BF16)
            junk2 = TH("junk2", [128, 128], BF16)
            junk2f = TH("junk2f", [128, 128], F32)
            pending = []

            def flush_pending():
                while pending:
                    pending.pop(0)()
            P.op("pool", lambda: nc.gpsimd.memset(VA[:, :, 128:130], 1.0), writes=["VAones"])
            epsb = TH("epsb", [128, 1], F32)
            P.op("pool", lambda: nc.gpsimd.memset(epsb[:], EPS), writes=["epsb"])
            accs = [[TH("accs%d_%d" % (i, k), [128, 390], F32) for k in range(3)] for i in range(2)]

            w_in = prm["l0_w_in"].rearrange("(j p) f -> p j f", p=128)

            def load_head_weights(h):
                wb = Wh[h % 2]
                for sec in range(4):
                    P.dma("pool", wb[:, :, sec, :], w_in[:, :, sec * E + h * 128: sec * E + (h + 1) * 128],
                          writes=[("Wh", h % 2, sec)])

            load_head_weights(heads[0])
            for hi_, h in enumerate(heads):
                wb = Wh[h % 2]
                if hi_ + 1 < len(heads):
                    load_head_weights(heads[hi_ + 1])
                for m in range(2):
                    P.dma("sp", QA[m][64:70, :], cst["c_qb"][h, :, :], writes=[("QAb", m)])
                    P.dma("sp", KA[m][64:70, :], cst["c_kb"][h, :, :], writes=[("KAb", m)])
                for nt in range(NQ):
                    tok = slice(nt * 512, (nt + 1) * 512)
                    hkeys = [("hT", nt)]
                    for sec, bk in ((0, 0), (1, 1), (3, 2)):
                        for j in range(8):
                            P.op("pe", lambda: nc.tensor.matmul(banks[bk][:], wb[:, j, sec, :], hT[:, j, tok],
                                                                start=(j == 0), stop=(j == 7)),
                                 reads=hkeys + [("Wh", h % 2, sec)], writes=[("bk", bk)])
                    for qi, bk in ((0, 0), (1, 1)):
                        P.op("act", lambda: nc.scalar.activation(out=sq[qi][:], in_=banks[bk][:], func=AF.Square),
                             reads=[("bk", bk)], writes=[("sq", qi)])
                    for qi in range(2):
                        P.op("pe", lambda: nc.tensor.matmul(banks[3 + qi][:], blk[:], sq[qi][:], start=True, stop=True),
                             reads=["blk", ("sq", qi)], writes=[("bk", 3 + qi)])
                    for qi, (bk, tiles, gcol, key) in enumerate(((0, QA, 0, "QA"), (1, KA, 1, "KA"))):
                        P.op("act", lambda: nc.scalar.activation(out=rst[qi][:], in_=banks[3 + qi][:], func=AF.Ln,
                                                                 scale=1.0, bias=epsb[:, 0:1]),
                             reads=[("bk", 3 + qi), "epsb"], writes=[("rst", qi)])
                        P.op("act", lambda: nc.scalar.activation(out=rst[qi][:], in_=rst[qi][:], func=AF.Exp, scale=-0.5),
                             reads=[("rst", qi)], writes=[("rst", qi)])
                        for m in range(2):
                            ps = slice(m * 64, (m + 1) * 64)
                            P.op("dve", lambda: nc.vector.scalar_tensor_tensor(
                                tiles[m][0:64, tok], banks[bk][ps, :], small[ps, gcol:gcol + 1], rst[qi][ps, :],
                                ALU.mult, ALU.mult),
                                reads=[("bk", bk), ("rst", qi), "small"], writes=[(key, m, nt)])
                    P.op("act", lambda: nc.scalar.activation(out=ZT[:, tok], in_=banks[2][:], func=AF.Silu),
                         reads=[("bk", 2)], writes=[("ZT", nt)])
                    for i in range(4):
                        kt = nt * 4 + i
                        for j in range(8):
                            P.op("pe", lambda: nc.tensor.matmul(banks[5][:, i * 128:(i + 1) * 128],
                                                                hT[:, j, kt * 128:(kt + 1) * 128], wb[:, j, 2, :],
                                                                start=(i == 0 and j == 0), stop=(j == 7),
                                                                skip_group_check=True),
                                 reads=hkeys + [("Wh", h % 2, 2)], writes=[("bk", 5)])
                    P.op("dve", lambda: nc.vector.tensor_copy(
                        VA[:, nt * 4:(nt + 1) * 4, 0:128],
                        banks[5][:, :].rearrange("p (i c) -> p i c", i=4)),
                        reads=[("bk", 5)], writes=[("VA", nt)])

                gt = GT[h % 2]
                ti = 0
                for Q in range(NQ):
                    qs = slice(Q * 512, (Q + 1) * 512)
                    tiles = [(m, kt) for m in range(2) for kt in range(4 * Q + 4)]
                    accreg = {}
                    slots = [(4, 0), (4, 1), (4, 2), (5, 0), (5, 1), (5, 2), (3, 0), (3, 1)]
                    for idx, (qb, m) in enumerate([(qb, m) for m in range(2) for qb in range(4)]):
                        accreg[(qb, m)] = slots[idx]
                    started = set()

                    def emit_S(i):
                        m, kt = tiles[i]
                        sb = (ti + i) % 3
                        diag = kt >= 4 * Q
                        P.op("pe", lambda: nc.tensor.matmul(banks[sb][:], KA[m][0:70, kt * 128:(kt + 1) * 128],
                                                            QA[m][0:70, qs], start=True, stop=not diag),
                             reads=[("KA", m, kt // 4), ("KAb", m), ("QA", m, Q), ("QAb", m)], writes=[("bk", sb)])
                        if diag:
                            P.op("pe", lambda: nc.tensor.matmul(banks[sb][:], ident[:], mask[:, kt - 4 * Q, :],
                                                                start=False, stop=True),
                                 reads=["ident", "mask"], writes=[("bk", sb)])

                    def emit_EP(i):
                        m, kt = tiles[i]
                        sb = (ti + i) % 3
                        pb = (ti + i) % 4
                        P.op("act", lambda: nc.scalar.activation(out=pbuf[pb][:], in_=banks[sb][:], func=AF.Exp),
                             reads=[("bk", sb)], writes=[("pbuf", pb)])
                        j = kt - 4 * Q
                        for qb in range(4):
                            if j >= 0 and qb < j:
                                continue
                            bk, r = accreg[(qb, m)]
                            first = bk not in started
                            started.add(bk)
                            P.op("pe", lambda: nc.tensor.matmul(banks[bk][:, r * 130:r * 130 + 129],
                                                                pbuf[pb][:, qb * 128:(qb + 1) * 128], VA[:, kt, 0:129],
                                                                start=first, stop=False, skip_group_check=True),
                                 reads=[("pbuf", pb), ("VA", kt // 4), "VAones"], writes=[("bk", bk)])

                    n = len(tiles)
                    emit_S(0)
                    if n > 1:
                        emit_S(1)
                    for i in range(n):
                        if i + 2 < n:
                            emit_S(i + 2)
                        emit_EP(i)
                        if i == min(n - 1, 24):
                            flush_pending()
                    ti += n
                    ab = Q % 2
                    sidx = {4: 0, 5: 1, 3: 2}
                    for bk_, ncol in ((4, 390), (5, 390), (3, 260)):
                        P.op("dve", lambda: nc.vector.tensor_copy(accs[ab][sidx[bk_]][:, 0:ncol], banks[bk_][:, 0:ncol]),
                             reads=[("bk", bk_)], writes=[("accs", ab, sidx[bk_])])
                    def tail(Q=Q, qs=qs, gt=gt, h=h, ab=ab, accreg=accreg):
                        ob = Q % 2
                        for qb in range(4):
                            b0, r0 = accreg[(qb, 0)]
                            b1, r1 = accreg[(qb, 1)]
                            s0, s1 = accs[ab][sidx[b0]], accs[ab][sidx[b1]]
                            k0, k1 = ("accs", ab, sidx[b0]), ("accs", ab, sidx[b1])
                            a0 = s0[:, r0 * 130:r0 * 130 + 128]
                            d0 = s0[:, r0 * 130 + 128:r0 * 130 + 129]
                            a1 = s1[:, r1 * 130:r1 * 130 + 128]
                            d1 = s1[:, r1 * 130 + 128:r1 * 130 + 129]
                            fk = ("fin", qb)
                            P.op("dve", lambda: nc.vector.reciprocal(fin[:, qb, 0:1], d0), reads=[k0], writes=[fk])
                            P.op("dve", lambda: nc.vector.reciprocal(fin[:, qb, 1:2], d1), reads=[k1], writes=[fk])
                            P.op("dve", lambda: nc.vector.tensor_tensor(fin[:, qb, 2:3], fin[:, qb, 1:2], small[:, 2:3], ALU.mult),
                                 reads=[fk, "small"], writes=[fk])
                            P.op("dve", lambda: nc.vector.tensor_scalar(o0[:, qb, :], a0, fin[:, qb, 0:1], None, ALU.mult),
                                 reads=[k0, fk], writes=[("o0", qb)])
                            P.op("dve", lambda: nc.vector.scalar_tensor_tensor(o1[:, qb, :], a1, fin[:, qb, 2:3], o0[:, qb, :],
                                                                               ALU.mult, ALU.add),
                                 reads=[k1, fk, ("o0", qb)], writes=[("o1", qb)])
                        ob = Q % 2
                        fks = [("fin", qb) for qb in range(4)]
                        for qb in range(4):
                            P.op("dve", lambda: nc.vector.scalar_tensor_tensor(junk2f[:], o1[:, qb, :], 1.0, o1[:, qb, :], ALU.mult, ALU.mult,
                                                                               accum_out=fin[:, qb, 3:4]),
                                 reads=[("o1", qb)], writes=["junk2f", ("fin", qb)])
                        P.op("act", lambda: nc.scalar.activation(out=fin[:, :, 4:5], in_=fin[:, :, 3:4], func=AF.Sqrt,
                                                                 scale=1.0 / 128, bias=EPS),
                             reads=fks, writes=fks)
                        P.op("dve", lambda: nc.vector.reciprocal(fin[:, :, 5:6], fin[:, :, 4:5]), reads=fks, writes=fks)
                        for qb in range(4):
                            P.op("dve", lambda: nc.vector.scalar_tensor_tensor(onb[:, ob, qb, :], o1[:, qb, :], fin[:, qb, 5:6], ghn[:],
                                                                               ALU.mult, ALU.mult),
                                 reads=[("o1", qb), ("fin", qb), "ghn"], writes=[("onb", ob, qb)])
                        tb = banksb[Q % 2]
                        for qb in range(4):
                            P.op("pe", lambda: nc.tensor.transpose(tb[:, qb * 128:(qb + 1) * 128], onb[:, ob, qb, :], ident[:]),
                                 reads=[("onb", ob, qb), "ident"], writes=[("pbb", Q % 2)])
                        P.op("dve", lambda: nc.vector.tensor_tensor(gt[:, qs], tb[:, 0:512], ZT[:, qs], ALU.mult),
                             reads=[("pbb", Q % 2), ("ZT", Q)], writes=[("GT", h % 2)])
                    pending.append(tail)
                flush_pending()
                P.dma("sp", gscr[:, h, 0:NQ * 512], gt[:, 0:NQ * 512], reads=[("GT", h % 2)], writes=[("gscr", h)])
                if dump is not None and h == heads[-1]:
                    n = NQ * 512
                    allk = list(P.lastw.keys())
                    P.dma("sp", dump["d_hT"][:, :, :], hT[:, :, 0:n], reads=allk)
                    for m in range(2):
                        P.dma("sp", dump["d_QA"][m, :, :], QA[m][0:70, 0:n], reads=allk)
                        P.dma("sp", dump["d_KA"][m, :, :], KA[m][0:70, 0:n], reads=allk)
                    P.dma("sp", dump["d_VA"][:, :, :], VA[:, 0:4 * NQ, :], reads=allk)
                    P.dma("sp", dump["d_ZT"][:, :], ZT[:, 0:n], reads=allk)
                    P.dma("sp", dump["d_GT"][:, :], gt[:, 0:n], reads=allk)

        if dump is not None:
            return
        P.barrier()
        with ExitStack() as stD:
            TD = lambda name, shape, dt: stD.enter_context(nc.sbuf_tensor(name, shape, dt))
            Wo = TD("Wo", [128, NH, D], BF16)
            Gt = [TD("Gt%d" % i, [128, NH, 512], BF16) for i in range(2)]
            xr = [TD("xr%d" % i, [128, D], F32) for i in range(2)]
            x1t = [TD("x1t%d" % i, [128, D], F32) for i in range(2)]
            w_out = prm["l0_w_out"].rearrange("(h p) f -> p h f", p=128)
            for hh in range(0, NH, 4):
                P.dma("pool", Wo[:, hh:hh + 4, :], w_out[:, hh:hh + 4, :], writes=[("Wo", hh // 4)])
            for Tt in range(NQ):
                g = Gt[Tt % 2]
                P.dma("sp", g[:], gscr[:, :, Tt * 512:(Tt + 1) * 512], reads=[("gscr", h) for h in heads],
                      writes=[("Gt", Tt % 2)])
                for tt in range(4):
                    t = Tt * 4 + tt
                    b = t % 2
                    P.dma("sp", xr[b][:], x[t * 128:(t + 1) * 128, :], writes=[("xr", b)])
                    for half in range(2):
                        bk = (t * 2 + half) % 4
                        for h in range(NH):
                            P.op("pe", lambda: nc.tensor.matmul(banks[bk][:], g[:, h, tt * 128:(tt + 1) * 128],
                                                                Wo[:, h, half * 512:(half + 1) * 512],
                                                                start=(h == 0), stop=(h == NH - 1)),
                                 reads=[("Gt", Tt % 2), ("Wo", h // 4)], writes=[("bk", bk)])
                        P.op("dve", lambda: nc.vector.tensor_tensor(x1t[b][:, half * 512:(half + 1) * 512], banks[bk][:],
                                                                    xr[b][:, half * 512:(half + 1) * 512], ALU.add),
                             reads=[("bk", bk), ("xr", b)], writes=[("x1t", b)])
                    P.dma("sp", x1_out[t * 128:(t + 1) * 128, :], x1t[b][:], reads=[("x1t", b)], writes=[("x1", t)])


def build_l0(NQ=8):
    nc = bass.Bass("TRN2", target_bir_lowering=False)
    x = nc.dram_tensor("x", [S, D], F32, kind="ExternalInput").ap()
    prm = {k: nc.dram_tensor(k, shp, F32, kind="ExternalInput").ap() for k, shp in L0_PARAMS.items()}
    cst = {k: nc.dram_tensor(k, shp, dt, kind="ExternalInput").ap() for k, (shp, dt) in CONST_SHAPES.items()}
    out = nc.dram_tensor("out", [S, D], F32, kind="ExternalOutput").ap()
    gscr = nc.dram_tensor("gscr", [128, NH, S], BF16, kind="Internal").ap()
    with ExitStack() as st:
        P = Prog(nc, st)
        emit_layer0(nc, P, x, prm, cst, out, gscr, NQ=NQ)
        P.finish()
        build_l0.stats = {"ninst": P.ninst, "nsem": P._nsem, "dma_uses": sum(sl[1] // 16 for sl in P.dma_sems),
                          "cur": {e: c[1] for e, c in P.cur.items()}}
    return nc


def build_l0_debug(heads=(0,), NQ=2):
    nc = bass.Bass("TRN2", target_bir_lowering=False)
    x = nc.dram_tensor("x", [S, D], F32, kind="ExternalInput").ap()
    prm = {k: nc.dram_tensor(k, shp, F32, kind="ExternalInput").ap() for k, shp in L0_PARAMS.items()}
    cst = {k: nc.dram_tensor(k, shp, dt, kind="ExternalInput").ap() for k, (shp, dt) in CONST_SHAPES.items()}
    n = NQ * 512
    dshapes = {"d_hT": [128, 8, n], "d_QA": [2, 70, n], "d_KA": [2, 70, n], "d_VA": [128, 4 * NQ, 130],
               "d_ZT": [128, n], "d_GT": [128, n]}
    dump = {k: nc.dram_tensor(k, shp, BF16, kind="ExternalOutput").ap() for k, shp in dshapes.items()}
    gscr = nc.dram_tensor("gscr", [128, NH, S], BF16, kind="Internal").ap()
    with ExitStack() as st:
        P = Prog(nc, st)
        emit_layer0(nc, P, x, prm, cst, None, gscr, heads=heads, NQ=NQ, dump=dump)
        P.finish()
    return nc


def run_layer0_spmd(inputs, n_cores=8):
    nc = build_l0()
    cst = host_consts()
    in_maps = []
    for b in range(n_cores):
        m = {"x": np.ascontiguousarray(inputs["x"][b])}
        for k in L0_PARAMS:
            m[k] = np.ascontiguousarray(inputs[k])
        m.update(cst)
        in_maps.append(m)
    res = run_bass_kernel_spmd(nc, in_maps, core_ids=list(range(n_cores)))
    return np.stack([np.asarray(r["out"]) for r in res.results], axis=0)


TWO_PI = 6.283185307179586
MAGIC = 12582912.0
L1_PARAMS = {
    "l1_norm_g": [D], "l1_w_in": [D, 2 * E], "l1_lam_re": [128, 64], "l1_lam_im": [128, 64], "l1_log_dt": [128],
    "l1_b_re": [128, 64, 16], "l1_b_im": [128, 64, 16], "l1_c_re": [128, 16, 64], "l1_c_im": [128, 16, 64],
    "l1_d": [E], "l1_w_glu": [E, E], "l1_b_glu": [E], "l1_w_out": [E, D],
}


def emit_s5_tables(nc, P, st, prm, NPOW=17):
    T = lambda name, shape, dt: st.enter_context(nc.sbuf_tensor(name, shape, dt))
    tb = T("s5tb", [128, 24, 64], F32)
    pw = T("s5pw", [128, NPOW, 2, 64], F32)
    LR, LI, LDT, DT, MAG, TH, K_, SN, CS, ABR, ABI, DEN, NR, GR, GI, T1, T2 = range(17)
    V = lambda i: tb[:, i, :]
    for g2 in range(2):
        ps = slice(g2 * 64, (g2 + 1) * 64)
        for i, nm in ((LR, "l1_lam_re"), (LI, "l1_lam_im")):
            src = prm[nm].rearrange("(pr g2) p -> g2 p pr", g2=2)
            P.dma("sp", tb[ps, i, :], src[g2], writes=[("tb", i, g2)], allow_slow_non_contiguous=True)
        src = prm["l1_log_dt"].rearrange("(pr g2) -> g2 pr", g2=2)
        P.dma("sp", tb[ps, LDT, :], src[g2:g2 + 1, :].to_broadcast([64, 64]), writes=[("tb", LDT, g2)],
              allow_slow_non_contiguous=True)
    rd = lambda *idx: [("tb", i, g2) for i in idx for g2 in range(2)] + [("tb", i) for i in idx]
    op = P.op
    op("act", lambda: nc.scalar.activation(out=V(DT), in_=V(LDT), func=AF.Exp), reads=rd(LDT), writes=[("tb", DT)])
    op("dve", lambda: nc.vector.tensor_tensor(V(T1), V(LR), V(DT), ALU.mult), reads=rd(LR, DT), writes=[("tb", T1)])
    op("act", lambda: nc.scalar.activation(out=V(MAG), in_=V(T1), func=AF.Exp), reads=rd(T1), writes=[("tb", MAG)])
    op("dve", lambda: nc.vector.tensor_tensor(V(TH), V(LI), V(DT), ALU.mult), reads=rd(LI, DT), writes=[("tb", TH)])
    for dst, shift in ((SN, 0.0), (CS, math.pi / 2)):
        op("dve", lambda: nc.vector.tensor_scalar(V(T2), V(TH), shift, 1.0 / TWO_PI, ALU.add, ALU.mult),
           reads=rd(TH), writes=[("tb", T2)])
        op("dve", lambda: nc.vector.tensor_scalar(V(K_), V(T2), MAGIC, None, ALU.add), reads=rd(T2), writes=[("tb", K_)])
        op("dve", lambda: nc.vector.tensor_scalar(V(K_), V(K_), MAGIC, None, ALU.subtract), reads=rd(K_), writes=[("tb", K_)])
        op("dve", lambda: nc.vector.scalar_tensor_tensor(V(T2), V(K_), -TWO_PI, V(TH), ALU.mult, ALU.add),
           reads=rd(K_, TH), writes=[("tb", T2)])
        op("act", lambda: nc.scalar.activation(out=V(dst), in_=V(T2), func=AF.Sin, scale=1.0, bias=shift),
           reads=rd(T2), writes=[("tb", dst)])
    op("dve", lambda: nc.vector.tensor_tensor(V(ABR), V(MAG), V(CS), ALU.mult), reads=rd(MAG, CS), writes=[("tb", ABR)])
    op("dve", lambda: nc.vector.tensor_tensor(V(ABI), V(MAG), V(SN), ALU.mult), reads=rd(MAG, SN), writes=[("tb", ABI)])
    op("dve", lambda: nc.vector.tensor_tensor(V(T1), V(LR), V(LR), ALU.mult), reads=rd(LR), writes=[("tb", T1)])
    op("dve", lambda: nc.vector.tensor_tensor(V(T2), V(LI), V(LI), ALU.mult), reads=rd(LI), writes=[("tb", T2)])
    op("dve", lambda: nc.vector.tensor_tensor(V(DEN), V(T1), V(T2), ALU.add), reads=rd(T1, T2), writes=[("tb", DEN)])
    op("dve", lambda: nc.vector.reciprocal(V(DEN), V(DEN)), reads=rd(DEN), writes=[("tb", DEN)])
    op("dve", lambda: nc.vector.tensor_scalar(V(NR), V(ABR), -1.0, None, ALU.add), reads=rd(ABR), writes=[("tb", NR)])
    op("dve", lambda: nc.vector.tensor_tensor(V(T1), V(NR), V(LR), ALU.mult), reads=rd(NR, LR), writes=[("tb", T1)])
    op("dve", lambda: nc.vector.tensor_tensor(V(T2), V(ABI), V(LI), ALU.mult), reads=rd(ABI, LI), writes=[("tb", T2)])
    op("dve", lambda: nc.vector.tensor_tensor(V(GR), V(T1), V(T2), ALU.add), reads=rd(T1, T2), writes=[("tb", GR)])
    op("dve", lambda: nc.vector.tensor_tensor(V(GR), V(GR), V(DEN), ALU.mult), reads=rd(GR, DEN), writes=[("tb", GR)])
    op("dve", lambda: nc.vector.tensor_tensor(V(T1), V(ABI), V(LR), ALU.mult), reads=rd(ABI, LR), writes=[("tb", T1)])
    op("dve", lambda: nc.vector.tensor_tensor(V(T2), V(NR), V(LI), ALU.mult), reads=rd(NR, LI), writes=[("tb", T2)])
    op("dve", lambda: nc.vector.tensor_tensor(V(GI), V(T1), V(T2), ALU.subtract), reads=rd(T1, T2), writes=[("tb", GI)])
    op("dve", lambda: nc.vector.tensor_tensor(V(GI), V(GI), V(DEN), ALU.mult), reads=rd(GI, DEN), writes=[("tb", GI)])
    op("pool", lambda: nc.gpsimd.memset(pw[:, 0, 0, :], 1.0), writes=[("pw", 0)])
    op("pool", lambda: nc.gpsimd.memset(pw[:, 0, 1, :], 0.0), writes=[("pw", 0)])
    for t in range(1, NPOW):
        pr_, pi_ = pw[:, t - 1, 0, :], pw[:, t - 1, 1, :]
        op("dve", lambda: nc.vector.tensor_tensor(V(T1), pr_, V(ABR), ALU.mult), reads=[("pw", t - 1)] + rd(ABR), writes=[("tb", T1)])
        op("dve", lambda: nc.vector.tensor_tensor(V(T2), pi_, V(ABI), ALU.mult), reads=[("pw", t - 1)] + rd(ABI), writes=[("tb", T2)])
        op("dve", lambda: nc.vector.tensor_tensor(pw[:, t, 0, :], V(T1), V(T2), ALU.subtract), reads=rd(T1, T2), writes=[("pw", t)])
        op("dve", lambda: nc.vector.tensor_tensor(V(T1), pr_, V(ABI), ALU.mult), reads=[("pw", t - 1)] + rd(ABI), writes=[("tb", T1)])
        op("dve", lambda: nc.vector.tensor_tensor(V(T2), pi_, V(ABR), ALU.mult), reads=[("pw", t - 1)] + rd(ABR), writes=[("tb", T2)])
        op("dve", lambda: nc.vector.tensor_tensor(pw[:, t, 1, :], V(T1), V(T2), ALU.add), reads=rd(T1, T2), writes=[("pw", t)])
    return {"tb": tb, "pw": pw, "idx": dict(ABR=ABR, ABI=ABI, GR=GR, GI=GI)}


def build_s5_tables_debug():
    nc = bass.Bass("TRN2", target_bir_lowering=False)
    prm = {k: nc.dram_tensor(k, shp, F32, kind="ExternalInput").ap() for k, shp in L1_PARAMS.items()
           if k in ("l1_lam_re", "l1_lam_im", "l1_log_dt")}
    d_tb = nc.dram_tensor("d_tb", [128, 24, 64], F32, kind="ExternalOutput").ap()
    d_pw = nc.dram_tensor("d_pw", [128, 17, 2, 64], F32, kind="ExternalOutput").ap()
    with ExitStack() as st:
        P = Prog(nc, st)
        r = emit_s5_tables(nc, P, st, prm)
        allk = list(P.lastw.keys())
        P.dma("sp", d_tb[:, :, :], r["tb"][:], reads=allk)
        P.dma("sp", d_pw[:, :, :, :], r["pw"][:], reads=allk)
        P.finish()
    return nc


def emit_s5_operands(nc, P, st, prm, tabs, ident, gts, o_K, o_WS, o_WC, banks, banksb, o_BB=None):
    T = lambda name, shape, dt: st.enter_context(nc.sbuf_tensor(name, shape, dt))
    tb, pw, ix = tabs["tb"], tabs["pw"], tabs["idx"]
    Bs = [T("s5B%d" % i, [128, 64, 16], F32) for i in range(2)]
    Cs = [T("s5C%d" % i, [128, 64, 16], F32) for i in range(2)]
    BB = [T("s5BB%d" % i, [128, 64, 16], F32) for i in range(2)]
    t1 = T("s5t1", [128, 64, 16], F32)
    t2 = T("s5t2", [128, 64, 16], F32)
    t3 = T("s5t3", [128, 64, 16], F32)
    t4 = T("s5t4", [128, 64, 16], F32)
    pwn = T("s5pwn", [128, 17, 64], F32)
    dsk = T("s5dsk", [128, 16], F32)
    XB2 = [T("s5XB%d" % i, [128, 16, 2, 4, 32], BF16) for i in range(2)]
    WC2 = [T("s5WC%d" % i, [128, 16, 2, 4, 32], BF16) for i in range(2)]
    CB2 = [T("s5CB%d" % i, [128, 2, 4, 32], BF16) for i in range(2)]
    Kbd2 = [T("s5Kbd%d" % i, [128, 16, 128], BF16) for i in range(2)]
    WS2 = [T("s5WS%d" % i, [128, 16, 2, 128], BF16) for i in range(2)]
    identf = T("s5idf", [128, 128], F32)
    op = P.op
    for g2 in range(2):
        ps = slice(g2 * 64, (g2 + 1) * 64)
        for i, nm in enumerate(("l1_b_re", "l1_b_im")):
            P.dma("sp", Bs[i][ps, :, :], prm[nm].rearrange("(pr g2) p c -> g2 p pr c", g2=2)[g2], writes=[("Bs", i, g2)])
    P.dma("sp", dsk[:], prm["l1_d"].rearrange("(gt p) -> p gt", p=128), writes=["dsk"], allow_slow_non_contiguous=True)
    for pr_ in range(64):
        for g2 in range(2):
            ps = slice(g2 * 64, (g2 + 1) * 64)
            for i, nm in enumerate(("l1_c_re", "l1_c_im")):
                src = prm[nm].rearrange("(pr g2) co p -> g2 p pr co", g2=2)[g2]
                P.dma("sp", Cs[i][ps, pr_, :], src[:, pr_, :],
                      writes=[("Cs", i, g2, pr_)], allow_slow_non_contiguous=True)
    op("dve", lambda: nc.vector.tensor_copy(identf[:], ident[:]), reads=["ident"], writes=["identf"])
    for i in range(2):
        for tl, key in ((XB2[i], "XB"), (WC2[i], "WC"), (CB2[i], "CB"), (Kbd2[i], "Kbd")):
            op("pool", lambda: nc.gpsimd.memset(tl[:], 0.0), writes=[key + str(i)])
    rB = [("Bs", i, g2) for i in range(2) for g2 in range(2)]
    rC_of = lambda gt: [("Cs", i, g2, pr_) for i in range(2) for g2 in range(2) for pr_ in range(gt * 4, gt * 4 + 4)]
    bc = lambda ap2, n: ap2.unsqueeze(2).to_broadcast([128, n, 16])
    GR, GI = tb[:, ix["GR"], :], tb[:, ix["GI"], :]
    op("dve", lambda: nc.vector.tensor_tensor(t1[:], Bs[0][:], bc(GR, 64), ALU.mult), reads=rB + [("tb", ix["GR"])], writes=["t1"])
    op("dve", lambda: nc.vector.tensor_tensor(t2[:], Bs[1][:], bc(GI, 64), ALU.mult), reads=rB + [("tb", ix["GI"])], writes=["t2"])
    op("dve", lambda: nc.vector.tensor_tensor(BB[0][:], t1[:], t2[:], ALU.subtract), reads=["t1", "t2"], writes=["BB0"])
    op("dve", lambda: nc.vector.tensor_tensor(t1[:], Bs[1][:], bc(GR, 64), ALU.mult), reads=rB + [("tb", ix["GR"])], writes=["t1"])
    op("dve", lambda: nc.vector.tensor_tensor(t2[:], Bs[0][:], bc(GI, 64), ALU.mult), reads=rB + [("tb", ix["GI"])], writes=["t2"])
    op("dve", lambda: nc.vector.tensor_tensor(BB[1][:], t1[:], t2[:], ALU.add), reads=["t1", "t2"], writes=["BB1"])
    if o_BB is not None:
        for i in range(2):
            P.dma("sp", o_BB[i, :, :, :], BB[i][:], reads=["BB%d" % i])

    def cmul_blocks(dst, k, Ar, Ai, Pr, Pi, prs, neg_im, rA, rP, wkey):
        a, b_ = t1[:, 0:4, :], t1[:, 4:8, :]
        c_, d_ = t2[:, 0:4, :], t2[:, 4:8, :]
        op("dve", lambda: nc.vector.tensor_tensor(a, Ar[:, prs, :], bc(Pr[:, prs], 4), ALU.mult), reads=rA + rP, writes=["t1"])
        op("dve", lambda: nc.vector.tensor_tensor(b_, Ai[:, prs, :], bc(Pi[:, prs], 4), ALU.mult), reads=rA + rP, writes=["t1"])
        op("dve", lambda: nc.vector.tensor_tensor(c_, Ar[:, prs, :], bc(Pi[:, prs], 4), ALU.mult), reads=rA + rP, writes=["t2"])
        op("dve", lambda: nc.vector.tensor_tensor(d_, Ai[:, prs, :], bc(Pr[:, prs], 4), ALU.mult), reads=rA + rP, writes=["t2"])
        for g2 in range(2):
            ps = slice(g2 * 64, (g2 + 1) * 64)
            cs = slice(g2 * 16, (g2 + 1) * 16)
            op("dve", lambda: nc.vector.tensor_tensor(dst[ps, k, 0, :, cs], a[ps], b_[ps], ALU.subtract),
               reads=["t1"], writes=[wkey])
            if neg_im:
                op("dve", lambda: nc.vector.scalar_tensor_tensor(dst[ps, k, 1, :, cs], c_[ps], -1.0, d_[ps], ALU.mult, ALU.subtract),
                   reads=["t2"], writes=[wkey])
            else:
                op("dve", lambda: nc.vector.tensor_tensor(dst[ps, k, 1, :, cs], c_[ps], d_[ps], ALU.add),
                   reads=["t2"], writes=[wkey])

    rPW = [("pw", t) for t in range(17)]
    op("dve", lambda: nc.vector.tensor_scalar(pwn[:], pw[:, :, 1, :], -1.0, None, ALU.mult), reads=rPW, writes=["pwn"])

    def cmul_all_d(dst, Ar, Ai, d0, prs, neg_im, rA, wkey):
        V4 = lambda t: t[:, :, :].rearrange("p (d j) c -> p d j c", d=16)
        a, b_, c_, d_ = V4(t1), V4(t2), V4(t3), V4(t4)
        bA = lambda A: A[:, prs, :].unsqueeze(1).to_broadcast([128, 16, 4, 16])
        Pr = pw[:, d0:d0 + 16, 0, prs].unsqueeze(3).to_broadcast([128, 16, 4, 16])
        Pi = pw[:, d0:d0 + 16, 1, prs].unsqueeze(3).to_broadcast([128, 16, 4, 16])
        Pin = pwn[:, d0:d0 + 16, prs].unsqueeze(3).to_broadcast([128, 16, 4, 16])
        op("dve", lambda: nc.vector.tensor_tensor(a, bA(Ar), Pr, ALU.mult), reads=rA + rPW, writes=["t1"])
        op("dve", lambda: nc.vector.tensor_tensor(b_, bA(Ai), Pi, ALU.mult), reads=rA + rPW, writes=["t2"])
        op("dve", lambda: nc.vector.tensor_tensor(c_, bA(Ar), Pin if neg_im else Pi, ALU.mult), reads=rA + rPW + ["pwn"], writes=["t3"])
        op("dve", lambda: nc.vector.tensor_tensor(d_, bA(Ai), Pr, ALU.mult), reads=rA + rPW, writes=["t4"])
        for g2 in range(2):
            ps = slice(g2 * 64, (g2 + 1) * 64)
            cs = slice(g2 * 16, (g2 + 1) * 16)
            op("dve", lambda: nc.vector.tensor_tensor(dst[ps, :, 0, :, cs], a[ps], b_[ps], ALU.subtract),
               reads=["t1", "t2"], writes=[wkey])
            op("dve", lambda: nc.vector.tensor_tensor(dst[ps, :, 1, :, cs], c_[ps], d_[ps], ALU.subtract if neg_im else ALU.add),
               reads=["t3", "t4"], writes=[wkey])

    for gi, gt in enumerate(gts):
        pb_ = str(gi % 2)
        XB, WC, CB, Kbd, WS = XB2[gi % 2], WC2[gi % 2], CB2[gi % 2], Kbd2[gi % 2], WS2[gi % 2]
        kXB, kWC, kCB, kKbd, kWS = "XB" + pb_, "WC" + pb_, "CB" + pb_, "Kbd" + pb_, "WS" + pb_
        prs = slice(gt * 4, gt * 4 + 4)
        rC = rC_of(gt)
        cmul_all_d(XB, BB[0], BB[1], 0, prs, False, ["BB0", "BB1"], kXB)
        cmul_all_d(WC, Cs[0], Cs[1], 1, prs, True, rC, kWC)
        for g2 in range(2):
            ps = slice(g2 * 64, (g2 + 1) * 64)
            cs = slice(g2 * 16, (g2 + 1) * 16)
            for i in range(2):
                op("dve", lambda: nc.vector.tensor_copy(CB[ps, i, :, cs], Cs[i][ps, prs, :]), reads=rC, writes=[kCB])
        op("dve", lambda: nc.vector.tensor_scalar(CB[:, 1, :, :], CB[:, 1, :, :], -1.0, None, ALU.mult), reads=[kCB], writes=[kCB])
        for j in range(4):
            kb = banks[j % 2]
            for d in range(16):
                for i in range(2):
                    op("pe", lambda: nc.tensor.matmul(kb[0:32, d * 32:(d + 1) * 32], XB[:, d, i, j, :], CB[:, i, j, :],
                                                      start=(d == 0 and i == 0), stop=(i == 1), skip_group_check=True),
                       reads=[kXB, kCB], writes=[("kbk", j % 2)])
            op("act", lambda: nc.scalar.activation(out=Kbd[j * 32:(j + 1) * 32, :, j * 32:(j + 1) * 32],
                                                   in_=kb[0:32, :].rearrange("p (d c) -> p d c", d=16), func=AF.Copy),
               reads=[("kbk", j % 2)], writes=[kKbd])
        op("dve", lambda: nc.vector.scalar_tensor_tensor(Kbd[:, 0, :], identf[:], dsk[:, gt:gt + 1], Kbd[:, 0, :], ALU.mult, ALU.add),
           reads=["identf", "dsk", kKbd], writes=[kKbd])
        P.dma("sp", o_K[gt, :, :, :], Kbd[:], reads=[kKbd], writes=[("oK", gt)])
        for j in range(4):
            for i in range(2):
                for half in range(2):
                    tbk = banksb[(j * 4 + i * 2 + half) % 2]
                    for mm in range(8):
                        m_ = half * 8 + mm
                        op("pe", lambda: nc.tensor.transpose(tbk[0:32, mm * 128:(mm + 1) * 128], XB[:, 15 - m_, i, j, :], ident[:]),
                           reads=[kXB, "ident"], writes=[("tbk", (j * 4 + i * 2 + half) % 2)])
                    op("act", lambda: nc.scalar.activation(out=WS[j * 32:(j + 1) * 32, half * 8:(half + 1) * 8, i, :],
                                                           in_=tbk[0:32, :].rearrange("p (m c) -> p m c", m=8), func=AF.Copy),
                       reads=[("tbk", (j * 4 + i * 2 + half) % 2)], writes=[kWS])
        P.dma("sp", o_WS[gt, :, :, :, :], WS[:], reads=[kWS], writes=[("oWS", gt)])
        P.dma("sp", o_WC[gt, :, :, :, :, :], WC[:], reads=[kWC], writes=[("oWC", gt)])


def build_s5_operands_debug(gts=(0, 5)):
    nc = bass.Bass("TRN2", target_bir_lowering=False)
    prm = {k: nc.dram_tensor(k, shp, F32, kind="ExternalInput").ap() for k, shp in L1_PARAMS.items()
           if k in ("l1_lam_re", "l1_lam_im", "l1_log_dt", "l1_b_re", "l1_b_im", "l1_c_re", "l1_c_im", "l1_d")}
    c_ident = nc.dram_tensor("c_ident", [128, 128], BF16, kind="ExternalInput").ap()
    o_K = nc.dram_tensor("o_K", [16, 128, 16, 128], BF16, kind="ExternalOutput").ap()
    o_WS = nc.dram_tensor("o_WS", [16, 128, 16, 2, 128], BF16, kind="ExternalOutput").ap()
    o_WC = nc.dram_tensor("o_WC", [16, 128, 16, 2, 4, 32], BF16, kind="ExternalOutput").ap()
    o_BB = nc.dram_tensor("o_BB", [2, 128, 64, 16], F32, kind="ExternalOutput").ap()
    with ExitStack() as st:
        P = Prog(nc, st)
        banks = [st.enter_context(nc.psum_tensor("bank%d" % i, [128, 512], F32)) for i in range(2)]
        banksb = [st.enter_context(nc.psum_tensor("bankb%d" % i, [128, 1024], BF16)) for i in range(2)]
        ident = st.enter_context(nc.sbuf_tensor("ident", [128, 128], BF16))
        P.dma("sp", ident[:], c_ident[:, :], writes=["ident"])
        tabs = emit_s5_tables(nc, P, st, prm)
        emit_s5_operands(nc, P, st, prm, tabs, ident, list(gts), o_K, o_WS, o_WC, banks, banksb, o_BB=o_BB)
        P.finish()
    return nc


NCH = 64
NTS = NCH * 16
KS_LEVELS = NCH.bit_length() - 1


def alloc_ks(nc, st):
    T = lambda name, shape, dt: st.enter_context(nc.sbuf_tensor(name, shape, dt))
    return {"ks": T("s5ks", [128, 6, 2, 64], F32), "kt1": T("s5kt1", [128, 64], F32), "kt2": T("s5kt2", [128, 64], F32),
            "ksn": T("s5ksn", [128, 6, 64], F32)}


def emit_ks_table(nc, P, st, tabs, bufs=None):
    if bufs is None:
        bufs = alloc_ks(nc, st)
    pw = tabs["pw"]
    ks, kt1, kt2 = bufs["ks"], bufs["kt1"], bufs["kt2"]
    op = P.op
    op("dve", lambda: nc.vector.tensor_copy(ks[:, 0, :, :], pw[:, 16, :, :]), reads=[("pw", 16)], writes=[("ks", 0)])
    for k in range(1, 6):
        a, b_ = ks[:, k - 1, 0, :], ks[:, k - 1, 1, :]
        op("dve", lambda: nc.vector.tensor_tensor(kt1[:], a, a, ALU.mult), reads=[("ks", k - 1)], writes=["kt1"])
        op("dve", lambda: nc.vector.tensor_tensor(kt2[:], b_, b_, ALU.mult), reads=[("ks", k - 1)], writes=["kt2"])
        op("dve", lambda: nc.vector.tensor_tensor(ks[:, k, 0, :], kt1[:], kt2[:], ALU.subtract), reads=["kt1", "kt2"], writes=[("ks", k)])
        op("dve", lambda: nc.vector.tensor_tensor(kt1[:], a, b_, ALU.mult), reads=[("ks", k - 1)], writes=["kt1"])
        op("dve", lambda: nc.vector.tensor_scalar(ks[:, k, 1, :], kt1[:], 2.0, None, ALU.mult), reads=["kt1"], writes=[("ks", k)])
    ksn = bufs["ksn"]
    op("dve", lambda: nc.vector.tensor_scalar(ksn[:], ks[:, :, 1, :], -1.0, None, ALU.mult), reads=[("ks", k) for k in range(6)], writes=["ksn"])
    return ks, ksn


def emit_s5_core(nc, P, gt, uT, Kbd, WS, WC, ks, ksn, carry, hbuf, hprev, Yb, Sb, yout, first_tile, tag="", ukey=("uT",), ykey=("yout",)):
    op = P.op
    uv = uT[:, :].rearrange("p (n m) -> p m n", m=16)
    h0 = hbuf[0]
    for j in range(4):
        rows = slice(j * 32, (j + 1) * 32)
        first = True
        for ri in range(2):
            for m in range(16):
                op("pe", lambda: nc.tensor.matmul(Sb[j][:, ri * NCH:(ri + 1) * NCH], WS[rows, m, ri, :], uv[rows, m, :],
                                                  start=first, stop=(m == 15), skip_group_check=True,
                                                  tile_position=(32 * j, 0)),
                   reads=[ukey, ("WS" + tag,)], writes=[("Sb", j)])
                first = False
        op("dve", lambda: nc.vector.tensor_copy(h0[:, j, :, :], Sb[j][:, 0:2 * NCH].rearrange("p (r n) -> p r n", r=2)),
           reads=[("Sb", j)], writes=[("hb", 0)])
    for j in range(4):
        pr = gt * 4 + j
        ar, ai, nai = ks[:, 0, 0, pr:pr + 1], ks[:, 0, 1, pr:pr + 1], ksn[:, 0, pr:pr + 1]
        if not first_tile:
            cr, ci = carry[:, j, 0:1], carry[:, j, 1:2]
            op("dve", lambda: nc.vector.scalar_tensor_tensor(h0[:, j, 0, 0:1], cr, ar, h0[:, j, 0, 0:1], ALU.mult, ALU.add),
               reads=[("carry", gt), ("hb", 0), ("ks", 0)], writes=[("hb", 0)])
            op("dve", lambda: nc.vector.scalar_tensor_tensor(h0[:, j, 0, 0:1], ci, nai, h0[:, j, 0, 0:1], ALU.mult, ALU.add),
               reads=[("carry", gt), ("hb", 0), "ksn"], writes=[("hb", 0)])
            op("dve", lambda: nc.vector.scalar_tensor_tensor(h0[:, j, 1, 0:1], ci, ar, h0[:, j, 1, 0:1], ALU.mult, ALU.add),
               reads=[("carry", gt), ("hb", 0), ("ks", 0)], writes=[("hb", 0)])
            op("dve", lambda: nc.vector.scalar_tensor_tensor(h0[:, j, 1, 0:1], cr, ai, h0[:, j, 1, 0:1], ALU.mult, ALU.add),
               reads=[("carry", gt), ("hb", 0), ("ks", 0)], writes=[("hb", 0)])
    cur = 0
    for k in range(KS_LEVELS):
        s = 1 << k
        src, dst = hbuf[cur], hbuf[1 - cur]
        rk = [("hb", cur), ("ks", k), "ksn"]
        wk = [("hb", 1 - cur)]
        op("dve", lambda: nc.vector.tensor_copy(dst[:, :, :, 0:s], src[:, :, :, 0:s]), reads=rk, writes=wk)
        for j in range(4):
            pr = gt * 4 + j
            ar, ai, nai = ks[:, k, 0, pr:pr + 1], ks[:, k, 1, pr:pr + 1], ksn[:, k, pr:pr + 1]
            sr, si = src[:, j, 0, 0:NCH - s], src[:, j, 1, 0:NCH - s]
            op("dve", lambda: nc.vector.scalar_tensor_tensor(dst[:, j, 0, s:NCH], sr, ar, src[:, j, 0, s:NCH], ALU.mult, ALU.add), reads=rk, writes=wk)
            op("dve", lambda: nc.vector.scalar_tensor_tensor(dst[:, j, 0, s:NCH], si, nai, dst[:, j, 0, s:NCH], ALU.mult, ALU.add), reads=rk + wk, writes=wk)
            op("dve", lambda: nc.vector.scalar_tensor_tensor(dst[:, j, 1, s:NCH], si, ar, src[:, j, 1, s:NCH], ALU.mult, ALU.add), reads=rk, writes=wk)
            op("dve", lambda: nc.vector.scalar_tensor_tensor(dst[:, j, 1, s:NCH], sr, ai, dst[:, j, 1, s:NCH], ALU.mult, ALU.add), reads=rk + wk, writes=wk)
        cur = 1 - cur
    H = hbuf[cur]
    hk = [("hp",)]
    if first_tile:
        op("pool", lambda: nc.gpsimd.memset(hprev[:, :, :, 0:1], 0.0), writes=hk)
    else:
        op("dve", lambda: nc.vector.tensor_copy(hprev[:, :, :, 0], carry[:, :, :]), reads=[("carry", gt)], writes=hk)
    op("dve", lambda: nc.vector.tensor_copy(hprev[:, :, :, 1:NCH], H[:, :, :, 0:NCH - 1]), reads=[("hb", cur)], writes=hk)
    op("dve", lambda: nc.vector.tensor_copy(carry[:, :, :], H[:, :, :, NCH - 1]), reads=[("hb", cur)] + hk, writes=[("carry", gt)])
    started = set()
    for m in range(16):
        bk = m // 8
        yb = Yb[bk]
        cols = slice((m % 8) * NCH, (m % 8 + 1) * NCH)
        for mp in range(m + 1):
            f_ = bk not in started
            started.add(bk)
            op("pe", lambda: nc.tensor.matmul(yb[:, cols], Kbd[:, m - mp, :], uv[:, mp, :], start=f_, stop=False, skip_group_check=True),
               reads=[ukey, ("Kbd" + tag,)], writes=[("Yb", bk)])
    for m in range(16):
        bk = m // 8
        yb = Yb[bk]
        cols = slice((m % 8) * NCH, (m % 8 + 1) * NCH)
        for j in range(4):
            for ri in range(2):
                op("pe", lambda: nc.tensor.matmul(yb[j * 32:(j + 1) * 32, cols], WC[:, m, ri, j, :], hprev[:, j, ri, :],
                                                  start=False, stop=(ri == 1), skip_group_check=True,
                                                  tile_position=(0, 32 * j)),
                   reads=hk + [("WC" + tag,)], writes=[("Yb", bk)])
    yv = yout[:, :].rearrange("p (n m) -> p m n", m=16)
    for bk in range(2):
        op("act", lambda: nc.scalar.activation(out=yv[:, bk * 8:(bk + 1) * 8, :], in_=Yb[bk][:, 0:8 * NCH].rearrange("p (m n) -> p m n", m=8), func=AF.Copy),
           reads=[("Yb", bk)], writes=[ykey])


def build_s5_core_debug(gt=0, ntiles=2):
    nc = bass.Bass("TRN2", target_bir_lowering=False)
    names = ("l1_lam_re", "l1_lam_im", "l1_log_dt", "l1_b_re", "l1_b_im", "l1_c_re", "l1_c_im", "l1_d")
    prm = {k: nc.dram_tensor(k, L1_PARAMS[k], F32, kind="ExternalInput").ap() for k in names}
    c_ident = nc.dram_tensor("c_ident", [128, 128], BF16, kind="ExternalInput").ap()
    u_in = nc.dram_tensor("u_in", [128, ntiles * NTS], BF16, kind="ExternalInput").ap()
    y_out = nc.dram_tensor("y_out", [128, ntiles * NTS], F32, kind="ExternalOutput").ap()
    o_K = nc.dram_tensor("o_K", [16, 128, 16, 128], BF16, kind="Internal").ap()
    o_WS = nc.dram_tensor("o_WS", [16, 128, 16, 2, 128], BF16, kind="Internal").ap()
    o_WC = nc.dram_tensor("o_WC", [16, 128, 16, 2, 4, 32], BF16, kind="Internal").ap()
    with ExitStack() as st:
        P = Prog(nc, st)
        T = lambda name, shape, dt: st.enter_context(nc.sbuf_tensor(name, shape, dt))
        banks = [st.enter_context(nc.psum_tensor("bank%d" % i, [128, 512], F32)) for i in range(6)]
        banksb = [st.enter_context(nc.psum_tensor("bankb%d" % i, [128, 1024], BF16)) for i in range(2)]
        ident = T("ident", [128, 128], BF16)
        P.dma("sp", ident[:], c_ident[:, :], writes=["ident"])
        tabs = emit_s5_tables(nc, P, st, prm)
        ks, ksn = emit_ks_table(nc, P, st, tabs)
        with ExitStack() as st2:
            emit_s5_operands(nc, P, st2, prm, tabs, ident, [gt], o_K, o_WS, o_WC, banks, banksb)
            P.barrier()
        Kbd = T("mKbd", [128, 16, 128], BF16); WS = T("mWS", [128, 16, 2, 128], BF16); WC = T("mWC", [128, 16, 2, 4, 32], BF16)
        P.dma("sp", Kbd[:], o_K[gt, :, :, :], reads=[("oK", gt)], writes=[("Kbd",)])
        P.dma("sp", WS[:], o_WS[gt, :, :, :, :], reads=[("oWS", gt)], writes=[("WS",)])
        P.dma("sp", WC[:], o_WC[gt, :, :, :, :, :], reads=[("oWC", gt)], writes=[("WC",)])
        uT = T("muT", [128, NTS], BF16); yo = T("myo", [128, NTS], F32)
        carry = T("mcarry", [128, 4, 2], F32)
        hbuf = [T("mhb%d" % i, [128, 4, 2, NCH], F32) for i in range(2)]
        hprev = T("mhp", [128, 4, 2, NCH], BF16)
        for t in range(ntiles):
            P.dma("sp", uT[:], u_in[:, t * NTS:(t + 1) * NTS], writes=[("uT",)])
            emit_s5_core(nc, P, gt, uT, Kbd, WS, WC, ks, ksn, carry, hbuf, hprev, banks[0:2], banks[2:6], yo, first_tile=(t == 0))
            P.dma("sp", y_out[:, t * NTS:(t + 1) * NTS], yo[:], reads=[("yout",)])
        P.finish()
    return nc


NSUP_FULL = S // NTS


def emit_layer1(nc, P, x1_src, prm, c_ident, out, o_K, o_WS, o_WC, NSUP=NSUP_FULL, d_ys5=None, probe_no_restream=False):
    with ExitStack() as st:
        T = lambda name, shape, dt: st.enter_context(nc.sbuf_tensor(name, shape, dt))
        banks = [st.enter_context(nc.psum_tensor("l1bank%d" % i, [128, 512], F32)) for i in range(6)]
        banksb = [st.enter_context(nc.psum_tensor("l1bankb%d" % i, [128, 1024], BF16)) for i in range(2)]
        ident = T("l1ident", [128, 128], BF16)
        gbc = T("l1gbc", [128, D], F32)
        bglu = T("l1bglu", [128, 16], F32)
        carry = T("l1carry", [128, 16, 4, 2], F32)
        Wo = T("l1Wo", [128, 16, D], BF16)
        P.dma("sp", ident[:], c_ident[:, :], writes=["ident"])
        P.dma("sp", gbc[:], prm["l1_norm_g"].partition_broadcast(128), writes=["gbc"])
        P.dma("sp", bglu[:], prm["l1_b_glu"].rearrange("(ft p) -> p ft", p=128), writes=["bglu"], allow_slow_non_contiguous=True)
        w_out = prm["l1_w_out"].rearrange("(k p) f -> p k f", p=128)
        for kk in range(0, 16, 4):
            P.dma("pool", Wo[:, kk:kk + 4, :], w_out[:, kk:kk + 4, :], writes=[("Wo", kk // 4)])
        ksb = alloc_ks(nc, st)
        with ExitStack() as st0:
            tabs = emit_s5_tables(nc, P, st0, prm)
            ks, ksn = emit_ks_table(nc, P, st0, tabs, bufs=ksb)
            emit_s5_operands(nc, P, st0, prm, tabs, ident, list(range(16)), o_K, o_WS, o_WC, banks, banksb)
            P.barrier()
        nb = alloc_norm_bufs(nc, st, "L1A")
        hT = T("l1hT", [128, 8, NTS], BF16)
        uT = [T("l1uT%d" % i, [128, NTS], BF16) for i in range(2)]
        yo = [T("l1yo%d" % i, [128, NTS], F32) for i in range(2)]
        YG = T("l1YG", [128, 16, NTS], BF16)
        G1 = T("l1G1", [128, 16, 512], BF16)
        hbuf = [T("l1hb%d" % i, [128, 4, 2, NCH], F32) for i in range(2)]
        hprev = [T("l1hp%d" % i, [128, 4, 2, NCH], BF16) for i in range(2)]
        Kbd = [T("l1Kbd%d" % i, [128, 16, 128], BF16) for i in range(2)]
        WS = [T("l1WS%d" % i, [128, 16, 2, 128], BF16) for i in range(2)]
        WC = [T("l1WC%d" % i, [128, 16, 2, 4, 32], BF16) for i in range(2)]
        Wu = [T("l1Wu%d" % i, [128, 8, 128], BF16) for i in range(2)]
        Wg = [T("l1Wg%d" % i, [128, 16, 128], BF16) for i in range(2)]
        Wz = [T("l1Wz%d" % i, [128, 8, 128], BF16) for i in range(2)]
        sg = T("l1sg", [128, 512], BF16)
        sz = T("l1sz", [128, 512], BF16)
        tg = T("l1tg", [128, 512], BF16)
        xr = [T("l1xr%d" % i, [128, D], F32) for i in range(2)]
        w_in = prm["l1_w_in"].rearrange("(j p) f -> p j f", p=128)
        w_glu = prm["l1_w_glu"].rearrange("(k p) f -> p k f", p=128)
        op = P.op

        loaded = set()

        def load_gt(gt, b):
            if probe_no_restream:
                if ("gt", b) in loaded:
                    return
                loaded.add(("gt", b))
            tag = str(b)
            P.dma("sp", Kbd[b][:], o_K[gt, :, :, :], reads=[("oK", gt)], writes=[("Kbd" + tag,)])
            P.dma("sp", WS[b][:], o_WS[gt, :, :, :, :], reads=[("oWS", gt)], writes=[("WS" + tag,)])
            P.dma("sp", WC[b][:], o_WC[gt, :, :, :, :, :], reads=[("oWC", gt)], writes=[("WC" + tag,)])
            P.dma("pool", Wu[b][:], w_in[:, :, gt * 128:(gt + 1) * 128], writes=[("Wu", b)])

        def load_ft(ft, b):
            if probe_no_restream:
                if ("ft", b) in loaded:
                    return
                loaded.add(("ft", b))
            P.dma("pool", Wg[b][:], w_glu[:, :, ft * 128:(ft + 1) * 128], writes=[("Wg", b)])
            P.dma("pool", Wz[b][:], w_in[:, :, E + ft * 128:E + (ft + 1) * 128], writes=[("Wz", b)])

        for Tt in range(NSUP):
            r0 = Tt * NTS
            emit_norm_transpose(nc, P, st, x1_src[r0:r0 + NTS, :], hT, gbc, ident, banksb, "L1A", NT=NTS // 128, x_keep=nb)
            hTk = [("hT", q) for q in range(NTS // 512)]
            Yb, Sb = banks[0:2], banks[2:6]
            Ub = [banksb[hf][:, :].bitcast(F32) for hf in range(2)]

            def U_stage(gt, b):
                for hf in range(NTS // 512):
                    cs = slice(hf * 512, (hf + 1) * 512)
                    for j in range(8):
                        op("pe", lambda: nc.tensor.matmul(Ub[hf][:, 0:512], Wu[b][:, j, :], hT[:, j, cs], start=(j == 0), stop=(j == 7)),
                           reads=hTk + [("Wu", b)], writes=[("pbb", hf)])
                    op("act", lambda: nc.scalar.activation(out=uT[b][:, cs], in_=Ub[hf][:, 0:512], func=AF.Copy),
                       reads=[("pbb", hf)], writes=[("uT", b)])

            def ST_stage(gt, b):
                uv = uT[b][:, :].rearrange("p (n m) -> p m n", m=16)
                h0 = hbuf[0]
                for j in range(4):
                    rows = slice(j * 32, (j + 1) * 32)
                    first = True
                    for ri in range(2):
                        for m in range(16):
                            op("pe", lambda: nc.tensor.matmul(Sb[j][:, ri * NCH:(ri + 1) * NCH], WS[b][rows, m, ri, :], uv[rows, m, :],
                                                              start=first, stop=(m == 15), skip_group_check=True, tile_position=(32 * j, 0)),
                               reads=[("uT", b), ("WS" + str(b),)], writes=[("Sb", j)])
                            first = False
                    op("dve", lambda: nc.vector.tensor_copy(h0[:, j, :, :], Sb[j][:, 0:2 * NCH].rearrange("p (r n) -> p r n", r=2)),
                       reads=[("Sb", j)], writes=[("hb", 0)])

            def SCAN_stage(gt, b, first_tile):
                h0 = hbuf[0]
                cg = carry[:, gt]
                ck = ("carry", gt)
                for j in range(4):
                    pr = gt * 4 + j
                    ar, ai, nai = ks[:, 0, 0, pr:pr + 1], ks[:, 0, 1, pr:pr + 1], ksn[:, 0, pr:pr + 1]
                    if not first_tile:
                        cr, ci = cg[:, j, 0:1], cg[:, j, 1:2]
                        for (dst_, src_, sc_) in ((0, cr, ar), (0, ci, nai), (1, ci, ar), (1, cr, ai)):
                            op("dve", lambda: nc.vector.scalar_tensor_tensor(h0[:, j, dst_, 0:1], src_, sc_, h0[:, j, dst_, 0:1], ALU.mult, ALU.add),
                               reads=[ck, ("hb", 0), ("ks", 0), "ksn"], writes=[("hb", 0)])
                cur = 0
                for k in range(KS_LEVELS):
                    sft = 1 << k
                    src, dst = hbuf[cur], hbuf[1 - cur]
                    rk = [("hb", cur), ("ks", k), "ksn"]
                    wk = [("hb", 1 - cur)]
                    op("dve", lambda: nc.vector.tensor_copy(dst[:, :, :, 0:sft], src[:, :, :, 0:sft]), reads=rk, writes=wk)
                    for j in range(4):
                        pr = gt * 4 + j
                        ar, ai, nai = ks[:, k, 0, pr:pr + 1], ks[:, k, 1, pr:pr + 1], ksn[:, k, pr:pr + 1]
                        sr, si = src[:, j, 0, 0:NCH - sft], src[:, j, 1, 0:NCH - sft]
                        op("dve", lambda: nc.vector.scalar_tensor_tensor(dst[:, j, 0, sft:NCH], sr, ar, src[:, j, 0, sft:NCH], ALU.mult, ALU.add), reads=rk, writes=wk)
                        op("dve", lambda: nc.vector.scalar_tensor_tensor(dst[:, j, 0, sft:NCH], si, nai, dst[:, j, 0, sft:NCH], ALU.mult, ALU.add), reads=rk + wk, writes=wk)
                        op("dve", lambda: nc.vector.scalar_tensor_tensor(dst[:, j, 1, sft:NCH], si, ar, src[:, j, 1, sft:NCH], ALU.mult, ALU.add), reads=rk, writes=wk)
                        op("dve", lambda: nc.vector.scalar_tensor_tensor(dst[:, j, 1, sft:NCH], sr, ai, dst[:, j, 1, sft:NCH], ALU.mult, ALU.add), reads=rk + wk, writes=wk)
                    cur = 1 - cur
                H = hbuf[cur]
                hp = hprev[b]
                hk = [("hp", b)]
                if first_tile:
                    op("pool", lambda: nc.gpsimd.memset(hp[:, :, :, 0:1], 0.0), writes=hk)
                else:
                    op("dve", lambda: nc.vector.tensor_copy(hp[:, :, :, 0], cg[:, :, :]), reads=[ck], writes=hk)
                op("dve", lambda: nc.vector.tensor_copy(hp[:, :, :, 1:NCH], H[:, :, :, 0:NCH - 1]), reads=[("hb", cur)], writes=hk)
                op("dve", lambda: nc.vector.tensor_copy(cg[:, :, :], H[:, :, :, NCH - 1]), reads=[("hb", cur)] + hk, writes=[ck])

            def INTRA_stage(gt, b):
                uv = uT[b][:, :].rearrange("p (n m) -> p m n", m=16)
                started = set()
                for m in range(16):
                    bk = m // 8
                    cols = slice((m % 8) * NCH, (m % 8 + 1) * NCH)
                    for mp in range(m + 1):
                        f_ = bk not in started
                        started.add(bk)
                        op("pe", lambda: nc.tensor.matmul(Yb[bk][:, cols], Kbd[b][:, m - mp, :], uv[:, mp, :], start=f_, stop=False, skip_group_check=True),
                           reads=[("uT", b), ("Kbd" + str(b),)], writes=[("Yb", bk)])

            def INTER_stage(gt, b):
                for m in range(16):
                    bk = m // 8
                    cols = slice((m % 8) * NCH, (m % 8 + 1) * NCH)
                    for j in range(4):
                        for ri in range(2):
                            op("pe", lambda: nc.tensor.matmul(Yb[bk][j * 32:(j + 1) * 32, cols], WC[b][:, m, ri, j, :], hprev[b][:, j, ri, :],
                                                              start=False, stop=(ri == 1), skip_group_check=True, tile_position=(0, 32 * j)),
                               reads=[("hp", b), ("WC" + str(b),)], writes=[("Yb", bk)])
                yv = yo[b][:, :].rearrange("p (n m) -> p m n", m=16)
                for bk in range(2):
                    op("act", lambda: nc.scalar.activation(out=yv[:, bk * 8:(bk + 1) * 8, :], in_=Yb[bk][:, 0:8 * NCH].rearrange("p (m n) -> p m n", m=8), func=AF.Copy),
                       reads=[("Yb", bk)], writes=[("yo", b)])

            load_gt(0, 0)
            U_stage(0, 0)
            ST_stage(0, 0)
            for gt in range(16):
                b = gt % 2
                if gt + 1 < 16:
                    load_gt(gt + 1, 1 - b)
                INTRA_stage(gt, b)
                SCAN_stage(gt, b, first_tile=(Tt == 0))
                if gt + 1 < 16:
                    U_stage(gt + 1, 1 - b)
                    ST_stage(gt + 1, 1 - b)
                INTER_stage(gt, b)
                if d_ys5 is not None:
                    P.dma("sp", d_ys5[gt * 128:(gt + 1) * 128, r0:r0 + NTS], yo[b][:], reads=[("yo", b)])
                op("act", lambda: nc.scalar.activation(out=YG[:, gt, :], in_=yo[b][:], func=AF.Gelu_apprx_tanh),
                   reads=[("yo", b)], writes=[("YG", gt)])
            ygk = [("YG", k) for k in range(16)]
            g1k = [("G1", k) for k in range(16)]
            for hf in range(NTS // 512):
                cs = slice(hf * 512, (hf + 1) * 512)
                load_ft(0, 0)
                for ft in range(16):
                    b = ft % 2
                    if ft + 1 < 16:
                        load_ft(ft + 1, 1 - b)
                    pg, pz = banks[b], banks[2 + b]
                    for k in range(16):
                        op("pe", lambda: nc.tensor.matmul(pg[:, :], Wg[b][:, k, :], YG[:, k, cs], start=(k == 0), stop=(k == 15)),
                           reads=ygk + [("Wg", b)], writes=[("Yb", b)])
                    for j in range(8):
                        op("pe", lambda: nc.tensor.matmul(pz[:, :], Wz[b][:, j, :], hT[:, j, cs], start=(j == 0), stop=(j == 7)),
                           reads=hTk + [("Wz", b)], writes=[("Sb", b)])
                    op("act", lambda: nc.scalar.activation(out=sg[:], in_=pg[:, :], func=AF.Sigmoid, bias=bglu[:, ft:ft + 1], scale=1.0),
                       reads=[("Yb", b), "bglu"], writes=["sg"])
                    op("act", lambda: nc.scalar.activation(out=sz[:], in_=pz[:, :], func=AF.Silu), reads=[("Sb", b)], writes=["sz"])
                    op("dve", lambda: nc.vector.tensor_tensor(tg[:], YG[:, ft, cs], sg[:], ALU.mult), reads=[("YG", ft), "sg"], writes=["tg"])
                    op("dve", lambda: nc.vector.tensor_tensor(G1[:, ft, :], tg[:], sz[:], ALU.mult), reads=["tg", "sz"], writes=[("G1", ft)])
                for tt in range(4):
                    t = (r0 + hf * 512) // 128 + tt
                    b = t % 2
                    P.dma("sp", xr[b][:], x1_src[t * 128:(t + 1) * 128, :], writes=[("xr", b)])
                    for half in range(2):
                        bk = 2 + (t * 2 + half) % 4
                        for k in range(16):
                            op("pe", lambda: nc.tensor.matmul(banks[bk][:], G1[:, k, tt * 128:(tt + 1) * 128], Wo[:, k, half * 512:(half + 1) * 512],
                                                              start=(k == 0), stop=(k == 15)),
                               reads=g1k + [("Wo", k // 4)], writes=[("Sb", bk - 2)])
                        op("dve", lambda: nc.vector.tensor_tensor(xr[b][:, half * 512:(half + 1) * 512], banks[bk][:],
                                                                  xr[b][:, half * 512:(half + 1) * 512], ALU.add),
                           reads=[("Sb", bk - 2), ("xr", b)], writes=[("xr", b)])
                    P.dma("sp", out[t * 128:(t + 1) * 128, :], xr[b][:], reads=[("xr", b)], writes=[("x2", t)])


def build_l1(NSUP=NSUP_FULL, debug=False, probe_no_restream=False):
    nc = bass.Bass("TRN2", target_bir_lowering=False)
    x1 = nc.dram_tensor("x1", [S, D], F32, kind="ExternalInput").ap()
    prm = {k: nc.dram_tensor(k, shp, F32, kind="ExternalInput").ap() for k, shp in L1_PARAMS.items()}
    c_ident = nc.dram_tensor("c_ident", [128, 128], BF16, kind="ExternalInput").ap()
    out = nc.dram_tensor("out", [S, D], F32, kind="ExternalOutput").ap()
    d_ys5 = nc.dram_tensor("d_ys5", [E, S], F32, kind="ExternalOutput").ap() if debug else None
    o_K = nc.dram_tensor("o_K", [16, 128, 16, 128], BF16, kind="Internal").ap()
    o_WS = nc.dram_tensor("o_WS", [16, 128, 16, 2, 128], BF16, kind="Internal").ap()
    o_WC = nc.dram_tensor("o_WC", [16, 128, 16, 2, 4, 32], BF16, kind="Internal").ap()
    with ExitStack() as st:
        P = Prog(nc, st)
        emit_layer1(nc, P, x1, prm, c_ident, out, o_K, o_WS, o_WC, NSUP=NSUP, d_ys5=d_ys5, probe_no_restream=probe_no_restream)
        P.finish()
        build_l1.stats = {"ninst": P.ninst, "nsem": P._nsem}
    return nc


def run_layer1_spmd(inputs, x1, n_cores=8):
    nc = build_l1()
    ident = host_consts()["c_ident"]
    in_maps = []
    for b in range(n_cores):
        m = {"x1": np.ascontiguousarray(x1[b]), "c_ident": ident}
        for k in L1_PARAMS:
            m[k] = np.ascontiguousarray(inputs[k])
        in_maps.append(m)
    res = run_bass_kernel_spmd(nc, in_maps, core_ids=list(range(n_cores)))
    return np.stack([np.asarray(r["out"]) for r in res.results], axis=0)


def build_fused(NQ=8, NSUP=NSUP_FULL):
    nc = bass.Bass("TRN2", target_bir_lowering=False)
    x = nc.dram_tensor("x", [S, D], F32, kind="ExternalInput").ap()
    prm = {k: nc.dram_tensor(k, shp, F32, kind="ExternalInput").ap() for k, shp in {**L0_PARAMS, **L1_PARAMS}.items()}
    cst = {k: nc.dram_tensor(k, shp, dt, kind="ExternalInput").ap() for k, (shp, dt) in CONST_SHAPES.items()}
    out = nc.dram_tensor("out", [S, D], F32, kind="ExternalOutput").ap()
    gscr = nc.dram_tensor("gscr", [128, NH, S], BF16, kind="Internal").ap()
    x1scr = nc.dram_tensor("x1scr", [S, D], F32, kind="Internal").ap()
    o_K = nc.dram_tensor("o_K", [16, 128, 16, 128], BF16, kind="Internal").ap()
    o_WS = nc.dram_tensor("o_WS", [16, 128, 16, 2, 128], BF16, kind="Internal").ap()
    o_WC = nc.dram_tensor("o_WC", [16, 128, 16, 2, 4, 32], BF16, kind="Internal").ap()
    with ExitStack() as st:
        P = Prog(nc, st)
        emit_layer0(nc, P, x, prm, cst, x1scr, gscr, NQ=NQ)
        P.barrier()
        emit_layer1(nc, P, x1scr, prm, cst["c_ident"], out, o_K, o_WS, o_WC, NSUP=NSUP)
        P.finish()
        build_fused.stats = {"ninst": P.ninst, "nsem": P._nsem}
    return nc


def fused_in_maps(inputs, n_cores=8):
    cst = host_consts()
    maps = []
    for b in range(n_cores):
        m = {"x": np.ascontiguousarray(inputs["x"][b])}
        for k in list(L0_PARAMS) + list(L1_PARAMS):
            m[k] = np.ascontiguousarray(inputs[k])
        m.update(cst)
        maps.append(m)
    return maps


FUSED = True


def kernel(**inputs):
    inputs = {k: np.asarray(v) for k, v in inputs.items()}
    if FUSED:
        nc = build_fused()
        res = run_bass_kernel_spmd(nc, fused_in_maps(inputs), core_ids=list(range(8)))
        return np.stack([np.asarray(r["out"]) for r in res.results], axis=0).astype(np.float32, copy=False)
    x1 = run_layer0_spmd(inputs)
    x2 = run_layer1_spmd(inputs, x1)
    return x2.astype(np.float32, copy=False)
```

```python
import math
from contextlib import ExitStack

import numpy as np
import ml_dtypes

import concourse.bass as bass
import concourse.mybir as mybir
from concourse.bass_utils import run_bass_kernel_spmd

F32 = mybir.dt.float32
BF16 = mybir.dt.bfloat16
AF = mybir.ActivationFunctionType
ALU = mybir.AluOpType
AX = mybir.AxisListType

S = 4096
D = 1024
E = 2048
NH = 16
EPS = 1e-6
LAM0 = 0.8 - 0.6 * math.exp(-0.3 * 0)
NEG = -30000.0
EPOCH = 12000


class Prog:
    def __init__(self, nc, stack, n_dma_sems=48):
        self.nc = nc
        self.stack = stack
        self.eng = {"pe": nc.tensor, "act": nc.scalar, "dve": nc.vector,
                    "pool": nc.gpsimd, "sp": nc.sync}
        self._nsem = 0
        self.cur = {}
        for e in ("pe", "act", "dve", "pool"):
            self.cur[e] = [self._newsem(e), 0]
        n_sw = max(8, n_dma_sems // 3)
        self.dma_pool = {"sp": [[self._newsem("dmah"), 0] for _ in range(n_dma_sems - n_sw)],
                         "pool": [[self._newsem("dmas"), 0] for _ in range(n_sw)]}
        self.dma_rr = {"sp": 0, "pool": 0}
        self.dma_sems = self.dma_pool["sp"] + self.dma_pool["pool"]
        self.waited = {}
        self.lastw = {}
        self.readers = {}
        self.ninst = 0

    def _newsem(self, name):
        self._nsem += 1
        return self.stack.enter_context(self.nc.semaphore("s_%s_%d" % (name, self._nsem)))

    def _wait(self, e, h):
        sem, val, src = h
        if src == "pe" and e == "pe":
            return
        k = (e, id(sem))
        if self.waited.get(k, 0) >= val:
            return
        self.waited[k] = val
        self.eng[e].wait_ge(sem, val)

    def deps(self, e, reads, writes):
        for r in reads:
            h = self.lastw.get(r)
            if h is not None:
                self._wait(e, h)
        for w in writes:
            h = self.lastw.get(w)
            if h is not None:
                self._wait(e, h)
            for h in self.readers.get(w, ()):
                self._wait(e, h)

    def _commit(self, h, reads, writes):
        for r in reads:
            lst = self.readers.setdefault(r, [])
            lst[:] = [x for x in lst if x[0] is not h[0]]
            lst.append(h)
        for w in writes:
            self.lastw[w] = h
            self.readers[w] = []

    def op(self, e, fn, reads=(), writes=()):
        self.deps(e, reads, writes)
        c = self.cur[e]
        if c[1] >= EPOCH:
            c = self.cur[e] = [self._newsem(e), 0]
        ins = fn()
        c[1] += 1
        ins.then_inc(c[0], 1)
        self.ninst += 1
        h = (c[0], c[1], e)
        self._commit(h, reads, writes)
        return h

    def dma(self, q, out, in_, reads=(), writes=(), **kw):
        self.deps(q, reads, writes)
        pool_ = self.dma_pool[q]
        slot = pool_[self.dma_rr[q]]
        self.dma_rr[q] = (self.dma_rr[q] + 1) % len(pool_)
        if slot[1] > 0:
            self._wait(q, (slot[0], slot[1], "dma"))
        ins = self.eng[q].dma_start(out=out, in_=in_, **kw)
        slot[1] += 16
        ins.then_inc(slot[0], 16)
        h = (slot[0], slot[1], "dma")
        self._commit(h, reads, writes)
        return h

    def barrier(self):
        hs = [(slot[0], slot[1], "dma") for slot in self.dma_sems if slot[1] > 0]
        hs += [(c[0], c[1], "bar") for c in self.cur.values() if c[1] > 0]
        for e in ("sp", "pe", "act", "dve", "pool"):
            for h in hs:
                self._wait(e, h)

    def finish(self):
        for slot in self.dma_sems:
            if slot[1] > 0:
                self._wait("sp", (slot[0], slot[1], "dma"))
        for e, c in self.cur.items():
            if c[1] > 0:
                self._wait("sp", (c[0], c[1], e))


def _split3(x):
    x = np.asarray(x, np.float32)
    hi = x.astype(ml_dtypes.bfloat16)
    r1 = x - hi.astype(np.float32)
    mid = r1.astype(ml_dtypes.bfloat16)
    r2 = r1 - mid.astype(np.float32)
    lo = r2.astype(ml_dtypes.bfloat16)
    return hi, mid, lo


def host_consts():
    bf = ml_dtypes.bfloat16
    c = {}
    c["c_ident"] = np.eye(128, dtype=np.float32).astype(bf)
    blk = np.zeros((128, 128), np.float32)
    blk[:64, :64] = 1.0 / 64
    blk[64:, 64:] = 1.0 / 64
    c["c_blk"] = blk.astype(bf)
    mask = np.zeros((128, 4, 512), np.float32)
    ki = np.arange(128)[:, None]
    qi = np.arange(512)[None, :]
    for j in range(4):
        mask[:, j, :] = np.where(128 * j + ki <= qi, 0.0, NEG)
    c["c_mask"] = mask.astype(bf)
    pos = np.arange(S, dtype=np.float64)
    qb = np.zeros((NH, 6, S), bf)
    kb = np.zeros((NH, 6, S), bf)
    for h in range(NH):
        slope = 2.0 ** (-8.0 * (h + 1) / NH)
        a, b_, c_ = _split3((-slope * pos).astype(np.float32))
        qb[h, 0], qb[h, 1], qb[h, 2] = a, b_, c_
        qb[h, 3:6] = 1.0
        a, b_, c_ = _split3((slope * pos).astype(np.float32))
        kb[h, 0:3] = 1.0
        kb[h, 3], kb[h, 4], kb[h, 5] = a, b_, c_
    c["c_qb"] = qb
    c["c_kb"] = kb
    return c


CONST_SHAPES = {
    "c_ident": ([128, 128], BF16), "c_blk": ([128, 128], BF16),
    "c_mask": ([128, 4, 512], BF16), "c_qb": ([NH, 6, S], BF16), "c_kb": ([NH, 6, S], BF16),
}

L0_PARAMS = {
    "l0_norm_g": [D], "l0_w_in": [D, 4 * E], "l0_q_norm_g": [64], "l0_k_norm_g": [64],
    "l0_lam_q1": [64], "l0_lam_k1": [64], "l0_lam_q2": [64], "l0_lam_k2": [64],
    "l0_head_norm_g": [128], "l0_w_out": [E, D],
}


def vec2(ap):
    return ap.rearrange("(n o) -> n o", o=1)


def alloc_norm_bufs(nc, st, tagp):
    T = lambda name, shape, dt: st.enter_context(nc.sbuf_tensor(name, shape, dt))
    return {"xt": [T(tagp + "xt%d" % i, [128, D], F32) for i in range(2)],
            "xn": [T(tagp + "xn%d" % i, [128, D], BF16) for i in range(2)],
            "junk": T(tagp + "junk", [128, D], BF16), "stat": T(tagp + "stat", [128, 8], F32)}


def emit_norm_transpose(nc, P, st, x_src, hT, gbc, ident, banksb, tagp, NT=32, x_keep=None):
    if x_keep is None:
        x_keep = alloc_norm_bufs(nc, st, tagp)
    xt, xn, junk, stat = x_keep["xt"], x_keep["xn"], x_keep["junk"], x_keep["stat"]
    for t in range(NT):
        b = t % 2
        P.dma("sp", xt[b][:], x_src[t * 128:(t + 1) * 128, :], writes=[(tagp, "xt", b)])
        P.op("act", lambda: nc.scalar.activation(out=junk[:], in_=xt[b][:], func=AF.Square,
                                                 accum_out=stat[:, b:b + 1]),
             reads=[(tagp, "xt", b)], writes=[(tagp, "junk"), (tagp, "ss", b)])
        P.op("act", lambda: nc.scalar.activation(out=stat[:, 2 + b:3 + b], in_=stat[:, b:b + 1], func=AF.Sqrt,
                                                 scale=1.0 / D, bias=EPS),
             reads=[(tagp, "ss", b)], writes=[(tagp, "sd", b)])
        P.op("dve", lambda: nc.vector.reciprocal(stat[:, 4 + b:5 + b], stat[:, 2 + b:3 + b]),
             reads=[(tagp, "sd", b)], writes=[(tagp, "rs", b)])
        P.op("dve", lambda: nc.vector.scalar_tensor_tensor(xn[b][:], xt[b][:], stat[:, 4 + b:5 + b], gbc[:],
                                                           ALU.mult, ALU.mult),
             reads=[(tagp, "xt", b), (tagp, "rs", b), "gbc"], writes=[(tagp, "xn", b)])
        pb = banksb[t % 2]
        for j in range(8):
            P.op("pe", lambda: nc.tensor.transpose(pb[:, j * 128:(j + 1) * 128], xn[b][:, j * 128:(j + 1) * 128],
                                                   ident[:]),
                 reads=[(tagp, "xn", b), "ident"], writes=[("pbb", t % 2)])
        P.op("dve", lambda: nc.vector.tensor_copy(hT[:, :, t * 128:(t + 1) * 128],
                                                  pb[:, :].rearrange("p (j c) -> p j c", j=8)),
             reads=[("pbb", t % 2)], writes=[("hT", t // 4)])


def emit_layer0(nc, P, x, prm, cst, x1_out, gscr, heads=None, NQ=8, dump=None):
    heads = list(range(NH)) if heads is None else list(heads)
    with ExitStack() as st:
        T = lambda name, shape, dt: st.enter_context(nc.sbuf_tensor(name, shape, dt))
        banks = [st.enter_context(nc.psum_tensor("bank%d" % i, [128, 512], F32)) for i in range(6)]
        banksb = [st.enter_context(nc.psum_tensor("bankb%d" % i, [128, 1024], BF16)) for i in range(2)]
        ident = T("ident", [128, 128], BF16)
        blk = T("blk", [128, 128], BF16)
        mask = T("mask", [128, 4, 512], BF16)
        gbc = T("gbc", [128, D], F32)
        small = T("small", [128, 16], F32)
        lamv = T("lamv", [128, 4, 64], F32)
        lamj = T("lamj", [128, 64], F32)
        ghn = T("ghn", [128, 128], F32)
        P.dma("sp", ident[:], cst["c_ident"][:, :], writes=["ident"])
        P.dma("sp", blk[:], cst["c_blk"][:, :], writes=["blk"])
        P.dma("sp", mask[:], cst["c_mask"][:, :, :], writes=["mask"])
        P.dma("sp", gbc[:], prm["l0_norm_g"].partition_broadcast(128), writes=["gbc"])
        P.dma("sp", ghn[:], prm["l0_head_norm_g"].partition_broadcast(128), writes=["ghn"])
        for m in range(2):
            P.dma("sp", small[m * 64:(m + 1) * 64, 0:1], vec2(prm["l0_q_norm_g"]), writes=["small"])
            P.dma("sp", small[m * 64:(m + 1) * 64, 1:2], vec2(prm["l0_k_norm_g"]), writes=["small"])
        for i, nm in enumerate(["l0_lam_q1", "l0_lam_k1", "l0_lam_q2", "l0_lam_k2"]):
            P.dma("sp", lamv[:, i, :], prm[nm].partition_broadcast(128), writes=["lamv"])
        P.op("dve", lambda: nc.vector.tensor_scalar(small[:, 0:1], small[:, 0:1], 0.125, None, ALU.mult),
             reads=["small"], writes=["small"])
        for i in range(2):
            P.op("dve", lambda: nc.vector.tensor_tensor(lamj[:], lamv[:, 2 * i, :], lamv[:, 2 * i + 1, :], ALU.mult),
                 reads=["lamv"], writes=["lamj"])
            P.op("dve", lambda: nc.vector.reduce_sum(small[:, 4 + i:5 + i], lamj[:], axis=AX.X),
                 reads=["lamj"], writes=["small"])
        P.op("act", lambda: nc.scalar.activation(out=small[:, 6:8], in_=small[:, 4:6], func=AF.Exp),
             reads=["small"], writes=["small"])
        P.op("dve", lambda: nc.vector.scalar_tensor_tensor(small[:, 2:3], small[:, 7:8], -LAM0, small[:, 6:7],
                                                           ALU.add, ALU.subtract),
             reads=["small"], writes=["small"])
        P.op("dve", lambda: nc.vector.tensor_scalar(ghn[:], ghn[:], 1.0 - LAM0, None, ALU.mult),
             reads=["ghn"], writes=["ghn"])

        hT = T("hT", [128, 8, S], BF16)
        with ExitStack() as stA:
            emit_norm_transpose(nc, P, stA, x, hT, gbc, ident, banksb, "A", NT=4 * NQ)
            P.barrier()

        with ExitStack() as stH:
            TH = lambda name, shape, dt: stH.enter_context(nc.sbuf_tensor(name, shape, dt))
            QA = [TH("QA%d" % m, [128, S], BF16) for m in range(2)]
            KA = [TH("KA%d" % m, [128, S], BF16) for m in range(2)]
            VA = TH("VA", [128, 32, 130], BF16)
            ZT = TH("ZT", [128, S], BF16)
            GT = [TH("GT%d" % i, [128, S], BF16) for i in range(2)]
            Wh = [TH("Wh%d" % i, [128, 8, 4, 128], BF16) for i in range(2)]
            sq = [TH("sq%d" % i, [128, 512], BF16) for i in range(2)]
            rst = [TH("rst%d" % i, [128, 512], F32) for i in range(2)]
            pbuf = [TH("pbuf%d" % i, [128, 512], BF16) for i in range(4)]
            fin = TH("fin", [128, 4, 8], F32)
            o0 = TH("o0", [128, 4, 128], F32)
            o1 = TH("o1", [128, 4, 128], F32)
            onb = TH("onb", [128, 2, 4, 128], BF16)
            junk2 = TH("junk2", [128, 128], BF16)
            junk2f = TH("junk2f", [128, 128], F32)
            pending = []

            def flush_pending():
                while pending:
                    pending.pop(0)()
            P.op("pool", lambda: nc.gpsimd.memset(VA[:, :, 128:130], 1.0), writes=["VAones"])
            epsb = TH("epsb", [128, 1], F32)
            P.op("pool", lambda: nc.gpsimd.memset(epsb[:], EPS), writes=["epsb"])
            accs = [[TH("accs%d_%d" % (i, k), [128, 390], F32) for k in range(3)] for i in range(2)]

            w_in = prm["l0_w_in"].rearrange("(j p) f -> p j f", p=128)

            def load_head_weights(h):
                wb = Wh[h % 2]
                for sec in range(4):
                    P.dma("pool", wb[:, :, sec, :], w_in[:, :, sec * E + h * 128: sec * E + (h + 1) * 128],
                          writes=[("Wh", h % 2, sec)])

            load_head_weights(heads[0])
            for hi_, h in enumerate(heads):
                wb = Wh[h % 2]
                if hi_ + 1 < len(heads):
                    load_head_weights(heads[hi_ + 1])
                for m in range(2):
                    P.dma("sp", QA[m][64:70, :], cst["c_qb"][h, :, :], writes=[("QAb", m)])
                    P.dma("sp", KA[m][64:70, :], cst["c_kb"][h, :, :], writes=[("KAb", m)])
                for nt in range(NQ):
                    tok = slice(nt * 512, (nt + 1) * 512)
                    hkeys = [("hT", nt)]
                    for sec, bk in ((0, 0), (1, 1), (3, 2)):
                        for j in range(8):
                            P.op("pe", lambda: nc.tensor.matmul(banks[bk][:], wb[:, j, sec, :], hT[:, j, tok],
                                                                start=(j == 0), stop=(j == 7)),
                                 reads=hkeys + [("Wh", h % 2, sec)], writes=[("bk", bk)])
                    for qi, bk in ((0, 0), (1, 1)):
                        P.op("act", lambda: nc.scalar.activation(out=sq[qi][:], in_=banks[bk][:], func=AF.Square),
                             reads=[("bk", bk)], writes=[("sq", qi)])
                    for qi in range(2):
                        P.op("pe", lambda: nc.tensor.matmul(banks[3 + qi][:], blk[:], sq[qi][:], start=True, stop=True),
                             reads=["blk", ("sq", qi)], writes=[("bk", 3 + qi)])
                    for qi, (bk, tiles, gcol, key) in enumerate(((0, QA, 0, "QA"), (1, KA, 1, "KA"))):
                        P.op("act", lambda: nc.scalar.activation(out=rst[qi][:], in_=banks[3 + qi][:], func=AF.Ln,
                                                                 scale=1.0, bias=epsb[:, 0:1]),
                             reads=[("bk", 3 + qi), "epsb"], writes=[("rst", qi)])
                        P.op("act", lambda: nc.scalar.activation(out=rst[qi][:], in_=rst[qi][:], func=AF.Exp, scale=-0.5),
                             reads=[("rst", qi)], writes=[("rst", qi)])
                        for m in range(2):
                            ps = slice(m * 64, (m + 1) * 64)
                            P.op("dve", lambda: nc.vector.scalar_tensor_tensor(
                                tiles[m][0:64, tok], banks[bk][ps, :], small[ps, gcol:gcol + 1], rst[qi][ps, :],
                                ALU.mult, ALU.mult),
                                reads=[("bk", bk), ("rst", qi), "small"], writes=[(key, m, nt)])
                    P.op("act", lambda: nc.scalar.activation(out=ZT[:, tok], in_=banks[2][:], func=AF.Silu),
                         reads=[("bk", 2)], writes=[("ZT", nt)])
                    for i in range(4):
                        kt = nt * 4 + i
                        for j in range(8):
                            P.op("pe", lambda: nc.tensor.matmul(banks[5][:, i * 128:(i + 1) * 128],
                                                                hT[:, j, kt * 128:(kt + 1) * 128], wb[:, j, 2, :],
                                                                start=(i == 0 and j == 0), stop=(j == 7),
                                                                skip_group_check=True),
                                 reads=hkeys + [("Wh", h % 2, 2)], writes=[("bk", 5)])
                    P.op("dve", lambda: nc.vector.tensor_copy(
                        VA[:, nt * 4:(nt + 1) * 4, 0:128],
                        banks[5][:, :].rearrange("p (i c) -> p i c", i=4)),
                        reads=[("bk", 5)], writes=[("VA", nt)])

                gt = GT[h % 2]
                ti = 0
                for Q in range(NQ):
                    qs = slice(Q * 512, (Q + 1) * 512)
                    tiles = [(m, kt) for m in range(2) for kt in range(4 * Q + 4)]
                    accreg = {}
                    slots = [(4, 0), (4, 1), (4, 2), (5, 0), (5, 1), (5, 2), (3, 0), (3, 1)]
                    for idx, (qb, m) in enumerate([(qb, m) for m in range(2) for qb in range(4)]):
                        accreg[(qb, m)] = slots[idx]
                    started = set()

                    def emit_S(i):
                        m, kt = tiles[i]
                        sb = (ti + i) % 3
                        diag = kt >= 4 * Q
                        P.op("pe", lambda: nc.tensor.matmul(banks[sb][:], KA[m][0:70, kt * 128:(kt + 1) * 128],
                                                            QA[m][0:70, qs], start=True, stop=not diag),
                             reads=[("KA", m, kt // 4), ("KAb", m), ("QA", m, Q), ("QAb", m)], writes=[("bk", sb)])
                        if diag:
                            P.op("pe", lambda: nc.tensor.matmul(banks[sb][:], ident[:], mask[:, kt - 4 * Q, :],
                                                                start=False, stop=True),
                                 reads=["ident", "mask"], writes=[("bk", sb)])

                    def emit_EP(i):
                        m, kt = tiles[i]
                        sb = (ti + i) % 3
                        pb = (ti + i) % 4
                        P.op("act", lambda: nc.scalar.activation(out=pbuf[pb][:], in_=banks[sb][:], func=AF.Exp),
                             reads=[("bk", sb)], writes=[("pbuf", pb)])
                        j = kt - 4 * Q
                        for qb in range(4):
                            if j >= 0 and qb < j:
                                continue
                            bk, r = accreg[(qb, m)]
                            first = bk not in started
                            started.add(bk)
                            P.op("pe", lambda: nc.tensor.matmul(banks[bk][:, r * 130:r * 130 + 129],
                                                                pbuf[pb][:, qb * 128:(qb + 1) * 128], VA[:, kt, 0:129],
                                                                start=first, stop=False, skip_group_check=True),
                                 reads=[("pbuf", pb), ("VA", kt // 4), "VAones"], writes=[("bk", bk)])

                    n = len(tiles)
                    emit_S(0)
                    if n > 1:
                        emit_S(1)
                    for i in range(n):
                        if i + 2 < n:
                            emit_S(i + 2)
                        emit_EP(i)
                        if i == min(n - 1, 24):
                            flush_pending()
                    ti += n
                    ab = Q % 2
                    sidx = {4: 0, 5: 1, 3: 2}
                    for bk_, ncol in ((4, 390), (5, 390), (3, 260)):
                        P.op("dve", lambda: nc.vector.tensor_copy(accs[ab][sidx[bk_]][:, 0:ncol], banks[bk_][:, 0:ncol]),
                             reads=[("bk", bk_)], writes=[("accs", ab, sidx[bk_])])
                    def tail(Q=Q, qs=qs, gt=gt, h=h, ab=ab, accreg=accreg):
                        ob = Q % 2
                        for qb in range(4):
                            b0, r0 = accreg[(qb, 0)]
                            b1, r1 = accreg[(qb, 1)]
                            s0, s1 = accs[ab][sidx[b0]], accs[ab][sidx[b1]]
                            k0, k1 = ("accs", ab, sidx[b0]), ("accs", ab, sidx[b1])
                            a0 = s0[:, r0 * 130:r0 * 130 + 128]
                            d0 = s0[:, r0 * 130 + 128:r0 * 130 + 129]
                            a1 = s1[:, r1 * 130:r1 * 130 + 128]
                            d1 = s1[:, r1 * 130 + 128:r1 * 130 + 129]
                            fk = ("fin", qb)
                            P.op("dve", lambda: nc.vector.reciprocal(fin[:, qb, 0:1], d0), reads=[k0], writes=[fk])
                            P.op("dve", lambda: nc.vector.reciprocal(fin[:, qb, 1:2], d1), reads=[k1], writes=[fk])
                            P.op("dve", lambda: nc.vector.tensor_tensor(fin[:, qb, 2:3], fin[:, qb, 1:2], small[:, 2:3], ALU.mult),
                                 reads=[fk, "small"], writes=[fk])
                            P.op("dve", lambda: nc.vector.tensor_scalar(o0[:, qb, :], a0, fin[:, qb, 0:1], None, ALU.mult),
                                 reads=[k0, fk], writes=[("o0", qb)])
                            P.op("dve", lambda: nc.vector.scalar_tensor_tensor(o1[:, qb, :], a1, fin[:, qb, 2:3], o0[:, qb, :],
                                                                               ALU.mult, ALU.add),
                                 reads=[k1, fk, ("o0", qb)], writes=[("o1", qb)])
                        ob = Q % 2
                        fks = [("fin", qb) for qb in range(4)]
                        for qb in range(4):
                            P.op("dve", lambda: nc.vector.scalar_tensor_tensor(junk2f[:], o1[:, qb, :], 1.0, o1[:, qb, :], ALU.mult, ALU.mult,
                                                                               accum_out=fin[:, qb, 3:4]),
                                 reads=[("o1", qb)], writes=["junk2f", ("fin", qb)])
                        P.op("act", lambda: nc.scalar.activation(out=fin[:, :, 4:5], in_=fin[:, :, 3:4], func=AF.Sqrt,
                                                                 scale=1.0 / 128, bias=EPS),
                             reads=fks, writes=fks)
                        P.op("dve", lambda: nc.vector.reciprocal(fin[:, :, 5:6], fin[:, :, 4:5]), reads=fks, writes=fks)
                        for qb in range(4):
                            P.op("dve", lambda: nc.vector.scalar_tensor_tensor(onb[:, ob, qb, :], o1[:, qb, :], fin[:, qb, 5:6], ghn[:],
                                                                               ALU.mult, ALU.mult),
                                 reads=[("o1", qb), ("fin", qb), "ghn"], writes=[("onb", ob, qb)])
                        tb = banksb[Q % 2]
                        for qb in range(4):
                            P.op("pe", lambda: nc.tensor.transpose(tb[:, qb * 128:(qb + 1) * 128], onb[:, ob, qb, :], ident[:]),
                                 reads=[("onb", ob, qb), "ident"], writes=[("pbb", Q % 2)])
                        P.op("dve", lambda: nc.vector.tensor_tensor(gt[:, qs], tb[:, 0:512], ZT[:, qs], ALU.mult),
                             reads=[("pbb", Q % 2), ("ZT", Q)], writes=[("GT", h % 2)])
                    pending.append(tail)
                flush_pending()
                P.dma("sp", gscr[:, h, 0:NQ * 512], gt[:, 0:NQ * 512], reads=[("GT", h % 2)], writes=[("gscr", h)])
                if dump is not None and h == heads[-1]:
                    n = NQ * 512
                    allk = list(P.lastw.keys())
                    P.dma("sp", dump["d_hT"][:, :, :], hT[:, :, 0:n], reads=allk)
                    for m in range(2):
                        P.dma("sp", dump["d_QA"][m, :, :], QA[m][0:70, 0:n], reads=allk)
                        P.dma("sp", dump["d_KA"][m, :, :], KA[m][0:70, 0:n], reads=allk)
                    P.dma("sp", dump["d_VA"][:, :, :], VA[:, 0:4 * NQ, :], reads=allk)
                    P.dma("sp", dump["d_ZT"][:, :], ZT[:, 0:n], reads=allk)
                    P.dma("sp", dump["d_GT"][:, :], gt[:, 0:n], reads=allk)

        if dump is not None:
            return
        P.barrier()
        with ExitStack() as stD:
            TD = lambda name, shape, dt: stD.enter_context(nc.sbuf_tensor(name, shape, dt))
            Wo = TD("Wo", [128, NH, D], BF16)
            Gt = [TD("Gt%d" % i, [128, NH, 512], BF16) for i in range(2)]
            xr = [TD("xr%d" % i, [128, D], F32) for i in range(2)]
            x1t = [TD("x1t%d" % i, [128, D], F32) for i in range(2)]
            w_out = prm["l0_w_out"].rearrange("(h p) f -> p h f", p=128)
            for hh in range(0, NH, 4):
                P.dma("pool", Wo[:, hh:hh + 4, :], w_out[:, hh:hh + 4, :], writes=[("Wo", hh // 4)])
            for Tt in range(NQ):
                g = Gt[Tt % 2]
                P.dma("sp", g[:], gscr[:, :, Tt * 512:(Tt + 1) * 512], reads=[("gscr", h) for h in heads],
                      writes=[("Gt", Tt % 2)])
                for tt in range(4):
                    t = Tt * 4 + tt
                    b = t % 2
                    P.dma("sp", xr[b][:], x[t * 128:(t + 1) * 128, :], writes=[("xr", b)])
                    for half in range(2):
                        bk = (t * 2 + half) % 4
                        for h in range(NH):
                            P.op("pe", lambda: nc.tensor.matmul(banks[bk][:], g[:, h, tt * 128:(tt + 1) * 128],
                                                                Wo[:, h, half * 512:(half + 1) * 512],
                                                                start=(h == 0), stop=(h == NH - 1)),
                                 reads=[("Gt", Tt % 2), ("Wo", h // 4)], writes=[("bk", bk)])
                        P.op("dve", lambda: nc.vector.tensor_tensor(x1t[b][:, half * 512:(half + 1) * 512], banks[bk][:],
                                                                    xr[b][:, half * 512:(half + 1) * 512], ALU.add),
                             reads=[("bk", bk), ("xr", b)], writes=[("x1t", b)])
                    P.dma("sp", x1_out[t * 128:(t + 1) * 128, :], x1t[b][:], reads=[("x1t", b)], writes=[("x1", t)])


def build_l0(NQ=8):
    nc = bass.Bass("TRN2", target_bir_lowering=False)
    x = nc.dram_tensor("x", [S, D], F32, kind="ExternalInput").ap()
    prm = {k: nc.dram_tensor(k, shp, F32, kind="ExternalInput").ap() for k, shp in L0_PARAMS.items()}
    cst = {k: nc.dram_tensor(k, shp, dt, kind="ExternalInput").ap() for k, (shp, dt) in CONST_SHAPES.items()}
    out = nc.dram_tensor("out", [S, D], F32, kind="ExternalOutput").ap()
    gscr = nc.dram_tensor("gscr", [128, NH, S], BF16, kind="Internal").ap()
    with ExitStack() as st:
        P = Prog(nc, st)
        emit_layer0(nc, P, x, prm, cst, out, gscr, NQ=NQ)
        P.finish()
        build_l0.stats = {"ninst": P.ninst, "nsem": P._nsem, "dma_uses": sum(sl[1] // 16 for sl in P.dma_sems),
                          "cur": {e: c[1] for e, c in P.cur.items()}}
    return nc


def build_l0_debug(heads=(0,), NQ=2):
    nc = bass.Bass("TRN2", target_bir_lowering=False)
    x = nc.dram_tensor("x", [S, D], F32, kind="ExternalInput").ap()
    prm = {k: nc.dram_tensor(k, shp, F32, kind="ExternalInput").ap() for k, shp in L0_PARAMS.items()}
    cst = {k: nc.dram_tensor(k, shp, dt, kind="ExternalInput").ap() for k, (shp, dt) in CONST_SHAPES.items()}
    n = NQ * 512
    dshapes = {"d_hT": [128, 8, n], "d_QA": [2, 70, n], "d_KA": [2, 70, n], "d_VA": [128, 4 * NQ, 130],
               "d_ZT": [128, n], "d_GT": [128, n]}
    dump = {k: nc.dram_tensor(k, shp, BF16, kind="ExternalOutput").ap() for k, shp in dshapes.items()}
    gscr = nc.dram_tensor("gscr", [128, NH, S], BF16, kind="Internal").ap()
    with ExitStack() as st:
        P = Prog(nc, st)
        emit_layer0(nc, P, x, prm, cst, None, gscr, heads=heads, NQ=NQ, dump=dump)
        P.finish()
    return nc


def run_layer0_spmd(inputs, n_cores=8):
    nc = build_l0()
    cst = host_consts()
    in_maps = []
    for b in range(n_cores):
        m = {"x": np.ascontiguousarray(inputs["x"][b])}
        for k in L0_PARAMS:
            m[k] = np.ascontiguousarray(inputs[k])
        m.update(cst)
        in_maps.append(m)
    res = run_bass_kernel_spmd(nc, in_maps, core_ids=list(range(n_cores)))
    return np.stack([np.asarray(r["out"]) for r in res.results], axis=0)


TWO_PI = 6.283185307179586
MAGIC = 12582912.0
L1_PARAMS = {
    "l1_norm_g": [D], "l1_w_in": [D, 2 * E], "l1_lam_re": [128, 64], "l1_lam_im": [128, 64], "l1_log_dt": [128],
    "l1_b_re": [128, 64, 16], "l1_b_im": [128, 64, 16], "l1_c_re": [128, 16, 64], "l1_c_im": [128, 16, 64],
    "l1_d": [E], "l1_w_glu": [E, E], "l1_b_glu": [E], "l1_w_out": [E, D],
}


def emit_s5_tables(nc, P, st, prm, NPOW=17):
    T = lambda name, shape, dt: st.enter_context(nc.sbuf_tensor(name, shape, dt))
    tb = T("s5tb", [128, 24, 64], F32)
    pw = T("s5pw", [128, NPOW, 2, 64], F32)
    LR, LI, LDT, DT, MAG, TH, K_, SN, CS, ABR, ABI, DEN, NR, GR, GI, T1, T2 = range(17)
    V = lambda i: tb[:, i, :]
    for g2 in range(2):
        ps = slice(g2 * 64, (g2 + 1) * 64)
        for i, nm in ((LR, "l1_lam_re"), (LI, "l1_lam_im")):
            src = prm[nm].rearrange("(pr g2) p -> g2 p pr", g2=2)
            P.dma("sp", tb[ps, i, :], src[g2], writes=[("tb", i, g2)], allow_slow_non_contiguous=True)
        src = prm["l1_log_dt"].rearrange("(pr g2) -> g2 pr", g2=2)
        P.dma("sp", tb[ps, LDT, :], src[g2:g2 + 1, :].to_broadcast([64, 64]), writes=[("tb", LDT, g2)],
              allow_slow_non_contiguous=True)
    rd = lambda *idx: [("tb", i, g2) for i in idx for g2 in range(2)] + [("tb", i) for i in idx]
    op = P.op
    op("act", lambda: nc.scalar.activation(out=V(DT), in_=V(LDT), func=AF.Exp), reads=rd(LDT), writes=[("tb", DT)])
    op("dve", lambda: nc.vector.tensor_tensor(V(T1), V(LR), V(DT), ALU.mult), reads=rd(LR, DT), writes=[("tb", T1)])
    op("act", lambda: nc.scalar.activation(out=V(MAG), in_=V(T1), func=AF.Exp), reads=rd(T1), writes=[("tb", MAG)])
    op("dve", lambda: nc.vector.tensor_tensor(V(TH), V(LI), V(DT), ALU.mult), reads=rd(LI, DT), writes=[("tb", TH)])
    for dst, shift in ((SN, 0.0), (CS, math.pi / 2)):
        op("dve", lambda: nc.vector.tensor_scalar(V(T2), V(TH), shift, 1.0 / TWO_PI, ALU.add, ALU.mult),
           reads=rd(TH), writes=[("tb", T2)])
        op("dve", lambda: nc.vector.tensor_scalar(V(K_), V(T2), MAGIC, None, ALU.add), reads=rd(T2), writes=[("tb", K_)])
        op("dve", lambda: nc.vector.tensor_scalar(V(K_), V(K_), MAGIC, None, ALU.subtract), reads=rd(K_), writes=[("tb", K_)])
        op("dve", lambda: nc.vector.scalar_tensor_tensor(V(T2), V(K_), -TWO_PI, V(TH), ALU.mult, ALU.add),
           reads=rd(K_, TH), writes=[("tb", T2)])
        op("act", lambda: nc.scalar.activation(out=V(dst), in_=V(T2), func=AF.Sin, scale=1.0, bias=shift),
           reads=rd(T2), writes=[("tb", dst)])
    op("dve", lambda: nc.vector.tensor_tensor(V(ABR), V(MAG), V(CS), ALU.mult), reads=rd(MAG, CS), writes=[("tb", ABR)])
    op("dve", lambda: nc.vector.tensor_tensor(V(ABI), V(MAG), V(SN), ALU.mult), reads=rd(MAG, SN), writes=[("tb", ABI)])
    op("dve", lambda: nc.vector.tensor_tensor(V(T1), V(LR), V(LR), ALU.mult), reads=rd(LR), writes=[("tb", T1)])
    op("dve", lambda: nc.vector.tensor_tensor(V(T2), V(LI), V(LI), ALU.mult), reads=rd(LI), writes=[("tb", T2)])
    op("dve", lambda: nc.vector.tensor_tensor(V(DEN), V(T1), V(T2), ALU.add), reads=rd(T1, T2), writes=[("tb", DEN)])
    op("dve", lambda: nc.vector.reciprocal(V(DEN), V(DEN)), reads=rd(DEN), writes=[("tb", DEN)])
    op("dve", lambda: nc.vector.tensor_scalar(V(NR), V(ABR), -1.0, None, ALU.add), reads=rd(ABR), writes=[("tb", NR)])
    op("dve", lambda: nc.vector.tensor_tensor(V(T1), V(NR), V(LR), ALU.mult), reads=rd(NR, LR), writes=[("tb", T1)])
    op("dve", lambda: nc.vector.tensor_tensor(V(T2), V(ABI), V(LI), ALU.mult), reads=rd(ABI, LI), writes=[("tb", T2)])
    op("dve", lambda: nc.vector.tensor_tensor(V(GR), V(T1), V(T2), ALU.add), reads=rd(T1, T2), writes=[("tb", GR)])
    op("dve", lambda: nc.vector.tensor_tensor(V(GR), V(GR), V(DEN), ALU.mult), reads=rd(GR, DEN), writes=[("tb", GR)])
    op("dve", lambda: nc.vector.tensor_tensor(V(T1), V(ABI), V(LR), ALU.mult), reads=rd(ABI, LR), writes=[("tb", T1)])
    op("dve", lambda: nc.vector.tensor_tensor(V(T2), V(NR), V(LI), ALU.mult), reads=rd(NR, LI), writes=[("tb", T2)])
    op("dve", lambda: nc.vector.tensor_tensor(V(GI), V(T1), V(T2), ALU.subtract), reads=rd(T1, T2), writes=[("tb", GI)])
    op("dve", lambda: nc.vector.tensor_tensor(V(GI), V(GI), V(DEN), ALU.mult), reads=rd(GI, DEN), writes=[("tb", GI)])
    op("pool", lambda: nc.gpsimd.memset(pw[:, 0, 0, :], 1.0), writes=[("pw", 0)])
    op("pool", lambda: nc.gpsimd.memset(pw[:, 0, 1, :], 0.0), writes=[("pw", 0)])
    for t in range(1, NPOW):
        pr_, pi_ = pw[:, t - 1, 0, :], pw[:, t - 1, 1, :]
        op("dve", lambda: nc.vector.tensor_tensor(V(T1), pr_, V(ABR), ALU.mult), reads=[("pw", t - 1)] + rd(ABR), writes=[("tb", T1)])
        op("dve", lambda: nc.vector.tensor_tensor(V(T2), pi_, V(ABI), ALU.mult), reads=[("pw", t - 1)] + rd(ABI), writes=[("tb", T2)])
        op("dve", lambda: nc.vector.tensor_tensor(pw[:, t, 0, :], V(T1), V(T2), ALU.subtract), reads=rd(T1, T2), writes=[("pw", t)])
        op("dve", lambda: nc.vector.tensor_tensor(V(T1), pr_, V(ABI), ALU.mult), reads=[("pw", t - 1)] + rd(ABI), writes=[("tb", T1)])
        op("dve", lambda: nc.vector.tensor_tensor(V(T2), pi_, V(ABR), ALU.mult), reads=[("pw", t - 1)] + rd(ABR), writes=[("tb", T2)])
        op("dve", lambda: nc.vector.tensor_tensor(pw[:, t, 1, :], V(T1), V(T2), ALU.add), reads=rd(T1, T2), writes=[("pw", t)])
    return {"tb": tb, "pw": pw, "idx": dict(ABR=ABR, ABI=ABI, GR=GR, GI=GI)}


def build_s5_tables_debug():
    nc = bass.Bass("TRN2", target_bir_lowering=False)
    prm = {k: nc.dram_tensor(k, shp, F32, kind="ExternalInput").ap() for k, shp in L1_PARAMS.items()
           if k in ("l1_lam_re", "l1_lam_im", "l1_log_dt")}
    d_tb = nc.dram_tensor("d_tb", [128, 24, 64], F32, kind="ExternalOutput").ap()
    d_pw = nc.dram_tensor("d_pw", [128, 17, 2, 64], F32, kind="ExternalOutput").ap()
    with ExitStack() as st:
        P = Prog(nc, st)
        r = emit_s5_tables(nc, P, st, prm)
        allk = list(P.lastw.keys())
        P.dma("sp", d_tb[:, :, :], r["tb"][:], reads=allk)
        P.dma("sp", d_pw[:, :, :, :], r["pw"][:], reads=allk)
        P.finish()
    return nc


def emit_s5_operands(nc, P, st, prm, tabs, ident, gts, o_K, o_WS, o_WC, banks, banksb, o_BB=None):
    T = lambda name, shape, dt: st.enter_context(nc.sbuf_tensor(name, shape, dt))
    tb, pw, ix = tabs["tb"], tabs["pw"], tabs["idx"]
    Bs = [T("s5B%d" % i, [128, 64, 16], F32) for i in range(2)]
    Cs = [T("s5C%d" % i, [128, 64, 16], F32) for i in range(2)]
    BB = [T("s5BB%d" % i, [128, 64, 16], F32) for i in range(2)]
    t1 = T("s5t1", [128, 64, 16], F32)
    t2 = T("s5t2", [128, 64, 16], F32)
    t3 = T("s5t3", [128, 64, 16], F32)
    t4 = T("s5t4", [128, 64, 16], F32)
    pwn = T("s5pwn", [128, 17, 64], F32)
    dsk = T("s5dsk", [128, 16], F32)
    XB2 = [T("s5XB%d" % i, [128, 16, 2, 4, 32], BF16) for i in range(2)]
    WC2 = [T("s5WC%d" % i, [128, 16, 2, 4, 32], BF16) for i in range(2)]
    CB2 = [T("s5CB%d" % i, [128, 2, 4, 32], BF16) for i in range(2)]
    Kbd2 = [T("s5Kbd%d" % i, [128, 16, 128], BF16) for i in range(2)]
    WS2 = [T("s5WS%d" % i, [128, 16, 2, 128], BF16) for i in range(2)]
    identf = T("s5idf", [128, 128], F32)
    op = P.op
    for g2 in range(2):
        ps = slice(g2 * 64, (g2 + 1) * 64)
        for i, nm in enumerate(("l1_b_re", "l1_b_im")):
            P.dma("sp", Bs[i][ps, :, :], prm[nm].rearrange("(pr g2) p c -> g2 p pr c", g2=2)[g2], writes=[("Bs", i, g2)])
    P.dma("sp", dsk[:], prm["l1_d"].rearrange("(gt p) -> p gt", p=128), writes=["dsk"], allow_slow_non_contiguous=True)
    def load_C(gt_):
        for pr_ in range(gt_ * 4, gt_ * 4 + 4):
            for g2 in range(2):
                ps = slice(g2 * 64, (g2 + 1) * 64)
                for i, nm in enumerate(("l1_c_re", "l1_c_im")):
                    src = prm[nm].rearrange("(pr g2) co p -> g2 p pr co", g2=2)[g2]
                    P.dma("sp", Cs[i][ps, pr_, :], src[:, pr_, :],
                          writes=[("Cs", i, g2, pr_)], allow_slow_non_contiguous=True)

    C_AHEAD = 3
    gts_all = list(gts)
    for gt_ in gts_all[:C_AHEAD]:
        load_C(gt_)
    op("dve", lambda: nc.vector.tensor_copy(identf[:], ident[:]), reads=["ident"], writes=["identf"])
    for i in range(2):
        for tl, key in ((XB2[i], "XB"), (WC2[i], "WC"), (CB2[i], "CB"), (Kbd2[i], "Kbd")):
            op("pool", lambda: nc.gpsimd.memset(tl[:], 0.0), writes=[key + str(i)])
    rB = [("Bs", i, g2) for i in range(2) for g2 in range(2)]
    rC_of = lambda gt: [("Cs", i, g2, pr_) for i in range(2) for g2 in range(2) for pr_ in range(gt * 4, gt * 4 + 4)]
    bc = lambda ap2, n: ap2.unsqueeze(2).to_broadcast([128, n, 16])
    GR, GI = tb[:, ix["GR"], :], tb[:, ix["GI"], :]
    op("dve", lambda: nc.vector.tensor_tensor(t1[:], Bs[0][:], bc(GR, 64), ALU.mult), reads=rB + [("tb", ix["GR"])], writes=["t1"])
    op("dve", lambda: nc.vector.tensor_tensor(t2[:], Bs[1][:], bc(GI, 64), ALU.mult), reads=rB + [("tb", ix["GI"])], writes=["t2"])
    op("dve", lambda: nc.vector.tensor_tensor(BB[0][:], t1[:], t2[:], ALU.subtract), reads=["t1", "t2"], writes=["BB0"])
    op("dve", lambda: nc.vector.tensor_tensor(t1[:], Bs[1][:], bc(GR, 64), ALU.mult), reads=rB + [("tb", ix["GR"])], writes=["t1"])
    op("dve", lambda: nc.vector.tensor_tensor(t2[:], Bs[0][:], bc(GI, 64), ALU.mult), reads=rB + [("tb", ix["GI"])], writes=["t2"])
    op("dve", lambda: nc.vector.tensor_tensor(BB[1][:], t1[:], t2[:], ALU.add), reads=["t1", "t2"], writes=["BB1"])
    if o_BB is not None:
        for i in range(2):
            P.dma("sp", o_BB[i, :, :, :], BB[i][:], reads=["BB%d" % i])

    def cmul_blocks(dst, k, Ar, Ai, Pr, Pi, prs, neg_im, rA, rP, wkey):
        a, b_ = t1[:, 0:4, :], t1[:, 4:8, :]
        c_, d_ = t2[:, 0:4, :], t2[:, 4:8, :]
        op("dve", lambda: nc.vector.tensor_tensor(a, Ar[:, prs, :], bc(Pr[:, prs], 4), ALU.mult), reads=rA + rP, writes=["t1"])
        op("dve", lambda: nc.vector.tensor_tensor(b_, Ai[:, prs, :], bc(Pi[:, prs], 4), ALU.mult), reads=rA + rP, writes=["t1"])
        op("dve", lambda: nc.vector.tensor_tensor(c_, Ar[:, prs, :], bc(Pi[:, prs], 4), ALU.mult), reads=rA + rP, writes=["t2"])
        op("dve", lambda: nc.vector.tensor_tensor(d_, Ai[:, prs, :], bc(Pr[:, prs], 4), ALU.mult), reads=rA + rP, writes=["t2"])
        for g2 in range(2):
            ps = slice(g2 * 64, (g2 + 1) * 64)
            cs = slice(g2 * 16, (g2 + 1) * 16)
            op("dve", lambda: nc.vector.tensor_tensor(dst[ps, k, 0, :, cs], a[ps], b_[ps], ALU.subtract),
               reads=["t1"], writes=[wkey])
            if neg_im:
                op("dve", lambda: nc.vector.scalar_tensor_tensor(dst[ps, k, 1, :, cs], c_[ps], -1.0, d_[ps], ALU.mult, ALU.subtract),
                   reads=["t2"], writes=[wkey])
            else:
                op("dve", lambda: nc.vector.tensor_tensor(dst[ps, k, 1, :, cs], c_[ps], d_[ps], ALU.add),
                   reads=["t2"], writes=[wkey])

    rPW = [("pw", t) for t in range(17)]
    op("dve", lambda: nc.vector.tensor_scalar(pwn[:], pw[:, :, 1, :], -1.0, None, ALU.mult), reads=rPW, writes=["pwn"])

    def cmul_all_d(dst, Ar, Ai, d0, prs, neg_im, rA, wkey):
        V4 = lambda t: t[:, :, :].rearrange("p (d j) c -> p d j c", d=16)
        a, b_, c_, d_ = V4(t1), V4(t2), V4(t3), V4(t4)
        bA = lambda A: A[:, prs, :].unsqueeze(1).to_broadcast([128, 16, 4, 16])
        Pr = pw[:, d0:d0 + 16, 0, prs].unsqueeze(3).to_broadcast([128, 16, 4, 16])
        Pi = pw[:, d0:d0 + 16, 1, prs].unsqueeze(3).to_broadcast([128, 16, 4, 16])
        Pin = pwn[:, d0:d0 + 16, prs].unsqueeze(3).to_broadcast([128, 16, 4, 16])
        op("dve", lambda: nc.vector.tensor_tensor(a, bA(Ar), Pr, ALU.mult), reads=rA + rPW, writes=["t1"])
        op("dve", lambda: nc.vector.tensor_tensor(b_, bA(Ai), Pi, ALU.mult), reads=rA + rPW, writes=["t2"])
        op("dve", lambda: nc.vector.tensor_tensor(c_, bA(Ar), Pin if neg_im else Pi, ALU.mult), reads=rA + rPW + ["pwn"], writes=["t3"])
        op("dve", lambda: nc.vector.tensor_tensor(d_, bA(Ai), Pr, ALU.mult), reads=rA + rPW, writes=["t4"])
        for g2 in range(2):
            ps = slice(g2 * 64, (g2 + 1) * 64)
            cs = slice(g2 * 16, (g2 + 1) * 16)
            op("dve", lambda: nc.vector.tensor_tensor(dst[ps, :, 0, :, cs], a[ps], b_[ps], ALU.subtract),
               reads=["t1", "t2"], writes=[wkey])
            op("dve", lambda: nc.vector.tensor_tensor(dst[ps, :, 1, :, cs], c_[ps], d_[ps], ALU.subtract if neg_im else ALU.add),
               reads=["t3", "t4"], writes=[wkey])

    for gi, gt in enumerate(gts):
        pb_ = str(gi % 2)
        XB, WC, CB, Kbd, WS = XB2[gi % 2], WC2[gi % 2], CB2[gi % 2], Kbd2[gi % 2], WS2[gi % 2]
        kXB, kWC, kCB, kKbd, kWS = "XB" + pb_, "WC" + pb_, "CB" + pb_, "Kbd" + pb_, "WS" + pb_
        prs = slice(gt * 4, gt * 4 + 4)
        rC = rC_of(gt)
        cmul_all_d(XB, BB[0], BB[1], 0, prs, False, ["BB0", "BB1"], kXB)
        cmul_all_d(WC, Cs[0], Cs[1], 1, prs, True, rC, kWC)
        for g2 in range(2):
            ps = slice(g2 * 64, (g2 + 1) * 64)
            cs = slice(g2 * 16, (g2 + 1) * 16)
            for i in range(2):
                op("dve", lambda: nc.vector.tensor_copy(CB[ps, i, :, cs], Cs[i][ps, prs, :]), reads=rC, writes=[kCB])
        op("dve", lambda: nc.vector.tensor_scalar(CB[:, 1, :, :], CB[:, 1, :, :], -1.0, None, ALU.mult), reads=[kCB], writes=[kCB])
        for j in range(4):
            kb = banks[j % 2]
            for d in range(16):
                for i in range(2):
                    op("pe", lambda: nc.tensor.matmul(kb[0:32, d * 32:(d + 1) * 32], XB[:, d, i, j, :], CB[:, i, j, :],
                                                      start=(d == 0 and i == 0), stop=(i == 1), skip_group_check=True),
                       reads=[kXB, kCB], writes=[("kbk", j % 2)])
            op("act", lambda: nc.scalar.activation(out=Kbd[j * 32:(j + 1) * 32, :, j * 32:(j + 1) * 32],
                                                   in_=kb[0:32, :].rearrange("p (d c) -> p d c", d=16), func=AF.Copy),
               reads=[("kbk", j % 2)], writes=[kKbd])
        op("dve", lambda: nc.vector.scalar_tensor_tensor(Kbd[:, 0, :], identf[:], dsk[:, gt:gt + 1], Kbd[:, 0, :], ALU.mult, ALU.add),
           reads=["identf", "dsk", kKbd], writes=[kKbd])
        if gi + C_AHEAD < len(gts_all):
            load_C(gts_all[gi + C_AHEAD])
        P.dma("sp", o_K[gt, :, :, :], Kbd[:], reads=[kKbd], writes=[("oK", gt)])
        for j in range(4):
            for i in range(2):
                for half in range(2):
                    tbk = banksb[(j * 4 + i * 2 + half) % 2]
                    for mm in range(8):
                        m_ = half * 8 + mm
                        op("pe", lambda: nc.tensor.transpose(tbk[0:32, mm * 128:(mm + 1) * 128], XB[:, 15 - m_, i, j, :], ident[:]),
                           reads=[kXB, "ident"], writes=[("tbk", (j * 4 + i * 2 + half) % 2)])
                    op("act", lambda: nc.scalar.activation(out=WS[j * 32:(j + 1) * 32, half * 8:(half + 1) * 8, i, :],
                                                           in_=tbk[0:32, :].rearrange("p (m c) -> p m c", m=8), func=AF.Copy),
                       reads=[("tbk", (j * 4 + i * 2 + half) % 2)], writes=[kWS])
        P.dma("sp", o_WS[gt, :, :, :, :], WS[:], reads=[kWS], writes=[("oWS", gt)])
        P.dma("sp", o_WC[gt, :, :, :, :, :], WC[:], reads=[kWC], writes=[("oWC", gt)])


def build_s5_operands_debug(gts=(0, 5)):
    nc = bass.Bass("TRN2", target_bir_lowering=False)
    prm = {k: nc.dram_tensor(k, shp, F32, kind="ExternalInput").ap() for k, shp in L1_PARAMS.items()
           if k in ("l1_lam_re", "l1_lam_im", "l1_log_dt", "l1_b_re", "l1_b_im", "l1_c_re", "l1_c_im", "l1_d")}
    c_ident = nc.dram_tensor("c_ident", [128, 128], BF16, kind="ExternalInput").ap()
    o_K = nc.dram_tensor("o_K", [16, 128, 16, 128], BF16, kind="ExternalOutput").ap()
    o_WS = nc.dram_tensor("o_WS", [16, 128, 16, 2, 128], BF16, kind="ExternalOutput").ap()
    o_WC = nc.dram_tensor("o_WC", [16, 128, 16, 2, 4, 32], BF16, kind="ExternalOutput").ap()
    o_BB = nc.dram_tensor("o_BB", [2, 128, 64, 16], F32, kind="ExternalOutput").ap()
    with ExitStack() as st:
        P = Prog(nc, st)
        banks = [st.enter_context(nc.psum_tensor("bank%d" % i, [128, 512], F32)) for i in range(2)]
        banksb = [st.enter_context(nc.psum_tensor("bankb%d" % i, [128, 1024], BF16)) for i in range(2)]
        ident = st.enter_context(nc.sbuf_tensor("ident", [128, 128], BF16))
        P.dma("sp", ident[:], c_ident[:, :], writes=["ident"])
        tabs = emit_s5_tables(nc, P, st, prm)
        emit_s5_operands(nc, P, st, prm, tabs, ident, list(gts), o_K, o_WS, o_WC, banks, banksb, o_BB=o_BB)
        P.finish()
    return nc


NCH = 64
NTS = NCH * 16
KS_LEVELS = NCH.bit_length() - 1


def alloc_ks(nc, st):
    T = lambda name, shape, dt: st.enter_context(nc.sbuf_tensor(name, shape, dt))
    return {"ks": T("s5ks", [128, 6, 2, 64], F32), "kt1": T("s5kt1", [128, 64], F32), "kt2": T("s5kt2", [128, 64], F32),
            "ksn": T("s5ksn", [128, 6, 64], F32)}


def emit_ks_table(nc, P, st, tabs, bufs=None):
    if bufs is None:
        bufs = alloc_ks(nc, st)
    pw = tabs["pw"]
    ks, kt1, kt2 = bufs["ks"], bufs["kt1"], bufs["kt2"]
    op = P.op
    op("dve", lambda: nc.vector.tensor_copy(ks[:, 0, :, :], pw[:, 16, :, :]), reads=[("pw", 16)], writes=[("ks", 0)])
    for k in range(1, 6):
        a, b_ = ks[:, k - 1, 0, :], ks[:, k - 1, 1, :]
        op("dve", lambda: nc.vector.tensor_tensor(kt1[:], a, a, ALU.mult), reads=[("ks", k - 1)], writes=["kt1"])
        op("dve", lambda: nc.vector.tensor_tensor(kt2[:], b_, b_, ALU.mult), reads=[("ks", k - 1)], writes=["kt2"])
        op("dve", lambda: nc.vector.tensor_tensor(ks[:, k, 0, :], kt1[:], kt2[:], ALU.subtract), reads=["kt1", "kt2"], writes=[("ks", k)])
        op("dve", lambda: nc.vector.tensor_tensor(kt1[:], a, b_, ALU.mult), reads=[("ks", k - 1)], writes=["kt1"])
        op("dve", lambda: nc.vector.tensor_scalar(ks[:, k, 1, :], kt1[:], 2.0, None, ALU.mult), reads=["kt1"], writes=[("ks", k)])
    ksn = bufs["ksn"]
    op("dve", lambda: nc.vector.tensor_scalar(ksn[:], ks[:, :, 1, :], -1.0, None, ALU.mult), reads=[("ks", k) for k in range(6)], writes=["ksn"])
    return ks, ksn


def emit_s5_core(nc, P, gt, uT, Kbd, WS, WC, ks, ksn, carry, hbuf, hprev, Yb, Sb, yout, first_tile, tag="", ukey=("uT",), ykey=("yout",)):
    op = P.op
    uv = uT[:, :].rearrange("p (n m) -> p m n", m=16)
    h0 = hbuf[0]
    for j in range(4):
        rows = slice(j * 32, (j + 1) * 32)
        first = True
        for ri in range(2):
            for m in range(16):
                op("pe", lambda: nc.tensor.matmul(Sb[j][:, ri * NCH:(ri + 1) * NCH], WS[rows, m, ri, :], uv[rows, m, :],
                                                  start=first, stop=(m == 15), skip_group_check=True,
                                                  tile_position=(32 * j, 0)),
                   reads=[ukey, ("WS" + tag,)], writes=[("Sb", j)])
                first = False
        op("dve", lambda: nc.vector.tensor_copy(h0[:, j, :, :], Sb[j][:, 0:2 * NCH].rearrange("p (r n) -> p r n", r=2)),
           reads=[("Sb", j)], writes=[("hb", 0)])
    for j in range(4):
        pr = gt * 4 + j
        ar, ai, nai = ks[:, 0, 0, pr:pr + 1], ks[:, 0, 1, pr:pr + 1], ksn[:, 0, pr:pr + 1]
        if not first_tile:
            cr, ci = carry[:, j, 0:1], carry[:, j, 1:2]
            op("dve", lambda: nc.vector.scalar_tensor_tensor(h0[:, j, 0, 0:1], cr, ar, h0[:, j, 0, 0:1], ALU.mult, ALU.add),
               reads=[("carry", gt), ("hb", 0), ("ks", 0)], writes=[("hb", 0)])
            op("dve", lambda: nc.vector.scalar_tensor_tensor(h0[:, j, 0, 0:1], ci, nai, h0[:, j, 0, 0:1], ALU.mult, ALU.add),
               reads=[("carry", gt), ("hb", 0), "ksn"], writes=[("hb", 0)])
            op("dve", lambda: nc.vector.scalar_tensor_tensor(h0[:, j, 1, 0:1], ci, ar, h0[:, j, 1, 0:1], ALU.mult, ALU.add),
               reads=[("carry", gt), ("hb", 0), ("ks", 0)], writes=[("hb", 0)])
            op("dve", lambda: nc.vector.scalar_tensor_tensor(h0[:, j, 1, 0:1], cr, ai, h0[:, j, 1, 0:1], ALU.mult, ALU.add),
               reads=[("carry", gt), ("hb", 0), ("ks", 0)], writes=[("hb", 0)])
    cur = 0
    for k in range(KS_LEVELS):
        s = 1 << k
        src, dst = hbuf[cur], hbuf[1 - cur]
        rk = [("hb", cur), ("ks", k), "ksn"]
        wk = [("hb", 1 - cur)]
        op("dve", lambda: nc.vector.tensor_copy(dst[:, :, :, 0:s], src[:, :, :, 0:s]), reads=rk, writes=wk)
        for j in range(4):
            pr = gt * 4 + j
            ar, ai, nai = ks[:, k, 0, pr:pr + 1], ks[:, k, 1, pr:pr + 1], ksn[:, k, pr:pr + 1]
            sr, si = src[:, j, 0, 0:NCH - s], src[:, j, 1, 0:NCH - s]
            op("dve", lambda: nc.vector.scalar_tensor_tensor(dst[:, j, 0, s:NCH], sr, ar, src[:, j, 0, s:NCH], ALU.mult, ALU.add), reads=rk, writes=wk)
            op("dve", lambda: nc.vector.scalar_tensor_tensor(dst[:, j, 0, s:NCH], si, nai, dst[:, j, 0, s:NCH], ALU.mult, ALU.add), reads=rk + wk, writes=wk)
            op("dve", lambda: nc.vector.scalar_tensor_tensor(dst[:, j, 1, s:NCH], si, ar, src[:, j, 1, s:NCH], ALU.mult, ALU.add), reads=rk, writes=wk)
            op("dve", lambda: nc.vector.scalar_tensor_tensor(dst[:, j, 1, s:NCH], sr, ai, dst[:, j, 1, s:NCH], ALU.mult, ALU.add), reads=rk + wk, writes=wk)
        cur = 1 - cur
    H = hbuf[cur]
    hk = [("hp",)]
    if first_tile:
        op("pool", lambda: nc.gpsimd.memset(hprev[:, :, :, 0:1], 0.0), writes=hk)
    else:
        op("dve", lambda: nc.vector.tensor_copy(hprev[:, :, :, 0], carry[:, :, :]), reads=[("carry", gt)], writes=hk)
    op("dve", lambda: nc.vector.tensor_copy(hprev[:, :, :, 1:NCH], H[:, :, :, 0:NCH - 1]), reads=[("hb", cur)], writes=hk)
    op("dve", lambda: nc.vector.tensor_copy(carry[:, :, :], H[:, :, :, NCH - 1]), reads=[("hb", cur)] + hk, writes=[("carry", gt)])
    started = set()
    for m in range(16):
        bk = m // 8
        yb = Yb[bk]
        cols = slice((m % 8) * NCH, (m % 8 + 1) * NCH)
        for mp in range(m + 1):
            f_ = bk not in started
            started.add(bk)
            op("pe", lambda: nc.tensor.matmul(yb[:, cols], Kbd[:, m - mp, :], uv[:, mp, :], start=f_, stop=False, skip_group_check=True),
               reads=[ukey, ("Kbd" + tag,)], writes=[("Yb", bk)])
    for m in range(16):
        bk = m // 8
        yb = Yb[bk]
        cols = slice((m % 8) * NCH, (m % 8 + 1) * NCH)
        for j in range(4):
            for ri in range(2):
                op("pe", lambda: nc.tensor.matmul(yb[j * 32:(j + 1) * 32, cols], WC[:, m, ri, j, :], hprev[:, j, ri, :],
                                                  start=False, stop=(ri == 1), skip_group_check=True,
                                                  tile_position=(0, 32 * j)),
                   reads=hk + [("WC" + tag,)], writes=[("Yb", bk)])
    yv = yout[:, :].rearrange("p (n m) -> p m n", m=16)
    for bk in range(2):
        op("act", lambda: nc.scalar.activation(out=yv[:, bk * 8:(bk + 1) * 8, :], in_=Yb[bk][:, 0:8 * NCH].rearrange("p (m n) -> p m n", m=8), func=AF.Copy),
           reads=[("Yb", bk)], writes=[ykey])


def build_s5_core_debug(gt=0, ntiles=2):
    nc = bass.Bass("TRN2", target_bir_lowering=False)
    names = ("l1_lam_re", "l1_lam_im", "l1_log_dt", "l1_b_re", "l1_b_im", "l1_c_re", "l1_c_im", "l1_d")
    prm = {k: nc.dram_tensor(k, L1_PARAMS[k], F32, kind="ExternalInput").ap() for k in names}
    c_ident = nc.dram_tensor("c_ident", [128, 128], BF16, kind="ExternalInput").ap()
    u_in = nc.dram_tensor("u_in", [128, ntiles * NTS], BF16, kind="ExternalInput").ap()
    y_out = nc.dram_tensor("y_out", [128, ntiles * NTS], F32, kind="ExternalOutput").ap()
    o_K = nc.dram_tensor("o_K", [16, 128, 16, 128], BF16, kind="Internal").ap()
    o_WS = nc.dram_tensor("o_WS", [16, 128, 16, 2, 128], BF16, kind="Internal").ap()
    o_WC = nc.dram_tensor("o_WC", [16, 128, 16, 2, 4, 32], BF16, kind="Internal").ap()
    with ExitStack() as st:
        P = Prog(nc, st)
        T = lambda name, shape, dt: st.enter_context(nc.sbuf_tensor(name, shape, dt))
        banks = [st.enter_context(nc.psum_tensor("bank%d" % i, [128, 512], F32)) for i in range(6)]
        banksb = [st.enter_context(nc.psum_tensor("bankb%d" % i, [128, 1024], BF16)) for i in range(2)]
        ident = T("ident", [128, 128], BF16)
        P.dma("sp", ident[:], c_ident[:, :], writes=["ident"])
        tabs = emit_s5_tables(nc, P, st, prm)
        ks, ksn = emit_ks_table(nc, P, st, tabs)
        with ExitStack() as st2:
            emit_s5_operands(nc, P, st2, prm, tabs, ident, [gt], o_K, o_WS, o_WC, banks, banksb)
            P.barrier()
        Kbd = T("mKbd", [128, 16, 128], BF16); WS = T("mWS", [128, 16, 2, 128], BF16); WC = T("mWC", [128, 16, 2, 4, 32], BF16)
        P.dma("sp", Kbd[:], o_K[gt, :, :, :], reads=[("oK", gt)], writes=[("Kbd",)])
        P.dma("sp", WS[:], o_WS[gt, :, :, :, :], reads=[("oWS", gt)], writes=[("WS",)])
        P.dma("sp", WC[:], o_WC[gt, :, :, :, :, :], reads=[("oWC", gt)], writes=[("WC",)])
        uT = T("muT", [128, NTS], BF16); yo = T("myo", [128, NTS], F32)
        carry = T("mcarry", [128, 4, 2], F32)
        hbuf = [T("mhb%d" % i, [128, 4, 2, NCH], F32) for i in range(2)]
        hprev = T("mhp", [128, 4, 2, NCH], BF16)
        for t in range(ntiles):
            P.dma("sp", uT[:], u_in[:, t * NTS:(t + 1) * NTS], writes=[("uT",)])
            emit_s5_core(nc, P, gt, uT, Kbd, WS, WC, ks, ksn, carry, hbuf, hprev, banks[0:2], banks[2:6], yo, first_tile=(t == 0))
            P.dma("sp", y_out[:, t * NTS:(t + 1) * NTS], yo[:], reads=[("yout",)])
        P.finish()
    return nc


NSUP_FULL = S // NTS


def emit_layer1(nc, P, x1_src, prm, c_ident, out, o_K, o_WS, o_WC, NSUP=NSUP_FULL, d_ys5=None, probe_no_restream=False):
    with ExitStack() as st:
        T = lambda name, shape, dt: st.enter_context(nc.sbuf_tensor(name, shape, dt))
        banks = [st.enter_context(nc.psum_tensor("l1bank%d" % i, [128, 512], F32)) for i in range(6)]
        banksb = [st.enter_context(nc.psum_tensor("l1bankb%d" % i, [128, 1024], BF16)) for i in range(2)]
        ident = T("l1ident", [128, 128], BF16)
        gbc = T("l1gbc", [128, D], F32)
        bglu = T("l1bglu", [128, 16], F32)
        carry = T("l1carry", [128, 16, 4, 2], F32)
        Wo = T("l1Wo", [128, 16, D], BF16)
        P.dma("sp", ident[:], c_ident[:, :], writes=["ident"])
        P.dma("sp", gbc[:], prm["l1_norm_g"].partition_broadcast(128), writes=["gbc"])
        P.dma("sp", bglu[:], prm["l1_b_glu"].rearrange("(ft p) -> p ft", p=128), writes=["bglu"], allow_slow_non_contiguous=True)
        w_out = prm["l1_w_out"].rearrange("(k p) f -> p k f", p=128)
        for kk in range(0, 16, 4):
            P.dma("pool", Wo[:, kk:kk + 4, :], w_out[:, kk:kk + 4, :], writes=[("Wo", kk // 4)])
        ksb = alloc_ks(nc, st)
        with ExitStack() as st0:
            tabs = emit_s5_tables(nc, P, st0, prm)
            ks, ksn = emit_ks_table(nc, P, st0, tabs, bufs=ksb)
            emit_s5_operands(nc, P, st0, prm, tabs, ident, list(range(16)), o_K, o_WS, o_WC, banks, banksb)
            P.barrier()
        nb = alloc_norm_bufs(nc, st, "L1A")
        hT = T("l1hT", [128, 8, NTS], BF16)
        uT = [T("l1uT%d" % i, [128, NTS], BF16) for i in range(2)]
        yo = [T("l1yo%d" % i, [128, NTS], F32) for i in range(2)]
        YG = T("l1YG", [128, 16, NTS], BF16)
        G1 = T("l1G1", [128, 16, 512], BF16)
        hbuf = [T("l1hb%d" % i, [128, 4, 2, NCH], F32) for i in range(2)]
        hprev = [T("l1hp%d" % i, [128, 4, 2, NCH], BF16) for i in range(2)]
        Kbd = [T("l1Kbd%d" % i, [128, 16, 128], BF16) for i in range(2)]
        WS = [T("l1WS%d" % i, [128, 16, 2, 128], BF16) for i in range(2)]
        WC = [T("l1WC%d" % i, [128, 16, 2, 4, 32], BF16) for i in range(2)]
        Wu = [T("l1Wu%d" % i, [128, 8, 128], BF16) for i in range(2)]
        Wg = [T("l1Wg%d" % i, [128, 16, 128], BF16) for i in range(2)]
        Wz = [T("l1Wz%d" % i, [128, 8, 128], BF16) for i in range(2)]
        sg = T("l1sg", [128, 512], BF16)
        sz = T("l1sz", [128, 512], BF16)
        tg = T("l1tg", [128, 512], BF16)
        xr = [T("l1xr%d" % i, [128, D], F32) for i in range(2)]
        w_in = prm["l1_w_in"].rearrange("(j p) f -> p j f", p=128)
        w_glu = prm["l1_w_glu"].rearrange("(k p) f -> p k f", p=128)
        op = P.op

        loaded = set()

        def load_gt(gt, b):
            if probe_no_restream:
                if ("gt", b) in loaded:
                    return
                loaded.add(("gt", b))
            tag = str(b)
            P.dma("sp", Kbd[b][:], o_K[gt, :, :, :], reads=[("oK", gt)], writes=[("Kbd" + tag,)])
            P.dma("sp", WS[b][:], o_WS[gt, :, :, :, :], reads=[("oWS", gt)], writes=[("WS" + tag,)])
            P.dma("sp", WC[b][:], o_WC[gt, :, :, :, :, :], reads=[("oWC", gt)], writes=[("WC" + tag,)])
            P.dma("pool", Wu[b][:], w_in[:, :, gt * 128:(gt + 1) * 128], writes=[("Wu", b)])

        def load_ft(ft, b):
            if probe_no_restream:
                if ("ft", b) in loaded:
                    return
                loaded.add(("ft", b))
            P.dma("pool", Wg[b][:], w_glu[:, :, ft * 128:(ft + 1) * 128], writes=[("Wg", b)])
            P.dma("pool", Wz[b][:], w_in[:, :, E + ft * 128:E + (ft + 1) * 128], writes=[("Wz", b)])

        for Tt in range(NSUP):
            r0 = Tt * NTS
            emit_norm_transpose(nc, P, st, x1_src[r0:r0 + NTS, :], hT, gbc, ident, banksb, "L1A", NT=NTS // 128, x_keep=nb)
            hTk = [("hT", q) for q in range(NTS // 512)]
            Yb, Sb = banks[0:2], banks[2:6]
            Ub = [banksb[hf][:, :].bitcast(F32) for hf in range(2)]

            def U_stage(gt, b):
                for hf in range(NTS // 512):
                    cs = slice(hf * 512, (hf + 1) * 512)
                    for j in range(8):
                        op("pe", lambda: nc.tensor.matmul(Ub[hf][:, 0:512], Wu[b][:, j, :], hT[:, j, cs], start=(j == 0), stop=(j == 7)),
                           reads=hTk + [("Wu", b)], writes=[("pbb", hf)])
                    op("act", lambda: nc.scalar.activation(out=uT[b][:, cs], in_=Ub[hf][:, 0:512], func=AF.Copy),
                       reads=[("pbb", hf)], writes=[("uT", b)])

            def ST_stage(gt, b):
                uv = uT[b][:, :].rearrange("p (n m) -> p m n", m=16)
                h0 = hbuf[0]
                for j in range(4):
                    rows = slice(j * 32, (j + 1) * 32)
                    first = True
                    for ri in range(2):
                        for m in range(16):
                            op("pe", lambda: nc.tensor.matmul(Sb[j][:, ri * NCH:(ri + 1) * NCH], WS[b][rows, m, ri, :], uv[rows, m, :],
                                                              start=first, stop=(m == 15), skip_group_check=True, tile_position=(32 * j, 0)),
                               reads=[("uT", b), ("WS" + str(b),)], writes=[("Sb", j)])
                            first = False
                    op("dve", lambda: nc.vector.tensor_copy(h0[:, j, :, :], Sb[j][:, 0:2 * NCH].rearrange("p (r n) -> p r n", r=2)),
                       reads=[("Sb", j)], writes=[("hb", 0)])

            def SCAN_stage(gt, b, first_tile):
                h0 = hbuf[0]
                cg = carry[:, gt]
                ck = ("carry", gt)
                for j in range(4):
                    pr = gt * 4 + j
                    ar, ai, nai = ks[:, 0, 0, pr:pr + 1], ks[:, 0, 1, pr:pr + 1], ksn[:, 0, pr:pr + 1]
                    if not first_tile:
                        cr, ci = cg[:, j, 0:1], cg[:, j, 1:2]
                        for (dst_, src_, sc_) in ((0, cr, ar), (0, ci, nai), (1, ci, ar), (1, cr, ai)):
                            op("dve", lambda: nc.vector.scalar_tensor_tensor(h0[:, j, dst_, 0:1], src_, sc_, h0[:, j, dst_, 0:1], ALU.mult, ALU.add),
                               reads=[ck, ("hb", 0), ("ks", 0), "ksn"], writes=[("hb", 0)])
                cur = 0
                for k in range(KS_LEVELS):
                    sft = 1 << k
                    src, dst = hbuf[cur], hbuf[1 - cur]
                    rk = [("hb", cur), ("ks", k), "ksn"]
                    wk = [("hb", 1 - cur)]
                    op("dve", lambda: nc.vector.tensor_copy(dst[:, :, :, 0:sft], src[:, :, :, 0:sft]), reads=rk, writes=wk)
                    for j in range(4):
                        pr = gt * 4 + j
                        ar, ai, nai = ks[:, k, 0, pr:pr + 1], ks[:, k, 1, pr:pr + 1], ksn[:, k, pr:pr + 1]
                        sr, si = src[:, j, 0, 0:NCH - sft], src[:, j, 1, 0:NCH - sft]
                        op("dve", lambda: nc.vector.scalar_tensor_tensor(dst[:, j, 0, sft:NCH], sr, ar, src[:, j, 0, sft:NCH], ALU.mult, ALU.add), reads=rk, writes=wk)
                        op("dve", lambda: nc.vector.scalar_tensor_tensor(dst[:, j, 0, sft:NCH], si, nai, dst[:, j, 0, sft:NCH], ALU.mult, ALU.add), reads=rk + wk, writes=wk)
                        op("dve", lambda: nc.vector.scalar_tensor_tensor(dst[:, j, 1, sft:NCH], si, ar, src[:, j, 1, sft:NCH], ALU.mult, ALU.add), reads=rk, writes=wk)
                        op("dve", lambda: nc.vector.scalar_tensor_tensor(dst[:, j, 1, sft:NCH], sr, ai, dst[:, j, 1, sft:NCH], ALU.mult, ALU.add), reads=rk + wk, writes=wk)
                    cur = 1 - cur
                H = hbuf[cur]
                hp = hprev[b]
                hk = [("hp", b)]
                if first_tile:
                    op("pool", lambda: nc.gpsimd.memset(hp[:, :, :, 0:1], 0.0), writes=hk)
                else:
                    op("dve", lambda: nc.vector.tensor_copy(hp[:, :, :, 0], cg[:, :, :]), reads=[ck], writes=hk)
                op("dve", lambda: nc.vector.tensor_copy(hp[:, :, :, 1:NCH], H[:, :, :, 0:NCH - 1]), reads=[("hb", cur)], writes=hk)
                op("dve", lambda: nc.vector.tensor_copy(cg[:, :, :], H[:, :, :, NCH - 1]), reads=[("hb", cur)] + hk, writes=[ck])

            def INTRA_stage(gt, b):
                uv = uT[b][:, :].rearrange("p (n m) -> p m n", m=16)
                started = set()
                for m in range(16):
                    bk = m // 8
                    cols = slice((m % 8) * NCH, (m % 8 + 1) * NCH)
                    for mp in range(m + 1):
                        f_ = bk not in started
                        started.add(bk)
                        op("pe", lambda: nc.tensor.matmul(Yb[bk][:, cols], Kbd[b][:, m - mp, :], uv[:, mp, :], start=f_, stop=False, skip_group_check=True),
                           reads=[("uT", b), ("Kbd" + str(b),)], writes=[("Yb", bk)])

            def INTER_stage(gt, b):
                for m in range(16):
                    bk = m // 8
                    cols = slice((m % 8) * NCH, (m % 8 + 1) * NCH)
                    for j in range(4):
                        for ri in range(2):
                            op("pe", lambda: nc.tensor.matmul(Yb[bk][j * 32:(j + 1) * 32, cols], WC[b][:, m, ri, j, :], hprev[b][:, j, ri, :],
                                                              start=False, stop=(ri == 1), skip_group_check=True, tile_position=(0, 32 * j)),
                               reads=[("hp", b), ("WC" + str(b),)], writes=[("Yb", bk)])
                yv = yo[b][:, :].rearrange("p (n m) -> p m n", m=16)
                for bk in range(2):
                    op("act", lambda: nc.scalar.activation(out=yv[:, bk * 8:(bk + 1) * 8, :], in_=Yb[bk][:, 0:8 * NCH].rearrange("p (m n) -> p m n", m=8), func=AF.Copy),
                       reads=[("Yb", bk)], writes=[("yo", b)])

            load_gt(0, 0)
            U_stage(0, 0)
            ST_stage(0, 0)
            for gt in range(16):
                b = gt % 2
                if gt + 1 < 16:
                    load_gt(gt + 1, 1 - b)
                INTRA_stage(gt, b)
                SCAN_stage(gt, b, first_tile=(Tt == 0))
                if gt + 1 < 16:
                    U_stage(gt + 1, 1 - b)
                    ST_stage(gt + 1, 1 - b)
                INTER_stage(gt, b)
                if d_ys5 is not None:
                    P.dma("sp", d_ys5[gt * 128:(gt + 1) * 128, r0:r0 + NTS], yo[b][:], reads=[("yo", b)])
                op("act", lambda: nc.scalar.activation(out=YG[:, gt, :], in_=yo[b][:], func=AF.Gelu_apprx_tanh),
                   reads=[("yo", b)], writes=[("YG", gt)])
            ygk = [("YG", k) for k in range(16)]
            g1k = [("G1", k) for k in range(16)]
            for hf in range(NTS // 512):
                cs = slice(hf * 512, (hf + 1) * 512)
                load_ft(0, 0)
                for ft in range(16):
                    b = ft % 2
                    if ft + 1 < 16:
                        load_ft(ft + 1, 1 - b)
                    pg, pz = banks[b], banks[2 + b]
                    for k in range(16):
                        op("pe", lambda: nc.tensor.matmul(pg[:, :], Wg[b][:, k, :], YG[:, k, cs], start=(k == 0), stop=(k == 15)),
                           reads=ygk + [("Wg", b)], writes=[("Yb", b)])
                    for j in range(8):
                        op("pe", lambda: nc.tensor.matmul(pz[:, :], Wz[b][:, j, :], hT[:, j, cs], start=(j == 0), stop=(j == 7)),
                           reads=hTk + [("Wz", b)], writes=[("Sb", b)])
                    op("act", lambda: nc.scalar.activation(out=sg[:], in_=pg[:, :], func=AF.Sigmoid, bias=bglu[:, ft:ft + 1], scale=1.0),
                       reads=[("Yb", b), "bglu"], writes=["sg"])
                    op("act", lambda: nc.scalar.activation(out=sz[:], in_=pz[:, :], func=AF.Silu), reads=[("Sb", b)], writes=["sz"])
                    op("dve", lambda: nc.vector.tensor_tensor(tg[:], YG[:, ft, cs], sg[:], ALU.mult), reads=[("YG", ft), "sg"], writes=["tg"])
                    op("dve", lambda: nc.vector.tensor_tensor(G1[:, ft, :], tg[:], sz[:], ALU.mult), reads=["tg", "sz"], writes=[("G1", ft)])
                for tt in range(4):
                    t = (r0 + hf * 512) // 128 + tt
                    b = t % 2
                    P.dma("sp", xr[b][:], x1_src[t * 128:(t + 1) * 128, :], writes=[("xr", b)])
                    for half in range(2):
                        bk = 2 + (t * 2 + half) % 4
                        for k in range(16):
                            op("pe", lambda: nc.tensor.matmul(banks[bk][:], G1[:, k, tt * 128:(tt + 1) * 128], Wo[:, k, half * 512:(half + 1) * 512],
                                                              start=(k == 0), stop=(k == 15)),
                               reads=g1k + [("Wo", k // 4)], writes=[("Sb", bk - 2)])
                        op("dve", lambda: nc.vector.tensor_tensor(xr[b][:, half * 512:(half + 1) * 512], banks[bk][:],
                                                                  xr[b][:, half * 512:(half + 1) * 512], ALU.add),
                           reads=[("Sb", bk - 2), ("xr", b)], writes=[("xr", b)])
                    P.dma("sp", out[t * 128:(t + 1) * 128, :], xr[b][:], reads=[("xr", b)], writes=[("x2", t)])


def build_l1(NSUP=NSUP_FULL, debug=False, probe_no_restream=False):
    nc = bass.Bass("TRN2", target_bir_lowering=False)
    x1 = nc.dram_tensor("x1", [S, D], F32, kind="ExternalInput").ap()
    prm = {k: nc.dram_tensor(k, shp, F32, kind="ExternalInput").ap() for k, shp in L1_PARAMS.items()}
    c_ident = nc.dram_tensor("c_ident", [128, 128], BF16, kind="ExternalInput").ap()
    out = nc.dram_tensor("out", [S, D], F32, kind="ExternalOutput").ap()
    d_ys5 = nc.dram_tensor("d_ys5", [E, S], F32, kind="ExternalOutput").ap() if debug else None
    o_K = nc.dram_tensor("o_K", [16, 128, 16, 128], BF16, kind="Internal").ap()
    o_WS = nc.dram_tensor("o_WS", [16, 128, 16, 2, 128], BF16, kind="Internal").ap()
    o_WC = nc.dram_tensor("o_WC", [16, 128, 16, 2, 4, 32], BF16, kind="Internal").ap()
    with ExitStack() as st:
        P = Prog(nc, st)
        emit_layer1(nc, P, x1, prm, c_ident, out, o_K, o_WS, o_WC, NSUP=NSUP, d_ys5=d_ys5, probe_no_restream=probe_no_restream)
        P.finish()
        build_l1.stats = {"ninst": P.ninst, "nsem": P._nsem}
    return nc


def run_layer1_spmd(inputs, x1, n_cores=8):
    nc = build_l1()
    ident = host_consts()["c_ident"]
    in_maps = []
    for b in range(n_cores):
        m = {"x1": np.ascontiguousarray(x1[b]), "c_ident": ident}
        for k in L1_PARAMS:
            m[k] = np.ascontiguousarray(inputs[k])
        in_maps.append(m)
    res = run_bass_kernel_spmd(nc, in_maps, core_ids=list(range(n_cores)))
    return np.stack([np.asarray(r["out"]) for r in res.results], axis=0)


def build_fused(NQ=8, NSUP=NSUP_FULL):
    nc = bass.Bass("TRN2", target_bir_lowering=False)
    x = nc.dram_tensor("x", [S, D], F32, kind="ExternalInput").ap()
    prm = {k: nc.dram_tensor(k, shp, F32, kind="ExternalInput").ap() for k, shp in {**L0_PARAMS, **L1_PARAMS}.items()}
    cst = {k: nc.dram_tensor(k, shp, dt, kind="ExternalInput").ap() for k, (shp, dt) in CONST_SHAPES.items()}
    out = nc.dram_tensor("out", [S, D], F32, kind="ExternalOutput").ap()
    gscr = nc.dram_tensor("gscr", [128, NH, S], BF16, kind="Internal").ap()
    x1scr = nc.dram_tensor("x1scr", [S, D], F32, kind="Internal").ap()
    o_K = nc.dram_tensor("o_K", [16, 128, 16, 128], BF16, kind="Internal").ap()
    o_WS = nc.dram_tensor("o_WS", [16, 128, 16, 2, 128], BF16, kind="Internal").ap()
    o_WC = nc.dram_tensor("o_WC", [16, 128, 16, 2, 4, 32], BF16, kind="Internal").ap()
    with ExitStack() as st:
        P = Prog(nc, st)
        emit_layer0(nc, P, x, prm, cst, x1scr, gscr, NQ=NQ)
        P.barrier()
        emit_layer1(nc, P, x1scr, prm, cst["c_ident"], out, o_K, o_WS, o_WC, NSUP=NSUP)
        P.finish()
        build_fused.stats = {"ninst": P.ninst, "nsem": P._nsem}
    return nc


def fused_in_maps(inputs, n_cores=8):
    cst = host_consts()
    maps = []
    for b in range(n_cores):
        m = {"x": np.ascontiguousarray(inputs["x"][b])}
        for k in list(L0_PARAMS) + list(L1_PARAMS):
            m[k] = np.ascontiguousarray(inputs[k])
        m.update(cst)
        maps.append(m)
    return maps


FUSED = True


def kernel(**inputs):
    inputs = {k: np.asarray(v) for k, v in inputs.items()}
    if FUSED:
        nc = build_fused()
        res = run_bass_kernel_spmd(nc, fused_in_maps(inputs), core_ids=list(range(8)))
        return np.stack([np.asarray(r["out"]) for r in res.results], axis=0).astype(np.float32, copy=False)
    x1 = run_layer0_spmd(inputs)
    x2 = run_layer1_spmd(inputs, x1)
    return x2.astype(np.float32, copy=False)
```
